# Optimizing a Trainium2 kernel written in Bass

```python
import jax, jax.numpy as jnp
from jax import lax
import numpy as np

D_MODEL = 1024
BATCH = 2
SEQ = 8192
DEPTH = 1

MEM_LEN = 256
EPS = 1e-6
POOL_WINDOWS = (2, 4, 8, 16)
POOL_GROUP = 128
POOL_WIDTH = POOL_GROUP * len(POOL_WINDOWS)
NSA_HEADS = 16
NSA_KV_GROUPS = 2
HEAD_DIM = 64
NSA_WIDTH = NSA_HEADS * HEAD_DIM
NSA_KV_WIDTH = NSA_KV_GROUPS * HEAD_DIM
CMP_BLOCK = 32
CMP_STRIDE = 16
CMP_HIDDEN = 256
SEL_BLOCK = 64
SEL_TOPK = 16
WINDOW = 512
Q_BLOCK = 128
ROPE_THETA = 500000.0
ROT_DIM = HEAD_DIM // 4
XA_HEADS = 4
XA_HEAD_DIM = 128
XA_WIDTH = XA_HEADS * XA_HEAD_DIM
N_BRANCHES = 3
D_FF = 2816
CONV_WIDTH = 3
IN_SIZES = (POOL_WIDTH, NSA_WIDTH, 6 * NSA_KV_WIDTH, 3 * NSA_HEADS, XA_WIDTH, N_BRANCHES * D_MODEL)
D_IN = POOL_WIDTH + NSA_WIDTH + 6 * NSA_KV_WIDTH + 3 * NSA_HEADS + XA_WIDTH + N_BRANCHES * D_MODEL

kernel_name = 'hybrid_pool_nsa_memory_block'


def rmsnorm(x, g):
    xf = x.astype(jnp.float32)
    y = xf * lax.rsqrt(jnp.mean(xf * xf, axis=-1, keepdims=True) + EPS)
    return (y * g.astype(jnp.float32)).astype(x.dtype)


def rotary(x, pos):
    half = ROT_DIM // 2
    inv_freq = ROPE_THETA ** (-jnp.arange(half, dtype=jnp.float32) * (2.0 / ROT_DIM))
    ang = pos.astype(jnp.float32)[..., None] * inv_freq
    cos = jnp.cos(ang)[:, :, None, :]
    sin = jnp.sin(ang)[:, :, None, :]
    xr = x[..., :ROT_DIM].astype(jnp.float32)
    x1, x2 = xr[..., :half], xr[..., half:]
    rot = jnp.concatenate([x1 * cos - x2 * sin, x2 * cos + x1 * sin], axis=-1).astype(x.dtype)
    return jnp.concatenate([rot, x[..., ROT_DIM:]], axis=-1)


def masked_softmax(s, mask):
    s = jnp.where(mask, s, -jnp.inf)
    m = jnp.max(s, axis=-1, keepdims=True)
    m = jnp.where(jnp.isfinite(m), m, 0.0)
    p = jnp.where(mask, jnp.exp(s - m), 0.0)
    return p / jnp.maximum(jnp.sum(p, axis=-1, keepdims=True), jnp.finfo(jnp.float32).tiny)


def multiscale_pool(u, pool_w, pool_scale):
    B, S, _ = u.shape
    uf = u.astype(jnp.float32)
    cs = jnp.concatenate([jnp.zeros((B, 1, POOL_WIDTH), jnp.float32), jnp.cumsum(uf, axis=1)], axis=1)
    t = jnp.arange(S)
    outs = []
    for gi, w in enumerate(POOL_WINDOWS):
        c = cs[:, :, gi * POOL_GROUP:(gi + 1) * POOL_GROUP]
        start = jnp.maximum(t + 1 - w, 0)
        cnt = (t + 1 - start).astype(jnp.float32)
        outs.append((c[:, 1:] - c[:, start]) / cnt[None, :, None] - uf[:, :, gi * POOL_GROUP:(gi + 1) * POOL_GROUP])
    p = jnp.stack(outs, axis=2).astype(u.dtype)
    y = jnp.einsum('bsgc,gcd->bsgd', p, pool_w).reshape(B, S, POOL_WIDTH)
    return y * pool_scale


def compress(kv, pe, w1, w2):
    B, S = kv.shape[:2]
    n_cmp = (S - CMP_BLOCK) // CMP_STRIDE + 1
    idx = jnp.arange(n_cmp)[:, None] * CMP_STRIDE + jnp.arange(CMP_BLOCK)[None, :]
    blk = kv[:, idx] + pe[None, None, :, None, :]
    blk = blk.transpose(0, 1, 3, 2, 4).reshape(B, n_cmp, NSA_KV_GROUPS, CMP_BLOCK * HEAD_DIM)
    return jax.nn.gelu(blk @ w1, approximate=True) @ w2


def nsa_attention(q, k_cmp, v_cmp, k_sel, v_sel, k_win, v_win, gates):
    B, S = q.shape[:2]
    G, HG = NSA_KV_GROUPS, NSA_HEADS // NSA_KV_GROUPS
    n_cmp = k_cmp.shape[1]
    n_sel = S // SEL_BLOCK
    top = min(SEL_TOPK, n_sel)
    nb = S // Q_BLOCK
    cmp_start = jnp.arange(n_cmp) * CMP_STRIDE
    cmp_end = cmp_start + CMP_BLOCK - 1
    sel_start = jnp.arange(n_sel) * SEL_BLOCK
    overlap = ((cmp_start[:, None] < sel_start[None, :] + SEL_BLOCK)
               & (cmp_start[:, None] + CMP_BLOCK > sel_start[None, :])).astype(jnp.float32)
    ks_blocks = k_sel.reshape(B, n_sel, SEL_BLOCK, G, HEAD_DIM).transpose(0, 3, 1, 2, 4)
    vs_blocks = v_sel.reshape(B, n_sel, SEL_BLOCK, G, HEAD_DIM).transpose(0, 3, 1, 2, 4)
    pad = ((0, 0), (WINDOW, 0), (0, 0), (0, 0))
    kw_pad = jnp.pad(k_win, pad)
    vw_pad = jnp.pad(v_win, pad)
    gather = jax.vmap(jax.vmap(lambda blocks, ix: blocks[ix]))
    blk_ids = jnp.arange(n_sel)[None, :]

    def block_fn(args):
        qb, gb, jb = args
        qb = qb.reshape(B, Q_BLOCK, G, HG, HEAD_DIM)
        tq = jb * Q_BLOCK + jnp.arange(Q_BLOCK)
        s_c = jnp.einsum('bqghd,bcgd->bghqc', qb, k_cmp).astype(jnp.float32)
        p_c = masked_softmax(s_c, cmp_end[None, :] <= tq[:, None])
        o_c = jnp.einsum('bghqc,bcgd->bqghd', p_c.astype(v_cmp.dtype), v_cmp)
        imp = jnp.einsum('bghqc,cj->bgqj', p_c, overlap)
        cur = (tq // SEL_BLOCK)[:, None]
        forced = (blk_ids == 0) | (blk_ids == cur) | (blk_ids == cur - 1)
        imp = jnp.where(blk_ids > cur, -jnp.inf, jnp.where(forced, jnp.inf, imp))
        _, idx = lax.top_k(imp, top)
        k_g = gather(ks_blocks, idx).reshape(B, G, Q_BLOCK, top * SEL_BLOCK, HEAD_DIM)
        v_g = gather(vs_blocks, idx).reshape(B, G, Q_BLOCK, top * SEL_BLOCK, HEAD_DIM)
        kpos = (idx[..., None] * SEL_BLOCK + jnp.arange(SEL_BLOCK)).reshape(B, G, Q_BLOCK, top * SEL_BLOCK)
        s_s = jnp.einsum('bqghd,bgqkd->bghqk', qb, k_g).astype(jnp.float32)
        p_s = masked_softmax(s_s, (kpos <= tq[None, None, :, None])[:, :, None])
        o_s = jnp.einsum('bghqk,bgqkd->bqghd', p_s.astype(v_g.dtype), v_g)
        k_w = lax.dynamic_slice_in_dim(kw_pad, jb * Q_BLOCK, WINDOW + Q_BLOCK, axis=1)
        v_w = lax.dynamic_slice_in_dim(vw_pad, jb * Q_BLOCK, WINDOW + Q_BLOCK, axis=1)
        kidx = jb * Q_BLOCK - WINDOW + jnp.arange(WINDOW + Q_BLOCK)
        diff = tq[:, None] - kidx[None, :]
        s_w = jnp.einsum('bqghd,bkgd->bghqk', qb, k_w).astype(jnp.float32)
        p_w = masked_softmax(s_w, (kidx[None, :] >= 0) & (diff >= 0) & (diff < WINDOW))
        o_w = jnp.einsum('bghqk,bkgd->bqghd', p_w.astype(v_w.dtype), v_w)
        gb = gb.reshape(B, Q_BLOCK, G, HG, 3)
        o = gb[..., 0:1] * o_c + gb[..., 1:2] * o_s + gb[..., 2:3] * o_w
        return o.reshape(B, Q_BLOCK, NSA_WIDTH)

    qs = q.reshape(B, nb, Q_BLOCK, NSA_HEADS, HEAD_DIM).swapaxes(0, 1)
    gs = gates.reshape(B, nb, Q_BLOCK, NSA_HEADS, 3).swapaxes(0, 1)
    out = lax.map(block_fn, (qs, gs, jnp.arange(nb)))
    return out.swapaxes(0, 1).reshape(B, S, NSA_WIDTH)


def memory_attention(q, mem_n, w_mem_kv):
    B, M, _ = mem_n.shape
    kv = (mem_n @ w_mem_kv).reshape(B, M, 2, XA_HEADS, XA_HEAD_DIM)
    k, v = kv[:, :, 0], kv[:, :, 1]
    s = jnp.einsum('bshd,bmhd->bhsm', q, k).astype(jnp.float32) * (XA_HEAD_DIM ** -0.5)
    p = jax.nn.softmax(s, axis=-1).astype(v.dtype)
    return jnp.einsum('bhsm,bmhd->bshd', p, v).reshape(B, q.shape[1], XA_WIDTH)


def hybrid_mixer(h, mem, positions, w_in, pool_w, pool_scale, cmp_pe, cmp_w1, cmp_w2, mem_norm_g,
                 w_mem_kv, w_br_pool, w_br_nsa, w_br_xa, w_out):
    B, S, _ = h.shape
    z = h @ w_in
    u_pool, q, kv, g_nsa, q_x, g_br = jnp.split(z, np.cumsum(IN_SIZES)[:-1].tolist(), axis=-1)
    y_pool = multiscale_pool(u_pool, pool_w, pool_scale)
    q = rotary(q.reshape(B, S, NSA_HEADS, HEAD_DIM), positions) * (HEAD_DIM ** -0.5)
    kv = kv.reshape(B, S, 6, NSA_KV_GROUPS, HEAD_DIM)
    k_c = rotary(kv[:, :, 0], positions)
    k_s = rotary(kv[:, :, 2], positions)
    k_w = rotary(kv[:, :, 4], positions)
    kc = compress(k_c, cmp_pe[0], cmp_w1[0], cmp_w2[0])
    vc = compress(kv[:, :, 1], cmp_pe[1], cmp_w1[1], cmp_w2[1])
    nsa_gates = jax.nn.sigmoid(g_nsa.reshape(B, S, NSA_HEADS, 3))
    y_nsa = nsa_attention(q, kc, vc, k_s, kv[:, :, 3], k_w, kv[:, :, 5], nsa_gates)
    y_mem = memory_attention(q_x.reshape(B, S, XA_HEADS, XA_HEAD_DIM), rmsnorm(mem, mem_norm_g), w_mem_kv)
    g = jax.nn.sigmoid(g_br.reshape(B, S, N_BRANCHES, D_MODEL))
    y = g[:, :, 0] * (y_pool @ w_br_pool) + g[:, :, 1] * (y_nsa @ w_br_nsa) + g[:, :, 2] * (y_mem @ w_br_xa)
    return y @ w_out


def conv_ffn(h, w_up, conv_w, conv_b, w_down):
    S = h.shape[1]
    u = h @ w_up
    up = jnp.pad(u, ((0, 0), (CONV_WIDTH - 1, 0), (0, 0)))
    c = conv_b
    for k in range(CONV_WIDTH):
        c = c + conv_w[k] * up[:, k:k + S]
    gate, val = jnp.split(c, 2, axis=-1)
    return (jax.nn.gelu(gate, approximate=True) * val) @ w_down


def setup_inputs(seed: int = 0) -> dict:
    key = jax.random.key(seed)
    ks = jax.random.split(key, 24)
    f32 = jnp.float32

    def nrm(k, shape, scale):
        return jax.random.normal(k, shape, f32) * scale

    def gain(k, shape):
        return 1.0 + 0.05 * jax.random.normal(k, shape, f32)

    L = DEPTH
    x = nrm(ks[0], (BATCH, SEQ, D_MODEL), 1.0)
    mem = nrm(ks[1], (BATCH, MEM_LEN, D_MODEL), 1.0)
    offset = jax.random.randint(ks[2], (BATCH, 1), 0, 4096, dtype=jnp.int32)
    positions = (offset + jnp.arange(SEQ, dtype=jnp.int32)[None, :]).astype(jnp.int32)
    return {
        'x': x,
        'mem': mem,
        'positions': positions,
        'pre_mix_g': gain(ks[3], (L, D_MODEL)),
        'w_in': nrm(ks[4], (L, D_MODEL, D_IN), D_MODEL ** -0.5),
        'pool_w': nrm(ks[5], (L, len(POOL_WINDOWS), POOL_GROUP, POOL_GROUP), POOL_GROUP ** -0.5),
        'pool_scale': 1.0 + 0.1 * jax.random.normal(ks[6], (L, POOL_WIDTH), f32),
        'cmp_pe': nrm(ks[7], (L, 2, CMP_BLOCK, HEAD_DIM), 0.1),
        'cmp_w1': nrm(ks[8], (L, 2, CMP_BLOCK * HEAD_DIM, CMP_HIDDEN), (CMP_BLOCK * HEAD_DIM) ** -0.5),
        'cmp_w2': nrm(ks[9], (L, 2, CMP_HIDDEN, HEAD_DIM), CMP_HIDDEN ** -0.5),
        'mem_norm_g': gain(ks[10], (L, D_MODEL)),
        'w_mem_kv': nrm(ks[11], (L, D_MODEL, 2 * XA_WIDTH), D_MODEL ** -0.5),
        'w_br_pool': nrm(ks[12], (L, POOL_WIDTH, D_MODEL), POOL_WIDTH ** -0.5),
        'w_br_nsa': nrm(ks[13], (L, NSA_WIDTH, D_MODEL), NSA_WIDTH ** -0.5),
        'w_br_xa': nrm(ks[14], (L, XA_WIDTH, D_MODEL), XA_WIDTH ** -0.5),
        'w_out': nrm(ks[15], (L, D_MODEL, D_MODEL), D_MODEL ** -0.5),
        'post_mix_g': gain(ks[16], (L, D_MODEL)),
        'pre_ffn_g': gain(ks[17], (L, D_MODEL)),
        'w_up': nrm(ks[18], (L, D_MODEL, 2 * D_FF), D_MODEL ** -0.5),
        'conv_w': nrm(ks[19], (L, CONV_WIDTH, 2 * D_FF), CONV_WIDTH ** -0.5),
        'conv_b': nrm(ks[20], (L, 2 * D_FF), 0.01),
        'w_down': nrm(ks[21], (L, D_FF, D_MODEL), D_FF ** -0.5),
        'post_ffn_g': gain(ks[22], (L, D_MODEL)),
    }


def reference(x, mem, positions, pre_mix_g, w_in, pool_w, pool_scale, cmp_pe, cmp_w1, cmp_w2,
              mem_norm_g, w_mem_kv, w_br_pool, w_br_nsa, w_br_xa, w_out, post_mix_g, pre_ffn_g,
              w_up, conv_w, conv_b, w_down, post_ffn_g):
    for l in range(DEPTH):
        y = hybrid_mixer(rmsnorm(x, pre_mix_g[l]), mem, positions, w_in[l], pool_w[l], pool_scale[l],
                         cmp_pe[l], cmp_w1[l], cmp_w2[l], mem_norm_g[l], w_mem_kv[l],
                         w_br_pool[l], w_br_nsa[l], w_br_xa[l], w_out[l])
        x = x + rmsnorm(y, post_mix_g[l])
        f = conv_ffn(rmsnorm(x, pre_ffn_g[l]), w_up[l], conv_w[l], conv_b[l], w_down[l])
        x = x + rmsnorm(f, post_ffn_g[l])
    return x
```

```python
import numpy as np
from contextlib import ExitStack
import concourse.bass as bass
import concourse.mybir as mybir
from concourse.bass_utils import run_bass_kernel_spmd

F32 = mybir.dt.float32
BF16 = mybir.dt.bfloat16
I32 = mybir.dt.int32
AF = mybir.ActivationFunctionType
ALU = mybir.AluOpType
AX = mybir.AxisListType


import sys as _sys


def _where():
    f = _sys._getframe(2)
    out = []
    while f is not None and len(out) < 4:
        if f.f_code.co_name != "<lambda>":
            out.append(f.f_lineno)
        f = f.f_back
    return out


class Buf:
    __slots__ = ("name", "writers", "readers", "excl", "last")

    def __init__(self, name, excl=False):
        self.name = name
        self.writers = []
        self.readers = []
        self.excl = excl
        self.last = {}


class Ins:
    __slots__ = ("eng", "fn", "deps", "idx", "flag", "tok", "dma", "pre", "where")

    def __init__(self, eng, fn, dma):
        self.eng = eng
        self.fn = fn
        self.deps = []
        self.flag = False
        self.tok = None
        self.dma = dma
        self.pre = None


class Sched:
    ENGS = ("pe", "dve", "act", "pool", "sp")
    EPOCH = 8000
    NDMA = 24

    def __init__(self, nc, stack):
        self.nc = nc
        self.stack = stack
        self.q = {e: [] for e in self.ENGS}
        self.nbuf = 0

    def buf(self, name=None, excl=False):
        self.nbuf += 1
        return Buf(name or f"b{self.nbuf}", excl)

    def op(self, eng, fn, r=(), w=(), dma=False):
        ins = Ins(eng, fn, dma)
        ins.where = _where()
        deps = []
        for b in r:
            deps.extend(b.writers)
        for b in w:
            deps.extend(b.readers)
        for b in list(r) + list(w):
            if b.excl:
                for en, li in b.last.items():
                    if en != eng:
                        deps.append(li)
                b.last[eng] = ins
        for b in w:
            if b.readers or (b in r):
                b.writers = [ins]
                b.readers = []
            else:
                b.writers.append(ins)
                if len(b.writers) > 48:
                    b.writers = b.writers[-48:]
        for b in r:
            if b not in w:
                b.readers.append(ins)
                if len(b.readers) > 48:
                    b.readers = b.readers[-48:]
        seen = set()
        for d in deps:
            if d is ins or id(d) in seen:
                continue
            if d.eng == "pe" and eng == "pe" and not d.dma:
                continue
            seen.add(id(d))
            ins.deps.append(d)
            d.flag = True
        self.q[eng].append(ins)
        return ins

    def emit(self, final_waits=()):
        nc = self.nc
        stack = self.stack
        sems = {}
        for e in self.ENGS:
            n = 0
            for ins in self.q[e]:
                if ins.dma:
                    continue
                if ins.flag:
                    n += 1
                    ins.idx = n
            nep = (n + self.EPOCH - 1) // self.EPOCH
            sems[e] = [stack.enter_context(nc.semaphore(f"s_{e}_{k}")) for k in range(max(nep, 1))]
        dsems = [stack.enter_context(nc.semaphore(f"s_dma_{k}")) for k in range(self.NDMA)]
        duse = [0] * self.NDMA
        dma_engs = [e for e in self.ENGS if any(i.dma for i in self.q[e])]
        share = {}
        if dma_engs:
            per = self.NDMA // len(dma_engs)
            for k, e in enumerate(dma_engs):
                share[e] = list(range(k * per, (k + 1) * per))
        for e in dma_engs:
            j = 0
            for ins in self.q[e]:
                if not ins.dma:
                    continue
                s = share[e][j % len(share[e])]
                j += 1
                prev = duse[s]
                duse[s] += 1
                ins.tok = (dsems[s], 16 * duse[s])
                ins.pre = (dsems[s], 16 * prev) if prev > 0 else None
        for e in self.ENGS:
            for ins in self.q[e]:
                if ins.dma or not ins.flag:
                    continue
                k = (ins.idx - 1) // self.EPOCH
                ins.tok = (sems[e][k], (ins.idx - 1) % self.EPOCH + 1)
        engobj = {"pe": "tensor", "dve": "vector", "act": "scalar", "pool": "gpsimd", "sp": "sync"}
        stats = {}
        with nc.Block() as block:
            for e in self.ENGS:
                lst = self.q[e]

                def body(eng, lst=lst, e=e):
                    waited = {}
                    nw = 0

                    def wait(tok):
                        nonlocal nw
                        sem, val = tok
                        key = id(sem)
                        if waited.get(key, 0) >= val:
                            return
                        waited[key] = val
                        eng.wait_ge(sem, val)
                        nw += 1

                    for ins in lst:
                        if ins.pre is not None:
                            wait(ins.pre)
                        for d in ins.deps:
                            wait(d.tok)
                        try:
                            bi = ins.fn(eng)
                        except BaseException:
                            print("FAILED op recorded at lines", ins.where, flush=True)
                            raise
                        if ins.dma:
                            bi.then_inc(ins.tok[0], 16)
                        elif ins.flag:
                            bi.then_inc(ins.tok[0], 1)
                    if e == "sp":
                        for fw in final_waits:
                            wait(fw.tok)
                    stats[e] = (len(lst), nw)

                getattr(block, engobj[e])(body)
        return stats


import os as _os
DUMPX = int(_os.environ.get('DUMPX', '0'))
NOBAR = int(_os.environ.get('NOBAR', '0'))
NEXT = 21
NALL = 64
BIG = 100.0
TWO_PI = float(2 * np.pi)


def build(dbg=None, ntile_b1=NEXT, na=NALL, skipcmp=False):
    nc = bass.Bass("TRN2", target_bir_lowering=False)

    def din(name, shape, dt=F32):
        return nc.dram_tensor(name, list(shape), dt, kind="ExternalInput").ap()

    x_all = din("x_all", [8192, 1024])
    x_ext = din("x_ext", [NEXT * 128, 1024])
    pos_all = din("pos_all", [128, NALL], I32)
    pos_ext = din("pos_ext", [128, NEXT], I32)
    tq_ext = din("tq_ext", [128, NEXT])
    qrow_d = din("qrow", [1, 128])
    tqst_d = din("tqstart", [1, NEXT])
    wvalid_d = din("wvalid", [1, NEXT])
    invcnt_d = din("invcnt", [128, NEXT * 4])
    hflag_d = din("hflag", [1, 1])
    mem_d = din("mem", [256, 1024])
    ident_d = din("ident", [128, 128])
    G_d = din("Gm", [128, 8192])
    invf_d = din("invf", [1, 8])
    bsrow_d = din("bsrow", [1, 128])
    cendrow_d = din("cendrow", [1, 512])
    kidx_d = din("kidx", [128, 64])
    cendcol_d = din("cendcol", [128, 4])
    Mw_d = din("Mw", [128, 256])
    e0_d = din("e0big", [1, 128])
    Ab_d = din("Aband", [128, 1024])
    pre_mix_g = din("pre_mix_g", [1, 1024])
    w_in = din("w_in", [1, 1024, 5936])
    pool_w = din("pool_w", [1, 4, 128, 128])
    pool_scale = din("pool_scale", [1, 512])
    cmp_pe = din("cmp_pe", [1, 2, 32, 64])
    cmp_w1 = din("cmp_w1", [1, 2, 2048, 256])
    cmp_w2 = din("cmp_w2", [1, 2, 256, 64])
    mem_norm_g = din("mem_norm_g", [1, 1024])
    w_mem_kv = din("w_mem_kv", [1, 1024, 1024])
    w_br_pool = din("w_br_pool", [1, 512, 1024])
    w_br_nsa = din("w_br_nsa", [1, 1024, 1024])
    w_br_xa = din("w_br_xa", [1, 512, 1024])
    w_out = din("w_out", [1, 1024, 1024])
    post_mix_g = din("post_mix_g", [1, 1024])
    pre_ffn_g = din("pre_ffn_g", [1, 1024])
    w_up = din("w_up", [1, 1024, 5632])
    conv_w = din("conv_w", [1, 3, 5632])
    conv_b = din("conv_b", [1, 5632])
    w_down = din("w_down", [1, 2816, 1024])
    post_ffn_g = din("post_ffn_g", [1, 1024])
    out_d = nc.dram_tensor("out", [16 * 128, 1024], F32, kind="ExternalOutput").ap()
    br_scr = nc.dram_tensor("br_scr", [17, 128, 24 * 128], BF16, kind="Internal").ap()
    x1_scr = nc.dram_tensor("x1_scr", [17, 128, 1024], F32, kind="Internal").ap()
    dbg_out = {}

    def dbg_t(name, shape, dt=F32):
        return nc.dram_tensor("dbg_" + name, list(shape), dt, kind="ExternalOutput").ap()

    with ExitStack() as st:
        S = Sched(nc, st)
        ARN = 95000
        arena = st.enter_context(nc.sbuf_tensor("arena", [128, ARN], BF16))
        PS = [st.enter_context(nc.psum_tensor(f"ps{i}", [128, 512], F32)) for i in range(8)]
        PB = [S.buf(f"ps{i}", excl=True) for i in range(8)]
        top = [0]

        def abf(n):
            a = arena[:, top[0]:top[0] + n]
            top[0] += (n + 31) // 32 * 32
            assert top[0] <= ARN, top[0]
            return a

        def af32(n):
            a = arena[:, top[0]:top[0] + 2 * n].bitcast(F32)
            top[0] += (n + 15) // 16 * 32
            assert top[0] <= ARN, top[0]
            return a

        def ai32(n):
            a = arena[:, top[0]:top[0] + 2 * n].bitcast(I32)
            top[0] += (n + 15) // 16 * 32
            assert top[0] <= ARN, top[0]
            return a

        def mm(out, lhsT, rhs, r, w, start=True, stop=True):
            return S.op("pe", lambda e: e.matmul(out, lhsT=lhsT, rhs=rhs, start=start, stop=stop,
                                                 skip_group_check=True), r, w)

        last_func = [None]

        def act(out, in_, func, r, w, bias=None, scale=None, accum=None):
            if func != last_func[0]:
                last_func[0] = func
                S.op("act", lambda e: e.activation(out=bsc[1][:, 0:1], in_=bsc[3][:, 0:1], func=func), [], [])
            kw = {}
            if bias is not None:
                kw["bias"] = bias
            if scale is not None:
                kw["scale"] = scale
            if accum is not None:
                kw["accum_out"] = accum
            return S.op("act", lambda e: e.activation(out=out, in_=in_, func=func, **kw), r, w)

        def ts(eng, out, in0, s1, s2, op0, op1, r, w):
            if s2 is None:
                return S.op(eng, lambda e: e.tensor_scalar(out=out, in0=in0, scalar1=s1, scalar2=None, op0=op0), r, w)
            return S.op(eng, lambda e: e.tensor_scalar(out=out, in0=in0, scalar1=s1, scalar2=s2, op0=op0, op1=op1), r, w)

        def tt(eng, out, in0, in1, op, r, w):
            return S.op(eng, lambda e: e.tensor_tensor(out=out, in0=in0, in1=in1, op=op), r, w)

        def stt(out, in0, scalar, in1, op0, op1, r, w, accum=None):
            if accum is None:
                return S.op("dve", lambda e: e.scalar_tensor_tensor(out=out, in0=in0, scalar=scalar, in1=in1, op0=op0, op1=op1), r, w)
            return S.op("dve", lambda e: e.scalar_tensor_tensor(out=out, in0=in0, scalar=scalar, in1=in1, op0=op0, op1=op1,
                                                                accum_out=accum), r, w)

        def cp(eng, out, in_, r, w, scale=None):
            if eng == "act":
                return act(out, in_, AF.Copy, r, w, scale=scale)
            if scale is not None:
                return ts(eng, out, in_, scale, None, ALU.mult, None, r, w)
            return S.op(eng, lambda e: e.tensor_copy(out=out, in_=in_), r, w)

        def memset(eng, ap, val, w):
            return S.op(eng, lambda e: e.memset(ap, val), (), w)

        dmas = []

        def dma(out, in_, r, w, slow=False):
            if slow:
                i = S.op("sp", lambda e: e.dma_start(out=out, in_=in_, allow_slow_non_contiguous=True), r, w, dma=True)
            else:
                i = S.op("sp", lambda e: e.dma_start(out=out, in_=in_), r, w, dma=True)
            dmas.append(i)
            return i

        bsc = [af32(16) for _ in range(4)]
        bbuf = {e: S.buf("bar_" + e) for e in S.ENGS}
        bar_scr = nc.dram_tensor("bar_scr", [128, 16], F32, kind="Internal").ap()

        def barrier():
            allb = list(bbuf.values())
            mm(PS[7][:, 0:1], ident_b[:, 0:128], ident_b[:, 0:1], [PB[7]], [bbuf["pe"], PB[7]])
            memset("dve", bsc[0], 0.0, [bbuf["dve"]])
            act(bsc[1], bsc[3], AF.Copy, [], [bbuf["act"]])
            memset("pool", bsc[2], 0.0, [bbuf["pool"]])
            i = S.op("sp", lambda e: e.dma_start(out=bar_scr, in_=bsc[3]), [], [bbuf["sp"]], dma=True)
            for d in dmas:
                if d not in i.deps:
                    i.deps.append(d)
            dmas.clear()
            mm(PS[7][:, 0:1], ident_b[:, 0:128], ident_b[:, 0:1], allb + [PB[7]], [PB[7]])
            S.op("dve", lambda e: e.memset(bsc[0], 0.0), allb, [])
            act(bsc[1], bsc[3], AF.Copy, allb, [])
            S.op("pool", lambda e: e.memset(bsc[2], 0.0), allb, [])
            S.op("sp", lambda e: e.dma_start(out=bar_scr, in_=bsc[3]), allb, [], dma=True)

        ident_b = abf(128)
        B_ident = S.buf("ident")
        stage = [af32(1024), af32(1024)]
        B_stage = [S.buf("stg0"), S.buf("stg1")]
        stg_i = [0]
        cast_i = [0]
        memset("dve", bsc[3], 0.0, [S.buf()])

        def loadw(dst_fn, src, ncols, scale=None, r_extra=(), w=None, perm_q=False):
            for c0 in range(0, ncols, 1024):
                n = min(1024, ncols - c0)
                k = stg_i[0] % 2
                stg_i[0] += 1
                np_ = src.shape[0]
                sl = stage[k][0:np_, 0:n]
                dma(sl, src[:, c0:c0 + n], [], [B_stage[k]])
                eng = ("pool", "dve")[cast_i[0] % 2]
                cast_i[0] += 1
                if perm_q:
                    for g in range(2):
                        o = dst_fn(c0, n).rearrange("p (c g d) -> p c g d", c=8, g=2)[:, :, g, :]
                        i_ = stage[k][0:np_, g * 512:(g + 1) * 512].rearrange("p (c d) -> p c d", c=8)
                        if scale is not None:
                            ts(eng, o, i_, scale, None, ALU.mult, None, [B_stage[k]] + list(r_extra), w)
                        else:
                            cp(eng, o, i_, [B_stage[k]] + list(r_extra), w)
                else:
                    if scale is not None:
                        ts(eng, dst_fn(c0, n), sl, scale, None, ALU.mult, None, [B_stage[k]] + list(r_extra), w)
                    else:
                        cp(eng, dst_fn(c0, n), sl, [B_stage[k]] + list(r_extra), w)

        def tr(out_ps, in_sb, r, w, start=True):
            return mm(out_ps, in_sb, ident_b[0:in_sb.shape[0], 0:in_sb.shape[0]], list(r) + [B_ident], w)

        dma(stage[0][:, 0:128], ident_d, [], [B_stage[0]])
        cp("dve", ident_b, stage[0][:, 0:128], [B_stage[0]], [B_ident])

        gpre = af32(8)
        gmem = af32(8)
        gffn = af32(8)
        B_small = S.buf("small")
        dma(gpre, pre_mix_g[0].rearrange("(k p) -> p k", p=128), [], [B_small], slow=True)
        dma(gmem, mem_norm_g[0].rearrange("(k p) -> p k", p=128), [], [B_small], slow=True)
        dma(gffn, pre_ffn_g[0].rearrange("(k p) -> p k", p=128), [], [B_small], slow=True)
        tqe = af32(NEXT)
        dma(tqe, tq_ext, [], [B_small])
        wval = af32(NEXT)
        dma(wval, wvalid_d.partition_broadcast(128), [], [B_small])
        hflag = af32(1)
        dma(hflag, hflag_d.partition_broadcast(128), [], [B_small])
        invcnt = af32(NEXT * 4)
        dma(invcnt, invcnt_d, [], [B_small])
        ss2 = af32(4)
        B_ss2 = S.buf("ss2")
        persist_top = top[0]

        def rms_scale(xt, xn, Bx, Bxn, ss, junk, Bss):
            act(junk, xt, AF.Square, [Bx], [Bss], accum=ss)
            ts("dve", ss, ss, 1.0 / 1024, 1e-6, ALU.mult, ALU.add, [Bss], [Bss])
            act(ss, ss, AF.Sqrt, [Bss], [Bss])
            S.op("dve", lambda e: e.reciprocal(out=ss, in_=ss), [Bss], [Bss])
            ts("dve", xn, xt, ss, None, ALU.mult, None, [Bx, Bss], [Bxn])

        def sincos(ang, n, osin, ocos, tmp, Bt, Bo):
            t, kf, g, ki = tmp
            ts("dve", ang, ang, 1.0 / TWO_PI, None, ALU.mult, None, [Bt], [Bt])
            for dst, off in ((osin, 0.0), (ocos, 0.25)):
                ts("dve", t, ang, off, None, ALU.add, None, [Bt], [Bt])
                cp("dve", ki, t, [Bt], [Bt])
                cp("dve", kf, ki, [Bt], [Bt])
                tt("dve", t, t, kf, ALU.subtract, [Bt], [Bt])
                ts("dve", g, t, 0.5, None, ALU.is_gt, None, [Bt], [Bt])
                tt("dve", t, t, g, ALU.subtract, [Bt], [Bt])
                ts("dve", g, t, -0.5, None, ALU.is_lt, None, [Bt], [Bt])
                tt("dve", t, t, g, ALU.add, [Bt], [Bt])
                act(dst, t, AF.Sin, [Bt], [Bo], scale=TWO_PI)

        def rotary(src4, dst4, cs, sn, tmp, rB, wB, Bt):
            a, b = src4.shape[1], src4.shape[2]
            n = a * b * 8
            x1 = src4[:, :, :, 0:8]
            x2 = src4[:, :, :, 8:16]
            csb = cs.unsqueeze(1).unsqueeze(1).to_broadcast([128, a, b, 8])
            snb = sn.unsqueeze(1).unsqueeze(1).to_broadcast([128, a, b, 8])
            t1 = tmp[:, 0:n].rearrange("p (a b d) -> p a b d", a=a, b=b)
            t2 = tmp[:, n:2 * n].rearrange("p (a b d) -> p a b d", a=a, b=b)
            tt("dve", t1, x1, csb, ALU.mult, rB, [Bt])
            tt("dve", t2, x2, snb, ALU.mult, rB, [Bt])
            tt("dve", dst4[:, :, :, 0:8], t1, t2, ALU.subtract, [Bt], wB)
            tt("dve", t1, x2, csb, ALU.mult, rB, [Bt])
            tt("dve", t2, x1, snb, ALU.mult, rB, [Bt])
            tt("dve", dst4[:, :, :, 8:16], t1, t2, ALU.add, [Bt], wB)

        KTs = abf(8192)
        Vs = abf(64 * 2 * 65).rearrange("p (t g d) -> p t g d", t=64, g=2)
        KTw = abf(NEXT * 128)
        Vw = abf(NEXT * 2 * 65).rearrange("p (t g d) -> p t g d", t=NEXT, g=2)
        KcT = abf(512)
        Vc = abf(4 * 2 * 65).rearrange("p (t g d) -> p t g d", t=4, g=2)
        Gb = abf(8192)
        B_KTs, B_Vs, B_KTw, B_Vw, B_KcT, B_Vc, B_G = [S.buf(n) for n in "KTs Vs KTw Vw KcT Vc G".split()]
        cosE = af32(NEXT * 8).rearrange("p (t f) -> p t f", f=8)
        sinE = af32(NEXT * 8).rearrange("p (t f) -> p t f", f=8)
        cos8 = af32(NEXT * 8).rearrange("p (t f) -> p t f", f=8)
        sin8 = af32(NEXT * 8).rearrange("p (t f) -> p t f", f=8)
        B_tabE = S.buf("tabE")
        kv_top = top[0]

        memset("pool", Vs, 1.0, [B_Vs])
        memset("pool", Vw, 1.0, [B_Vw])
        memset("pool", Vc, 0.0, [B_Vc])
        memset("pool", Vc[:, :, :, 64:65], 1.0, [B_Vc])
        loadw(lambda c0, n: Gb[:, c0:c0 + n], G_d, 8192, w=[B_G])

        cosA = af32(NALL * 8).rearrange("p (t f) -> p t f", f=8)
        sinA = af32(NALL * 8).rearrange("p (t f) -> p t f", f=8)
        B_tabA = S.buf("tabA")
        KcRaw = abf(8192)
        VcRaw = abf(8192)
        B_KcRaw, B_VcRaw = S.buf("KcRaw"), S.buf("VcRaw")
        wkvA = abf(8 * 512).rearrange("p (k c) -> p k c", k=8)
        B_wkvA = S.buf("wkvA")
        for k in range(8):
            loadw(lambda c0, n, k=k: wkvA[:, k, c0:c0 + n], w_in[0][k * 128:(k + 1) * 128, 1536:2048], 512,
                  scale=gpre[:, k:k + 1], r_extra=[B_small], w=[B_wkvA])
        posi = ai32(512)
        posf = af32(512)
        invf = af32(8)
        ang = af32(512)
        tmp3 = (af32(512), af32(512), af32(512), ai32(512))
        B_t = S.buf("tabtmp")
        dma(invf, invf_d.partition_broadcast(128), [], [B_t])
        dma(posi[:, 0:NALL], pos_all, [], [B_t])
        cp("dve", posf[:, 0:NALL], posi[:, 0:NALL], [B_t], [B_t])
        tt("dve", ang.rearrange("p (t f) -> p t f", f=8), posf[:, 0:NALL].unsqueeze(2).to_broadcast([128, NALL, 8]),
           invf.unsqueeze(1).to_broadcast([128, NALL, 8]), ALU.mult, [B_t], [B_t])
        sincos(ang, 512, sinA.rearrange("p t f -> p (t f)"), cosA.rearrange("p t f -> p (t f)"),
               tmp3, B_t, B_tabA)
        dma(posi[:, 0:NEXT], pos_ext, [B_t], [B_t])
        cp("dve", posf[:, 0:NEXT], posi[:, 0:NEXT], [B_t], [B_t])
        ne = NEXT * 8
        tt("dve", ang[:, 0:ne].rearrange("p (t f) -> p t f", f=8), posf[:, 0:NEXT].unsqueeze(2).to_broadcast([128, NEXT, 8]),
           invf.unsqueeze(1).to_broadcast([128, NEXT, 8]), ALU.mult, [B_t], [B_t])
        sincos(ang[:, 0:ne], ne, sinE.rearrange("p t f -> p (t f)"), cosE.rearrange("p t f -> p (t f)"),
               tuple(a[:, 0:ne] for a in tmp3), B_t, B_tabE)
        ts("dve", cos8.rearrange("p t f -> p (t f)"), cosE.rearrange("p t f -> p (t f)"), 0.125, None, ALU.mult, None, [B_tabE], [B_tabE])
        ts("dve", sin8.rearrange("p t f -> p (t f)"), sinE.rearrange("p t f -> p (t f)"), 0.125, None, ALU.mult, None, [B_tabE], [B_tabE])

        if dbg == "0":
            f1 = dma(dbg_t("cosA", [128, NALL * 8]), cosA.rearrange("p t f -> p (t f)"), [B_tabA], [])
            f2 = dma(dbg_t("sinA", [128, NALL * 8]), sinA.rearrange("p t f -> p (t f)"), [B_tabA], [])
            f3 = dma(dbg_t("cos8", [128, NEXT * 8]), cos8.rearrange("p t f -> p (t f)"), [B_tabE], [])
            f4 = dma(dbg_t("Gb", [128, 8192], BF16), Gb, [B_G], [])
            print(S.emit(final_waits=[f1, f2, f3, f4]))
            return nc
        xt2 = [af32(1024), af32(1024)]
        B_xt = [S.buf("xt0"), S.buf("xt1")]
        xn = abf(1024)
        B_xn = S.buf("xn")
        junk = abf(1024)
        ssA = af32(1)
        B_ss = S.buf("ss")
        hT = abf(1024).rearrange("p (k t) -> p k t", k=8)
        B_hT = S.buf("hT")
        kb = abf(512)
        B_kb = S.buf("kb")
        rtmp = af32(2 * 16 * 8)
        B_rt = S.buf("rtmp")

        def norm_T(xt, Bx, pa, pb, hTd, BhT):
            rms_scale(xt, xn, Bx, B_xn, ssA, junk, B_ss)
            for k in range(8):
                bank = pa if k < 4 else pb
                tr(PS[bank][:, (k % 4) * 128:(k % 4 + 1) * 128], xn[:, k * 128:(k + 1) * 128], [B_xn], [PB[bank]])
            cp("act", hTd[:, 0:4, :], PS[pa][:, :].rearrange("p (k t) -> p k t", k=4), [PB[pa]], [BhT])
            cp("dve", hTd[:, 4:8, :], PS[pb][:, :].rearrange("p (k t) -> p k t", k=4), [PB[pb]], [BhT])

        dma(xt2[0], x_all[0:128, :], [], [B_xt[0]])
        for T in range(na):
            s = T % 2
            if T + 1 < NALL:
                dma(xt2[1 - s], x_all[(T + 1) * 128:(T + 2) * 128, :], [], [B_xt[1 - s]])
            norm_T(xt2[s], B_xt[s], 0, 1, hT, B_hT)
            for k in range(8):
                mm(PS[2][:, 0:512], hT[:, k, :], wkvA[:, k, :], [B_hT, B_wkvA], [PB[2]], start=(k == 0), stop=(k == 7))
            cp("act", kb, PS[2][:, 0:512], [PB[2]], [B_kb])
            v5 = PS[2][:, 0:512].rearrange("p (j2 jj g d) -> p j2 jj g d", j2=2, jj=2, g=2)
            k5 = kb.rearrange("p (j2 jj g d) -> p j2 jj g d", j2=2, jj=2, g=2)
            rotary(v5[:, :, 0, :, :], k5[:, :, 0, :, :], cosA[:, T, :], sinA[:, T, :], rtmp, [PB[2], B_tabA], [B_kb], B_rt)
            cp("pool", Vs[:, T, :, 0:64], kb[:, 384:512].rearrange("p (g d) -> p g d", g=2), [B_kb], [B_Vs])
            tr(PS[3][:, 0:128], kb[:, 0:128], [B_kb], [PB[3]])
            tr(PS[3][:, 128:256], kb[:, 128:256], [B_kb], [PB[3]])
            tr(PS[3][:, 256:384], kb[:, 256:384], [B_kb], [PB[3]])
            cp("act", KcRaw[:, T * 128:(T + 1) * 128], PS[3][:, 0:128], [PB[3]], [B_KcRaw])
            cp("dve", VcRaw[:, T * 128:(T + 1) * 128], PS[3][:, 128:256], [PB[3]], [B_VcRaw])
            cp("act", KTs[:, T * 128:(T + 1) * 128], PS[3][:, 256:384], [PB[3]], [B_KTs])

        w1z = [abf(32 * 256).rearrange("p (l m) -> p l m", l=32) for _ in range(2)]
        B_w1 = S.buf("w1b")
        memset("pool", w1z[0][64:128, :, :], 0.0, [B_w1])
        memset("pool", w1z[1][0:64, :, :], 0.0, [B_w1])
        w2b = abf(2 * 128).rearrange("p (h d) -> p h d", h=2)
        B_w2 = S.buf("w2b")
        peT = abf(32)
        pef = af32(32)
        B_pe = S.buf("pe")
        hid2 = [abf(2 * 512).rearrange("p (h c) -> p h c", h=2) for _ in range(2)]
        B_hid2 = [S.buf("hid0"), S.buf("hid1")]
        cbias = af32(2)
        B_cb = S.buf("cbias")
        for kvi, raw, Braw in (() if skipcmp else ((0, KcRaw, B_KcRaw), (1, VcRaw, B_VcRaw))):
            w1v = cmp_w1[0][kvi].rearrange("(l d) m -> d l m", d=64)
            for half in range(2):
                for l0 in range(0, 32, 4):
                    k = stg_i[0] % 2
                    stg_i[0] += 1
                    sl = stage[k][64 * half:64 * half + 64, 0:1024]
                    dma(sl.rearrange("p (l m) -> p l m", l=4), w1v[:, l0:l0 + 4, :], [], [B_stage[k]])
                    cp(("pool", "dve")[(l0 // 4) % 2], w1z[half][64 * half:64 * half + 64, l0:l0 + 4, :],
                       sl.rearrange("p (l m) -> p l m", l=4), [B_stage[k]], [B_w1])
            k = stg_i[0] % 2
            stg_i[0] += 1
            dma(stage[k][:, 0:128].rearrange("p (h d) -> p h d", h=2), cmp_w2[0][kvi].rearrange("(h p) d -> p h d", p=128),
                [], [B_stage[k]])
            cp("dve", w2b[:, :, 0:64], stage[k][:, 0:128].rearrange("p (h d) -> p h d", h=2), [B_stage[k]], [B_w2])
            cp("dve", w2b[:, :, 64:128], stage[k][:, 0:128].rearrange("p (h d) -> p h d", h=2), [B_stage[k]], [B_w2])
            dma(pef[0:64, :], cmp_pe[0][kvi].rearrange("l d -> d l"), [], [B_pe], slow=True)
            memset("dve", peT[64:128, :], 0.0, [B_pe])
            cp("dve", peT[0:64, :], pef[0:64, :], [B_pe], [B_pe])
            for half in range(2):
                for l in range(32):
                    mm(PS[4][:, half:half + 1], w1z[0][:, l, half * 128:(half + 1) * 128], peT[:, l:l + 1],
                       [B_w1, B_pe], [PB[4]], start=(l == 0 and half == 0), stop=(l == 31))
            cp("dve", cbias, PS[4][:, 0:2], [PB[4]], [B_cb])
            if not NOBAR:
                barrier()
            if dbg == "A" and kvi == 0 and (DUMPX & 1):
                dma(dbg_t("w1b", [128, 8192], BF16), w1z[0].rearrange("p l m -> p (l m)"), [B_w1], [])
                dma(dbg_t("cbias", [128, 2]), cbias, [B_cb], [])
            rawv = raw.rearrange("p (i s) -> p i s", s=16)
            for g in range(2):
                if not NOBAR:
                    barrier()
                hid, B_hid = hid2[g], B_hid2[g]
                pr = slice(64 * g, 64 * g + 64)
                for half in range(2):
                    for l in range(32):
                        rhs = rawv[:, 0:511, l] if l < 16 else rawv[:, 1:512, l - 16]
                        mm(PS[half][:, 0:511], w1z[g][:, l, half * 128:(half + 1) * 128], rhs, [B_w1, Braw], [PB[half]],
                           start=(l == 0), stop=(l == 31))
                    act(hid[:, half, 0:511], PS[half][:, 0:511], AF.Gelu_apprx_tanh, [PB[half], B_cb], [B_hid],
                        bias=cbias[:, half:half + 1])
                if dbg == "A" and kvi == 0 and (DUMPX & 2):
                    dma(dbg_t(f"hid{g}", [128, 1024], BF16), hid.rearrange("p h c -> p (h c)"), [B_hid], [])
                if kvi == 0:
                    for half in range(2):
                        mm(PS[2][:, 0:511], w2b[:, half, :], hid[:, half, 0:511], [B_w2, B_hid], [PB[2]],
                           start=(half == 0), stop=(half == 1))
                    cp("act", KcT[pr, 0:511], PS[2][pr, 0:511], [PB[2]], [B_KcT])
                else:
                    for c in range(4):
                        m = 128 if c < 3 else 127
                        for half in range(2):
                            mm(PS[2][0:m, c * 64:(c + 1) * 64], hid[:, half, c * 128:c * 128 + m], w2b[:, half, 0:64],
                               [B_w2, B_hid], [PB[2]], start=(half == 0 and c == 0), stop=(half == 1))
                    for c in range(4):
                        m = 128 if c < 3 else 127
                        cp("act", Vc[0:m, c, g, 0:64], PS[2][0:m, c * 64:(c + 1) * 64], [PB[2]], [B_Vc])
        memset("dve", KcT[:, 511:512], 0.0, [B_KcT])
        if dbg == "A":
            fl = [dma(dbg_t("KTs", [128, 8192], BF16), KTs, [B_KTs], []),
                  dma(dbg_t("Vs", [128, 64 * 130], BF16), Vs.rearrange("p t g d -> p (t g d)"), [B_Vs], []),
                  dma(dbg_t("KcT", [128, 512], BF16), KcT, [B_KcT], []),
                  dma(dbg_t("Vc", [128, 4 * 130], BF16), Vc.rearrange("p t g d -> p (t g d)"), [B_Vc], []),
                  dma(dbg_t("KcRaw", [128, 8192], BF16), KcRaw, [B_KcRaw], [])]
            print(S.emit(final_waits=fl))
            return nc
        barrier()
        top[0] = kv_top
        wb1 = abf(8 * 2352).rearrange("p (k c) -> p k c", k=8)
        B_wb1 = S.buf("wb1")
        for k in range(8):
            rows = w_in[0][k * 128:(k + 1) * 128, :]
            sc = gpre[:, k:k + 1]
            loadw(lambda c0, n, k=k: wb1[:, k, c0:c0 + n], rows[:, 0:512], 512, scale=sc, r_extra=[B_small], w=[B_wb1])
            loadw(lambda c0, n, k=k: wb1[:, k, 512:1536], rows[:, 512:1536], 1024, scale=sc, r_extra=[B_small], w=[B_wb1], perm_q=True)
            loadw(lambda c0, n, k=k: wb1[:, k, 1536 + c0:1536 + c0 + n], rows[:, 2048:2864], 816, scale=sc, r_extra=[B_small], w=[B_wb1])
        poolw = abf(4 * 128).rearrange("p (g d) -> p g d", g=4)
        B_cst = S.buf("cst")
        k_ = stg_i[0] % 2
        stg_i[0] += 1
        dma(stage[k_][:, 0:512].rearrange("p (g d) -> p g d", g=4), pool_w[0].rearrange("g c d -> c g d"), [], [B_stage[k_]])
        cp("dve", poolw, stage[k_][:, 0:512].rearrange("p (g d) -> p g d", g=4), [B_stage[k_]], [B_cst])
        pscale = af32(4)
        dma(pscale, pool_scale[0].rearrange("(g d) -> d g", d=128), [], [B_cst], slow=True)
        Ab = abf(1024).rearrange("p (g c t) -> p g c t", g=4, c=2)
        k_ = stg_i[0] % 2
        stg_i[0] += 1
        dma(stage[k_][:, 0:1024], Ab_d, [], [B_stage[k_]])
        cp("dve", Ab.rearrange("p g c t -> p (g c t)"), stage[k_][:, 0:1024], [B_stage[k_]], [B_cst])
        Mw3 = abf(384).rearrange("p (j q) -> p j q", j=3)
        k_ = stg_i[0] % 2
        stg_i[0] += 1
        dma(stage[k_][:, 0:256], Mw_d, [], [B_stage[k_]])
        cp("dve", Mw3[:, 0, :], stage[k_][:, 0:128], [B_stage[k_]], [B_cst])
        cp("dve", Mw3[:, 2, :], stage[k_][:, 128:256], [B_stage[k_]], [B_cst])
        memset("dve", Mw3[:, 1, :], 1.0, [B_cst])
        onesb = abf(1)
        memset("dve", onesb, 1.0, [B_cst])
        qrow = af32(128)
        dma(qrow, qrow_d.partition_broadcast(128), [], [B_cst])
        tqst = af32(NEXT)
        dma(tqst, tqst_d.partition_broadcast(128), [], [B_cst])
        tqt = af32(128)
        B_tqt = S.buf("tqt")
        bsrow = af32(128)
        dma(bsrow, bsrow_d.partition_broadcast(128), [], [B_cst])
        cendrow = af32(512)
        dma(cendrow, cendrow_d.partition_broadcast(128), [], [B_cst])
        kidx = af32(64)
        dma(kidx, kidx_d, [], [B_cst])
        cendcol = af32(4)
        dma(cendcol, cendcol_d, [], [B_cst])
        e0big = af32(128)
        dma(e0big, e0_d.partition_broadcast(128), [], [B_cst])

        xt2 = [af32(1024), af32(1024)]
        B_xt = [S.buf("bxt0"), S.buf("bxt1")]
        xn = abf(1024)
        B_xn = S.buf("bxn")
        junk = abf(1024)
        ssA = af32(1)
        B_ss = S.buf("bss")
        brT0 = abf(24 * 128).rearrange("p (k t) -> p k t", k=24)
        brT = [brT0, brT0]
        B_br0 = S.buf("br0")
        B_br = [B_br0, B_br0]
        rtmp = af32(256)
        B_rt = S.buf("brtmp")

        KmT = abf(4 * 256).rearrange("p (h m) -> p h m", h=4)
        Vm = abf(2 * 4 * 128).rearrange("p (c h d) -> p c h d", c=2, h=4)
        kmb = abf(512)
        mark_m = top[0]
        wmem = abf(8 * 1024).rearrange("p (k c) -> p k c", k=8)
        B_wmem = S.buf("wmem")
        for k in range(8):
            loadw(lambda c0, n, k=k: wmem[:, k, c0:c0 + n], w_mem_kv[0][k * 128:(k + 1) * 128, :], 1024,
                  scale=gmem[:, k:k + 1], r_extra=[B_small], w=[B_wmem])
        B_km, B_vm, B_kmb = S.buf("KmT"), S.buf("Vm"), S.buf("kmb")
        for c in range(2):
            dma(xt2[c], mem_d[c * 128:(c + 1) * 128, :], [], [B_xt[c]])
            norm_T(xt2[c], B_xt[c], 0, 1, brT[c][:, 16:24, :], B_br[c])
            for nb in range(2):
                for k in range(8):
                    mm(PS[2 + nb][:, :], brT[c][:, 16 + k, :], wmem[:, k, nb * 512:(nb + 1) * 512], [B_br[c], B_wmem], [PB[2 + nb]],
                       start=(k == 0), stop=(k == 7))
            cp("act", kmb, PS[2][:, :], [PB[2]], [B_kmb], scale=float(128 ** -0.5))
            cp("dve", Vm[:, c, :, :], PS[3][:, :].rearrange("p (h d) -> p h d", h=4), [PB[3]], [B_vm])
            for h in range(4):
                tr(PS[4][:, h * 128:(h + 1) * 128], kmb[:, h * 128:(h + 1) * 128], [B_kmb], [PB[4]])
            cp("act", KmT[:, :, c * 128:(c + 1) * 128], PS[4][:, :].rearrange("p (h m) -> p h m", h=4), [PB[4]], [B_km])

        barrier()
        top[0] = mark_m
        kwb = abf(256)
        B_kwb = S.buf("kwb")
        ub = [abf(512), abf(512)]
        B_ub = [S.buf("ub0"), S.buf("ub1")]
        uf = af32(512)
        B_uf = S.buf("uf")
        pbb = abf(512)
        B_pbb = S.buf("pbb")
        pT = abf(512).rearrange("p (g t) -> p g t", g=4)
        B_pT = S.buf("pT")
        qb = abf(1024)
        B_qb = S.buf("qb")
        qTz = [abf(1024).rearrange("p (k t) -> p k t", k=8) for _ in range(2)]
        B_qT = S.buf("qT")
        memset("pool", qTz[0], 0.0, [B_qT])
        memset("pool", qTz[1], 0.0, [B_qT])
        gn = af32(48)
        B_gn = S.buf("gn")
        qxb = abf(512)
        B_qxb = S.buf("qxb")
        qxT = abf(512).rearrange("p (h t) -> p h t", h=4)
        B_qxT = S.buf("qxT")
        mpT = [abf(512).rearrange("p (h t) -> p h t", h=4) for _ in range(2)]
        B_mpT = [S.buf("mpT0"), S.buf("mpT1")]
        rsm = af32(4)
        B_rsm = S.buf("rsm")
        ymemb = abf(512)
        B_ymem = S.buf("ymem")
        ef0 = af32(512)
        ef = [ef0, ef0]
        B_ef0 = S.buf("ef0")
        B_ef = [B_ef0, B_ef0]
        em = af32(512)
        B_em = S.buf("em")
        ssum = af32(2)
        B_ssum = S.buf("ssum")
        Pb = [af32(516), af32(516)]
        B_Pb = [S.buf("Pb0"), S.buf("Pb1")]
        cmrow = af32(512)
        B_cm = S.buf("cmrow")
        imp = af32(128)
        nd = af32(128)
        itmp = af32(128)
        wk = af32(128)
        mx = af32(16)
        B_imp = S.buf("imp")
        selb = abf(128)
        B_sel = S.buf("sel")
        selT = [abf(128), abf(128)]
        B_selT = [S.buf("selT0"), S.buf("selT1")]
        mk = [abf(128), abf(128)]
        B_mk = [S.buf("mk0"), S.buf("mk1")]
        pTa = [abf(1024).rearrange("p (h t) -> p h t", h=8) for _ in range(2)]
        B_pTa = [S.buf("pTa0"), S.buf("pTa1")]
        ynsa = af32(1024)
        B_yn = S.buf("ynsa")
        ytmp = af32(256)
        B_yt = S.buf("ytmp")
        rs = af32(8)
        B_rs = S.buf("rs")
        ynb = abf(1024)
        B_ynb = S.buf("ynb")
        memset("dve", Pb[0], 0.0, [B_Pb[0]])
        memset("dve", Pb[1], 0.0, [B_Pb[1]])
        nchunk = [0]

        def attend(e, g, br, chunks):
            last = len(chunks) - 1

            def stage_scores(n):
                sl = nchunk[0] % 2
                nchunk[0] += 1
                KT, V, mk_pe, mk_dve = chunks[n]
                sb0, sb1 = 2 + 2 * sl, 3 + 2 * sl
                mm(PS[sb0][:, :], KT, qTz[g][:, 0:4, :], [B_qT, B_KTs, B_KTw, B_KcT], [PB[sb0]])
                mm(PS[sb1][:, :], KT, qTz[g][:, 4:8, :], [B_qT, B_KTs, B_KTw, B_KcT], [PB[sb1]])
                if mk_pe is not None:
                    mk_pe(sl)
                return sl

            sl_next = stage_scores(0)
            for n, (KT, V, mk_pe, mk_dve) in enumerate(chunks):
                sl = sl_next
                if n < last:
                    sl_next = stage_scores(n + 1)
                sb0, sb1 = 2 + 2 * sl, 3 + 2 * sl
                pt = pTa[sl]
                act(pt[:, 0:4, :], PS[sb0][:, :].rearrange("p (h t) -> p h t", h=4), AF.Exp, [PB[sb0]], [B_pTa[sl]])
                act(pt[:, 4:8, :], PS[sb1][:, :].rearrange("p (h t) -> p h t", h=4), AF.Exp, [PB[sb1]], [B_pTa[sl]])
                mk_dve(sl)
                mb = mk[sl].unsqueeze(1).to_broadcast([128, 4, 128])
                tt("dve", pt[:, 0:4, :], pt[:, 0:4, :], mb, ALU.mult, [B_pTa[sl], B_mk[sl]], [B_pTa[sl]])
                tt("pool", pt[:, 4:8, :], pt[:, 4:8, :], mb, ALU.mult, [B_pTa[sl], B_mk[sl]], [B_pTa[sl]])
                for hh in range(8):
                    b = hh // 4
                    mm(PS[b][:, (hh % 4) * 128:(hh % 4) * 128 + 65], pt[:, hh, :], V, [B_pTa[sl], B_Vs, B_Vw, B_Vc], [PB[b]],
                       start=(n == 0 and hh % 4 == 0), stop=(n == last))
            gn3 = gn.rearrange("p (h b) -> p h b", b=3)
            for b in range(2):
                Ov = PS[b][:, :].rearrange("p (h d) -> p h d", h=4)
                h0 = 8 * g + 4 * b
                rsb = rs[:, 4 * b:4 * b + 4]
                ts("dve", rsb, Ov[:, :, 64], 1e-30, None, ALU.max, None, [PB[b]], [B_rs])
                S.op("dve", lambda e_, rsb=rsb: e_.reciprocal(out=rsb, in_=rsb), [B_rs], [B_rs])
                tt("dve", rsb, rsb, gn3[:, h0:h0 + 4, br], ALU.mult, [B_rs, B_gn], [B_rs])
                dst = ynsa.rearrange("p (h d) -> p h d", h=16)[:, h0:h0 + 4, :]
                rb = rsb.unsqueeze(2).to_broadcast([128, 4, 64])
                if br == 0:
                    tt("dve", dst, Ov[:, :, 0:64], rb, ALU.mult, [PB[b], B_rs], [B_yn])
                else:
                    yt = ytmp.rearrange("p (h d) -> p h d", h=4)
                    tt("dve", yt, Ov[:, :, 0:64], rb, ALU.mult, [PB[b], B_rs], [B_yt])
                    tt("pool", dst, dst, yt, ALU.add, [B_yn, B_yt], [B_yn])

        dma(xt2[0], x_ext[0:128, :], [], [B_xt[0]])
        for e in range(ntile_b1):
            s = e % 2
            if e + 1 < NEXT:
                dma(xt2[1 - s], x_ext[(e + 1) * 128:(e + 2) * 128, :], [], [B_xt[1 - s]])
            bt = brT[s]
            hTd = bt[:, 16:24, :]
            norm_T(xt2[s], B_xt[s], 0, 1, hTd, B_br[s])
            for k in range(8):
                mm(PS[2][:, 0:256], hTd[:, k, :], wb1[:, k, 1536:1792], [B_br[s], B_wb1], [PB[2]], start=(k == 0), stop=(k == 7))
            cp("act", kwb, PS[2][:, 0:256], [PB[2]], [B_kwb])
            rotary(PS[2][:, 0:128].rearrange("p (a g d) -> p a g d", a=1, g=2), kwb[:, 0:128].rearrange("p (a g d) -> p a g d", a=1, g=2),
                   cosE[:, e, :], sinE[:, e, :], rtmp, [PB[2], B_tabE], [B_kwb], B_rt)
            cp("pool", Vw[:, e, :, 0:64], kwb[:, 128:256].rearrange("p (g d) -> p g d", g=2), [B_kwb], [B_Vw])
            tr(PS[3][:, 0:128], kwb[:, 0:128], [B_kwb], [PB[3]])
            cp("act", KTw[:, e * 128:(e + 1) * 128], PS[3][:, 0:128], [PB[3]], [B_KTw])
            for k in range(8):
                mm(PS[4][:, :], hTd[:, k, :], wb1[:, k, 0:512], [B_br[s], B_wb1], [PB[4]], start=(k == 0), stop=(k == 7))
            cp("act", ub[s], PS[4][:, :], [PB[4]], [B_ub[s]])
            if e < 4:
                continue
            cp("dve", uf, PS[4][:, :], [PB[4]], [B_uf])
            i = e - 4
            for gi in range(4):
                blk = slice(gi * 128, (gi + 1) * 128)
                mm(PS[5][:, blk], Ab[:, gi, 0, :], ub[s][:, blk], [B_cst, B_ub[s]], [PB[5]], start=True, stop=False)
                mm(PS[5][:, blk], Ab[:, gi, 1, :], ub[1 - s][:, blk], [B_cst, B_ub[1 - s]], [PB[5]], start=False, stop=True)
            for gi in range(4):
                blk = slice(gi * 128, (gi + 1) * 128)
                stt(pbb[:, blk], PS[5][:, blk], invcnt[:, e * 4 + gi:e * 4 + gi + 1], uf[:, blk], ALU.mult, ALU.subtract,
                    [PB[5], B_uf, B_small], [B_pbb])
            for gi in range(4):
                blk = slice(gi * 128, (gi + 1) * 128)
                tr(PS[6][:, blk], pbb[:, blk], [B_pbb], [PB[6]])
            cp("act", pT, PS[6][:, :].rearrange("p (g t) -> p g t", g=4), [PB[6]], [B_pT])
            for gi in range(4):
                blk = slice(gi * 128, (gi + 1) * 128)
                mm(PS[5][:, blk], poolw[:, gi, :], pT[:, gi, :], [B_cst, B_pT], [PB[5]])
            for gi in range(4):
                blk = slice(gi * 128, (gi + 1) * 128)
                ts("dve", bt[:, gi, :], PS[5][:, blk], pscale[:, gi:gi + 1], None, ALU.mult, None, [PB[5], B_cst], [B_br[s]])
            for nb in range(2):
                for k in range(8):
                    mm(PS[nb][:, :], hTd[:, k, :], wb1[:, k, 512 + nb * 512:1024 + nb * 512], [B_br[s], B_wb1], [PB[nb]],
                       start=(k == 0), stop=(k == 7))
            for nb in range(2):
                qv = qb[:, nb * 512:(nb + 1) * 512]
                cp("act", qv, PS[nb][:, :], [PB[nb]], [B_qb], scale=0.125)
                rotary(PS[nb][:, :].rearrange("p (c g d) -> p c g d", c=4, g=2), qv.rearrange("p (c g d) -> p c g d", c=4, g=2),
                       cos8[:, e, :], sin8[:, e, :], rtmp, [PB[nb], B_tabE], [B_qb], B_rt)
            for k in range(8):
                tr(PS[2 + k // 4][:, (k % 4) * 128:(k % 4 + 1) * 128], qb[:, k * 128:(k + 1) * 128], [B_qb], [PB[2 + k // 4]])
            for (bk, c0) in ((2, 0), (3, 4)):
                cp("act", qTz[0][0:64, c0:c0 + 4, :], PS[bk][0:64, :].rearrange("p (k t) -> p k t", k=4), [PB[bk]], [B_qT])
                cp("dve", qTz[1][64:128, c0:c0 + 4, :], PS[bk][64:128, :].rearrange("p (k t) -> p k t", k=4), [PB[bk]], [B_qT])
            for k in range(8):
                mm(PS[4][:, 0:48], hTd[:, k, :], wb1[:, k, 1792:1840], [B_br[s], B_wb1], [PB[4]], start=(k == 0), stop=(k == 7))
            act(gn, PS[4][:, 0:48], AF.Sigmoid, [PB[4]], [B_gn])
            for k in range(8):
                mm(PS[5][:, :], hTd[:, k, :], wb1[:, k, 1840:2352], [B_br[s], B_wb1], [PB[5]], start=(k == 0), stop=(k == 7))
            cp("act", qxb, PS[5][:, :], [PB[5]], [B_qxb])
            for h in range(4):
                tr(PS[6][:, h * 128:(h + 1) * 128], qxb[:, h * 128:(h + 1) * 128], [B_qxb], [PB[6]])
            cp("dve", qxT, PS[6][:, :].rearrange("p (h t) -> p h t", h=4), [PB[6]], [B_qxT])
            for c in range(2):
                for h in range(4):
                    mm(PS[7][:, h * 128:(h + 1) * 128], KmT[:, h, c * 128:(c + 1) * 128], qxT[:, h, :], [B_km, B_qxT], [PB[7]])
                act(mpT[c], PS[7][:, :].rearrange("p (h t) -> p h t", h=4), AF.Exp, [PB[7]], [B_mpT[c]])
            for h in range(4):
                for c in range(2):
                    mm(PS[5][:, h * 128:(h + 1) * 128], mpT[c][:, h, :], Vm[:, c, h, :], [B_mpT[c], B_vm], [PB[5]],
                       start=(c == 0), stop=(c == 1))
            for h in range(4):
                for c in range(2):
                    mm(PS[6][:, h:h + 1], mpT[c][:, h, :], onesb[:, 0:1], [B_mpT[c], B_cst], [PB[6]],
                       start=(c == 0), stop=(c == 1))
            S.op("dve", lambda e_: e_.reciprocal(out=rsm, in_=PS[6][:, 0:4]), [PB[6]], [B_rsm])
            tt("dve", ymemb.rearrange("p (h d) -> p h d", h=4), PS[5][:, :].rearrange("p (h d) -> p h d", h=4),
               rsm.unsqueeze(2).to_broadcast([128, 4, 128]), ALU.mult, [PB[5], B_rsm], [B_ymem])
            for h in range(4):
                tr(PS[7][:, h * 128:(h + 1) * 128], ymemb[:, h * 128:(h + 1) * 128], [B_ymem], [PB[7]])
            cp("act", bt[:, 12:16, :], PS[7][:, :].rearrange("p (h t) -> p h t", h=4), [PB[7]], [B_br[s]])
            tqs = tqe[:, e:e + 1]
            ts("dve", cmrow, cendrow, tqs, None, ALU.is_le, None, [B_cst, B_small], [B_cm])
            ts("dve", tqt, qrow, tqst[:, e:e + 1], None, ALU.add, None, [B_cst], [B_tqt])
            for g in range(2):
                pr = slice(64 * g, 64 * g + 64)
                Pv = Pb[g][:, 1:513]
                for c_ in range(8):
                    a = c_ % 2
                    mm(PS[4 + a][:, :], qTz[g][:, c_, :], KcT[:, 0:512], [B_qT, B_KcT], [PB[4 + a]])
                    act(ef[a], PS[4 + a][:, :], AF.Exp, [PB[4 + a]], [B_ef[a]])
                    stt(em, ef[a], 1.0, cmrow, ALU.mult, ALU.mult, [B_ef[a], B_cm], [B_em, B_ssum], accum=ssum[:, 0:1])
                    ts("dve", ssum[:, 1:2], ssum[:, 0:1], 1e-30, None, ALU.max, None, [B_ssum], [B_ssum])
                    S.op("dve", lambda e_: e_.reciprocal(out=ssum[:, 1:2], in_=ssum[:, 1:2]), [B_ssum], [B_ssum])
                    if c_ == 0:
                        ts("dve", Pv, em, ssum[:, 1:2], None, ALU.mult, None, [B_em, B_ssum], [B_Pb[g]])
                    else:
                        stt(Pv, em, ssum[:, 1:2], Pv, ALU.mult, ALU.add, [B_em, B_ssum, B_Pb[g]], [B_Pb[g]])
                S.op("dve", lambda e_, g=g: e_.tensor_reduce(out=imp, in_=Pb[g][:, 0:512].rearrange("p (j s) -> p j s", s=4),
                                                          axis=AX.X, op=ALU.add), [B_Pb[g]], [B_imp])
                tt("dve", imp, imp, Pb[g][:, 4:516].rearrange("p (j s) -> p j s", s=4)[:, :, 0], ALU.add, [B_Pb[g], B_imp], [B_imp])
                ts("dve", nd, bsrow, tqs, None, ALU.subtract, None, [B_cst, B_small, B_imp], [B_imp])
                ts("dve", itmp, nd, -128.0, BIG, ALU.is_gt, ALU.mult, [B_imp], [B_imp])
                tt("dve", imp, imp, itmp, ALU.add, [B_imp], [B_imp])
                ts("dve", itmp, nd, 0.0, -3.0 * BIG, ALU.is_gt, ALU.mult, [B_imp], [B_imp])
                tt("dve", imp, imp, itmp, ALU.add, [B_imp], [B_imp])
                tt("dve", imp, imp, e0big, ALU.add, [B_imp, B_cst], [B_imp])
                S.op("dve", lambda e_: e_.max(out=mx[:, 0:8], in_=imp), [B_imp], [B_imp])
                S.op("dve", lambda e_: e_.match_replace(out=wk, in_to_replace=mx[:, 0:8], in_values=imp, imm_value=-1e30), [B_imp], [B_imp])
                S.op("dve", lambda e_: e_.max(out=mx[:, 8:16], in_=wk), [B_imp], [B_imp])
                ts("dve", selb, imp, mx[:, 15:16], None, ALU.is_ge, None, [B_imp], [B_sel])
                tr(PS[6][:, g * 128:(g + 1) * 128], selb, [B_sel], [PB[6]])
                cp("act", selT[g], PS[6][:, g * 128:(g + 1) * 128], [PB[6]], [B_selT[g]])
            for g in range(2):
                pr = slice(64 * g, 64 * g + 64)

                def mk_cmp(c):
                    return (None, lambda sl: ts("dve", mk[sl], tqt, cendcol[:, c:c + 1], None, ALU.is_ge, None, [B_cst, B_tqt], [B_mk[sl]]))

                def mk_win(j, ee):
                    v = 0 if j == 0 else (2 if j == 4 else 1)
                    return (None, lambda sl: ts("dve", mk[sl], Mw3[:, v, :], wval[:, ee:ee + 1], None, ALU.mult, None, [B_cst, B_small], [B_mk[sl]]))

                def mk_sel(cc, g=g):
                    def f_pe(sl):
                        mm(PS[6][:, 256 + sl * 128:256 + (sl + 1) * 128], Gb[:, cc * 128:(cc + 1) * 128], selT[g], [B_G, B_selT[g]], [PB[6]])

                    def f_dve(sl):
                        stt(mk[sl], tqt, kidx[:, cc:cc + 1], PS[6][:, 256 + sl * 128:256 + (sl + 1) * 128], ALU.is_ge, ALU.mult,
                            [B_cst, B_tqt, PB[6]], [B_mk[sl]])
                    return (f_pe, f_dve)

                attend(e, g, 0, [(KcT[:, c * 128:(c + 1) * 128], Vc[:, c, g, :]) + mk_cmp(c) for c in range(4)])
                attend(e, g, 1, [(KTs[:, cc * 128:(cc + 1) * 128], Vs[:, cc, g, :]) + mk_sel(cc) for cc in range(48 + i)])
                attend(e, g, 2, [(KTw[:, (e - 4 + j) * 128:(e - 3 + j) * 128], Vw[:, e - 4 + j, g, :]) + mk_win(j, e - 4 + j)
                                 for j in range(5)])
            cp("act", ynb, ynsa, [B_yn], [B_ynb])
            for k in range(8):
                tr(PS[2 + k // 4][:, (k % 4) * 128:(k % 4 + 1) * 128], ynb[:, k * 128:(k + 1) * 128], [B_ynb], [PB[2 + k // 4]])
            cp("act", bt[:, 4:8, :], PS[2][:, :].rearrange("p (k t) -> p k t", k=4), [PB[2]], [B_br[s]])
            cp("dve", bt[:, 8:12, :], PS[3][:, :].rearrange("p (k t) -> p k t", k=4), [PB[3]], [B_br[s]])
            lastbr = dma(br_scr[i], bt.rearrange("p k t -> p (k t)"), [B_br[s]], [])
            if dbg == "B1":
                lastbr = dma(dbg_t(f"br{i}", [128, 24 * 128], BF16), bt.rearrange("p k t -> p (k t)"), [B_br[s]], [])
                fl1 = [lastbr, dma(dbg_t(f"ynsa{i}", [128, 1024]), ynsa, [B_yn], []),
                       dma(dbg_t(f"qb{i}", [128, 1024], BF16), qb, [B_qb], []),
                       dma(dbg_t(f"gn{i}", [128, 48]), gn, [B_gn], []),
                       dma(dbg_t(f"selT{i}", [128, 128], BF16), selT[1], [B_selT[1]], []),
                       dma(dbg_t(f"Pb{i}", [128, 516]), Pb[1], [B_Pb[1]], [])]
        if dbg == "B1":
            print(S.emit(final_waits=fl1))
            return nc
        barrier()
        top[0] = persist_top
        wg = abf(8 * 3072).rearrange("p (k c) -> p k c", k=8)
        wbp = abf(4 * 1024).rearrange("p (k c) -> p k c", k=4)
        wbn = abf(8 * 1024).rearrange("p (k c) -> p k c", k=8)
        wbx = abf(4 * 1024).rearrange("p (k c) -> p k c", k=4)
        wo = abf(8 * 1024).rearrange("p (k c) -> p k c", k=8)
        B_w2p = S.buf("w_b2")
        for k in range(8):
            loadw(lambda c0, n, k=k: wg[:, k, c0:c0 + n], w_in[0][k * 128:(k + 1) * 128, 2864:5936], 3072,
                  scale=gpre[:, k:k + 1], r_extra=[B_small], w=[B_w2p])
            loadw(lambda c0, n, k=k: wbn[:, k, c0:c0 + n], w_br_nsa[0][k * 128:(k + 1) * 128, :], 1024, w=[B_w2p])
            loadw(lambda c0, n, k=k: wo[:, k, c0:c0 + n], w_out[0][k * 128:(k + 1) * 128, :], 1024, w=[B_w2p])
            if k < 4:
                loadw(lambda c0, n, k=k: wbp[:, k, c0:c0 + n], w_br_pool[0][k * 128:(k + 1) * 128, :], 1024, w=[B_w2p])
                loadw(lambda c0, n, k=k: wbx[:, k, c0:c0 + n], w_br_xa[0][k * 128:(k + 1) * 128, :], 1024, w=[B_w2p])
        gpost = af32(1024)
        B_gp = S.buf("gpost")
        dma(gpost, post_mix_g.partition_broadcast(128), [], [B_gp])
        brT = [abf(24 * 128).rearrange("p (k t) -> p k t", k=24) for _ in range(2)]
        B_br = [S.buf("c_br0"), S.buf("c_br1")]
        xt2 = [af32(1024), af32(1024)]
        B_xt = [S.buf("c_xt0"), S.buf("c_xt1")]
        sg = af32(1024)
        B_sg = S.buf("sg")
        yy = af32(1024)
        B_y = S.buf("yy")
        ytm = af32(1024)
        B_ytm = S.buf("ytm")
        yb = abf(1024)
        B_yb = S.buf("yb")
        yT = abf(1024).rearrange("p (k t) -> p k t", k=8)
        B_yT = S.buf("yT")
        junk = abf(512)
        x1t = [af32(1024), af32(1024)]
        B_x1 = [S.buf("x1t0"), S.buf("x1t1")]

        def post_norm_res(pa, pb, gp, Bg, xres, Bxres, dst, Bdst):
            act(junk[:, 0:512], PS[pa][:, :], AF.Square, [PB[pa]], [B_ss2], accum=ss2[:, 0:1])
            act(junk[:, 0:512], PS[pb][:, :], AF.Square, [PB[pb]], [B_ss2], accum=ss2[:, 1:2])
            tt("dve", ss2[:, 2:3], ss2[:, 0:1], ss2[:, 1:2], ALU.add, [B_ss2], [B_ss2])
            ts("dve", ss2[:, 2:3], ss2[:, 2:3], 1.0 / 1024, 1e-6, ALU.mult, ALU.add, [B_ss2], [B_ss2])
            act(ss2[:, 2:3], ss2[:, 2:3], AF.Sqrt, [B_ss2], [B_ss2])
            S.op("dve", lambda e_: e_.reciprocal(out=ss2[:, 3:4], in_=ss2[:, 2:3]), [B_ss2], [B_ss2])
            for nb, bank in enumerate((pa, pb)):
                blk = slice(nb * 512, (nb + 1) * 512)
                stt(dst[:, blk], PS[bank][:, :], ss2[:, 3:4], gp[:, blk], ALU.mult, ALU.mult, [PB[bank], B_ss2, Bg], [Bdst])
                tt("pool", dst[:, blk], dst[:, blk], xres[:, blk], ALU.add, [Bdst, Bxres], [Bdst])

        dma(brT[0].rearrange("p k t -> p (k t)"), br_scr[0], [], [B_br[0]])
        dma(xt2[0], x_ext[4 * 128:5 * 128, :], [], [B_xt[0]])
        for i in range(17):
            s = i % 2
            e = i + 4
            if i + 1 < 17:
                dma(brT[1 - s].rearrange("p k t -> p (k t)"), br_scr[i + 1], [], [B_br[1 - s]])
                dma(xt2[1 - s], x_ext[(e + 1) * 128:(e + 2) * 128, :], [], [B_xt[1 - s]])
            bt = brT[s]
            for br in range(3):
                for nb in range(2):
                    for k in range(8):
                        mm(PS[nb][:, :], bt[:, 16 + k, :], wg[:, k, br * 1024 + nb * 512:br * 1024 + (nb + 1) * 512],
                           [B_br[s], B_w2p], [PB[nb]], start=(k == 0), stop=(k == 7))
                wsel, off, nk = ((wbp, 0, 4), (wbn, 4, 8), (wbx, 12, 4))[br]
                for nb in range(2):
                    for k in range(nk):
                        mm(PS[2 + nb][:, :], bt[:, off + k, :], wsel[:, k, nb * 512:(nb + 1) * 512],
                           [B_br[s], B_w2p], [PB[2 + nb]], start=(k == 0), stop=(k == nk - 1))
                for nb in range(2):
                    blk = slice(nb * 512, (nb + 1) * 512)
                    act(sg[:, blk], PS[nb][:, :], AF.Sigmoid, [PB[nb]], [B_sg])
                    if br == 0:
                        tt("dve", yy[:, blk], sg[:, blk], PS[2 + nb][:, :], ALU.mult, [B_sg, PB[2 + nb]], [B_y])
                    else:
                        tt("dve", ytm[:, blk], sg[:, blk], PS[2 + nb][:, :], ALU.mult, [B_sg, PB[2 + nb]], [B_ytm])
                        tt("pool", yy[:, blk], yy[:, blk], ytm[:, blk], ALU.add, [B_y, B_ytm], [B_y])
            cp("act", yb, yy, [B_y], [B_yb])
            for k in range(8):
                tr(PS[4 + k // 4][:, (k % 4) * 128:(k % 4 + 1) * 128], yb[:, k * 128:(k + 1) * 128], [B_yb], [PB[4 + k // 4]])
            cp("act", yT[:, 0:4, :], PS[4][:, :].rearrange("p (k t) -> p k t", k=4), [PB[4]], [B_yT])
            cp("dve", yT[:, 4:8, :], PS[5][:, :].rearrange("p (k t) -> p k t", k=4), [PB[5]], [B_yT])
            for nb in range(2):
                for k in range(8):
                    mm(PS[6 + nb][:, :], yT[:, k, :], wo[:, k, nb * 512:(nb + 1) * 512], [B_yT, B_w2p], [PB[6 + nb]],
                       start=(k == 0), stop=(k == 7))
            post_norm_res(6, 7, gpost, B_gp, xt2[s], B_xt[s], x1t[s], B_x1[s])
            lx = dma(x1_scr[i], x1t[s], [B_x1[s]], [])
            if dbg == "B2":
                lx = dma(dbg_t(f"x1_{i}", [128, 1024]), x1t[s], [B_x1[s]], [])
        if dbg == "B2":
            print(S.emit(final_waits=[lx]))
            return nc
        barrier()
        top[0] = persist_top

        wup = abf(8 * 5632).rearrange("p (k c) -> p k c", k=8)
        wdn = abf(22 * 1024).rearrange("p (k c) -> p k c", k=22)
        B_w3 = S.buf("w_c")
        for k in range(8):
            loadw(lambda c0, n, k=k: wup[:, k, c0:c0 + n], w_up[0][k * 128:(k + 1) * 128, :], 5632,
                  scale=gffn[:, k:k + 1], r_extra=[B_small], w=[B_w3])
        for k in range(22):
            loadw(lambda c0, n, k=k: wdn[:, k, c0:c0 + n], w_down[0][k * 128:(k + 1) * 128, :], 1024, w=[B_w3])
        convp = af32(44 * 4).rearrange("p (j c) -> p j c", c=4)
        B_cv = S.buf("convp")
        for kk in range(3):
            dma(convp[:, :, kk], conv_w[0][kk].rearrange("(j p) -> p j", p=128), [], [B_cv], slow=True)
        dma(convp[:, :, 3], conv_b[0].rearrange("(j p) -> p j", p=128), [], [B_cv], slow=True)
        gpost2 = af32(1024)
        B_gp2 = S.buf("gpost2")
        dma(gpost2, post_ffn_g.partition_broadcast(128), [], [B_gp2])
        xt2 = [af32(1024), af32(1024)]
        B_xt = [S.buf("d_xt0"), S.buf("d_xt1")]
        xn = abf(1024)
        B_xn = S.buf("d_xn")
        junk = abf(1024)
        ssA = af32(2)
        B_ss = S.buf("d_ss")
        h2T = abf(8 * 130).rearrange("p (k t) -> p k t", k=8)
        B_h2 = S.buf("h2T")
        halo = abf(16).rearrange("p (k t) -> p k t", k=8)
        B_halo = S.buf("halo")
        aT = abf(22 * 128).rearrange("p (k t) -> p k t", k=22)
        B_aT = S.buf("aT")
        cg = [af32(128), af32(128)]
        cv = [af32(128), af32(128)]
        gl = [af32(128), af32(128)]
        B_cg = [S.buf("cg0"), S.buf("cg1")]
        B_cvb = [S.buf("cv0"), S.buf("cv1")]
        B_gl = [S.buf("gl0"), S.buf("gl1")]
        ot = [af32(1024), af32(1024)]
        B_ot = [S.buf("ot0"), S.buf("ot1")]
        junk2 = junk[:, 0:512]
        fins = []
        dma(xt2[0], x1_scr[0], [], [B_xt[0]])
        for i in range(17):
            s = i % 2
            if i + 1 < 17:
                dma(xt2[1 - s], x1_scr[i + 1], [], [B_xt[1 - s]])
            if i > 0:
                cp("pool", h2T[:, :, 0:2], halo, [B_halo], [B_h2])
            rms_scale(xt2[s], xn, B_xt[s], B_xn, ssA[:, 0:1], junk, B_ss)
            for k in range(8):
                bank = k // 4
                tr(PS[bank][:, (k % 4) * 128:(k % 4 + 1) * 128], xn[:, k * 128:(k + 1) * 128], [B_xn], [PB[bank]])
            cp("act", h2T[:, 0:4, 2:130], PS[0][:, :].rearrange("p (k t) -> p k t", k=4), [PB[0]], [B_h2])
            cp("dve", h2T[:, 4:8, 2:130], PS[1][:, :].rearrange("p (k t) -> p k t", k=4), [PB[1]], [B_h2])
            if i == 0:
                ts("dve", halo, h2T[:, :, 128:130], hflag[:, 0:1], None, ALU.mult, None, [B_h2, B_small], [B_halo])
                continue
            cp("pool", halo, h2T[:, :, 128:130], [B_h2], [B_halo])
            for j in range(22):
                a = j % 2
                bg, bv = 2 + 2 * a, 3 + 2 * a
                for k in range(8):
                    mm(PS[bg][:, 0:130], wup[:, k, j * 128:(j + 1) * 128], h2T[:, k, :], [B_w3, B_h2], [PB[bg]],
                       start=(k == 0), stop=(k == 7))
                for k in range(8):
                    mm(PS[bv][:, 0:130], wup[:, k, (22 + j) * 128:(23 + j) * 128], h2T[:, k, :], [B_w3, B_h2], [PB[bv]],
                       start=(k == 0), stop=(k == 7))
                for (bank, dstc, Bd, jj) in ((bg, cg[a], B_cg[a], j), (bv, cv[a], B_cvb[a], 22 + j)):
                    act(dstc, PS[bank][:, 2:130], AF.Identity, [PB[bank], B_cv], [Bd], bias=convp[:, jj, 3:4], scale=convp[:, jj, 2:3])
                    stt(dstc, PS[bank][:, 1:129], convp[:, jj, 1:2], dstc, ALU.mult, ALU.add, [PB[bank], B_cv, Bd], [Bd])
                    stt(dstc, PS[bank][:, 0:128], convp[:, jj, 0:1], dstc, ALU.mult, ALU.add, [PB[bank], B_cv, Bd], [Bd])
                act(gl[a], cg[a], AF.Gelu_apprx_tanh, [B_cg[a]], [B_gl[a]])
                tt("pool", aT[:, j, :], gl[a], cv[a], ALU.mult, [B_gl[a], B_cvb[a]], [B_aT])
            for nb in range(2):
                for j in range(22):
                    mm(PS[6 + nb][:, :], aT[:, j, :], wdn[:, j, nb * 512:(nb + 1) * 512], [B_aT, B_w3], [PB[6 + nb]],
                       start=(j == 0), stop=(j == 21))
            post_norm_res(6, 7, gpost2, B_gp2, xt2[s], B_xt[s], ot[s], B_ot[s])
            fins.append(dma(out_d[(i - 1) * 128:i * 128, :], ot[s], [B_ot[s]], []))
        stats = S.emit(final_waits=fins)
        print("emit stats", stats, flush=True)
    return nc


def _consts():
    c = {}
    c["ident"] = np.eye(128, dtype=np.float32)
    k = np.arange(8192)
    c["Gm"] = (k[None, :] // 64 == np.arange(128)[:, None]).astype(np.float32)
    c["invf"] = (np.float32(500000.0) ** (-np.arange(8, dtype=np.float32) * np.float32(2.0 / 16))).astype(np.float32).reshape(1, 8)
    c["bsrow"] = (64.0 * np.arange(128, dtype=np.float32)).reshape(1, 128)
    ce = (16.0 * np.arange(512, dtype=np.float32) + 31.0)
    ce[511] = 1e9
    c["cendrow"] = ce.reshape(1, 512)
    c["kidx"] = (128.0 * np.arange(64)[None, :] + np.arange(128)[:, None]).astype(np.float32)
    c["cendcol"] = np.ascontiguousarray(ce.reshape(4, 128).T)
    p = np.arange(128)[:, None]
    q = np.arange(128)[None, :]
    c["Mw"] = np.concatenate([(q < p), (q >= p)], axis=1).astype(np.float32)
    e0 = np.zeros((1, 128), np.float32)
    e0[0, 0] = BIG
    c["e0big"] = e0
    A = np.zeros((128, 4, 2, 128), np.float32)
    for gi, w in enumerate((2, 4, 8, 16)):
        tp = np.arange(128)[:, None]
        t = np.arange(128)[None, :]
        A[:, gi, 0, :] = ((t - tp >= 0) & (t - tp < w))
        A[:, gi, 1, :] = ((t + 128 - tp >= 0) & (t + 128 - tp < w))
    c["Aband"] = A.reshape(128, 1024)
    c["qrow"] = np.arange(128, dtype=np.float32).reshape(1, 128)
    return c


_PROG = {}


def kernel(**inputs):
    x = np.asarray(inputs["x"], dtype=np.float32)
    mem = np.asarray(inputs["mem"], dtype=np.float32)
    positions = np.asarray(inputs["positions"]).astype(np.int32)
    if inputs.get("_return_maps"):
        nc = None
    else:
        if "nc" not in _PROG:
            _PROG["nc"] = build()
        nc = _PROG["nc"]
    consts = _consts()
    wnames = ["pre_mix_g", "w_in", "pool_w", "pool_scale", "cmp_pe", "cmp_w1", "cmp_w2", "mem_norm_g", "w_mem_kv",
              "w_br_pool", "w_br_nsa", "w_br_xa", "w_out", "post_mix_g", "pre_ffn_g", "w_up", "conv_w", "conv_b",
              "w_down", "post_ffn_g"]
    shared = {n: np.ascontiguousarray(np.asarray(inputs[n], dtype=np.float32)) for n in wnames}
    in_maps = []
    for core in range(8):
        b, r = core // 4, core % 4
        m = dict(shared)
        m.update(consts)
        m["x_all"] = np.ascontiguousarray(x[b])
        m["mem"] = np.ascontiguousarray(mem[b])
        xe = np.zeros((NEXT * 128, 1024), np.float32)
        pe = np.zeros((NEXT, 128), np.int32)
        tq = np.zeros((NEXT, 128), np.float32)
        wv = np.zeros((1, NEXT), np.float32)
        ic = np.ones((128, NEXT, 4), np.float32)
        tst = np.zeros((1, NEXT), np.float32)
        for e in range(NEXT):
            ge = 16 * r - 5 + e
            tq[e] = ge * 128 + np.arange(128)
            tst[0, e] = ge * 128
            if ge >= 0:
                xe[e * 128:(e + 1) * 128] = x[b, ge * 128:(ge + 1) * 128]
                pe[e] = positions[b, ge * 128:(ge + 1) * 128]
                wv[0, e] = 1.0
                t = ge * 128 + np.arange(128)
                for gi, w in enumerate((2, 4, 8, 16)):
                    ic[:, e, gi] = 1.0 / np.minimum(t + 1, w).astype(np.float32)
        m["x_ext"] = xe
        m["pos_ext"] = np.ascontiguousarray(pe.T)
        m["pos_all"] = np.ascontiguousarray(positions[b].reshape(NALL, 128).T)
        m["tq_ext"] = np.ascontiguousarray(tq.T)
        m["tqstart"] = tst
        m["wvalid"] = wv
        m["invcnt"] = np.ascontiguousarray(ic.reshape(128, NEXT * 4))
        m["hflag"] = np.array([[0.0 if r == 0 else 1.0]], np.float32)
        in_maps.append(m)
    if inputs.get("_return_maps"):
        return in_maps
    res = run_bass_kernel_spmd(nc, in_maps, core_ids=list(range(8)))
    out = np.zeros((2, 8192, 1024), np.float32)
    for core in range(8):
        b, r = core // 4, core % 4
        out[b, r * 2048:(r + 1) * 2048] = res.results[core]["out"]
    return out
```

```python
import numpy as np
from contextlib import ExitStack
import concourse.bass as bass
import concourse.mybir as mybir
from concourse.bass_utils import run_bass_kernel_spmd

F32 = mybir.dt.float32
BF16 = mybir.dt.bfloat16
I32 = mybir.dt.int32
AF = mybir.ActivationFunctionType
ALU = mybir.AluOpType
AX = mybir.AxisListType


import sys as _sys


def _where():
    f = _sys._getframe(2)
    out = []
    while f is not None and len(out) < 4:
        if f.f_code.co_name != "<lambda>":
            out.append(f.f_lineno)
        f = f.f_back
    return out


class Buf:
    __slots__ = ("name", "writers", "readers", "excl", "last")

    def __init__(self, name, excl=False):
        self.name = name
        self.writers = []
        self.readers = []
        self.excl = excl
        self.last = {}


class Ins:
    __slots__ = ("eng", "fn", "deps", "idx", "flag", "tok", "dma", "pre", "where")

    def __init__(self, eng, fn, dma):
        self.eng = eng
        self.fn = fn
        self.deps = []
        self.flag = False
        self.tok = None
        self.dma = dma
        self.pre = None


class Sched:
    ENGS = ("pe", "dve", "act", "pool", "sp")
    EPOCH = 8000
    NDMA = 24

    def __init__(self, nc, stack):
        self.nc = nc
        self.stack = stack
        self.q = {e: [] for e in self.ENGS}
        self.nbuf = 0

    def buf(self, name=None, excl=False):
        self.nbuf += 1
        return Buf(name or f"b{self.nbuf}", excl)

    def op(self, eng, fn, r=(), w=(), dma=False):
        ins = Ins(eng, fn, dma)
        ins.where = _where()
        deps = []
        for b in r:
            deps.extend(b.writers)
        for b in w:
            deps.extend(b.readers)
        for b in list(r) + list(w):
            if b.excl:
                for en, li in b.last.items():
                    if en != eng:
                        deps.append(li)
                b.last[eng] = ins
        for b in w:
            if b.readers or (b in r):
                b.writers = [ins]
                b.readers = []
            else:
                b.writers.append(ins)
                if len(b.writers) > 48:
                    b.writers = b.writers[-48:]
        for b in r:
            if b not in w:
                b.readers.append(ins)
                if len(b.readers) > 48:
                    b.readers = b.readers[-48:]
        seen = set()
        for d in deps:
            if d is ins or id(d) in seen:
                continue
            if d.eng == "pe" and eng == "pe" and not d.dma:
                continue
            seen.add(id(d))
            ins.deps.append(d)
            d.flag = True
        self.q[eng].append(ins)
        return ins

    def emit(self, final_waits=()):
        nc = self.nc
        stack = self.stack
        sems = {}
        for e in self.ENGS:
            n = 0
            for ins in self.q[e]:
                if ins.dma:
                    continue
                if ins.flag:
                    n += 1
                    ins.idx = n
            nep = (n + self.EPOCH - 1) // self.EPOCH
            sems[e] = [stack.enter_context(nc.semaphore(f"s_{e}_{k}")) for k in range(max(nep, 1))]
        dsems = [stack.enter_context(nc.semaphore(f"s_dma_{k}")) for k in range(self.NDMA)]
        duse = [0] * self.NDMA
        dma_engs = [e for e in self.ENGS if any(i.dma for i in self.q[e])]
        share = {}
        if dma_engs:
            per = self.NDMA // len(dma_engs)
            for k, e in enumerate(dma_engs):
                share[e] = list(range(k * per, (k + 1) * per))
        for e in dma_engs:
            j = 0
            for ins in self.q[e]:
                if not ins.dma:
                    continue
                s = share[e][j % len(share[e])]
                j += 1
                prev = duse[s]
                duse[s] += 1
                ins.tok = (dsems[s], 16 * duse[s])
                ins.pre = (dsems[s], 16 * prev) if prev > 0 else None
        for e in self.ENGS:
            for ins in self.q[e]:
                if ins.dma or not ins.flag:
                    continue
                k = (ins.idx - 1) // self.EPOCH
                ins.tok = (sems[e][k], (ins.idx - 1) % self.EPOCH + 1)
        engobj = {"pe": "tensor", "dve": "vector", "act": "scalar", "pool": "gpsimd", "sp": "sync"}
        stats = {}
        with nc.Block() as block:
            for e in self.ENGS:
                lst = self.q[e]

                def body(eng, lst=lst, e=e):
                    waited = {}
                    nw = 0

                    def wait(tok):
                        nonlocal nw
                        sem, val = tok
                        key = id(sem)
                        if waited.get(key, 0) >= val:
                            return
                        waited[key] = val
                        eng.wait_ge(sem, val)
                        nw += 1

                    for ins in lst:
                        if ins.pre is not None:
                            wait(ins.pre)
                        for d in ins.deps:
                            wait(d.tok)
                        try:
                            bi = ins.fn(eng)
                        except BaseException:
                            print("FAILED op recorded at lines", ins.where, flush=True)
                            raise
                        if ins.dma:
                            bi.then_inc(ins.tok[0], 16)
                        elif ins.flag:
                            bi.then_inc(ins.tok[0], 1)
                    if e == "sp":
                        for fw in final_waits:
                            wait(fw.tok)
                    stats[e] = (len(lst), nw)

                getattr(block, engobj[e])(body)
        return stats


import os as _os
DUMPX = int(_os.environ.get('DUMPX', '0'))
NOBAR = int(_os.environ.get('NOBAR', '0'))
NEXT = 21
NALL = 64
BIG = 100.0
TWO_PI = float(2 * np.pi)


def build(dbg=None, ntile_b1=NEXT, na=NALL, skipcmp=False):
    nc = bass.Bass("TRN2", target_bir_lowering=False)

    def din(name, shape, dt=F32):
        return nc.dram_tensor(name, list(shape), dt, kind="ExternalInput").ap()

    x_all = din("x_all", [8192, 1024])
    x_ext = din("x_ext", [NEXT * 128, 1024])
    pos_all = din("pos_all", [128, NALL], I32)
    pos_ext = din("pos_ext", [128, NEXT], I32)
    tq_ext = din("tq_ext", [128, NEXT])
    qrow_d = din("qrow", [1, 128])
    tqst_d = din("tqstart", [1, NEXT])
    wvalid_d = din("wvalid", [1, NEXT])
    invcnt_d = din("invcnt", [128, NEXT * 4])
    hflag_d = din("hflag", [1, 1])
    mem_d = din("mem", [256, 1024])
    ident_d = din("ident", [128, 128])
    G_d = din("Gm", [128, 8192])
    invf_d = din("invf", [1, 8])
    bsrow_d = din("bsrow", [1, 128])
    cendrow_d = din("cendrow", [1, 512])
    kidx_d = din("kidx", [128, 64])
    cendcol_d = din("cendcol", [128, 4])
    Mw_d = din("Mw", [128, 256])
    e0_d = din("e0big", [1, 128])
    Ab_d = din("Aband", [128, 1024])
    pre_mix_g = din("pre_mix_g", [1, 1024])
    w_in = din("w_in", [1, 1024, 5936])
    pool_w = din("pool_w", [1, 4, 128, 128])
    pool_scale = din("pool_scale", [1, 512])
    cmp_pe = din("cmp_pe", [1, 2, 32, 64])
    cmp_w1 = din("cmp_w1", [1, 2, 2048, 256])
    cmp_w2 = din("cmp_w2", [1, 2, 256, 64])
    mem_norm_g = din("mem_norm_g", [1, 1024])
    w_mem_kv = din("w_mem_kv", [1, 1024, 1024])
    w_br_pool = din("w_br_pool", [1, 512, 1024])
    w_br_nsa = din("w_br_nsa", [1, 1024, 1024])
    w_br_xa = din("w_br_xa", [1, 512, 1024])
    w_out = din("w_out", [1, 1024, 1024])
    post_mix_g = din("post_mix_g", [1, 1024])
    pre_ffn_g = din("pre_ffn_g", [1, 1024])
    w_up = din("w_up", [1, 1024, 5632])
    conv_w = din("conv_w", [1, 3, 5632])
    conv_b = din("conv_b", [1, 5632])
    w_down = din("w_down", [1, 2816, 1024])
    post_ffn_g = din("post_ffn_g", [1, 1024])
    out_d = nc.dram_tensor("out", [16 * 128, 1024], F32, kind="ExternalOutput").ap()
    br_scr = nc.dram_tensor("br_scr", [17, 128, 24 * 128], BF16, kind="Internal").ap()
    x1_scr = nc.dram_tensor("x1_scr", [17, 128, 1024], F32, kind="Internal").ap()
    dbg_out = {}

    def dbg_t(name, shape, dt=F32):
        return nc.dram_tensor("dbg_" + name, list(shape), dt, kind="ExternalOutput").ap()

    with ExitStack() as st:
        S = Sched(nc, st)
        ARN = 95000
        arena = st.enter_context(nc.sbuf_tensor("arena", [128, ARN], BF16))
        PS = [st.enter_context(nc.psum_tensor(f"ps{i}", [128, 512], F32)) for i in range(8)]
        PB = [S.buf(f"ps{i}", excl=True) for i in range(8)]
        top = [0]

        def abf(n):
            a = arena[:, top[0]:top[0] + n]
            top[0] += (n + 31) // 32 * 32
            assert top[0] <= ARN, top[0]
            return a

        def af32(n):
            a = arena[:, top[0]:top[0] + 2 * n].bitcast(F32)
            top[0] += (n + 15) // 16 * 32
            assert top[0] <= ARN, top[0]
            return a

        def ai32(n):
            a = arena[:, top[0]:top[0] + 2 * n].bitcast(I32)
            top[0] += (n + 15) // 16 * 32
            assert top[0] <= ARN, top[0]
            return a

        def mm(out, lhsT, rhs, r, w, start=True, stop=True):
            return S.op("pe", lambda e: e.matmul(out, lhsT=lhsT, rhs=rhs, start=start, stop=stop,
                                                 skip_group_check=True), r, w)

        last_func = [None]

        def act(out, in_, func, r, w, bias=None, scale=None, accum=None):
            if func != last_func[0]:
                last_func[0] = func
                S.op("act", lambda e: e.activation(out=bsc[1][:, 0:1], in_=bsc[3][:, 0:1], func=func), [], [])
            kw = {}
            if bias is not None:
                kw["bias"] = bias
            if scale is not None:
                kw["scale"] = scale
            if accum is not None:
                kw["accum_out"] = accum
            return S.op("act", lambda e: e.activation(out=out, in_=in_, func=func, **kw), r, w)

        def ts(eng, out, in0, s1, s2, op0, op1, r, w):
            if s2 is None:
                return S.op(eng, lambda e: e.tensor_scalar(out=out, in0=in0, scalar1=s1, scalar2=None, op0=op0), r, w)
            return S.op(eng, lambda e: e.tensor_scalar(out=out, in0=in0, scalar1=s1, scalar2=s2, op0=op0, op1=op1), r, w)

        def tt(eng, out, in0, in1, op, r, w):
            return S.op(eng, lambda e: e.tensor_tensor(out=out, in0=in0, in1=in1, op=op), r, w)

        def stt(out, in0, scalar, in1, op0, op1, r, w, accum=None):
            if accum is None:
                return S.op("dve", lambda e: e.scalar_tensor_tensor(out=out, in0=in0, scalar=scalar, in1=in1, op0=op0, op1=op1), r, w)
            return S.op("dve", lambda e: e.scalar_tensor_tensor(out=out, in0=in0, scalar=scalar, in1=in1, op0=op0, op1=op1,
                                                                accum_out=accum), r, w)

        def cp(eng, out, in_, r, w, scale=None):
            if eng == "act":
                return act(out, in_, AF.Copy, r, w, scale=scale)
            if scale is not None:
                return ts(eng, out, in_, scale, None, ALU.mult, None, r, w)
            return S.op(eng, lambda e: e.tensor_copy(out=out, in_=in_), r, w)

        def memset(eng, ap, val, w):
            return S.op(eng, lambda e: e.memset(ap, val), (), w)

        dmas = []

        def dma(out, in_, r, w, slow=False):
            if slow:
                i = S.op("sp", lambda e: e.dma_start(out=out, in_=in_, allow_slow_non_contiguous=True), r, w, dma=True)
            else:
                i = S.op("sp", lambda e: e.dma_start(out=out, in_=in_), r, w, dma=True)
            dmas.append(i)
            return i

        bsc = [af32(16) for _ in range(4)]
        bbuf = {e: S.buf("bar_" + e) for e in S.ENGS}
        bar_scr = nc.dram_tensor("bar_scr", [128, 16], F32, kind="Internal").ap()

        def barrier():
            allb = list(bbuf.values())
            mm(PS[7][:, 0:1], ident_b[:, 0:128], ident_b[:, 0:1], [PB[7]], [bbuf["pe"], PB[7]])
            memset("dve", bsc[0], 0.0, [bbuf["dve"]])
            act(bsc[1], bsc[3], AF.Copy, [], [bbuf["act"]])
            memset("pool", bsc[2], 0.0, [bbuf["pool"]])
            i = S.op("sp", lambda e: e.dma_start(out=bar_scr, in_=bsc[3]), [], [bbuf["sp"]], dma=True)
            for d in dmas:
                if d not in i.deps:
                    i.deps.append(d)
            dmas.clear()
            mm(PS[7][:, 0:1], ident_b[:, 0:128], ident_b[:, 0:1], allb + [PB[7]], [PB[7]])
            S.op("dve", lambda e: e.memset(bsc[0], 0.0), allb, [])
            act(bsc[1], bsc[3], AF.Copy, allb, [])
            S.op("pool", lambda e: e.memset(bsc[2], 0.0), allb, [])
            S.op("sp", lambda e: e.dma_start(out=bar_scr, in_=bsc[3]), allb, [], dma=True)

        ident_b = abf(128)
        B_ident = S.buf("ident")
        stage = [af32(1024), af32(1024)]
        B_stage = [S.buf("stg0"), S.buf("stg1")]
        stg_i = [0]
        cast_i = [0]
        memset("dve", bsc[3], 0.0, [S.buf()])

        def loadw(dst_fn, src, ncols, scale=None, r_extra=(), w=None, perm_q=False):
            for c0 in range(0, ncols, 1024):
                n = min(1024, ncols - c0)
                k = stg_i[0] % 2
                stg_i[0] += 1
                np_ = src.shape[0]
                sl = stage[k][0:np_, 0:n]
                dma(sl, src[:, c0:c0 + n], [], [B_stage[k]])
                eng = ("pool", "dve")[cast_i[0] % 2]
                cast_i[0] += 1
                if perm_q:
                    for g in range(2):
                        o = dst_fn(c0, n).rearrange("p (c g d) -> p c g d", c=8, g=2)[:, :, g, :]
                        i_ = stage[k][0:np_, g * 512:(g + 1) * 512].rearrange("p (c d) -> p c d", c=8)
                        if scale is not None:
                            ts(eng, o, i_, scale, None, ALU.mult, None, [B_stage[k]] + list(r_extra), w)
                        else:
                            cp(eng, o, i_, [B_stage[k]] + list(r_extra), w)
                else:
                    if scale is not None:
                        ts(eng, dst_fn(c0, n), sl, scale, None, ALU.mult, None, [B_stage[k]] + list(r_extra), w)
                    else:
                        cp(eng, dst_fn(c0, n), sl, [B_stage[k]] + list(r_extra), w)

        def tr(out_ps, in_sb, r, w, start=True):
            return mm(out_ps, in_sb, ident_b[0:in_sb.shape[0], 0:in_sb.shape[0]], list(r) + [B_ident], w)

        dma(stage[0][:, 0:128], ident_d, [], [B_stage[0]])
        cp("dve", ident_b, stage[0][:, 0:128], [B_stage[0]], [B_ident])

        gpre = af32(8)
        gmem = af32(8)
        gffn = af32(8)
        B_small = S.buf("small")
        dma(gpre, pre_mix_g[0].rearrange("(k p) -> p k", p=128), [], [B_small], slow=True)
        dma(gmem, mem_norm_g[0].rearrange("(k p) -> p k", p=128), [], [B_small], slow=True)
        dma(gffn, pre_ffn_g[0].rearrange("(k p) -> p k", p=128), [], [B_small], slow=True)
        tqe = af32(NEXT)
        dma(tqe, tq_ext, [], [B_small])
        wval = af32(NEXT)
        dma(wval, wvalid_d.partition_broadcast(128), [], [B_small])
        hflag = af32(1)
        dma(hflag, hflag_d.partition_broadcast(128), [], [B_small])
        invcnt = af32(NEXT * 4)
        dma(invcnt, invcnt_d, [], [B_small])
        ss2 = af32(4)
        B_ss2 = S.buf("ss2")
        persist_top = top[0]

        def rms_scale(xt, xn, Bx, Bxn, ss, junk, Bss):
            act(junk, xt, AF.Square, [Bx], [Bss], accum=ss)
            ts("dve", ss, ss, 1.0 / 1024, 1e-6, ALU.mult, ALU.add, [Bss], [Bss])
            act(ss, ss, AF.Sqrt, [Bss], [Bss])
            S.op("dve", lambda e: e.reciprocal(out=ss, in_=ss), [Bss], [Bss])
            ts("dve", xn, xt, ss, None, ALU.mult, None, [Bx, Bss], [Bxn])

        def sincos(ang, n, osin, ocos, tmp, Bt, Bo):
            t, kf, g, ki = tmp
            ts("dve", ang, ang, 1.0 / TWO_PI, None, ALU.mult, None, [Bt], [Bt])
            for dst, off in ((osin, 0.0), (ocos, 0.25)):
                ts("dve", t, ang, off, None, ALU.add, None, [Bt], [Bt])
                cp("dve", ki, t, [Bt], [Bt])
                cp("dve", kf, ki, [Bt], [Bt])
                tt("dve", t, t, kf, ALU.subtract, [Bt], [Bt])
                ts("dve", g, t, 0.5, None, ALU.is_gt, None, [Bt], [Bt])
                tt("dve", t, t, g, ALU.subtract, [Bt], [Bt])
                ts("dve", g, t, -0.5, None, ALU.is_lt, None, [Bt], [Bt])
                tt("dve", t, t, g, ALU.add, [Bt], [Bt])
                act(dst, t, AF.Sin, [Bt], [Bo], scale=TWO_PI)

        def rotary(src4, dst4, cs, sn, tmp, rB, wB, Bt):
            a, b = src4.shape[1], src4.shape[2]
            n = a * b * 8
            x1 = src4[:, :, :, 0:8]
            x2 = src4[:, :, :, 8:16]
            csb = cs.unsqueeze(1).unsqueeze(1).to_broadcast([128, a, b, 8])
            snb = sn.unsqueeze(1).unsqueeze(1).to_broadcast([128, a, b, 8])
            t1 = tmp[:, 0:n].rearrange("p (a b d) -> p a b d", a=a, b=b)
            t2 = tmp[:, n:2 * n].rearrange("p (a b d) -> p a b d", a=a, b=b)
            tt("dve", t1, x1, csb, ALU.mult, rB, [Bt])
            tt("dve", t2, x2, snb, ALU.mult, rB, [Bt])
            tt("dve", dst4[:, :, :, 0:8], t1, t2, ALU.subtract, [Bt], wB)
            tt("dve", t1, x2, csb, ALU.mult, rB, [Bt])
            tt("dve", t2, x1, snb, ALU.mult, rB, [Bt])
            tt("dve", dst4[:, :, :, 8:16], t1, t2, ALU.add, [Bt], wB)

        KTs = abf(8192)
        Vs = abf(64 * 2 * 65).rearrange("p (t g d) -> p t g d", t=64, g=2)
        KTw = abf(NEXT * 128)
        Vw = abf(NEXT * 2 * 65).rearrange("p (t g d) -> p t g d", t=NEXT, g=2)
        KcT = abf(512)
        Vc = abf(4 * 2 * 65).rearrange("p (t g d) -> p t g d", t=4, g=2)
        Gb = abf(8192)
        B_KTs, B_Vs, B_KTw, B_Vw, B_KcT, B_Vc, B_G = [S.buf(n) for n in "KTs Vs KTw Vw KcT Vc G".split()]
        cosE = af32(NEXT * 8).rearrange("p (t f) -> p t f", f=8)
        sinE = af32(NEXT * 8).rearrange("p (t f) -> p t f", f=8)
        cos8 = af32(NEXT * 8).rearrange("p (t f) -> p t f", f=8)
        sin8 = af32(NEXT * 8).rearrange("p (t f) -> p t f", f=8)
        B_tabE = S.buf("tabE")
        kv_top = top[0]

        memset("pool", Vs, 1.0, [B_Vs])
        memset("pool", Vw, 1.0, [B_Vw])
        memset("pool", Vc, 0.0, [B_Vc])
        memset("pool", Vc[:, :, :, 64:65], 1.0, [B_Vc])
        loadw(lambda c0, n: Gb[:, c0:c0 + n], G_d, 8192, w=[B_G])

        cosA = af32(NALL * 8).rearrange("p (t f) -> p t f", f=8)
        sinA = af32(NALL * 8).rearrange("p (t f) -> p t f", f=8)
        B_tabA = S.buf("tabA")
        KcRaw = abf(8192)
        VcRaw = abf(8192)
        B_KcRaw, B_VcRaw = S.buf("KcRaw"), S.buf("VcRaw")
        wkvA = abf(8 * 512).rearrange("p (k c) -> p k c", k=8)
        B_wkvA = S.buf("wkvA")
        for k in range(8):
            loadw(lambda c0, n, k=k: wkvA[:, k, c0:c0 + n], w_in[0][k * 128:(k + 1) * 128, 1536:2048], 512,
                  scale=gpre[:, k:k + 1], r_extra=[B_small], w=[B_wkvA])
        posi = ai32(512)
        posf = af32(512)
        invf = af32(8)
        ang = af32(512)
        tmp3 = (af32(512), af32(512), af32(512), ai32(512))
        B_t = S.buf("tabtmp")
        dma(invf, invf_d.partition_broadcast(128), [], [B_t])
        dma(posi[:, 0:NALL], pos_all, [], [B_t])
        cp("dve", posf[:, 0:NALL], posi[:, 0:NALL], [B_t], [B_t])
        tt("dve", ang.rearrange("p (t f) -> p t f", f=8), posf[:, 0:NALL].unsqueeze(2).to_broadcast([128, NALL, 8]),
           invf.unsqueeze(1).to_broadcast([128, NALL, 8]), ALU.mult, [B_t], [B_t])
        sincos(ang, 512, sinA.rearrange("p t f -> p (t f)"), cosA.rearrange("p t f -> p (t f)"),
               tmp3, B_t, B_tabA)
        dma(posi[:, 0:NEXT], pos_ext, [B_t], [B_t])
        cp("dve", posf[:, 0:NEXT], posi[:, 0:NEXT], [B_t], [B_t])
        ne = NEXT * 8
        tt("dve", ang[:, 0:ne].rearrange("p (t f) -> p t f", f=8), posf[:, 0:NEXT].unsqueeze(2).to_broadcast([128, NEXT, 8]),
           invf.unsqueeze(1).to_broadcast([128, NEXT, 8]), ALU.mult, [B_t], [B_t])
        sincos(ang[:, 0:ne], ne, sinE.rearrange("p t f -> p (t f)"), cosE.rearrange("p t f -> p (t f)"),
               tuple(a[:, 0:ne] for a in tmp3), B_t, B_tabE)
        ts("dve", cos8.rearrange("p t f -> p (t f)"), cosE.rearrange("p t f -> p (t f)"), 0.125, None, ALU.mult, None, [B_tabE], [B_tabE])
        ts("dve", sin8.rearrange("p t f -> p (t f)"), sinE.rearrange("p t f -> p (t f)"), 0.125, None, ALU.mult, None, [B_tabE], [B_tabE])

        if dbg == "0":
            f1 = dma(dbg_t("cosA", [128, NALL * 8]), cosA.rearrange("p t f -> p (t f)"), [B_tabA], [])
            f2 = dma(dbg_t("sinA", [128, NALL * 8]), sinA.rearrange("p t f -> p (t f)"), [B_tabA], [])
            f3 = dma(dbg_t("cos8", [128, NEXT * 8]), cos8.rearrange("p t f -> p (t f)"), [B_tabE], [])
            f4 = dma(dbg_t("Gb", [128, 8192], BF16), Gb, [B_G], [])
            print(S.emit(final_waits=[f1, f2, f3, f4]))
            return nc
        xt2 = [af32(1024), af32(1024)]
        B_xt = [S.buf("xt0"), S.buf("xt1")]
        xn = abf(1024)
        B_xn = S.buf("xn")
        junk = abf(1024)
        ssA = af32(1)
        B_ss = S.buf("ss")
        hT = abf(1024).rearrange("p (k t) -> p k t", k=8)
        B_hT = S.buf("hT")
        kb = abf(512)
        B_kb = S.buf("kb")
        rtmp = af32(2 * 16 * 8)
        B_rt = S.buf("rtmp")

        def norm_T(xt, Bx, pa, pb, hTd, BhT):
            rms_scale(xt, xn, Bx, B_xn, ssA, junk, B_ss)
            for k in range(8):
                bank = pa if k < 4 else pb
                tr(PS[bank][:, (k % 4) * 128:(k % 4 + 1) * 128], xn[:, k * 128:(k + 1) * 128], [B_xn], [PB[bank]])
            cp("act", hTd[:, 0:4, :], PS[pa][:, :].rearrange("p (k t) -> p k t", k=4), [PB[pa]], [BhT])
            cp("dve", hTd[:, 4:8, :], PS[pb][:, :].rearrange("p (k t) -> p k t", k=4), [PB[pb]], [BhT])

        dma(xt2[0], x_all[0:128, :], [], [B_xt[0]])
        for T in range(na):
            s = T % 2
            if T + 1 < NALL:
                dma(xt2[1 - s], x_all[(T + 1) * 128:(T + 2) * 128, :], [], [B_xt[1 - s]])
            norm_T(xt2[s], B_xt[s], 0, 1, hT, B_hT)
            for k in range(8):
                mm(PS[2][:, 0:512], hT[:, k, :], wkvA[:, k, :], [B_hT, B_wkvA], [PB[2]], start=(k == 0), stop=(k == 7))
            cp("act", kb, PS[2][:, 0:512], [PB[2]], [B_kb])
            v5 = PS[2][:, 0:512].rearrange("p (j2 jj g d) -> p j2 jj g d", j2=2, jj=2, g=2)
            k5 = kb.rearrange("p (j2 jj g d) -> p j2 jj g d", j2=2, jj=2, g=2)
            rotary(v5[:, :, 0, :, :], k5[:, :, 0, :, :], cosA[:, T, :], sinA[:, T, :], rtmp, [PB[2], B_tabA], [B_kb], B_rt)
            cp("pool", Vs[:, T, :, 0:64], kb[:, 384:512].rearrange("p (g d) -> p g d", g=2), [B_kb], [B_Vs])
            tr(PS[3][:, 0:128], kb[:, 0:128], [B_kb], [PB[3]])
            tr(PS[3][:, 128:256], kb[:, 128:256], [B_kb], [PB[3]])
            tr(PS[3][:, 256:384], kb[:, 256:384], [B_kb], [PB[3]])
            cp("act", KcRaw[:, T * 128:(T + 1) * 128], PS[3][:, 0:128], [PB[3]], [B_KcRaw])
            cp("dve", VcRaw[:, T * 128:(T + 1) * 128], PS[3][:, 128:256], [PB[3]], [B_VcRaw])
            cp("act", KTs[:, T * 128:(T + 1) * 128], PS[3][:, 256:384], [PB[3]], [B_KTs])

        w1z = [abf(32 * 256).rearrange("p (l m) -> p l m", l=32) for _ in range(2)]
        B_w1 = S.buf("w1b")
        memset("pool", w1z[0][64:128, :, :], 0.0, [B_w1])
        memset("pool", w1z[1][0:64, :, :], 0.0, [B_w1])
        w2b = abf(2 * 128).rearrange("p (h d) -> p h d", h=2)
        B_w2 = S.buf("w2b")
        peT = abf(32)
        pef = af32(32)
        B_pe = S.buf("pe")
        hid2 = [abf(2 * 512).rearrange("p (h c) -> p h c", h=2) for _ in range(2)]
        B_hid2 = [S.buf("hid0"), S.buf("hid1")]
        cbias = af32(2)
        B_cb = S.buf("cbias")
        for kvi, raw, Braw in (() if skipcmp else ((0, KcRaw, B_KcRaw), (1, VcRaw, B_VcRaw))):
            w1v = cmp_w1[0][kvi].rearrange("(l d) m -> d l m", d=64)
            for half in range(2):
                for l0 in range(0, 32, 4):
                    k = stg_i[0] % 2
                    stg_i[0] += 1
                    sl = stage[k][64 * half:64 * half + 64, 0:1024]
                    dma(sl.rearrange("p (l m) -> p l m", l=4), w1v[:, l0:l0 + 4, :], [], [B_stage[k]])
                    cp(("pool", "dve")[(l0 // 4) % 2], w1z[half][64 * half:64 * half + 64, l0:l0 + 4, :],
                       sl.rearrange("p (l m) -> p l m", l=4), [B_stage[k]], [B_w1])
            k = stg_i[0] % 2
            stg_i[0] += 1
            dma(stage[k][:, 0:128].rearrange("p (h d) -> p h d", h=2), cmp_w2[0][kvi].rearrange("(h p) d -> p h d", p=128),
                [], [B_stage[k]])
            cp("dve", w2b[:, :, 0:64], stage[k][:, 0:128].rearrange("p (h d) -> p h d", h=2), [B_stage[k]], [B_w2])
            cp("dve", w2b[:, :, 64:128], stage[k][:, 0:128].rearrange("p (h d) -> p h d", h=2), [B_stage[k]], [B_w2])
            dma(pef[0:64, :], cmp_pe[0][kvi].rearrange("l d -> d l"), [], [B_pe], slow=True)
            memset("dve", peT[64:128, :], 0.0, [B_pe])
            cp("dve", peT[0:64, :], pef[0:64, :], [B_pe], [B_pe])
            for half in range(2):
                for l in range(32):
                    mm(PS[4][:, half:half + 1], w1z[0][:, l, half * 128:(half + 1) * 128], peT[:, l:l + 1],
                       [B_w1, B_pe], [PB[4]], start=(l == 0 and half == 0), stop=(l == 31))
            cp("dve", cbias, PS[4][:, 0:2], [PB[4]], [B_cb])
            if not NOBAR:
                barrier()
            if dbg == "A" and kvi == 0 and (DUMPX & 1):
                dma(dbg_t("w1b", [128, 8192], BF16), w1z[0].rearrange("p l m -> p (l m)"), [B_w1], [])
                dma(dbg_t("cbias", [128, 2]), cbias, [B_cb], [])
            rawv = raw.rearrange("p (i s) -> p i s", s=16)
            for g in range(2):
                if not NOBAR:
                    barrier()
                hid, B_hid = hid2[g], B_hid2[g]
                pr = slice(64 * g, 64 * g + 64)
                for half in range(2):
                    for l in range(32):
                        rhs = rawv[:, 0:511, l] if l < 16 else rawv[:, 1:512, l - 16]
                        mm(PS[half][:, 0:511], w1z[g][:, l, half * 128:(half + 1) * 128], rhs, [B_w1, Braw], [PB[half]],
                           start=(l == 0), stop=(l == 31))
                    act(hid[:, half, 0:511], PS[half][:, 0:511], AF.Gelu_apprx_tanh, [PB[half], B_cb], [B_hid],
                        bias=cbias[:, half:half + 1])
                if dbg == "A" and kvi == 0 and (DUMPX & 2):
                    dma(dbg_t(f"hid{g}", [128, 1024], BF16), hid.rearrange("p h c -> p (h c)"), [B_hid], [])
                if kvi == 0:
                    for half in range(2):
                        mm(PS[2][:, 0:511], w2b[:, half, :], hid[:, half, 0:511], [B_w2, B_hid], [PB[2]],
                           start=(half == 0), stop=(half == 1))
                    cp("act", KcT[pr, 0:511], PS[2][pr, 0:511], [PB[2]], [B_KcT])
                else:
                    for c in range(4):
                        m = 128 if c < 3 else 127
                        for half in range(2):
                            mm(PS[2][0:m, c * 64:(c + 1) * 64], hid[:, half, c * 128:c * 128 + m], w2b[:, half, 0:64],
                               [B_w2, B_hid], [PB[2]], start=(half == 0 and c == 0), stop=(half == 1))
                    for c in range(4):
                        m = 128 if c < 3 else 127
                        cp("act", Vc[0:m, c, g, 0:64], PS[2][0:m, c * 64:(c + 1) * 64], [PB[2]], [B_Vc])
        memset("dve", KcT[:, 511:512], 0.0, [B_KcT])
        if dbg == "A":
            fl = [dma(dbg_t("KTs", [128, 8192], BF16), KTs, [B_KTs], []),
                  dma(dbg_t("Vs", [128, 64 * 130], BF16), Vs.rearrange("p t g d -> p (t g d)"), [B_Vs], []),
                  dma(dbg_t("KcT", [128, 512], BF16), KcT, [B_KcT], []),
                  dma(dbg_t("Vc", [128, 4 * 130], BF16), Vc.rearrange("p t g d -> p (t g d)"), [B_Vc], []),
                  dma(dbg_t("KcRaw", [128, 8192], BF16), KcRaw, [B_KcRaw], [])]
            print(S.emit(final_waits=fl))
            return nc
        barrier()
        top[0] = kv_top
        wb1 = abf(8 * 2352).rearrange("p (k c) -> p k c", k=8)
        B_wb1 = S.buf("wb1")
        for k in range(8):
            rows = w_in[0][k * 128:(k + 1) * 128, :]
            sc = gpre[:, k:k + 1]
            loadw(lambda c0, n, k=k: wb1[:, k, c0:c0 + n], rows[:, 0:512], 512, scale=sc, r_extra=[B_small], w=[B_wb1])
            loadw(lambda c0, n, k=k: wb1[:, k, 512:1536], rows[:, 512:1536], 1024, scale=sc, r_extra=[B_small], w=[B_wb1], perm_q=True)
            loadw(lambda c0, n, k=k: wb1[:, k, 1536 + c0:1536 + c0 + n], rows[:, 2048:2864], 816, scale=sc, r_extra=[B_small], w=[B_wb1])
        poolw = abf(4 * 128).rearrange("p (g d) -> p g d", g=4)
        B_cst = S.buf("cst")
        k_ = stg_i[0] % 2
        stg_i[0] += 1
        dma(stage[k_][:, 0:512].rearrange("p (g d) -> p g d", g=4), pool_w[0].rearrange("g c d -> c g d"), [], [B_stage[k_]])
        cp("dve", poolw, stage[k_][:, 0:512].rearrange("p (g d) -> p g d", g=4), [B_stage[k_]], [B_cst])
        pscale = af32(4)
        dma(pscale, pool_scale[0].rearrange("(g d) -> d g", d=128), [], [B_cst], slow=True)
        Ab = abf(1024).rearrange("p (g c t) -> p g c t", g=4, c=2)
        k_ = stg_i[0] % 2
        stg_i[0] += 1
        dma(stage[k_][:, 0:1024], Ab_d, [], [B_stage[k_]])
        cp("dve", Ab.rearrange("p g c t -> p (g c t)"), stage[k_][:, 0:1024], [B_stage[k_]], [B_cst])
        Mw3 = abf(384).rearrange("p (j q) -> p j q", j=3)
        k_ = stg_i[0] % 2
        stg_i[0] += 1
        dma(stage[k_][:, 0:256], Mw_d, [], [B_stage[k_]])
        cp("dve", Mw3[:, 0, :], stage[k_][:, 0:128], [B_stage[k_]], [B_cst])
        cp("dve", Mw3[:, 2, :], stage[k_][:, 128:256], [B_stage[k_]], [B_cst])
        memset("dve", Mw3[:, 1, :], 1.0, [B_cst])
        onesb = abf(1)
        memset("dve", onesb, 1.0, [B_cst])
        qrow = af32(128)
        dma(qrow, qrow_d.partition_broadcast(128), [], [B_cst])
        tqst = af32(NEXT)
        dma(tqst, tqst_d.partition_broadcast(128), [], [B_cst])
        tqt = af32(128)
        B_tqt = S.buf("tqt")
        bsrow = af32(128)
        dma(bsrow, bsrow_d.partition_broadcast(128), [], [B_cst])
        cendrow = af32(512)
        dma(cendrow, cendrow_d.partition_broadcast(128), [], [B_cst])
        kidx = af32(64)
        dma(kidx, kidx_d, [], [B_cst])
        cendcol = af32(4)
        dma(cendcol, cendcol_d, [], [B_cst])
        e0big = af32(128)
        dma(e0big, e0_d.partition_broadcast(128), [], [B_cst])

        xt2 = [af32(1024), af32(1024)]
        B_xt = [S.buf("bxt0"), S.buf("bxt1")]
        xn = abf(1024)
        B_xn = S.buf("bxn")
        junk = abf(1024)
        ssA = af32(1)
        B_ss = S.buf("bss")
        brT0 = abf(24 * 128).rearrange("p (k t) -> p k t", k=24)
        brT = [brT0, brT0]
        B_br0 = S.buf("br0")
        B_br = [B_br0, B_br0]
        rtmp = af32(256)
        B_rt = S.buf("brtmp")

        KmT = abf(4 * 256).rearrange("p (h m) -> p h m", h=4)
        Vm = abf(2 * 4 * 128).rearrange("p (c h d) -> p c h d", c=2, h=4)
        kmb = abf(512)
        mark_m = top[0]
        wmem = abf(8 * 1024).rearrange("p (k c) -> p k c", k=8)
        B_wmem = S.buf("wmem")
        for k in range(8):
            loadw(lambda c0, n, k=k: wmem[:, k, c0:c0 + n], w_mem_kv[0][k * 128:(k + 1) * 128, :], 1024,
                  scale=gmem[:, k:k + 1], r_extra=[B_small], w=[B_wmem])
        B_km, B_vm, B_kmb = S.buf("KmT"), S.buf("Vm"), S.buf("kmb")
        for c in range(2):
            dma(xt2[c], mem_d[c * 128:(c + 1) * 128, :], [], [B_xt[c]])
            norm_T(xt2[c], B_xt[c], 0, 1, brT[c][:, 16:24, :], B_br[c])
            for nb in range(2):
                for k in range(8):
                    mm(PS[2 + nb][:, :], brT[c][:, 16 + k, :], wmem[:, k, nb * 512:(nb + 1) * 512], [B_br[c], B_wmem], [PB[2 + nb]],
                       start=(k == 0), stop=(k == 7))
            cp("act", kmb, PS[2][:, :], [PB[2]], [B_kmb], scale=float(128 ** -0.5))
            cp("dve", Vm[:, c, :, :], PS[3][:, :].rearrange("p (h d) -> p h d", h=4), [PB[3]], [B_vm])
            for h in range(4):
                tr(PS[4][:, h * 128:(h + 1) * 128], kmb[:, h * 128:(h + 1) * 128], [B_kmb], [PB[4]])
            cp("act", KmT[:, :, c * 128:(c + 1) * 128], PS[4][:, :].rearrange("p (h m) -> p h m", h=4), [PB[4]], [B_km])

        barrier()
        top[0] = mark_m
        kwb = abf(256)
        B_kwb = S.buf("kwb")
        ub = [abf(512), abf(512)]
        B_ub = [S.buf("ub0"), S.buf("ub1")]
        uf = af32(512)
        B_uf = S.buf("uf")
        pbb = abf(512)
        B_pbb = S.buf("pbb")
        pT = abf(512).rearrange("p (g t) -> p g t", g=4)
        B_pT = S.buf("pT")
        qb = abf(1024)
        B_qb = S.buf("qb")
        qTz = [abf(1024).rearrange("p (k t) -> p k t", k=8) for _ in range(2)]
        B_qT = S.buf("qT")
        memset("pool", qTz[0], 0.0, [B_qT])
        memset("pool", qTz[1], 0.0, [B_qT])
        gn = af32(48)
        B_gn = S.buf("gn")
        qxb = abf(512)
        B_qxb = S.buf("qxb")
        qxT = abf(512).rearrange("p (h t) -> p h t", h=4)
        B_qxT = S.buf("qxT")
        mpT = [abf(512).rearrange("p (h t) -> p h t", h=4) for _ in range(2)]
        B_mpT = [S.buf("mpT0"), S.buf("mpT1")]
        rsm = af32(4)
        B_rsm = S.buf("rsm")
        ymemb = abf(512)
        B_ymem = S.buf("ymem")
        ef0 = af32(512)
        ef = [ef0, ef0]
        B_ef0 = S.buf("ef0")
        B_ef = [B_ef0, B_ef0]
        em = af32(512)
        B_em = S.buf("em")
        ssum = af32(2)
        B_ssum = S.buf("ssum")
        Pb = [af32(516), af32(516)]
        B_Pb = [S.buf("Pb0"), S.buf("Pb1")]
        cmrow = af32(512)
        B_cm = S.buf("cmrow")
        imp = af32(128)
        nd = af32(128)
        itmp = af32(128)
        wk = af32(128)
        mx = af32(16)
        B_imp = S.buf("imp")
        selb = abf(128)
        B_sel = S.buf("sel")
        selT = [abf(128), abf(128)]
        B_selT = [S.buf("selT0"), S.buf("selT1")]
        mk = [abf(128), abf(128)]
        B_mk = [S.buf("mk0"), S.buf("mk1")]
        pTa = [abf(1024).rearrange("p (h t) -> p h t", h=8) for _ in range(2)]
        B_pTa = [S.buf("pTa0"), S.buf("pTa1")]
        B_pTh = [S.buf("pTh0"), S.buf("pTh1")]
        ynsa = af32(1024)
        B_yn = S.buf("ynsa")
        ytmp = af32(256)
        B_yt = S.buf("ytmp")
        rs = af32(8)
        B_rs = S.buf("rs")
        ynb = abf(1024)
        B_ynb = S.buf("ynb")
        memset("dve", Pb[0], 0.0, [B_Pb[0]])
        memset("dve", Pb[1], 0.0, [B_Pb[1]])
        nchunk = [0]

        def attend(e, g, br, chunks):
            last = len(chunks) - 1

            def stage_scores(n):
                sl = nchunk[0] % 2
                nchunk[0] += 1
                KT, V, mk_pe, mk_dve = chunks[n]
                sb0, sb1 = 2 + 2 * sl, 3 + 2 * sl
                mm(PS[sb0][:, :], KT, qTz[g][:, 0:4, :], [B_qT, B_KTs, B_KTw, B_KcT], [PB[sb0]])
                mm(PS[sb1][:, :], KT, qTz[g][:, 4:8, :], [B_qT, B_KTs, B_KTw, B_KcT], [PB[sb1]])
                if mk_pe is not None:
                    mk_pe(sl)
                return sl

            sl_next = stage_scores(0)
            for n, (KT, V, mk_pe, mk_dve) in enumerate(chunks):
                sl = sl_next
                if n < last:
                    sl_next = stage_scores(n + 1)
                sb0, sb1 = 2 + 2 * sl, 3 + 2 * sl
                pt = pTa[sl]
                act(pt[:, 0:4, :], PS[sb0][:, :].rearrange("p (h t) -> p h t", h=4), AF.Exp, [PB[sb0]], [B_pTa[sl]])
                act(pt[:, 4:8, :], PS[sb1][:, :].rearrange("p (h t) -> p h t", h=4), AF.Exp, [PB[sb1]], [B_pTh[sl]])
                mk_dve(sl)
                mb = mk[sl].unsqueeze(1).to_broadcast([128, 4, 128])
                tt("dve", pt[:, 0:4, :], pt[:, 0:4, :], mb, ALU.mult, [B_pTa[sl], B_mk[sl]], [B_pTa[sl]])
                tt("pool", pt[:, 4:8, :], pt[:, 4:8, :], mb, ALU.mult, [B_pTh[sl], B_mk[sl]], [B_pTh[sl]])
                for hh in range(8):
                    b = hh // 4
                    mm(PS[b][:, (hh % 4) * 128:(hh % 4) * 128 + 65], pt[:, hh, :], V, [(B_pTa, B_pTh)[b][sl], B_Vs, B_Vw, B_Vc], [PB[b]],
                       start=(n == 0 and hh % 4 == 0), stop=(n == last))
            gn3 = gn.rearrange("p (h b) -> p h b", b=3)
            for b in range(2):
                Ov = PS[b][:, :].rearrange("p (h d) -> p h d", h=4)
                h0 = 8 * g + 4 * b
                rsb = rs[:, 4 * b:4 * b + 4]
                ts("dve", rsb, Ov[:, :, 64], 1e-30, None, ALU.max, None, [PB[b]], [B_rs])
                S.op("dve", lambda e_, rsb=rsb: e_.reciprocal(out=rsb, in_=rsb), [B_rs], [B_rs])
                tt("dve", rsb, rsb, gn3[:, h0:h0 + 4, br], ALU.mult, [B_rs, B_gn], [B_rs])
                dst = ynsa.rearrange("p (h d) -> p h d", h=16)[:, h0:h0 + 4, :]
                rb = rsb.unsqueeze(2).to_broadcast([128, 4, 64])
                if br == 0:
                    tt("dve", dst, Ov[:, :, 0:64], rb, ALU.mult, [PB[b], B_rs], [B_yn])
                else:
                    yt = ytmp.rearrange("p (h d) -> p h d", h=4)
                    tt("dve", yt, Ov[:, :, 0:64], rb, ALU.mult, [PB[b], B_rs], [B_yt])
                    tt("pool", dst, dst, yt, ALU.add, [B_yn, B_yt], [B_yn])

        dma(xt2[0], x_ext[0:128, :], [], [B_xt[0]])
        for e in range(ntile_b1):
            s = e % 2
            if e + 1 < NEXT:
                dma(xt2[1 - s], x_ext[(e + 1) * 128:(e + 2) * 128, :], [], [B_xt[1 - s]])
            bt = brT[s]
            hTd = bt[:, 16:24, :]
            norm_T(xt2[s], B_xt[s], 0, 1, hTd, B_br[s])
            for k in range(8):
                mm(PS[2][:, 0:256], hTd[:, k, :], wb1[:, k, 1536:1792], [B_br[s], B_wb1], [PB[2]], start=(k == 0), stop=(k == 7))
            cp("act", kwb, PS[2][:, 0:256], [PB[2]], [B_kwb])
            rotary(PS[2][:, 0:128].rearrange("p (a g d) -> p a g d", a=1, g=2), kwb[:, 0:128].rearrange("p (a g d) -> p a g d", a=1, g=2),
                   cosE[:, e, :], sinE[:, e, :], rtmp, [PB[2], B_tabE], [B_kwb], B_rt)
            cp("pool", Vw[:, e, :, 0:64], kwb[:, 128:256].rearrange("p (g d) -> p g d", g=2), [B_kwb], [B_Vw])
            tr(PS[3][:, 0:128], kwb[:, 0:128], [B_kwb], [PB[3]])
            cp("act", KTw[:, e * 128:(e + 1) * 128], PS[3][:, 0:128], [PB[3]], [B_KTw])
            for k in range(8):
                mm(PS[4][:, :], hTd[:, k, :], wb1[:, k, 0:512], [B_br[s], B_wb1], [PB[4]], start=(k == 0), stop=(k == 7))
            cp("act", ub[s], PS[4][:, :], [PB[4]], [B_ub[s]])
            if e < 4:
                continue
            cp("dve", uf, PS[4][:, :], [PB[4]], [B_uf])
            i = e - 4
            for gi in range(4):
                blk = slice(gi * 128, (gi + 1) * 128)
                mm(PS[5][:, blk], Ab[:, gi, 0, :], ub[s][:, blk], [B_cst, B_ub[s]], [PB[5]], start=True, stop=False)
                mm(PS[5][:, blk], Ab[:, gi, 1, :], ub[1 - s][:, blk], [B_cst, B_ub[1 - s]], [PB[5]], start=False, stop=True)
            for gi in range(4):
                blk = slice(gi * 128, (gi + 1) * 128)
                stt(pbb[:, blk], PS[5][:, blk], invcnt[:, e * 4 + gi:e * 4 + gi + 1], uf[:, blk], ALU.mult, ALU.subtract,
                    [PB[5], B_uf, B_small], [B_pbb])
            for gi in range(4):
                blk = slice(gi * 128, (gi + 1) * 128)
                tr(PS[6][:, blk], pbb[:, blk], [B_pbb], [PB[6]])
            cp("act", pT, PS[6][:, :].rearrange("p (g t) -> p g t", g=4), [PB[6]], [B_pT])
            for gi in range(4):
                blk = slice(gi * 128, (gi + 1) * 128)
                mm(PS[5][:, blk], poolw[:, gi, :], pT[:, gi, :], [B_cst, B_pT], [PB[5]])
            for gi in range(4):
                blk = slice(gi * 128, (gi + 1) * 128)
                ts("dve", bt[:, gi, :], PS[5][:, blk], pscale[:, gi:gi + 1], None, ALU.mult, None, [PB[5], B_cst], [B_br[s]])
            for nb in range(2):
                for k in range(8):
                    mm(PS[nb][:, :], hTd[:, k, :], wb1[:, k, 512 + nb * 512:1024 + nb * 512], [B_br[s], B_wb1], [PB[nb]],
                       start=(k == 0), stop=(k == 7))
            for nb in range(2):
                qv = qb[:, nb * 512:(nb + 1) * 512]
                cp("act", qv, PS[nb][:, :], [PB[nb]], [B_qb], scale=0.125)
                rotary(PS[nb][:, :].rearrange("p (c g d) -> p c g d", c=4, g=2), qv.rearrange("p (c g d) -> p c g d", c=4, g=2),
                       cos8[:, e, :], sin8[:, e, :], rtmp, [PB[nb], B_tabE], [B_qb], B_rt)
            for k in range(8):
                tr(PS[2 + k // 4][:, (k % 4) * 128:(k % 4 + 1) * 128], qb[:, k * 128:(k + 1) * 128], [B_qb], [PB[2 + k // 4]])
            for (bk, c0) in ((2, 0), (3, 4)):
                cp("act", qTz[0][0:64, c0:c0 + 4, :], PS[bk][0:64, :].rearrange("p (k t) -> p k t", k=4), [PB[bk]], [B_qT])
                cp("dve", qTz[1][64:128, c0:c0 + 4, :], PS[bk][64:128, :].rearrange("p (k t) -> p k t", k=4), [PB[bk]], [B_qT])
            for k in range(8):
                mm(PS[4][:, 0:48], hTd[:, k, :], wb1[:, k, 1792:1840], [B_br[s], B_wb1], [PB[4]], start=(k == 0), stop=(k == 7))
            act(gn, PS[4][:, 0:48], AF.Sigmoid, [PB[4]], [B_gn])
            for k in range(8):
                mm(PS[5][:, :], hTd[:, k, :], wb1[:, k, 1840:2352], [B_br[s], B_wb1], [PB[5]], start=(k == 0), stop=(k == 7))
            cp("act", qxb, PS[5][:, :], [PB[5]], [B_qxb])
            for h in range(4):
                tr(PS[6][:, h * 128:(h + 1) * 128], qxb[:, h * 128:(h + 1) * 128], [B_qxb], [PB[6]])
            cp("dve", qxT, PS[6][:, :].rearrange("p (h t) -> p h t", h=4), [PB[6]], [B_qxT])
            for c in range(2):
                for h in range(4):
                    mm(PS[7][:, h * 128:(h + 1) * 128], KmT[:, h, c * 128:(c + 1) * 128], qxT[:, h, :], [B_km, B_qxT], [PB[7]])
                act(mpT[c], PS[7][:, :].rearrange("p (h t) -> p h t", h=4), AF.Exp, [PB[7]], [B_mpT[c]])
            for h in range(4):
                for c in range(2):
                    mm(PS[5][:, h * 128:(h + 1) * 128], mpT[c][:, h, :], Vm[:, c, h, :], [B_mpT[c], B_vm], [PB[5]],
                       start=(c == 0), stop=(c == 1))
            for h in range(4):
                for c in range(2):
                    mm(PS[6][:, h:h + 1], mpT[c][:, h, :], onesb[:, 0:1], [B_mpT[c], B_cst], [PB[6]],
                       start=(c == 0), stop=(c == 1))
            S.op("dve", lambda e_: e_.reciprocal(out=rsm, in_=PS[6][:, 0:4]), [PB[6]], [B_rsm])
            tt("dve", ymemb.rearrange("p (h d) -> p h d", h=4), PS[5][:, :].rearrange("p (h d) -> p h d", h=4),
               rsm.unsqueeze(2).to_broadcast([128, 4, 128]), ALU.mult, [PB[5], B_rsm], [B_ymem])
            for h in range(4):
                tr(PS[7][:, h * 128:(h + 1) * 128], ymemb[:, h * 128:(h + 1) * 128], [B_ymem], [PB[7]])
            cp("act", bt[:, 12:16, :], PS[7][:, :].rearrange("p (h t) -> p h t", h=4), [PB[7]], [B_br[s]])
            tqs = tqe[:, e:e + 1]
            ts("dve", cmrow, cendrow, tqs, None, ALU.is_le, None, [B_cst, B_small], [B_cm])
            ts("dve", tqt, qrow, tqst[:, e:e + 1], None, ALU.add, None, [B_cst], [B_tqt])
            for g in range(2):
                pr = slice(64 * g, 64 * g + 64)
                Pv = Pb[g][:, 1:513]
                for c_ in range(8):
                    a = c_ % 2
                    mm(PS[4 + a][:, :], qTz[g][:, c_, :], KcT[:, 0:512], [B_qT, B_KcT], [PB[4 + a]])
                    act(ef[a], PS[4 + a][:, :], AF.Exp, [PB[4 + a]], [B_ef[a]])
                    stt(em, ef[a], 1.0, cmrow, ALU.mult, ALU.mult, [B_ef[a], B_cm], [B_em, B_ssum], accum=ssum[:, 0:1])
                    ts("dve", ssum[:, 1:2], ssum[:, 0:1], 1e-30, None, ALU.max, None, [B_ssum], [B_ssum])
                    S.op("dve", lambda e_: e_.reciprocal(out=ssum[:, 1:2], in_=ssum[:, 1:2]), [B_ssum], [B_ssum])
                    if c_ == 0:
                        ts("dve", Pv, em, ssum[:, 1:2], None, ALU.mult, None, [B_em, B_ssum], [B_Pb[g]])
                    else:
                        stt(Pv, em, ssum[:, 1:2], Pv, ALU.mult, ALU.add, [B_em, B_ssum, B_Pb[g]], [B_Pb[g]])
                S.op("dve", lambda e_, g=g: e_.tensor_reduce(out=imp, in_=Pb[g][:, 0:512].rearrange("p (j s) -> p j s", s=4),
                                                          axis=AX.X, op=ALU.add), [B_Pb[g]], [B_imp])
                tt("dve", imp, imp, Pb[g][:, 4:516].rearrange("p (j s) -> p j s", s=4)[:, :, 0], ALU.add, [B_Pb[g], B_imp], [B_imp])
                ts("dve", nd, bsrow, tqs, None, ALU.subtract, None, [B_cst, B_small, B_imp], [B_imp])
                ts("dve", itmp, nd, -128.0, BIG, ALU.is_gt, ALU.mult, [B_imp], [B_imp])
                tt("dve", imp, imp, itmp, ALU.add, [B_imp], [B_imp])
                ts("dve", itmp, nd, 0.0, -3.0 * BIG, ALU.is_gt, ALU.mult, [B_imp], [B_imp])
                tt("dve", imp, imp, itmp, ALU.add, [B_imp], [B_imp])
                tt("dve", imp, imp, e0big, ALU.add, [B_imp, B_cst], [B_imp])
                S.op("dve", lambda e_: e_.max(out=mx[:, 0:8], in_=imp), [B_imp], [B_imp])
                S.op("dve", lambda e_: e_.match_replace(out=wk, in_to_replace=mx[:, 0:8], in_values=imp, imm_value=-1e30), [B_imp], [B_imp])
                S.op("dve", lambda e_: e_.max(out=mx[:, 8:16], in_=wk), [B_imp], [B_imp])
                ts("dve", selb, imp, mx[:, 15:16], None, ALU.is_ge, None, [B_imp], [B_sel])
                tr(PS[6][:, g * 128:(g + 1) * 128], selb, [B_sel], [PB[6]])
                cp("act", selT[g], PS[6][:, g * 128:(g + 1) * 128], [PB[6]], [B_selT[g]])
            for g in range(2):
                pr = slice(64 * g, 64 * g + 64)

                def mk_cmp(c):
                    return (None, lambda sl: ts("dve", mk[sl], tqt, cendcol[:, c:c + 1], None, ALU.is_ge, None, [B_cst, B_tqt], [B_mk[sl]]))

                def mk_win(j, ee):
                    v = 0 if j == 0 else (2 if j == 4 else 1)
                    return (None, lambda sl: ts("dve", mk[sl], Mw3[:, v, :], wval[:, ee:ee + 1], None, ALU.mult, None, [B_cst, B_small], [B_mk[sl]]))

                def mk_sel(cc, g=g):
                    def f_pe(sl):
                        mm(PS[6 + sl][:, 256:384], Gb[:, cc * 128:(cc + 1) * 128], selT[g], [B_G, B_selT[g]], [PB[6 + sl]])

                    def f_dve(sl):
                        stt(mk[sl], tqt, kidx[:, cc:cc + 1], PS[6 + sl][:, 256:384], ALU.is_ge, ALU.mult,
                            [B_cst, B_tqt, PB[6 + sl]], [B_mk[sl]])
                    return (f_pe, f_dve)

                attend(e, g, 0, [(KcT[:, c * 128:(c + 1) * 128], Vc[:, c, g, :]) + mk_cmp(c) for c in range(4)])
                attend(e, g, 1, [(KTs[:, cc * 128:(cc + 1) * 128], Vs[:, cc, g, :]) + mk_sel(cc) for cc in range(48 + i)])
                attend(e, g, 2, [(KTw[:, (e - 4 + j) * 128:(e - 3 + j) * 128], Vw[:, e - 4 + j, g, :]) + mk_win(j, e - 4 + j)
                                 for j in range(5)])
            cp("act", ynb, ynsa, [B_yn], [B_ynb])
            for k in range(8):
                tr(PS[2 + k // 4][:, (k % 4) * 128:(k % 4 + 1) * 128], ynb[:, k * 128:(k + 1) * 128], [B_ynb], [PB[2 + k // 4]])
            cp("act", bt[:, 4:8, :], PS[2][:, :].rearrange("p (k t) -> p k t", k=4), [PB[2]], [B_br[s]])
            cp("dve", bt[:, 8:12, :], PS[3][:, :].rearrange("p (k t) -> p k t", k=4), [PB[3]], [B_br[s]])
            lastbr = dma(br_scr[i], bt.rearrange("p k t -> p (k t)"), [B_br[s]], [])
            if dbg == "B1":
                lastbr = dma(dbg_t(f"br{i}", [128, 24 * 128], BF16), bt.rearrange("p k t -> p (k t)"), [B_br[s]], [])
                fl1 = [lastbr, dma(dbg_t(f"ynsa{i}", [128, 1024]), ynsa, [B_yn], []),
                       dma(dbg_t(f"qb{i}", [128, 1024], BF16), qb, [B_qb], []),
                       dma(dbg_t(f"gn{i}", [128, 48]), gn, [B_gn], []),
                       dma(dbg_t(f"selT{i}", [128, 128], BF16), selT[1], [B_selT[1]], []),
                       dma(dbg_t(f"Pb{i}", [128, 516]), Pb[1], [B_Pb[1]], [])]
        if dbg == "B1":
            print(S.emit(final_waits=fl1))
            return nc
        barrier()
        top[0] = persist_top
        wg = abf(8 * 3072).rearrange("p (k c) -> p k c", k=8)
        wbp = abf(4 * 1024).rearrange("p (k c) -> p k c", k=4)
        wbn = abf(8 * 1024).rearrange("p (k c) -> p k c", k=8)
        wbx = abf(4 * 1024).rearrange("p (k c) -> p k c", k=4)
        wo = abf(8 * 1024).rearrange("p (k c) -> p k c", k=8)
        B_w2p = S.buf("w_b2")
        for k in range(8):
            loadw(lambda c0, n, k=k: wg[:, k, c0:c0 + n], w_in[0][k * 128:(k + 1) * 128, 2864:5936], 3072,
                  scale=gpre[:, k:k + 1], r_extra=[B_small], w=[B_w2p])
            loadw(lambda c0, n, k=k: wbn[:, k, c0:c0 + n], w_br_nsa[0][k * 128:(k + 1) * 128, :], 1024, w=[B_w2p])
            loadw(lambda c0, n, k=k: wo[:, k, c0:c0 + n], w_out[0][k * 128:(k + 1) * 128, :], 1024, w=[B_w2p])
            if k < 4:
                loadw(lambda c0, n, k=k: wbp[:, k, c0:c0 + n], w_br_pool[0][k * 128:(k + 1) * 128, :], 1024, w=[B_w2p])
                loadw(lambda c0, n, k=k: wbx[:, k, c0:c0 + n], w_br_xa[0][k * 128:(k + 1) * 128, :], 1024, w=[B_w2p])
        gpost = af32(1024)
        B_gp = S.buf("gpost")
        dma(gpost, post_mix_g.partition_broadcast(128), [], [B_gp])
        brT = [abf(24 * 128).rearrange("p (k t) -> p k t", k=24) for _ in range(2)]
        B_br = [S.buf("c_br0"), S.buf("c_br1")]
        xt2 = [af32(1024), af32(1024)]
        B_xt = [S.buf("c_xt0"), S.buf("c_xt1")]
        sg = af32(1024)
        B_sg = S.buf("sg")
        yy = af32(1024)
        B_y = S.buf("yy")
        ytm = af32(1024)
        B_ytm = S.buf("ytm")
        yb = abf(1024)
        B_yb = S.buf("yb")
        yT = abf(1024).rearrange("p (k t) -> p k t", k=8)
        B_yT = S.buf("yT")
        junk = abf(512)
        x1t = [af32(1024), af32(1024)]
        B_x1 = [S.buf("x1t0"), S.buf("x1t1")]

        def post_norm_res(pa, pb, gp, Bg, xres, Bxres, dst, Bdst):
            act(junk[:, 0:512], PS[pa][:, :], AF.Square, [PB[pa]], [B_ss2], accum=ss2[:, 0:1])
            act(junk[:, 0:512], PS[pb][:, :], AF.Square, [PB[pb]], [B_ss2], accum=ss2[:, 1:2])
            tt("dve", ss2[:, 2:3], ss2[:, 0:1], ss2[:, 1:2], ALU.add, [B_ss2], [B_ss2])
            ts("dve", ss2[:, 2:3], ss2[:, 2:3], 1.0 / 1024, 1e-6, ALU.mult, ALU.add, [B_ss2], [B_ss2])
            act(ss2[:, 2:3], ss2[:, 2:3], AF.Sqrt, [B_ss2], [B_ss2])
            S.op("dve", lambda e_: e_.reciprocal(out=ss2[:, 3:4], in_=ss2[:, 2:3]), [B_ss2], [B_ss2])
            for nb, bank in enumerate((pa, pb)):
                blk = slice(nb * 512, (nb + 1) * 512)
                stt(dst[:, blk], PS[bank][:, :], ss2[:, 3:4], gp[:, blk], ALU.mult, ALU.mult, [PB[bank], B_ss2, Bg], [Bdst])
                tt("pool", dst[:, blk], dst[:, blk], xres[:, blk], ALU.add, [Bdst, Bxres], [Bdst])

        dma(brT[0].rearrange("p k t -> p (k t)"), br_scr[0], [], [B_br[0]])
        dma(xt2[0], x_ext[4 * 128:5 * 128, :], [], [B_xt[0]])
        for i in range(17):
            s = i % 2
            e = i + 4
            if i + 1 < 17:
                dma(brT[1 - s].rearrange("p k t -> p (k t)"), br_scr[i + 1], [], [B_br[1 - s]])
                dma(xt2[1 - s], x_ext[(e + 1) * 128:(e + 2) * 128, :], [], [B_xt[1 - s]])
            bt = brT[s]
            for br in range(3):
                for nb in range(2):
                    for k in range(8):
                        mm(PS[nb][:, :], bt[:, 16 + k, :], wg[:, k, br * 1024 + nb * 512:br * 1024 + (nb + 1) * 512],
                           [B_br[s], B_w2p], [PB[nb]], start=(k == 0), stop=(k == 7))
                wsel, off, nk = ((wbp, 0, 4), (wbn, 4, 8), (wbx, 12, 4))[br]
                for nb in range(2):
                    for k in range(nk):
                        mm(PS[2 + nb][:, :], bt[:, off + k, :], wsel[:, k, nb * 512:(nb + 1) * 512],
                           [B_br[s], B_w2p], [PB[2 + nb]], start=(k == 0), stop=(k == nk - 1))
                for nb in range(2):
                    blk = slice(nb * 512, (nb + 1) * 512)
                    act(sg[:, blk], PS[nb][:, :], AF.Sigmoid, [PB[nb]], [B_sg])
                    if br == 0:
                        tt("dve", yy[:, blk], sg[:, blk], PS[2 + nb][:, :], ALU.mult, [B_sg, PB[2 + nb]], [B_y])
                    else:
                        tt("dve", ytm[:, blk], sg[:, blk], PS[2 + nb][:, :], ALU.mult, [B_sg, PB[2 + nb]], [B_ytm])
                        tt("pool", yy[:, blk], yy[:, blk], ytm[:, blk], ALU.add, [B_y, B_ytm], [B_y])
            cp("act", yb, yy, [B_y], [B_yb])
            for k in range(8):
                tr(PS[4 + k // 4][:, (k % 4) * 128:(k % 4 + 1) * 128], yb[:, k * 128:(k + 1) * 128], [B_yb], [PB[4 + k // 4]])
            cp("act", yT[:, 0:4, :], PS[4][:, :].rearrange("p (k t) -> p k t", k=4), [PB[4]], [B_yT])
            cp("dve", yT[:, 4:8, :], PS[5][:, :].rearrange("p (k t) -> p k t", k=4), [PB[5]], [B_yT])
            for nb in range(2):
                for k in range(8):
                    mm(PS[6 + nb][:, :], yT[:, k, :], wo[:, k, nb * 512:(nb + 1) * 512], [B_yT, B_w2p], [PB[6 + nb]],
                       start=(k == 0), stop=(k == 7))
            post_norm_res(6, 7, gpost, B_gp, xt2[s], B_xt[s], x1t[s], B_x1[s])
            lx = dma(x1_scr[i], x1t[s], [B_x1[s]], [])
            if dbg == "B2":
                lx = dma(dbg_t(f"x1_{i}", [128, 1024]), x1t[s], [B_x1[s]], [])
        if dbg == "B2":
            print(S.emit(final_waits=[lx]))
            return nc
        barrier()
        top[0] = persist_top

        wup = abf(8 * 5632).rearrange("p (k c) -> p k c", k=8)
        wdn = abf(22 * 1024).rearrange("p (k c) -> p k c", k=22)
        B_w3 = S.buf("w_c")
        for k in range(8):
            loadw(lambda c0, n, k=k: wup[:, k, c0:c0 + n], w_up[0][k * 128:(k + 1) * 128, :], 5632,
                  scale=gffn[:, k:k + 1], r_extra=[B_small], w=[B_w3])
        for k in range(22):
            loadw(lambda c0, n, k=k: wdn[:, k, c0:c0 + n], w_down[0][k * 128:(k + 1) * 128, :], 1024, w=[B_w3])
        convp = af32(44 * 4).rearrange("p (j c) -> p j c", c=4)
        B_cv = S.buf("convp")
        for kk in range(3):
            dma(convp[:, :, kk], conv_w[0][kk].rearrange("(j p) -> p j", p=128), [], [B_cv], slow=True)
        dma(convp[:, :, 3], conv_b[0].rearrange("(j p) -> p j", p=128), [], [B_cv], slow=True)
        gpost2 = af32(1024)
        B_gp2 = S.buf("gpost2")
        dma(gpost2, post_ffn_g.partition_broadcast(128), [], [B_gp2])
        xt2 = [af32(1024), af32(1024)]
        B_xt = [S.buf("d_xt0"), S.buf("d_xt1")]
        xn = abf(1024)
        B_xn = S.buf("d_xn")
        junk = abf(1024)
        ssA = af32(2)
        B_ss = S.buf("d_ss")
        h2T = abf(8 * 130).rearrange("p (k t) -> p k t", k=8)
        B_h2 = S.buf("h2T")
        halo = abf(16).rearrange("p (k t) -> p k t", k=8)
        B_halo = S.buf("halo")
        aT = abf(22 * 128).rearrange("p (k t) -> p k t", k=22)
        B_aT = S.buf("aT")
        cg = [af32(128), af32(128)]
        cv = [af32(128), af32(128)]
        gl = [af32(128), af32(128)]
        B_cg = [S.buf("cg0"), S.buf("cg1")]
        B_cvb = [S.buf("cv0"), S.buf("cv1")]
        B_gl = [S.buf("gl0"), S.buf("gl1")]
        ot = [af32(1024), af32(1024)]
        B_ot = [S.buf("ot0"), S.buf("ot1")]
        junk2 = junk[:, 0:512]
        fins = []
        dma(xt2[0], x1_scr[0], [], [B_xt[0]])
        for i in range(17):
            s = i % 2
            if i + 1 < 17:
                dma(xt2[1 - s], x1_scr[i + 1], [], [B_xt[1 - s]])
            if i > 0:
                cp("pool", h2T[:, :, 0:2], halo, [B_halo], [B_h2])
            rms_scale(xt2[s], xn, B_xt[s], B_xn, ssA[:, 0:1], junk, B_ss)
            for k in range(8):
                bank = k // 4
                tr(PS[bank][:, (k % 4) * 128:(k % 4 + 1) * 128], xn[:, k * 128:(k + 1) * 128], [B_xn], [PB[bank]])
            cp("act", h2T[:, 0:4, 2:130], PS[0][:, :].rearrange("p (k t) -> p k t", k=4), [PB[0]], [B_h2])
            cp("dve", h2T[:, 4:8, 2:130], PS[1][:, :].rearrange("p (k t) -> p k t", k=4), [PB[1]], [B_h2])
            if i == 0:
                ts("dve", halo, h2T[:, :, 128:130], hflag[:, 0:1], None, ALU.mult, None, [B_h2, B_small], [B_halo])
                continue
            cp("pool", halo, h2T[:, :, 128:130], [B_h2], [B_halo])
            for j in range(22):
                a = j % 2
                bg, bv = 2 + 2 * a, 3 + 2 * a
                for k in range(8):
                    mm(PS[bg][:, 0:130], wup[:, k, j * 128:(j + 1) * 128], h2T[:, k, :], [B_w3, B_h2], [PB[bg]],
                       start=(k == 0), stop=(k == 7))
                for k in range(8):
                    mm(PS[bv][:, 0:130], wup[:, k, (22 + j) * 128:(23 + j) * 128], h2T[:, k, :], [B_w3, B_h2], [PB[bv]],
                       start=(k == 0), stop=(k == 7))
                for (bank, dstc, Bd, jj) in ((bg, cg[a], B_cg[a], j), (bv, cv[a], B_cvb[a], 22 + j)):
                    act(dstc, PS[bank][:, 2:130], AF.Identity, [PB[bank], B_cv], [Bd], bias=convp[:, jj, 3:4], scale=convp[:, jj, 2:3])
                    stt(dstc, PS[bank][:, 1:129], convp[:, jj, 1:2], dstc, ALU.mult, ALU.add, [PB[bank], B_cv, Bd], [Bd])
                    stt(dstc, PS[bank][:, 0:128], convp[:, jj, 0:1], dstc, ALU.mult, ALU.add, [PB[bank], B_cv, Bd], [Bd])
                act(gl[a], cg[a], AF.Gelu_apprx_tanh, [B_cg[a]], [B_gl[a]])
                tt("pool", aT[:, j, :], gl[a], cv[a], ALU.mult, [B_gl[a], B_cvb[a]], [B_aT])
            for nb in range(2):
                for j in range(22):
                    mm(PS[6 + nb][:, :], aT[:, j, :], wdn[:, j, nb * 512:(nb + 1) * 512], [B_aT, B_w3], [PB[6 + nb]],
                       start=(j == 0), stop=(j == 21))
            post_norm_res(6, 7, gpost2, B_gp2, xt2[s], B_xt[s], ot[s], B_ot[s])
            fins.append(dma(out_d[(i - 1) * 128:i * 128, :], ot[s], [B_ot[s]], []))
        stats = S.emit(final_waits=fins)
        print("emit stats", stats, flush=True)
    return nc


def _consts():
    c = {}
    c["ident"] = np.eye(128, dtype=np.float32)
    k = np.arange(8192)
    c["Gm"] = (k[None, :] // 64 == np.arange(128)[:, None]).astype(np.float32)
    c["invf"] = (np.float32(500000.0) ** (-np.arange(8, dtype=np.float32) * np.float32(2.0 / 16))).astype(np.float32).reshape(1, 8)
    c["bsrow"] = (64.0 * np.arange(128, dtype=np.float32)).reshape(1, 128)
    ce = (16.0 * np.arange(512, dtype=np.float32) + 31.0)
    ce[511] = 1e9
    c["cendrow"] = ce.reshape(1, 512)
    c["kidx"] = (128.0 * np.arange(64)[None, :] + np.arange(128)[:, None]).astype(np.float32)
    c["cendcol"] = np.ascontiguousarray(ce.reshape(4, 128).T)
    p = np.arange(128)[:, None]
    q = np.arange(128)[None, :]
    c["Mw"] = np.concatenate([(q < p), (q >= p)], axis=1).astype(np.float32)
    e0 = np.zeros((1, 128), np.float32)
    e0[0, 0] = BIG
    c["e0big"] = e0
    A = np.zeros((128, 4, 2, 128), np.float32)
    for gi, w in enumerate((2, 4, 8, 16)):
        tp = np.arange(128)[:, None]
        t = np.arange(128)[None, :]
        A[:, gi, 0, :] = ((t - tp >= 0) & (t - tp < w))
        A[:, gi, 1, :] = ((t + 128 - tp >= 0) & (t + 128 - tp < w))
    c["Aband"] = A.reshape(128, 1024)
    c["qrow"] = np.arange(128, dtype=np.float32).reshape(1, 128)
    return c


_PROG = {}


def kernel(**inputs):
    x = np.asarray(inputs["x"], dtype=np.float32)
    mem = np.asarray(inputs["mem"], dtype=np.float32)
    positions = np.asarray(inputs["positions"]).astype(np.int32)
    if inputs.get("_return_maps"):
        nc = None
    else:
        if "nc" not in _PROG:
            _PROG["nc"] = build()
        nc = _PROG["nc"]
    consts = _consts()
    wnames = ["pre_mix_g", "w_in", "pool_w", "pool_scale", "cmp_pe", "cmp_w1", "cmp_w2", "mem_norm_g", "w_mem_kv",
              "w_br_pool", "w_br_nsa", "w_br_xa", "w_out", "post_mix_g", "pre_ffn_g", "w_up", "conv_w", "conv_b",
              "w_down", "post_ffn_g"]
    shared = {n: np.ascontiguousarray(np.asarray(inputs[n], dtype=np.float32)) for n in wnames}
    in_maps = []
    for core in range(8):
        b, r = core // 4, core % 4
        m = dict(shared)
        m.update(consts)
        m["x_all"] = np.ascontiguousarray(x[b])
        m["mem"] = np.ascontiguousarray(mem[b])
        xe = np.zeros((NEXT * 128, 1024), np.float32)
        pe = np.zeros((NEXT, 128), np.int32)
        tq = np.zeros((NEXT, 128), np.float32)
        wv = np.zeros((1, NEXT), np.float32)
        ic = np.ones((128, NEXT, 4), np.float32)
        tst = np.zeros((1, NEXT), np.float32)
        for e in range(NEXT):
            ge = 16 * r - 5 + e
            tq[e] = ge * 128 + np.arange(128)
            tst[0, e] = ge * 128
            if ge >= 0:
                xe[e * 128:(e + 1) * 128] = x[b, ge * 128:(ge + 1) * 128]
                pe[e] = positions[b, ge * 128:(ge + 1) * 128]
                wv[0, e] = 1.0
                t = ge * 128 + np.arange(128)
                for gi, w in enumerate((2, 4, 8, 16)):
                    ic[:, e, gi] = 1.0 / np.minimum(t + 1, w).astype(np.float32)
        m["x_ext"] = xe
        m["pos_ext"] = np.ascontiguousarray(pe.T)
        m["pos_all"] = np.ascontiguousarray(positions[b].reshape(NALL, 128).T)
        m["tq_ext"] = np.ascontiguousarray(tq.T)
        m["tqstart"] = tst
        m["wvalid"] = wv
        m["invcnt"] = np.ascontiguousarray(ic.reshape(128, NEXT * 4))
        m["hflag"] = np.array([[0.0 if r == 0 else 1.0]], np.float32)
        in_maps.append(m)
    if inputs.get("_return_maps"):
        return in_maps
    res = run_bass_kernel_spmd(nc, in_maps, core_ids=list(range(8)))
    out = np.zeros((2, 8192, 1024), np.float32)
    for core in range(8):
        b, r = core // 4, core % 4
        out[b, r * 2048:(r + 1) * 2048] = res.results[core]["out"]
    return out
```

```python
import numpy as np
from contextlib import ExitStack
import concourse.bass as bass
import concourse.mybir as mybir
from concourse.bass_utils import run_bass_kernel_spmd

F32 = mybir.dt.float32
BF16 = mybir.dt.bfloat16
I32 = mybir.dt.int32
AF = mybir.ActivationFunctionType
ALU = mybir.AluOpType
AX = mybir.AxisListType


import sys as _sys


def _where():
    f = _sys._getframe(2)
    out = []
    while f is not None and len(out) < 4:
        if f.f_code.co_name != "<lambda>":
            out.append(f.f_lineno)
        f = f.f_back
    return out


class Buf:
    __slots__ = ("name", "writers", "readers", "excl", "last")

    def __init__(self, name, excl=False):
        self.name = name
        self.writers = []
        self.readers = []
        self.excl = excl
        self.last = {}


class Ins:
    __slots__ = ("eng", "fn", "deps", "idx", "flag", "tok", "dma", "pre", "where")

    def __init__(self, eng, fn, dma):
        self.eng = eng
        self.fn = fn
        self.deps = []
        self.flag = False
        self.tok = None
        self.dma = dma
        self.pre = None


class Sched:
    ENGS = ("pe", "dve", "act", "pool", "sp")
    EPOCH = 8000
    NDMA = 24

    def __init__(self, nc, stack):
        self.nc = nc
        self.stack = stack
        self.q = {e: [] for e in self.ENGS}
        self.nbuf = 0

    def buf(self, name=None, excl=False):
        self.nbuf += 1
        return Buf(name or f"b{self.nbuf}", excl)

    def op(self, eng, fn, r=(), w=(), dma=False):
        ins = Ins(eng, fn, dma)
        ins.where = _where()
        deps = []
        for b in r:
            deps.extend(b.writers)
        for b in w:
            deps.extend(b.readers)
        for b in list(r) + list(w):
            if b.excl:
                for en, li in b.last.items():
                    if en != eng:
                        deps.append(li)
                b.last[eng] = ins
        for b in w:
            if b.readers or (b in r):
                b.writers = [ins]
                b.readers = []
            else:
                b.writers.append(ins)
                if len(b.writers) > 48:
                    b.writers = b.writers[-48:]
        for b in r:
            if b not in w:
                b.readers.append(ins)
                if len(b.readers) > 48:
                    b.readers = b.readers[-48:]
        seen = set()
        for d in deps:
            if d is ins or id(d) in seen:
                continue
            if d.eng == "pe" and eng == "pe" and not d.dma:
                continue
            seen.add(id(d))
            ins.deps.append(d)
            d.flag = True
        self.q[eng].append(ins)
        return ins

    def emit(self, final_waits=()):
        nc = self.nc
        stack = self.stack
        sems = {}
        for e in self.ENGS:
            n = 0
            for ins in self.q[e]:
                if ins.dma:
                    continue
                if ins.flag:
                    n += 1
                    ins.idx = n
            nep = (n + self.EPOCH - 1) // self.EPOCH
            sems[e] = [stack.enter_context(nc.semaphore(f"s_{e}_{k}")) for k in range(max(nep, 1))]
        dsems = [stack.enter_context(nc.semaphore(f"s_dma_{k}")) for k in range(self.NDMA)]
        duse = [0] * self.NDMA
        dma_engs = [e for e in self.ENGS if any(i.dma for i in self.q[e])]
        share = {}
        if dma_engs:
            per = self.NDMA // len(dma_engs)
            for k, e in enumerate(dma_engs):
                share[e] = list(range(k * per, (k + 1) * per))
        for e in dma_engs:
            j = 0
            for ins in self.q[e]:
                if not ins.dma:
                    continue
                s = share[e][j % len(share[e])]
                j += 1
                prev = duse[s]
                duse[s] += 1
                ins.tok = (dsems[s], 16 * duse[s])
                ins.pre = (dsems[s], 16 * prev) if prev > 0 else None
        for e in self.ENGS:
            for ins in self.q[e]:
                if ins.dma or not ins.flag:
                    continue
                k = (ins.idx - 1) // self.EPOCH
                ins.tok = (sems[e][k], (ins.idx - 1) % self.EPOCH + 1)
        engobj = {"pe": "tensor", "dve": "vector", "act": "scalar", "pool": "gpsimd", "sp": "sync"}
        stats = {}
        with nc.Block() as block:
            for e in self.ENGS:
                lst = self.q[e]

                def body(eng, lst=lst, e=e):
                    waited = {}
                    nw = 0

                    def wait(tok):
                        nonlocal nw
                        sem, val = tok
                        key = id(sem)
                        if waited.get(key, 0) >= val:
                            return
                        waited[key] = val
                        eng.wait_ge(sem, val)
                        nw += 1

                    for ins in lst:
                        if ins.pre is not None:
                            wait(ins.pre)
                        for d in ins.deps:
                            wait(d.tok)
                        try:
                            bi = ins.fn(eng)
                        except BaseException:
                            print("FAILED op recorded at lines", ins.where, flush=True)
                            raise
                        if ins.dma:
                            bi.then_inc(ins.tok[0], 16)
                        elif ins.flag:
                            bi.then_inc(ins.tok[0], 1)
                    if e == "sp":
                        for fw in final_waits:
                            wait(fw.tok)
                    stats[e] = (len(lst), nw)

                getattr(block, engobj[e])(body)
        return stats


import os as _os
DUMPX = int(_os.environ.get('DUMPX', '0'))
NOBAR = int(_os.environ.get('NOBAR', '0'))
NEXT = 21
NALL = 64
BIG = 100.0
TWO_PI = float(2 * np.pi)


def build(dbg=None, ntile_b1=NEXT, na=NALL, skipcmp=False):
    nc = bass.Bass("TRN2", target_bir_lowering=False)

    def din(name, shape, dt=F32):
        return nc.dram_tensor(name, list(shape), dt, kind="ExternalInput").ap()

    x_all = din("x_all", [8192, 1024])
    x_ext = din("x_ext", [NEXT * 128, 1024])
    pos_all = din("pos_all", [128, NALL], I32)
    pos_ext = din("pos_ext", [128, NEXT], I32)
    tq_ext = din("tq_ext", [128, NEXT])
    qrow_d = din("qrow", [1, 128])
    tqst_d = din("tqstart", [1, NEXT])
    wvalid_d = din("wvalid", [1, NEXT])
    invcnt_d = din("invcnt", [128, NEXT * 4])
    hflag_d = din("hflag", [1, 1])
    mem_d = din("mem", [256, 1024])
    ident_d = din("ident", [128, 128])
    G_d = din("Gm", [128, 8192])
    invf_d = din("invf", [1, 8])
    bsrow_d = din("bsrow", [1, 128])
    cendrow_d = din("cendrow", [1, 512])
    kidx_d = din("kidx", [128, 64])
    cendcol_d = din("cendcol", [128, 4])
    Mw_d = din("Mw", [128, 256])
    e0_d = din("e0big", [1, 128])
    Ab_d = din("Aband", [128, 1024])
    pre_mix_g = din("pre_mix_g", [1, 1024])
    w_in = din("w_in", [1, 1024, 5936])
    pool_w = din("pool_w", [1, 4, 128, 128])
    pool_scale = din("pool_scale", [1, 512])
    cmp_pe = din("cmp_pe", [1, 2, 32, 64])
    cmp_w1 = din("cmp_w1", [1, 2, 2048, 256])
    cmp_w2 = din("cmp_w2", [1, 2, 256, 64])
    mem_norm_g = din("mem_norm_g", [1, 1024])
    w_mem_kv = din("w_mem_kv", [1, 1024, 1024])
    w_br_pool = din("w_br_pool", [1, 512, 1024])
    w_br_nsa = din("w_br_nsa", [1, 1024, 1024])
    w_br_xa = din("w_br_xa", [1, 512, 1024])
    w_out = din("w_out", [1, 1024, 1024])
    post_mix_g = din("post_mix_g", [1, 1024])
    pre_ffn_g = din("pre_ffn_g", [1, 1024])
    w_up = din("w_up", [1, 1024, 5632])
    conv_w = din("conv_w", [1, 3, 5632])
    conv_b = din("conv_b", [1, 5632])
    w_down = din("w_down", [1, 2816, 1024])
    post_ffn_g = din("post_ffn_g", [1, 1024])
    out_d = nc.dram_tensor("out", [16 * 128, 1024], F32, kind="ExternalOutput").ap()
    br_scr = nc.dram_tensor("br_scr", [17, 128, 24 * 128], BF16, kind="Internal").ap()
    x1_scr = nc.dram_tensor("x1_scr", [17, 128, 1024], F32, kind="Internal").ap()
    dbg_out = {}

    def dbg_t(name, shape, dt=F32):
        return nc.dram_tensor("dbg_" + name, list(shape), dt, kind="ExternalOutput").ap()

    with ExitStack() as st:
        S = Sched(nc, st)
        ARN = 95800
        arena = st.enter_context(nc.sbuf_tensor("arena", [128, ARN], BF16))
        PS = [st.enter_context(nc.psum_tensor(f"ps{i}", [128, 512], F32)) for i in range(8)]
        PB = [S.buf(f"ps{i}", excl=True) for i in range(8)]
        top = [0]

        def abf(n):
            a = arena[:, top[0]:top[0] + n]
            top[0] += (n + 31) // 32 * 32
            assert top[0] <= ARN, top[0]
            return a

        def af32(n):
            a = arena[:, top[0]:top[0] + 2 * n].bitcast(F32)
            top[0] += (n + 15) // 16 * 32
            assert top[0] <= ARN, top[0]
            return a

        def ai32(n):
            a = arena[:, top[0]:top[0] + 2 * n].bitcast(I32)
            top[0] += (n + 15) // 16 * 32
            assert top[0] <= ARN, top[0]
            return a

        def mm(out, lhsT, rhs, r, w, start=True, stop=True):
            return S.op("pe", lambda e: e.matmul(out, lhsT=lhsT, rhs=rhs, start=start, stop=stop,
                                                 skip_group_check=True), r, w)

        last_func = [None]

        def act(out, in_, func, r, w, bias=None, scale=None, accum=None):
            if func != last_func[0]:
                last_func[0] = func
                S.op("act", lambda e: e.activation(out=bsc[1][:, 0:1], in_=bsc[3][:, 0:1], func=func), [B_bsc3], [])
            kw = {}
            if bias is not None:
                kw["bias"] = bias
            if scale is not None:
                kw["scale"] = scale
            if accum is not None:
                kw["accum_out"] = accum
            return S.op("act", lambda e: e.activation(out=out, in_=in_, func=func, **kw), r, w)

        def ts(eng, out, in0, s1, s2, op0, op1, r, w):
            if s2 is None:
                return S.op(eng, lambda e: e.tensor_scalar(out=out, in0=in0, scalar1=s1, scalar2=None, op0=op0), r, w)
            return S.op(eng, lambda e: e.tensor_scalar(out=out, in0=in0, scalar1=s1, scalar2=s2, op0=op0, op1=op1), r, w)

        def tt(eng, out, in0, in1, op, r, w):
            return S.op(eng, lambda e: e.tensor_tensor(out=out, in0=in0, in1=in1, op=op), r, w)

        def stt(out, in0, scalar, in1, op0, op1, r, w, accum=None):
            if accum is None:
                return S.op("dve", lambda e: e.scalar_tensor_tensor(out=out, in0=in0, scalar=scalar, in1=in1, op0=op0, op1=op1), r, w)
            return S.op("dve", lambda e: e.scalar_tensor_tensor(out=out, in0=in0, scalar=scalar, in1=in1, op0=op0, op1=op1,
                                                                accum_out=accum), r, w)

        def cp(eng, out, in_, r, w, scale=None):
            if eng == "act":
                return act(out, in_, AF.Copy, r, w, scale=scale)
            if scale is not None:
                return ts(eng, out, in_, scale, None, ALU.mult, None, r, w)
            return S.op(eng, lambda e: e.tensor_copy(out=out, in_=in_), r, w)

        def memset(eng, ap, val, w):
            return S.op(eng, lambda e: e.memset(ap, val), (), w)

        dmas = []

        def dma(out, in_, r, w, slow=False):
            if slow:
                i = S.op("sp", lambda e: e.dma_start(out=out, in_=in_, allow_slow_non_contiguous=True), r, w, dma=True)
            else:
                i = S.op("sp", lambda e: e.dma_start(out=out, in_=in_), r, w, dma=True)
            dmas.append(i)
            return i

        bsc = [af32(16) for _ in range(4)]
        bbuf = {e: S.buf("bar_" + e) for e in S.ENGS}
        bar_scr = nc.dram_tensor("bar_scr", [128, 16], F32, kind="Internal").ap()

        def barrier():
            allb = list(bbuf.values())
            mm(PS[7][:, 0:1], ident_b[:, 0:128], ident_b[:, 0:1], [PB[7]], [bbuf["pe"], PB[7]])
            memset("dve", bsc[0], 0.0, [bbuf["dve"]])
            act(bsc[1], bsc[3], AF.Copy, [B_bsc3], [bbuf["act"]])
            memset("pool", bsc[2], 0.0, [bbuf["pool"]])
            i = S.op("sp", lambda e: e.dma_start(out=bar_scr, in_=bsc[3]), [B_bsc3], [bbuf["sp"]], dma=True)
            for d in dmas:
                if d not in i.deps:
                    i.deps.append(d)
            dmas.clear()
            mm(PS[7][:, 0:1], ident_b[:, 0:128], ident_b[:, 0:1], allb + [PB[7]], [PB[7]])
            S.op("dve", lambda e: e.memset(bsc[0], 0.0), allb, [])
            act(bsc[1], bsc[3], AF.Copy, allb + [B_bsc3], [])
            S.op("pool", lambda e: e.memset(bsc[2], 0.0), allb, [])
            S.op("sp", lambda e: e.dma_start(out=bar_scr, in_=bsc[3]), allb + [B_bsc3], [], dma=True)

        ident_b = abf(128)
        B_ident = S.buf("ident")
        stage = [af32(1024), af32(1024)]
        B_stage = [S.buf("stg0"), S.buf("stg1")]
        stg_i = [0]
        cast_i = [0]
        B_bsc3 = S.buf("bsc3")
        memset("dve", bsc[3], 0.0, [B_bsc3])

        def loadw(dst_fn, src, ncols, scale=None, r_extra=(), w=None, perm_q=False):
            for c0 in range(0, ncols, 1024):
                n = min(1024, ncols - c0)
                k = stg_i[0] % 2
                stg_i[0] += 1
                np_ = src.shape[0]
                sl = stage[k][0:np_, 0:n]
                dma(sl, src[:, c0:c0 + n], [], [B_stage[k]])
                eng = ("pool", "dve")[cast_i[0] % 2]
                cast_i[0] += 1
                if perm_q:
                    for g in range(2):
                        o = dst_fn(c0, n).rearrange("p (c g d) -> p c g d", c=8, g=2)[:, :, g, :]
                        i_ = stage[k][0:np_, g * 512:(g + 1) * 512].rearrange("p (c d) -> p c d", c=8)
                        if scale is not None:
                            ts(eng, o, i_, scale, None, ALU.mult, None, [B_stage[k]] + list(r_extra), w)
                        else:
                            cp(eng, o, i_, [B_stage[k]] + list(r_extra), w)
                else:
                    if scale is not None:
                        ts(eng, dst_fn(c0, n), sl, scale, None, ALU.mult, None, [B_stage[k]] + list(r_extra), w)
                    else:
                        cp(eng, dst_fn(c0, n), sl, [B_stage[k]] + list(r_extra), w)

        def tr(out_ps, in_sb, r, w, start=True):
            return mm(out_ps, in_sb, ident_b[0:in_sb.shape[0], 0:in_sb.shape[0]], list(r) + [B_ident], w)

        dma(stage[0][:, 0:128], ident_d, [], [B_stage[0]])
        cp("dve", ident_b, stage[0][:, 0:128], [B_stage[0]], [B_ident])

        gpre = af32(8)
        gmem = af32(8)
        gffn = af32(8)
        B_small = S.buf("small")
        dma(gpre, pre_mix_g[0].rearrange("(k p) -> p k", p=128), [], [B_small], slow=True)
        dma(gmem, mem_norm_g[0].rearrange("(k p) -> p k", p=128), [], [B_small], slow=True)
        dma(gffn, pre_ffn_g[0].rearrange("(k p) -> p k", p=128), [], [B_small], slow=True)
        tqe = af32(NEXT)
        dma(tqe, tq_ext, [], [B_small])
        wval = af32(NEXT)
        dma(wval, wvalid_d.partition_broadcast(128), [], [B_small])
        hflag = af32(1)
        dma(hflag, hflag_d.partition_broadcast(128), [], [B_small])
        invcnt = af32(NEXT * 4)
        dma(invcnt, invcnt_d, [], [B_small])
        ss2 = af32(4)
        B_ss2 = S.buf("ss2")
        persist_top = top[0]

        def rms_scale(xt, xn, Bx, Bxn, ss, junk, Bss):
            act(junk, xt, AF.Square, [Bx], [Bss], accum=ss)
            ts("dve", ss, ss, 1.0 / 1024, 1e-6, ALU.mult, ALU.add, [Bss], [Bss])
            act(ss, ss, AF.Sqrt, [Bss], [Bss])
            S.op("dve", lambda e: e.reciprocal(out=ss, in_=ss), [Bss], [Bss])
            ts("dve", xn, xt, ss, None, ALU.mult, None, [Bx, Bss], [Bxn])

        def sincos(ang, n, osin, ocos, tmp, Bt, Bo):
            t, kf, g, ki = tmp
            ts("dve", ang, ang, 1.0 / TWO_PI, None, ALU.mult, None, [Bt], [Bt])
            for dst, off in ((osin, 0.0), (ocos, 0.25)):
                ts("dve", t, ang, off, None, ALU.add, None, [Bt], [Bt])
                cp("dve", ki, t, [Bt], [Bt])
                cp("dve", kf, ki, [Bt], [Bt])
                tt("dve", t, t, kf, ALU.subtract, [Bt], [Bt])
                ts("dve", g, t, 0.5, None, ALU.is_gt, None, [Bt], [Bt])
                tt("dve", t, t, g, ALU.subtract, [Bt], [Bt])
                ts("dve", g, t, -0.5, None, ALU.is_lt, None, [Bt], [Bt])
                tt("dve", t, t, g, ALU.add, [Bt], [Bt])
                act(dst, t, AF.Sin, [Bt], [Bo], scale=TWO_PI)

        def rotary(src4, dst4, cs, sn, tmp, rB, wB, Bt):
            a, b = src4.shape[1], src4.shape[2]
            n = a * b * 8
            x1 = src4[:, :, :, 0:8]
            x2 = src4[:, :, :, 8:16]
            csb = cs.unsqueeze(1).unsqueeze(1).to_broadcast([128, a, b, 8])
            snb = sn.unsqueeze(1).unsqueeze(1).to_broadcast([128, a, b, 8])
            t1 = tmp[:, 0:n].rearrange("p (a b d) -> p a b d", a=a, b=b)
            t2 = tmp[:, n:2 * n].rearrange("p (a b d) -> p a b d", a=a, b=b)
            tt("dve", t1, x1, csb, ALU.mult, rB, [Bt])
            tt("dve", t2, x2, snb, ALU.mult, rB, [Bt])
            tt("dve", dst4[:, :, :, 0:8], t1, t2, ALU.subtract, [Bt], wB)
            tt("dve", t1, x2, csb, ALU.mult, rB, [Bt])
            tt("dve", t2, x1, snb, ALU.mult, rB, [Bt])
            tt("dve", dst4[:, :, :, 8:16], t1, t2, ALU.add, [Bt], wB)

        KTs = abf(8192)
        Vs = abf(64 * 2 * 65).rearrange("p (t g d) -> p t g d", t=64, g=2)
        KTw = abf(NEXT * 128)
        Vw = abf(NEXT * 2 * 65).rearrange("p (t g d) -> p t g d", t=NEXT, g=2)
        KcT = abf(512)
        Vc = abf(4 * 2 * 65).rearrange("p (t g d) -> p t g d", t=4, g=2)
        Gb = abf(8192)
        B_KTs, B_Vs, B_KTw, B_Vw, B_KcT, B_Vc, B_G = [S.buf(n) for n in "KTs Vs KTw Vw KcT Vc G".split()]
        cosE = af32(NEXT * 8).rearrange("p (t f) -> p t f", f=8)
        sinE = af32(NEXT * 8).rearrange("p (t f) -> p t f", f=8)
        cos8 = af32(NEXT * 8).rearrange("p (t f) -> p t f", f=8)
        sin8 = af32(NEXT * 8).rearrange("p (t f) -> p t f", f=8)
        B_tabE = S.buf("tabE")
        kv_top = top[0]

        memset("pool", Vs, 1.0, [B_Vs])
        memset("pool", Vw, 1.0, [B_Vw])
        memset("pool", Vc, 0.0, [B_Vc])
        memset("pool", Vc[:, :, :, 64:65], 1.0, [B_Vc])
        loadw(lambda c0, n: Gb[:, c0:c0 + n], G_d, 8192, w=[B_G])

        cosA = af32(NALL * 8).rearrange("p (t f) -> p t f", f=8)
        sinA = af32(NALL * 8).rearrange("p (t f) -> p t f", f=8)
        B_tabA = S.buf("tabA")
        KcRaw = abf(8192)
        VcRaw = abf(8192)
        B_KcRaw, B_VcRaw = S.buf("KcRaw"), S.buf("VcRaw")
        wkvA = abf(8 * 512).rearrange("p (k c) -> p k c", k=8)
        B_wkvA = S.buf("wkvA")
        for k in range(8):
            loadw(lambda c0, n, k=k: wkvA[:, k, c0:c0 + n], w_in[0][k * 128:(k + 1) * 128, 1536:2048], 512,
                  scale=gpre[:, k:k + 1], r_extra=[B_small], w=[B_wkvA])
        posi = ai32(512)
        posf = af32(512)
        invf = af32(8)
        ang = af32(512)
        tmp3 = (af32(512), af32(512), af32(512), ai32(512))
        B_t = S.buf("tabtmp")
        dma(invf, invf_d.partition_broadcast(128), [], [B_t])
        dma(posi[:, 0:NALL], pos_all, [], [B_t])
        cp("dve", posf[:, 0:NALL], posi[:, 0:NALL], [B_t], [B_t])
        tt("dve", ang.rearrange("p (t f) -> p t f", f=8), posf[:, 0:NALL].unsqueeze(2).to_broadcast([128, NALL, 8]),
           invf.unsqueeze(1).to_broadcast([128, NALL, 8]), ALU.mult, [B_t], [B_t])
        sincos(ang, 512, sinA.rearrange("p t f -> p (t f)"), cosA.rearrange("p t f -> p (t f)"),
               tmp3, B_t, B_tabA)
        dma(posi[:, 0:NEXT], pos_ext, [B_t], [B_t])
        cp("dve", posf[:, 0:NEXT], posi[:, 0:NEXT], [B_t], [B_t])
        ne = NEXT * 8
        tt("dve", ang[:, 0:ne].rearrange("p (t f) -> p t f", f=8), posf[:, 0:NEXT].unsqueeze(2).to_broadcast([128, NEXT, 8]),
           invf.unsqueeze(1).to_broadcast([128, NEXT, 8]), ALU.mult, [B_t], [B_t])
        sincos(ang[:, 0:ne], ne, sinE.rearrange("p t f -> p (t f)"), cosE.rearrange("p t f -> p (t f)"),
               tuple(a[:, 0:ne] for a in tmp3), B_t, B_tabE)
        ts("dve", cos8.rearrange("p t f -> p (t f)"), cosE.rearrange("p t f -> p (t f)"), 0.125, None, ALU.mult, None, [B_tabE], [B_tabE])
        ts("dve", sin8.rearrange("p t f -> p (t f)"), sinE.rearrange("p t f -> p (t f)"), 0.125, None, ALU.mult, None, [B_tabE], [B_tabE])

        if dbg == "0":
            f1 = dma(dbg_t("cosA", [128, NALL * 8]), cosA.rearrange("p t f -> p (t f)"), [B_tabA], [])
            f2 = dma(dbg_t("sinA", [128, NALL * 8]), sinA.rearrange("p t f -> p (t f)"), [B_tabA], [])
            f3 = dma(dbg_t("cos8", [128, NEXT * 8]), cos8.rearrange("p t f -> p (t f)"), [B_tabE], [])
            f4 = dma(dbg_t("Gb", [128, 8192], BF16), Gb, [B_G], [])
            print(S.emit(final_waits=[f1, f2, f3, f4]))
            return nc
        xt2 = [af32(1024), af32(1024)]
        B_xt = [S.buf("xt0"), S.buf("xt1")]
        xn = abf(1024)
        B_xn = S.buf("xn")
        junk = abf(1024)
        ssA = af32(1)
        B_ss = S.buf("ss")
        hT = abf(1024).rearrange("p (k t) -> p k t", k=8)
        B_hT = S.buf("hT")
        kb = abf(512)
        B_kb = S.buf("kb")
        rtmp = af32(2 * 16 * 8)
        B_rt = S.buf("rtmp")

        def norm_T(xt, Bx, pa, pb, hTd, BhT):
            rms_scale(xt, xn, Bx, B_xn, ssA, junk, B_ss)
            for k in range(8):
                bank = pa if k < 4 else pb
                tr(PS[bank][:, (k % 4) * 128:(k % 4 + 1) * 128], xn[:, k * 128:(k + 1) * 128], [B_xn], [PB[bank]])
            cp("act", hTd[:, 0:4, :], PS[pa][:, :].rearrange("p (k t) -> p k t", k=4), [PB[pa]], [BhT])
            cp("dve", hTd[:, 4:8, :], PS[pb][:, :].rearrange("p (k t) -> p k t", k=4), [PB[pb]], [BhT])

        dma(xt2[0], x_all[0:128, :], [], [B_xt[0]])
        for T in range(na):
            s = T % 2
            if T + 1 < NALL:
                dma(xt2[1 - s], x_all[(T + 1) * 128:(T + 2) * 128, :], [], [B_xt[1 - s]])
            norm_T(xt2[s], B_xt[s], 0, 1, hT, B_hT)
            for k in range(8):
                mm(PS[2][:, 0:512], hT[:, k, :], wkvA[:, k, :], [B_hT, B_wkvA], [PB[2]], start=(k == 0), stop=(k == 7))
            cp("act", kb, PS[2][:, 0:512], [PB[2]], [B_kb])
            v5 = PS[2][:, 0:512].rearrange("p (j2 jj g d) -> p j2 jj g d", j2=2, jj=2, g=2)
            k5 = kb.rearrange("p (j2 jj g d) -> p j2 jj g d", j2=2, jj=2, g=2)
            rotary(v5[:, :, 0, :, :], k5[:, :, 0, :, :], cosA[:, T, :], sinA[:, T, :], rtmp, [PB[2], B_tabA], [B_kb], B_rt)
            cp("pool", Vs[:, T, :, 0:64], kb[:, 384:512].rearrange("p (g d) -> p g d", g=2), [B_kb], [B_Vs])
            tr(PS[3][:, 0:128], kb[:, 0:128], [B_kb], [PB[3]])
            tr(PS[3][:, 128:256], kb[:, 128:256], [B_kb], [PB[3]])
            tr(PS[3][:, 256:384], kb[:, 256:384], [B_kb], [PB[3]])
            cp("act", KcRaw[:, T * 128:(T + 1) * 128], PS[3][:, 0:128], [PB[3]], [B_KcRaw])
            cp("dve", VcRaw[:, T * 128:(T + 1) * 128], PS[3][:, 128:256], [PB[3]], [B_VcRaw])
            cp("act", KTs[:, T * 128:(T + 1) * 128], PS[3][:, 256:384], [PB[3]], [B_KTs])

        w1z = [abf(32 * 256).rearrange("p (l m) -> p l m", l=32) for _ in range(2)]
        B_w1 = S.buf("w1b")
        memset("pool", w1z[0][64:128, :, :], 0.0, [B_w1])
        memset("pool", w1z[1][0:64, :, :], 0.0, [B_w1])
        w2b = abf(2 * 128).rearrange("p (h d) -> p h d", h=2)
        B_w2 = S.buf("w2b")
        peT = abf(32)
        pef = af32(32)
        B_pe = S.buf("pe")
        hid2 = [abf(2 * 512).rearrange("p (h c) -> p h c", h=2) for _ in range(2)]
        B_hid2 = [S.buf("hid0"), S.buf("hid1")]
        cbias = af32(2)
        B_cb = S.buf("cbias")
        for kvi, raw, Braw in (() if skipcmp else ((0, KcRaw, B_KcRaw), (1, VcRaw, B_VcRaw))):
            w1v = cmp_w1[0][kvi].rearrange("(l d) m -> d l m", d=64)
            for half in range(2):
                for l0 in range(0, 32, 4):
                    k = stg_i[0] % 2
                    stg_i[0] += 1
                    sl = stage[k][64 * half:64 * half + 64, 0:1024]
                    dma(sl.rearrange("p (l m) -> p l m", l=4), w1v[:, l0:l0 + 4, :], [], [B_stage[k]])
                    cp(("pool", "dve")[(l0 // 4) % 2], w1z[half][64 * half:64 * half + 64, l0:l0 + 4, :],
                       sl.rearrange("p (l m) -> p l m", l=4), [B_stage[k]], [B_w1])
            k = stg_i[0] % 2
            stg_i[0] += 1
            dma(stage[k][:, 0:128].rearrange("p (h d) -> p h d", h=2), cmp_w2[0][kvi].rearrange("(h p) d -> p h d", p=128),
                [], [B_stage[k]])
            cp("dve", w2b[:, :, 0:64], stage[k][:, 0:128].rearrange("p (h d) -> p h d", h=2), [B_stage[k]], [B_w2])
            cp("dve", w2b[:, :, 64:128], stage[k][:, 0:128].rearrange("p (h d) -> p h d", h=2), [B_stage[k]], [B_w2])
            dma(pef[0:64, :], cmp_pe[0][kvi].rearrange("l d -> d l"), [], [B_pe], slow=True)
            memset("dve", peT[64:128, :], 0.0, [B_pe])
            cp("dve", peT[0:64, :], pef[0:64, :], [B_pe], [B_pe])
            for half in range(2):
                for l in range(32):
                    mm(PS[4][:, half:half + 1], w1z[0][:, l, half * 128:(half + 1) * 128], peT[:, l:l + 1],
                       [B_w1, B_pe], [PB[4]], start=(l == 0 and half == 0), stop=(l == 31))
            cp("dve", cbias, PS[4][:, 0:2], [PB[4]], [B_cb])
            if not NOBAR:
                barrier()
            if dbg == "A" and kvi == 0 and (DUMPX & 1):
                dma(dbg_t("w1b", [128, 8192], BF16), w1z[0].rearrange("p l m -> p (l m)"), [B_w1], [])
                dma(dbg_t("cbias", [128, 2]), cbias, [B_cb], [])
            rawv = raw.rearrange("p (i s) -> p i s", s=16)
            for g in range(2):
                if not NOBAR:
                    barrier()
                hid, B_hid = hid2[g], B_hid2[g]
                pr = slice(64 * g, 64 * g + 64)
                for half in range(2):
                    for l in range(32):
                        rhs = rawv[:, 0:511, l] if l < 16 else rawv[:, 1:512, l - 16]
                        mm(PS[half][:, 0:511], w1z[g][:, l, half * 128:(half + 1) * 128], rhs, [B_w1, Braw], [PB[half]],
                           start=(l == 0), stop=(l == 31))
                    act(hid[:, half, 0:511], PS[half][:, 0:511], AF.Gelu_apprx_tanh, [PB[half], B_cb], [B_hid],
                        bias=cbias[:, half:half + 1])
                if dbg == "A" and kvi == 0 and (DUMPX & 2):
                    dma(dbg_t(f"hid{g}", [128, 1024], BF16), hid.rearrange("p h c -> p (h c)"), [B_hid], [])
                if kvi == 0:
                    for half in range(2):
                        mm(PS[2][:, 0:511], w2b[:, half, :], hid[:, half, 0:511], [B_w2, B_hid], [PB[2]],
                           start=(half == 0), stop=(half == 1))
                    cp("act", KcT[pr, 0:511], PS[2][pr, 0:511], [PB[2]], [B_KcT])
                else:
                    for c in range(4):
                        m = 128 if c < 3 else 127
                        for half in range(2):
                            mm(PS[2][0:m, c * 64:(c + 1) * 64], hid[:, half, c * 128:c * 128 + m], w2b[:, half, 0:64],
                               [B_w2, B_hid], [PB[2]], start=(half == 0 and c == 0), stop=(half == 1))
                    for c in range(4):
                        m = 128 if c < 3 else 127
                        cp("act", Vc[0:m, c, g, 0:64], PS[2][0:m, c * 64:(c + 1) * 64], [PB[2]], [B_Vc])
        memset("dve", KcT[:, 511:512], 0.0, [B_KcT])
        if dbg == "A":
            fl = [dma(dbg_t("KTs", [128, 8192], BF16), KTs, [B_KTs], []),
                  dma(dbg_t("Vs", [128, 64 * 130], BF16), Vs.rearrange("p t g d -> p (t g d)"), [B_Vs], []),
                  dma(dbg_t("KcT", [128, 512], BF16), KcT, [B_KcT], []),
                  dma(dbg_t("Vc", [128, 4 * 130], BF16), Vc.rearrange("p t g d -> p (t g d)"), [B_Vc], []),
                  dma(dbg_t("KcRaw", [128, 8192], BF16), KcRaw, [B_KcRaw], [])]
            print(S.emit(final_waits=fl))
            return nc
        barrier()
        top[0] = kv_top
        wb1 = abf(8 * 2352).rearrange("p (k c) -> p k c", k=8)
        B_wb1 = S.buf("wb1")
        for k in range(8):
            rows = w_in[0][k * 128:(k + 1) * 128, :]
            sc = gpre[:, k:k + 1]
            loadw(lambda c0, n, k=k: wb1[:, k, c0:c0 + n], rows[:, 0:512], 512, scale=sc, r_extra=[B_small], w=[B_wb1])
            loadw(lambda c0, n, k=k: wb1[:, k, 512:1536], rows[:, 512:1536], 1024, scale=sc, r_extra=[B_small], w=[B_wb1], perm_q=True)
            loadw(lambda c0, n, k=k: wb1[:, k, 1536 + c0:1536 + c0 + n], rows[:, 2048:2864], 816, scale=sc, r_extra=[B_small], w=[B_wb1])
        poolw = abf(4 * 128).rearrange("p (g d) -> p g d", g=4)
        B_cst = S.buf("cst")
        k_ = stg_i[0] % 2
        stg_i[0] += 1
        dma(stage[k_][:, 0:512].rearrange("p (g d) -> p g d", g=4), pool_w[0].rearrange("g c d -> c g d"), [], [B_stage[k_]])
        cp("dve", poolw, stage[k_][:, 0:512].rearrange("p (g d) -> p g d", g=4), [B_stage[k_]], [B_cst])
        pscale = af32(4)
        dma(pscale, pool_scale[0].rearrange("(g d) -> d g", d=128), [], [B_cst], slow=True)
        Ab = abf(1024).rearrange("p (g c t) -> p g c t", g=4, c=2)
        k_ = stg_i[0] % 2
        stg_i[0] += 1
        dma(stage[k_][:, 0:1024], Ab_d, [], [B_stage[k_]])
        cp("dve", Ab.rearrange("p g c t -> p (g c t)"), stage[k_][:, 0:1024], [B_stage[k_]], [B_cst])
        Mw3 = abf(384).rearrange("p (j q) -> p j q", j=3)
        k_ = stg_i[0] % 2
        stg_i[0] += 1
        dma(stage[k_][:, 0:256], Mw_d, [], [B_stage[k_]])
        cp("dve", Mw3[:, 0, :], stage[k_][:, 0:128], [B_stage[k_]], [B_cst])
        cp("dve", Mw3[:, 2, :], stage[k_][:, 128:256], [B_stage[k_]], [B_cst])
        memset("dve", Mw3[:, 1, :], 1.0, [B_cst])
        onesb = abf(1)
        memset("dve", onesb, 1.0, [B_cst])
        qrow = af32(128)
        dma(qrow, qrow_d.partition_broadcast(128), [], [B_cst])
        tqst = af32(NEXT)
        dma(tqst, tqst_d.partition_broadcast(128), [], [B_cst])
        tqt = af32(128)
        B_tqt = S.buf("tqt")
        bsrow = af32(128)
        dma(bsrow, bsrow_d.partition_broadcast(128), [], [B_cst])
        cendrow = af32(512)
        dma(cendrow, cendrow_d.partition_broadcast(128), [], [B_cst])
        kidx = af32(64)
        dma(kidx, kidx_d, [], [B_cst])
        cendcol = af32(4)
        dma(cendcol, cendcol_d, [], [B_cst])
        e0big = af32(128)
        dma(e0big, e0_d.partition_broadcast(128), [], [B_cst])

        xt2 = [af32(1024), af32(1024)]
        B_xt = [S.buf("bxt0"), S.buf("bxt1")]
        xn = abf(1024)
        B_xn = S.buf("bxn")
        junk = abf(1024)
        ssA = af32(1)
        B_ss = S.buf("bss")
        brT0 = abf(24 * 128).rearrange("p (k t) -> p k t", k=24)
        brT = [brT0, brT0]
        B_br0 = S.buf("br0")
        B_br = [B_br0, B_br0]
        rtmp = af32(256)
        B_rt = S.buf("brtmp")

        KmT = abf(4 * 256).rearrange("p (h m) -> p h m", h=4)
        Vm = abf(2 * 4 * 128).rearrange("p (c h d) -> p c h d", c=2, h=4)
        kmb = abf(512)
        mark_m = top[0]
        wmem = abf(8 * 1024).rearrange("p (k c) -> p k c", k=8)
        B_wmem = S.buf("wmem")
        for k in range(8):
            loadw(lambda c0, n, k=k: wmem[:, k, c0:c0 + n], w_mem_kv[0][k * 128:(k + 1) * 128, :], 1024,
                  scale=gmem[:, k:k + 1], r_extra=[B_small], w=[B_wmem])
        B_km, B_vm, B_kmb = S.buf("KmT"), S.buf("Vm"), S.buf("kmb")
        for c in range(2):
            dma(xt2[c], mem_d[c * 128:(c + 1) * 128, :], [], [B_xt[c]])
            norm_T(xt2[c], B_xt[c], 0, 1, brT[c][:, 16:24, :], B_br[c])
            for nb in range(2):
                for k in range(8):
                    mm(PS[2 + nb][:, :], brT[c][:, 16 + k, :], wmem[:, k, nb * 512:(nb + 1) * 512], [B_br[c], B_wmem], [PB[2 + nb]],
                       start=(k == 0), stop=(k == 7))
            cp("act", kmb, PS[2][:, :], [PB[2]], [B_kmb], scale=float(128 ** -0.5))
            cp("dve", Vm[:, c, :, :], PS[3][:, :].rearrange("p (h d) -> p h d", h=4), [PB[3]], [B_vm])
            for h in range(4):
                tr(PS[4][:, h * 128:(h + 1) * 128], kmb[:, h * 128:(h + 1) * 128], [B_kmb], [PB[4]])
            cp("act", KmT[:, :, c * 128:(c + 1) * 128], PS[4][:, :].rearrange("p (h m) -> p h m", h=4), [PB[4]], [B_km])

        barrier()
        top[0] = mark_m
        kwb = abf(256)
        B_kwb = S.buf("kwb")
        ub = [abf(512), abf(512)]
        B_ub = [S.buf("ub0"), S.buf("ub1")]
        uf = af32(512)
        B_uf = S.buf("uf")
        pbb = abf(512)
        B_pbb = S.buf("pbb")
        pT = abf(512).rearrange("p (g t) -> p g t", g=4)
        B_pT = S.buf("pT")
        qb = abf(1024)
        B_qb = S.buf("qb")
        qTz = [abf(1024).rearrange("p (k t) -> p k t", k=8) for _ in range(2)]
        B_qT = S.buf("qT")
        memset("pool", qTz[0], 0.0, [B_qT])
        memset("pool", qTz[1], 0.0, [B_qT])
        gn = af32(48)
        B_gn = S.buf("gn")
        qxb = abf(512)
        B_qxb = S.buf("qxb")
        qxT = abf(512).rearrange("p (h t) -> p h t", h=4)
        B_qxT = S.buf("qxT")
        mpT = [abf(512).rearrange("p (h t) -> p h t", h=4) for _ in range(2)]
        B_mpT = [S.buf("mpT0"), S.buf("mpT1")]
        rsm = af32(4)
        B_rsm = S.buf("rsm")
        ymemb = abf(512)
        B_ymem = S.buf("ymem")
        ef0 = af32(512)
        ef = [ef0, ef0]
        B_ef0 = S.buf("ef0")
        B_ef = [B_ef0, B_ef0]
        em = af32(512)
        B_em = S.buf("em")
        ssum = af32(2)
        B_ssum = S.buf("ssum")
        Pb = [af32(516), af32(516)]
        B_Pb = [S.buf("Pb0"), S.buf("Pb1")]
        cmrow = af32(512)
        B_cm = S.buf("cmrow")
        imp = af32(128)
        nd = af32(128)
        itmp = af32(128)
        wk = af32(128)
        mx = af32(16)
        B_imp = S.buf("imp")
        selb = abf(128)
        B_sel = S.buf("sel")
        selT = [abf(128), abf(128)]
        B_selT = [S.buf("selT0"), S.buf("selT1")]
        NSL = 3
        NSU = 6
        ucount = [0]
        mk = [abf(128) for _ in range(NSL)]
        B_mk = [S.buf(f"mk{i}") for i in range(NSL)]
        pTu = [abf(512).rearrange("p (h t) -> p h t", h=4) for _ in range(NSU)]
        B_pTu = [S.buf(f"pTu{i}") for i in range(NSU)]
        ynsa = af32(1024)
        B_yn = S.buf("ynsa")
        ytmp = af32(256)
        B_yt = S.buf("ytmp")
        rs = af32(8)
        B_rs = S.buf("rs")
        ynb = abf(1024)
        B_ynb = S.buf("ynb")
        memset("dve", Pb[0], 0.0, [B_Pb[0]])
        memset("dve", Pb[1], 0.0, [B_Pb[1]])
        nchunk = [0]

        def attend(e, g, br, chunks):
            LOOK = 3
            units = [(n, q) for n in range(len(chunks)) for q in range(2)]
            nu = len(units)
            info = {}

            def stage_scores(u):
                n, q = units[u]
                KT, V, mk_pe, mk_dve = chunks[n]
                if q == 0:
                    cs = nchunk[0]
                    nchunk[0] += 1
                    info[n] = (cs % 2, cs % NSL)
                    if mk_pe is not None:
                        mk_pe(cs % 2)
                bank = 2 + (ucount[0] % 4)
                ub = ucount[0] % NSU
                ucount[0] += 1
                mm(PS[bank][:, :], KT, qTz[g][:, 4 * q:4 * q + 4, :], [B_qT, B_KTs, B_KTw, B_KcT], [PB[bank]])
                return (bank, ub)

            pend = [stage_scores(u) for u in range(min(LOOK, nu))]
            for u, (n, q) in enumerate(units):
                bank, ub = pend.pop(0)
                if u + LOOK < nu:
                    pend.append(stage_scores(u + LOOK))
                KT, V, mk_pe, mk_dve = chunks[n]
                sl, bs = info[n]
                pt = pTu[ub]
                act(pt, PS[bank][:, :].rearrange("p (h t) -> p h t", h=4), AF.Exp, [PB[bank]], [B_pTu[ub]])
                if q == 0:
                    mk_dve(sl, bs)
                mb = mk[bs].unsqueeze(1).to_broadcast([128, 4, 128])
                tt(("dve", "pool")[q], pt, pt, mb, ALU.mult, [B_pTu[ub], B_mk[bs]], [B_pTu[ub]])
                for hh in range(4 * q, 4 * q + 4):
                    mm(PS[q][:, (hh % 4) * 128:(hh % 4) * 128 + 65], pt[:, hh % 4, :], V, [B_pTu[ub], B_Vs, B_Vw, B_Vc], [PB[q]],
                       start=(n == 0 and hh % 4 == 0), stop=(n == len(chunks) - 1))
            gn3 = gn.rearrange("p (h b) -> p h b", b=3)
            for b in range(2):
                Ov = PS[b][:, :].rearrange("p (h d) -> p h d", h=4)
                h0 = 8 * g + 4 * b
                rsb = rs[:, 4 * b:4 * b + 4]
                ts("dve", rsb, Ov[:, :, 64], 1e-30, None, ALU.max, None, [PB[b]], [B_rs])
                S.op("dve", lambda e_, rsb=rsb: e_.reciprocal(out=rsb, in_=rsb), [B_rs], [B_rs])
                tt("dve", rsb, rsb, gn3[:, h0:h0 + 4, br], ALU.mult, [B_rs, B_gn], [B_rs])
                dst = ynsa.rearrange("p (h d) -> p h d", h=16)[:, h0:h0 + 4, :]
                rb = rsb.unsqueeze(2).to_broadcast([128, 4, 64])
                if br == 0:
                    tt("dve", dst, Ov[:, :, 0:64], rb, ALU.mult, [PB[b], B_rs], [B_yn])
                else:
                    yt = ytmp.rearrange("p (h d) -> p h d", h=4)
                    tt("dve", yt, Ov[:, :, 0:64], rb, ALU.mult, [PB[b], B_rs], [B_yt])
                    tt("pool", dst, dst, yt, ALU.add, [B_yn, B_yt], [B_yn])

        dma(xt2[0], x_ext[0:128, :], [], [B_xt[0]])
        for e in range(ntile_b1):
            s = e % 2
            if e + 1 < NEXT:
                dma(xt2[1 - s], x_ext[(e + 1) * 128:(e + 2) * 128, :], [], [B_xt[1 - s]])
            bt = brT[s]
            hTd = bt[:, 16:24, :]
            norm_T(xt2[s], B_xt[s], 0, 1, hTd, B_br[s])
            for k in range(8):
                mm(PS[2][:, 0:256], hTd[:, k, :], wb1[:, k, 1536:1792], [B_br[s], B_wb1], [PB[2]], start=(k == 0), stop=(k == 7))
            cp("act", kwb, PS[2][:, 0:256], [PB[2]], [B_kwb])
            rotary(PS[2][:, 0:128].rearrange("p (a g d) -> p a g d", a=1, g=2), kwb[:, 0:128].rearrange("p (a g d) -> p a g d", a=1, g=2),
                   cosE[:, e, :], sinE[:, e, :], rtmp, [PB[2], B_tabE], [B_kwb], B_rt)
            cp("pool", Vw[:, e, :, 0:64], kwb[:, 128:256].rearrange("p (g d) -> p g d", g=2), [B_kwb], [B_Vw])
            tr(PS[3][:, 0:128], kwb[:, 0:128], [B_kwb], [PB[3]])
            cp("act", KTw[:, e * 128:(e + 1) * 128], PS[3][:, 0:128], [PB[3]], [B_KTw])
            for k in range(8):
                mm(PS[4][:, :], hTd[:, k, :], wb1[:, k, 0:512], [B_br[s], B_wb1], [PB[4]], start=(k == 0), stop=(k == 7))
            cp("act", ub[s], PS[4][:, :], [PB[4]], [B_ub[s]])
            if e < 4:
                continue
            cp("dve", uf, PS[4][:, :], [PB[4]], [B_uf])
            i = e - 4
            for gi in range(4):
                blk = slice(gi * 128, (gi + 1) * 128)
                mm(PS[5][:, blk], Ab[:, gi, 0, :], ub[s][:, blk], [B_cst, B_ub[s]], [PB[5]], start=True, stop=False)
                mm(PS[5][:, blk], Ab[:, gi, 1, :], ub[1 - s][:, blk], [B_cst, B_ub[1 - s]], [PB[5]], start=False, stop=True)
            for gi in range(4):
                blk = slice(gi * 128, (gi + 1) * 128)
                stt(pbb[:, blk], PS[5][:, blk], invcnt[:, e * 4 + gi:e * 4 + gi + 1], uf[:, blk], ALU.mult, ALU.subtract,
                    [PB[5], B_uf, B_small], [B_pbb])
            for gi in range(4):
                blk = slice(gi * 128, (gi + 1) * 128)
                tr(PS[6][:, blk], pbb[:, blk], [B_pbb], [PB[6]])
            cp("act", pT, PS[6][:, :].rearrange("p (g t) -> p g t", g=4), [PB[6]], [B_pT])
            for gi in range(4):
                blk = slice(gi * 128, (gi + 1) * 128)
                mm(PS[5][:, blk], poolw[:, gi, :], pT[:, gi, :], [B_cst, B_pT], [PB[5]])
            for gi in range(4):
                blk = slice(gi * 128, (gi + 1) * 128)
                ts("dve", bt[:, gi, :], PS[5][:, blk], pscale[:, gi:gi + 1], None, ALU.mult, None, [PB[5], B_cst], [B_br[s]])
            for nb in range(2):
                for k in range(8):
                    mm(PS[nb][:, :], hTd[:, k, :], wb1[:, k, 512 + nb * 512:1024 + nb * 512], [B_br[s], B_wb1], [PB[nb]],
                       start=(k == 0), stop=(k == 7))
            for nb in range(2):
                qv = qb[:, nb * 512:(nb + 1) * 512]
                cp("act", qv, PS[nb][:, :], [PB[nb]], [B_qb], scale=0.125)
                rotary(PS[nb][:, :].rearrange("p (c g d) -> p c g d", c=4, g=2), qv.rearrange("p (c g d) -> p c g d", c=4, g=2),
                       cos8[:, e, :], sin8[:, e, :], rtmp, [PB[nb], B_tabE], [B_qb], B_rt)
            for k in range(8):
                tr(PS[2 + k // 4][:, (k % 4) * 128:(k % 4 + 1) * 128], qb[:, k * 128:(k + 1) * 128], [B_qb], [PB[2 + k // 4]])
            for (bk, c0) in ((2, 0), (3, 4)):
                cp("act", qTz[0][0:64, c0:c0 + 4, :], PS[bk][0:64, :].rearrange("p (k t) -> p k t", k=4), [PB[bk]], [B_qT])
                cp("dve", qTz[1][64:128, c0:c0 + 4, :], PS[bk][64:128, :].rearrange("p (k t) -> p k t", k=4), [PB[bk]], [B_qT])
            for k in range(8):
                mm(PS[4][:, 0:48], hTd[:, k, :], wb1[:, k, 1792:1840], [B_br[s], B_wb1], [PB[4]], start=(k == 0), stop=(k == 7))
            act(gn, PS[4][:, 0:48], AF.Sigmoid, [PB[4]], [B_gn])
            for k in range(8):
                mm(PS[5][:, :], hTd[:, k, :], wb1[:, k, 1840:2352], [B_br[s], B_wb1], [PB[5]], start=(k == 0), stop=(k == 7))
            cp("act", qxb, PS[5][:, :], [PB[5]], [B_qxb])
            for h in range(4):
                tr(PS[6][:, h * 128:(h + 1) * 128], qxb[:, h * 128:(h + 1) * 128], [B_qxb], [PB[6]])
            cp("dve", qxT, PS[6][:, :].rearrange("p (h t) -> p h t", h=4), [PB[6]], [B_qxT])
            for c in range(2):
                for h in range(4):
                    mm(PS[7][:, h * 128:(h + 1) * 128], KmT[:, h, c * 128:(c + 1) * 128], qxT[:, h, :], [B_km, B_qxT], [PB[7]])
                act(mpT[c], PS[7][:, :].rearrange("p (h t) -> p h t", h=4), AF.Exp, [PB[7]], [B_mpT[c]])
            for h in range(4):
                for c in range(2):
                    mm(PS[5][:, h * 128:(h + 1) * 128], mpT[c][:, h, :], Vm[:, c, h, :], [B_mpT[c], B_vm], [PB[5]],
                       start=(c == 0), stop=(c == 1))
            for h in range(4):
                for c in range(2):
                    mm(PS[6][:, h:h + 1], mpT[c][:, h, :], onesb[:, 0:1], [B_mpT[c], B_cst], [PB[6]],
                       start=(c == 0), stop=(c == 1))
            S.op("dve", lambda e_: e_.reciprocal(out=rsm, in_=PS[6][:, 0:4]), [PB[6]], [B_rsm])
            tt("dve", ymemb.rearrange("p (h d) -> p h d", h=4), PS[5][:, :].rearrange("p (h d) -> p h d", h=4),
               rsm.unsqueeze(2).to_broadcast([128, 4, 128]), ALU.mult, [PB[5], B_rsm], [B_ymem])
            for h in range(4):
                tr(PS[7][:, h * 128:(h + 1) * 128], ymemb[:, h * 128:(h + 1) * 128], [B_ymem], [PB[7]])
            cp("act", bt[:, 12:16, :], PS[7][:, :].rearrange("p (h t) -> p h t", h=4), [PB[7]], [B_br[s]])
            tqs = tqe[:, e:e + 1]
            ts("dve", cmrow, cendrow, tqs, None, ALU.is_le, None, [B_cst, B_small], [B_cm])
            ts("dve", tqt, qrow, tqst[:, e:e + 1], None, ALU.add, None, [B_cst], [B_tqt])
            for g in range(2):
                pr = slice(64 * g, 64 * g + 64)
                Pv = Pb[g][:, 1:513]
                for c_ in range(8):
                    a = c_ % 2
                    mm(PS[4 + a][:, :], qTz[g][:, c_, :], KcT[:, 0:512], [B_qT, B_KcT], [PB[4 + a]])
                    act(ef[a], PS[4 + a][:, :], AF.Exp, [PB[4 + a]], [B_ef[a]])
                    stt(em, ef[a], 1.0, cmrow, ALU.mult, ALU.mult, [B_ef[a], B_cm], [B_em, B_ssum], accum=ssum[:, 0:1])
                    ts("dve", ssum[:, 1:2], ssum[:, 0:1], 1e-30, None, ALU.max, None, [B_ssum], [B_ssum])
                    S.op("dve", lambda e_: e_.reciprocal(out=ssum[:, 1:2], in_=ssum[:, 1:2]), [B_ssum], [B_ssum])
                    if c_ == 0:
                        ts("dve", Pv, em, ssum[:, 1:2], None, ALU.mult, None, [B_em, B_ssum], [B_Pb[g]])
                    else:
                        stt(Pv, em, ssum[:, 1:2], Pv, ALU.mult, ALU.add, [B_em, B_ssum, B_Pb[g]], [B_Pb[g]])
                S.op("dve", lambda e_, g=g: e_.tensor_reduce(out=imp, in_=Pb[g][:, 0:512].rearrange("p (j s) -> p j s", s=4),
                                                          axis=AX.X, op=ALU.add), [B_Pb[g]], [B_imp])
                tt("dve", imp, imp, Pb[g][:, 4:516].rearrange("p (j s) -> p j s", s=4)[:, :, 0], ALU.add, [B_Pb[g], B_imp], [B_imp])
                ts("dve", nd, bsrow, tqs, None, ALU.subtract, None, [B_cst, B_small, B_imp], [B_imp])
                ts("dve", itmp, nd, -128.0, BIG, ALU.is_gt, ALU.mult, [B_imp], [B_imp])
                tt("dve", imp, imp, itmp, ALU.add, [B_imp], [B_imp])
                ts("dve", itmp, nd, 0.0, -3.0 * BIG, ALU.is_gt, ALU.mult, [B_imp], [B_imp])
                tt("dve", imp, imp, itmp, ALU.add, [B_imp], [B_imp])
                tt("dve", imp, imp, e0big, ALU.add, [B_imp, B_cst], [B_imp])
                S.op("dve", lambda e_: e_.max(out=mx[:, 0:8], in_=imp), [B_imp], [B_imp])
                S.op("dve", lambda e_: e_.match_replace(out=wk, in_to_replace=mx[:, 0:8], in_values=imp, imm_value=-1e30), [B_imp], [B_imp])
                S.op("dve", lambda e_: e_.max(out=mx[:, 8:16], in_=wk), [B_imp], [B_imp])
                ts("dve", selb, imp, mx[:, 15:16], None, ALU.is_ge, None, [B_imp], [B_sel])
                tr(PS[6][:, g * 128:(g + 1) * 128], selb, [B_sel], [PB[6]])
                cp("act", selT[g], PS[6][:, g * 128:(g + 1) * 128], [PB[6]], [B_selT[g]])
            for g in range(2):
                pr = slice(64 * g, 64 * g + 64)

                def mk_cmp(c):
                    return (None, lambda sl, bs: ts("dve", mk[bs], tqt, cendcol[:, c:c + 1], None, ALU.is_ge, None, [B_cst, B_tqt], [B_mk[bs]]))

                def mk_win(j, ee):
                    v = 0 if j == 0 else (2 if j == 4 else 1)
                    return (None, lambda sl, bs: ts("dve", mk[bs], Mw3[:, v, :], wval[:, ee:ee + 1], None, ALU.mult, None, [B_cst, B_small], [B_mk[bs]]))

                def mk_sel(cc, g=g):
                    def f_pe(sl):
                        mm(PS[6 + sl][:, 256:384], Gb[:, cc * 128:(cc + 1) * 128], selT[g], [B_G, B_selT[g]], [PB[6 + sl]])

                    def f_dve(sl, bs):
                        stt(mk[bs], tqt, kidx[:, cc:cc + 1], PS[6 + sl][:, 256:384], ALU.is_ge, ALU.mult,
                            [B_cst, B_tqt, PB[6 + sl]], [B_mk[bs]])
                    return (f_pe, f_dve)

                attend(e, g, 0, [(KcT[:, c * 128:(c + 1) * 128], Vc[:, c, g, :]) + mk_cmp(c) for c in range(4)])
                attend(e, g, 1, [(KTs[:, cc * 128:(cc + 1) * 128], Vs[:, cc, g, :]) + mk_sel(cc) for cc in range(48 + i)])
                attend(e, g, 2, [(KTw[:, (e - 4 + j) * 128:(e - 3 + j) * 128], Vw[:, e - 4 + j, g, :]) + mk_win(j, e - 4 + j)
                                 for j in range(5)])
            cp("act", ynb, ynsa, [B_yn], [B_ynb])
            for k in range(8):
                tr(PS[2 + k // 4][:, (k % 4) * 128:(k % 4 + 1) * 128], ynb[:, k * 128:(k + 1) * 128], [B_ynb], [PB[2 + k // 4]])
            cp("act", bt[:, 4:8, :], PS[2][:, :].rearrange("p (k t) -> p k t", k=4), [PB[2]], [B_br[s]])
            cp("dve", bt[:, 8:12, :], PS[3][:, :].rearrange("p (k t) -> p k t", k=4), [PB[3]], [B_br[s]])
            lastbr = dma(br_scr[i], bt.rearrange("p k t -> p (k t)"), [B_br[s]], [])
            if dbg == "B1":
                lastbr = dma(dbg_t(f"br{i}", [128, 24 * 128], BF16), bt.rearrange("p k t -> p (k t)"), [B_br[s]], [])
                fl1 = [lastbr, dma(dbg_t(f"ynsa{i}", [128, 1024]), ynsa, [B_yn], []),
                       dma(dbg_t(f"qb{i}", [128, 1024], BF16), qb, [B_qb], []),
                       dma(dbg_t(f"gn{i}", [128, 48]), gn, [B_gn], []),
                       dma(dbg_t(f"selT{i}", [128, 128], BF16), selT[1], [B_selT[1]], []),
                       dma(dbg_t(f"Pb{i}", [128, 516]), Pb[1], [B_Pb[1]], [])]
        if dbg == "B1":
            print(S.emit(final_waits=fl1))
            return nc
        barrier()
        top[0] = persist_top
        wg = abf(8 * 3072).rearrange("p (k c) -> p k c", k=8)
        wbp = abf(4 * 1024).rearrange("p (k c) -> p k c", k=4)
        wbn = abf(8 * 1024).rearrange("p (k c) -> p k c", k=8)
        wbx = abf(4 * 1024).rearrange("p (k c) -> p k c", k=4)
        wo = abf(8 * 1024).rearrange("p (k c) -> p k c", k=8)
        B_w2p = S.buf("w_b2")
        for k in range(8):
            loadw(lambda c0, n, k=k: wg[:, k, c0:c0 + n], w_in[0][k * 128:(k + 1) * 128, 2864:5936], 3072,
                  scale=gpre[:, k:k + 1], r_extra=[B_small], w=[B_w2p])
            loadw(lambda c0, n, k=k: wbn[:, k, c0:c0 + n], w_br_nsa[0][k * 128:(k + 1) * 128, :], 1024, w=[B_w2p])
            loadw(lambda c0, n, k=k: wo[:, k, c0:c0 + n], w_out[0][k * 128:(k + 1) * 128, :], 1024, w=[B_w2p])
            if k < 4:
                loadw(lambda c0, n, k=k: wbp[:, k, c0:c0 + n], w_br_pool[0][k * 128:(k + 1) * 128, :], 1024, w=[B_w2p])
                loadw(lambda c0, n, k=k: wbx[:, k, c0:c0 + n], w_br_xa[0][k * 128:(k + 1) * 128, :], 1024, w=[B_w2p])
        gpost = af32(1024)
        B_gp = S.buf("gpost")
        dma(gpost, post_mix_g.partition_broadcast(128), [], [B_gp])
        brT = [abf(24 * 128).rearrange("p (k t) -> p k t", k=24) for _ in range(2)]
        B_br = [S.buf("c_br0"), S.buf("c_br1")]
        xt2 = [af32(1024), af32(1024)]
        B_xt = [S.buf("c_xt0"), S.buf("c_xt1")]
        sg = af32(1024)
        B_sg = S.buf("sg")
        yy = af32(1024)
        B_y = S.buf("yy")
        ytm = af32(1024)
        B_ytm = S.buf("ytm")
        yb = abf(1024)
        B_yb = S.buf("yb")
        yT = abf(1024).rearrange("p (k t) -> p k t", k=8)
        B_yT = S.buf("yT")
        junk = abf(512)
        x1t = [af32(1024), af32(1024)]
        B_x1 = [S.buf("x1t0"), S.buf("x1t1")]

        def post_norm_res(pa, pb, gp, Bg, xres, Bxres, dst, Bdst):
            act(junk[:, 0:512], PS[pa][:, :], AF.Square, [PB[pa]], [B_ss2], accum=ss2[:, 0:1])
            act(junk[:, 0:512], PS[pb][:, :], AF.Square, [PB[pb]], [B_ss2], accum=ss2[:, 1:2])
            tt("dve", ss2[:, 2:3], ss2[:, 0:1], ss2[:, 1:2], ALU.add, [B_ss2], [B_ss2])
            ts("dve", ss2[:, 2:3], ss2[:, 2:3], 1.0 / 1024, 1e-6, ALU.mult, ALU.add, [B_ss2], [B_ss2])
            act(ss2[:, 2:3], ss2[:, 2:3], AF.Sqrt, [B_ss2], [B_ss2])
            S.op("dve", lambda e_: e_.reciprocal(out=ss2[:, 3:4], in_=ss2[:, 2:3]), [B_ss2], [B_ss2])
            for nb, bank in enumerate((pa, pb)):
                blk = slice(nb * 512, (nb + 1) * 512)
                stt(dst[:, blk], PS[bank][:, :], ss2[:, 3:4], gp[:, blk], ALU.mult, ALU.mult, [PB[bank], B_ss2, Bg], [Bdst])
                tt("pool", dst[:, blk], dst[:, blk], xres[:, blk], ALU.add, [Bdst, Bxres], [Bdst])

        dma(brT[0].rearrange("p k t -> p (k t)"), br_scr[0], [], [B_br[0]])
        dma(xt2[0], x_ext[4 * 128:5 * 128, :], [], [B_xt[0]])
        for i in range(17):
            s = i % 2
            e = i + 4
            if i + 1 < 17:
                dma(brT[1 - s].rearrange("p k t -> p (k t)"), br_scr[i + 1], [], [B_br[1 - s]])
                dma(xt2[1 - s], x_ext[(e + 1) * 128:(e + 2) * 128, :], [], [B_xt[1 - s]])
            bt = brT[s]
            for br in range(3):
                for nb in range(2):
                    for k in range(8):
                        mm(PS[nb][:, :], bt[:, 16 + k, :], wg[:, k, br * 1024 + nb * 512:br * 1024 + (nb + 1) * 512],
                           [B_br[s], B_w2p], [PB[nb]], start=(k == 0), stop=(k == 7))
                wsel, off, nk = ((wbp, 0, 4), (wbn, 4, 8), (wbx, 12, 4))[br]
                for nb in range(2):
                    for k in range(nk):
                        mm(PS[2 + nb][:, :], bt[:, off + k, :], wsel[:, k, nb * 512:(nb + 1) * 512],
                           [B_br[s], B_w2p], [PB[2 + nb]], start=(k == 0), stop=(k == nk - 1))
                for nb in range(2):
                    blk = slice(nb * 512, (nb + 1) * 512)
                    act(sg[:, blk], PS[nb][:, :], AF.Sigmoid, [PB[nb]], [B_sg])
                    if br == 0:
                        tt("dve", yy[:, blk], sg[:, blk], PS[2 + nb][:, :], ALU.mult, [B_sg, PB[2 + nb]], [B_y])
                    else:
                        tt("dve", ytm[:, blk], sg[:, blk], PS[2 + nb][:, :], ALU.mult, [B_sg, PB[2 + nb]], [B_ytm])
                        tt("pool", yy[:, blk], yy[:, blk], ytm[:, blk], ALU.add, [B_y, B_ytm], [B_y])
            cp("act", yb, yy, [B_y], [B_yb])
            for k in range(8):
                tr(PS[4 + k // 4][:, (k % 4) * 128:(k % 4 + 1) * 128], yb[:, k * 128:(k + 1) * 128], [B_yb], [PB[4 + k // 4]])
            cp("act", yT[:, 0:4, :], PS[4][:, :].rearrange("p (k t) -> p k t", k=4), [PB[4]], [B_yT])
            cp("dve", yT[:, 4:8, :], PS[5][:, :].rearrange("p (k t) -> p k t", k=4), [PB[5]], [B_yT])
            for nb in range(2):
                for k in range(8):
                    mm(PS[6 + nb][:, :], yT[:, k, :], wo[:, k, nb * 512:(nb + 1) * 512], [B_yT, B_w2p], [PB[6 + nb]],
                       start=(k == 0), stop=(k == 7))
            post_norm_res(6, 7, gpost, B_gp, xt2[s], B_xt[s], x1t[s], B_x1[s])
            lx = dma(x1_scr[i], x1t[s], [B_x1[s]], [])
            if dbg == "B2":
                lx = dma(dbg_t(f"x1_{i}", [128, 1024]), x1t[s], [B_x1[s]], [])
        if dbg == "B2":
            print(S.emit(final_waits=[lx]))
            return nc
        barrier()
        top[0] = persist_top

        wup = abf(8 * 5632).rearrange("p (k c) -> p k c", k=8)
        wdn = abf(22 * 1024).rearrange("p (k c) -> p k c", k=22)
        B_w3 = S.buf("w_c")
        for k in range(8):
            loadw(lambda c0, n, k=k: wup[:, k, c0:c0 + n], w_up[0][k * 128:(k + 1) * 128, :], 5632,
                  scale=gffn[:, k:k + 1], r_extra=[B_small], w=[B_w3])
        for k in range(22):
            loadw(lambda c0, n, k=k: wdn[:, k, c0:c0 + n], w_down[0][k * 128:(k + 1) * 128, :], 1024, w=[B_w3])
        convp = af32(44 * 4).rearrange("p (j c) -> p j c", c=4)
        B_cv = S.buf("convp")
        for kk in range(3):
            dma(convp[:, :, kk], conv_w[0][kk].rearrange("(j p) -> p j", p=128), [], [B_cv], slow=True)
        dma(convp[:, :, 3], conv_b[0].rearrange("(j p) -> p j", p=128), [], [B_cv], slow=True)
        gpost2 = af32(1024)
        B_gp2 = S.buf("gpost2")
        dma(gpost2, post_ffn_g.partition_broadcast(128), [], [B_gp2])
        xt2 = [af32(1024), af32(1024)]
        B_xt = [S.buf("d_xt0"), S.buf("d_xt1")]
        xn = abf(1024)
        B_xn = S.buf("d_xn")
        junk = abf(1024)
        ssA = af32(2)
        B_ss = S.buf("d_ss")
        h2T = abf(8 * 130).rearrange("p (k t) -> p k t", k=8)
        B_h2 = S.buf("h2T")
        halo = abf(16).rearrange("p (k t) -> p k t", k=8)
        B_halo = S.buf("halo")
        aT = abf(22 * 128).rearrange("p (k t) -> p k t", k=22)
        B_aT = S.buf("aT")
        cg = [af32(128), af32(128)]
        cv = [af32(128), af32(128)]
        gl = [af32(128), af32(128)]
        B_cg = [S.buf("cg0"), S.buf("cg1")]
        B_cvb = [S.buf("cv0"), S.buf("cv1")]
        B_gl = [S.buf("gl0"), S.buf("gl1")]
        ot = [af32(1024), af32(1024)]
        B_ot = [S.buf("ot0"), S.buf("ot1")]
        junk2 = junk[:, 0:512]
        fins = []
        dma(xt2[0], x1_scr[0], [], [B_xt[0]])
        for i in range(17):
            s = i % 2
            if i + 1 < 17:
                dma(xt2[1 - s], x1_scr[i + 1], [], [B_xt[1 - s]])
            if i > 0:
                cp("pool", h2T[:, :, 0:2], halo, [B_halo], [B_h2])
            rms_scale(xt2[s], xn, B_xt[s], B_xn, ssA[:, 0:1], junk, B_ss)
            for k in range(8):
                bank = k // 4
                tr(PS[bank][:, (k % 4) * 128:(k % 4 + 1) * 128], xn[:, k * 128:(k + 1) * 128], [B_xn], [PB[bank]])
            cp("act", h2T[:, 0:4, 2:130], PS[0][:, :].rearrange("p (k t) -> p k t", k=4), [PB[0]], [B_h2])
            cp("dve", h2T[:, 4:8, 2:130], PS[1][:, :].rearrange("p (k t) -> p k t", k=4), [PB[1]], [B_h2])
            if i == 0:
                ts("dve", halo, h2T[:, :, 128:130], hflag[:, 0:1], None, ALU.mult, None, [B_h2, B_small], [B_halo])
                continue
            cp("pool", halo, h2T[:, :, 128:130], [B_h2], [B_halo])
            for j in range(22):
                a = j % 2
                bg, bv = 2 + 2 * a, 3 + 2 * a
                for k in range(8):
                    mm(PS[bg][:, 0:130], wup[:, k, j * 128:(j + 1) * 128], h2T[:, k, :], [B_w3, B_h2], [PB[bg]],
                       start=(k == 0), stop=(k == 7))
                for k in range(8):
                    mm(PS[bv][:, 0:130], wup[:, k, (22 + j) * 128:(23 + j) * 128], h2T[:, k, :], [B_w3, B_h2], [PB[bv]],
                       start=(k == 0), stop=(k == 7))
                for (bank, dstc, Bd, jj) in ((bg, cg[a], B_cg[a], j), (bv, cv[a], B_cvb[a], 22 + j)):
                    act(dstc, PS[bank][:, 2:130], AF.Identity, [PB[bank], B_cv], [Bd], bias=convp[:, jj, 3:4], scale=convp[:, jj, 2:3])
                    stt(dstc, PS[bank][:, 1:129], convp[:, jj, 1:2], dstc, ALU.mult, ALU.add, [PB[bank], B_cv, Bd], [Bd])
                    stt(dstc, PS[bank][:, 0:128], convp[:, jj, 0:1], dstc, ALU.mult, ALU.add, [PB[bank], B_cv, Bd], [Bd])
                act(gl[a], cg[a], AF.Gelu_apprx_tanh, [B_cg[a]], [B_gl[a]])
                tt("pool", aT[:, j, :], gl[a], cv[a], ALU.mult, [B_gl[a], B_cvb[a]], [B_aT])
            for nb in range(2):
                for j in range(22):
                    mm(PS[6 + nb][:, :], aT[:, j, :], wdn[:, j, nb * 512:(nb + 1) * 512], [B_aT, B_w3], [PB[6 + nb]],
                       start=(j == 0), stop=(j == 21))
            post_norm_res(6, 7, gpost2, B_gp2, xt2[s], B_xt[s], ot[s], B_ot[s])
            fins.append(dma(out_d[(i - 1) * 128:i * 128, :], ot[s], [B_ot[s]], []))
        stats = S.emit(final_waits=fins)
        print("emit stats", stats, flush=True)
    return nc


def _consts():
    c = {}
    c["ident"] = np.eye(128, dtype=np.float32)
    k = np.arange(8192)
    c["Gm"] = (k[None, :] // 64 == np.arange(128)[:, None]).astype(np.float32)
    c["invf"] = (np.float32(500000.0) ** (-np.arange(8, dtype=np.float32) * np.float32(2.0 / 16))).astype(np.float32).reshape(1, 8)
    c["bsrow"] = (64.0 * np.arange(128, dtype=np.float32)).reshape(1, 128)
    ce = (16.0 * np.arange(512, dtype=np.float32) + 31.0)
    ce[511] = 1e9
    c["cendrow"] = ce.reshape(1, 512)
    c["kidx"] = (128.0 * np.arange(64)[None, :] + np.arange(128)[:, None]).astype(np.float32)
    c["cendcol"] = np.ascontiguousarray(ce.reshape(4, 128).T)
    p = np.arange(128)[:, None]
    q = np.arange(128)[None, :]
    c["Mw"] = np.concatenate([(q < p), (q >= p)], axis=1).astype(np.float32)
    e0 = np.zeros((1, 128), np.float32)
    e0[0, 0] = BIG
    c["e0big"] = e0
    A = np.zeros((128, 4, 2, 128), np.float32)
    for gi, w in enumerate((2, 4, 8, 16)):
        tp = np.arange(128)[:, None]
        t = np.arange(128)[None, :]
        A[:, gi, 0, :] = ((t - tp >= 0) & (t - tp < w))
        A[:, gi, 1, :] = ((t + 128 - tp >= 0) & (t + 128 - tp < w))
    c["Aband"] = A.reshape(128, 1024)
    c["qrow"] = np.arange(128, dtype=np.float32).reshape(1, 128)
    return c


_PROG = {}


def kernel(**inputs):
    x = np.asarray(inputs["x"], dtype=np.float32)
    mem = np.asarray(inputs["mem"], dtype=np.float32)
    positions = np.asarray(inputs["positions"]).astype(np.int32)
    if inputs.get("_return_maps"):
        nc = None
    else:
        if "nc" not in _PROG:
            _PROG["nc"] = build()
        nc = _PROG["nc"]
    consts = _consts()
    wnames = ["pre_mix_g", "w_in", "pool_w", "pool_scale", "cmp_pe", "cmp_w1", "cmp_w2", "mem_norm_g", "w_mem_kv",
              "w_br_pool", "w_br_nsa", "w_br_xa", "w_out", "post_mix_g", "pre_ffn_g", "w_up", "conv_w", "conv_b",
              "w_down", "post_ffn_g"]
    shared = {n: np.ascontiguousarray(np.asarray(inputs[n], dtype=np.float32)) for n in wnames}
    in_maps = []
    for core in range(8):
        b, r = core // 4, core % 4
        m = dict(shared)
        m.update(consts)
        m["x_all"] = np.ascontiguousarray(x[b])
        m["mem"] = np.ascontiguousarray(mem[b])
        xe = np.zeros((NEXT * 128, 1024), np.float32)
        pe = np.zeros((NEXT, 128), np.int32)
        tq = np.zeros((NEXT, 128), np.float32)
        wv = np.zeros((1, NEXT), np.float32)
        ic = np.ones((128, NEXT, 4), np.float32)
        tst = np.zeros((1, NEXT), np.float32)
        for e in range(NEXT):
            ge = 16 * r - 5 + e
            tq[e] = ge * 128 + np.arange(128)
            tst[0, e] = ge * 128
            if ge >= 0:
                xe[e * 128:(e + 1) * 128] = x[b, ge * 128:(ge + 1) * 128]
                pe[e] = positions[b, ge * 128:(ge + 1) * 128]
                wv[0, e] = 1.0
                t = ge * 128 + np.arange(128)
                for gi, w in enumerate((2, 4, 8, 16)):
                    ic[:, e, gi] = 1.0 / np.minimum(t + 1, w).astype(np.float32)
        m["x_ext"] = xe
        m["pos_ext"] = np.ascontiguousarray(pe.T)
        m["pos_all"] = np.ascontiguousarray(positions[b].reshape(NALL, 128).T)
        m["tq_ext"] = np.ascontiguousarray(tq.T)
        m["tqstart"] = tst
        m["wvalid"] = wv
        m["invcnt"] = np.ascontiguousarray(ic.reshape(128, NEXT * 4))
        m["hflag"] = np.array([[0.0 if r == 0 else 1.0]], np.float32)
        in_maps.append(m)
    if inputs.get("_return_maps"):
        return in_maps
    res = run_bass_kernel_spmd(nc, in_maps, core_ids=list(range(8)))
    out = np.zeros((2, 8192, 1024), np.float32)
    for core in range(8):
        b, r = core // 4, core % 4
        out[b, r * 2048:(r + 1) * 2048] = res.results[core]["out"]
    return out
```

```python
import numpy as np
from contextlib import ExitStack
import concourse.bass as bass
import concourse.mybir as mybir
from concourse.bass_utils import run_bass_kernel_spmd

F32 = mybir.dt.float32
BF16 = mybir.dt.bfloat16
I32 = mybir.dt.int32
AF = mybir.ActivationFunctionType
ALU = mybir.AluOpType
AX = mybir.AxisListType


import sys as _sys


def _where():
    f = _sys._getframe(2)
    out = []
    while f is not None and len(out) < 4:
        if f.f_code.co_name != "<lambda>":
            out.append(f.f_lineno)
        f = f.f_back
    return out


class Buf:
    __slots__ = ("name", "writers", "readers", "excl", "last")

    def __init__(self, name, excl=False):
        self.name = name
        self.writers = []
        self.readers = []
        self.excl = excl
        self.last = {}


class Ins:
    __slots__ = ("eng", "fn", "deps", "idx", "flag", "tok", "dma", "pre", "where")

    def __init__(self, eng, fn, dma):
        self.eng = eng
        self.fn = fn
        self.deps = []
        self.flag = False
        self.tok = None
        self.dma = dma
        self.pre = None


class Sched:
    ENGS = ("pe", "dve", "act", "pool", "sp")
    EPOCH = 8000
    NDMA = 24

    def __init__(self, nc, stack):
        self.nc = nc
        self.stack = stack
        self.q = {e: [] for e in self.ENGS}
        self.nbuf = 0

    def buf(self, name=None, excl=False):
        self.nbuf += 1
        return Buf(name or f"b{self.nbuf}", excl)

    def op(self, eng, fn, r=(), w=(), dma=False):
        ins = Ins(eng, fn, dma)
        ins.where = _where()
        deps = []
        for b in r:
            deps.extend(b.writers)
        for b in w:
            deps.extend(b.readers)
        for b in list(r) + list(w):
            if b.excl:
                for en, li in b.last.items():
                    if en != eng:
                        deps.append(li)
                b.last[eng] = ins
        for b in w:
            if b.readers or (b in r):
                b.writers = [ins]
                b.readers = []
            else:
                b.writers.append(ins)
                if len(b.writers) > 48:
                    b.writers = b.writers[-48:]
        for b in r:
            if b not in w:
                b.readers.append(ins)
                if len(b.readers) > 48:
                    b.readers = b.readers[-48:]
        seen = set()
        for d in deps:
            if d is ins or id(d) in seen:
                continue
            if d.eng == "pe" and eng == "pe" and not d.dma:
                continue
            seen.add(id(d))
            ins.deps.append(d)
            d.flag = True
        self.q[eng].append(ins)
        return ins

    def emit(self, final_waits=()):
        nc = self.nc
        stack = self.stack
        sems = {}
        for e in self.ENGS:
            n = 0
            for ins in self.q[e]:
                if ins.dma:
                    continue
                if ins.flag:
                    n += 1
                    ins.idx = n
            nep = (n + self.EPOCH - 1) // self.EPOCH
            sems[e] = [stack.enter_context(nc.semaphore(f"s_{e}_{k}")) for k in range(max(nep, 1))]
        dsems = [stack.enter_context(nc.semaphore(f"s_dma_{k}")) for k in range(self.NDMA)]
        duse = [0] * self.NDMA
        dma_engs = [e for e in self.ENGS if any(i.dma for i in self.q[e])]
        share = {}
        if dma_engs:
            per = self.NDMA // len(dma_engs)
            for k, e in enumerate(dma_engs):
                share[e] = list(range(k * per, (k + 1) * per))
        for e in dma_engs:
            j = 0
            for ins in self.q[e]:
                if not ins.dma:
                    continue
                s = share[e][j % len(share[e])]
                j += 1
                prev = duse[s]
                duse[s] += 1
                ins.tok = (dsems[s], 16 * duse[s])
                ins.pre = (dsems[s], 16 * prev) if prev > 0 else None
        for e in self.ENGS:
            for ins in self.q[e]:
                if ins.dma or not ins.flag:
                    continue
                k = (ins.idx - 1) // self.EPOCH
                ins.tok = (sems[e][k], (ins.idx - 1) % self.EPOCH + 1)
        engobj = {"pe": "tensor", "dve": "vector", "act": "scalar", "pool": "gpsimd", "sp": "sync"}
        stats = {}
        with nc.Block() as block:
            for e in self.ENGS:
                lst = self.q[e]

                def body(eng, lst=lst, e=e):
                    waited = {}
                    nw = 0

                    def wait(tok):
                        nonlocal nw
                        sem, val = tok
                        key = id(sem)
                        if waited.get(key, 0) >= val:
                            return
                        waited[key] = val
                        eng.wait_ge(sem, val)
                        nw += 1

                    for ins in lst:
                        if ins.pre is not None:
                            wait(ins.pre)
                        for d in ins.deps:
                            wait(d.tok)
                        try:
                            bi = ins.fn(eng)
                        except BaseException:
                            print("FAILED op recorded at lines", ins.where, flush=True)
                            raise
                        if ins.dma:
                            bi.then_inc(ins.tok[0], 16)
                        elif ins.flag:
                            bi.then_inc(ins.tok[0], 1)
                    if e == "sp":
                        for fw in final_waits:
                            wait(fw.tok)
                    stats[e] = (len(lst), nw)

                getattr(block, engobj[e])(body)
        return stats


import os as _os
DUMPX = int(_os.environ.get('DUMPX', '0'))
NOBAR = int(_os.environ.get('NOBAR', '0'))
NEXT = 21
NALL = 64
BIG = 100.0
TWO_PI = float(2 * np.pi)


def build(dbg=None, ntile_b1=NEXT, na=NALL, skipcmp=False):
    nc = bass.Bass("TRN2", target_bir_lowering=False)

    def din(name, shape, dt=F32):
        return nc.dram_tensor(name, list(shape), dt, kind="ExternalInput").ap()

    x_all = din("x_all", [8192, 1024])
    x_ext = din("x_ext", [NEXT * 128, 1024])
    pos_all = din("pos_all", [128, NALL], I32)
    pos_ext = din("pos_ext", [128, NEXT], I32)
    tq_ext = din("tq_ext", [128, NEXT])
    qrow_d = din("qrow", [1, 128])
    tqst_d = din("tqstart", [1, NEXT])
    wvalid_d = din("wvalid", [1, NEXT])
    invcnt_d = din("invcnt", [128, NEXT * 4])
    hflag_d = din("hflag", [1, 1])
    mem_d = din("mem", [256, 1024])
    ident_d = din("ident", [128, 128])
    G_d = din("Gm", [128, 8192])
    invf_d = din("invf", [1, 8])
    bsrow_d = din("bsrow", [1, 128])
    cendrow_d = din("cendrow", [1, 512])
    kidx_d = din("kidx", [128, 64])
    cendcol_d = din("cendcol", [128, 4])
    Mw_d = din("Mw", [128, 256])
    e0_d = din("e0big", [1, 128])
    Ab_d = din("Aband", [128, 1024])
    pre_mix_g = din("pre_mix_g", [1, 1024])
    w_in = din("w_in", [1, 1024, 5936])
    pool_w = din("pool_w", [1, 4, 128, 128])
    pool_scale = din("pool_scale", [1, 512])
    cmp_pe = din("cmp_pe", [1, 2, 32, 64])
    cmp_w1 = din("cmp_w1", [1, 2, 2048, 256])
    cmp_w2 = din("cmp_w2", [1, 2, 256, 64])
    mem_norm_g = din("mem_norm_g", [1, 1024])
    w_mem_kv = din("w_mem_kv", [1, 1024, 1024])
    w_br_pool = din("w_br_pool", [1, 512, 1024])
    w_br_nsa = din("w_br_nsa", [1, 1024, 1024])
    w_br_xa = din("w_br_xa", [1, 512, 1024])
    w_out = din("w_out", [1, 1024, 1024])
    post_mix_g = din("post_mix_g", [1, 1024])
    pre_ffn_g = din("pre_ffn_g", [1, 1024])
    w_up = din("w_up", [1, 1024, 5632])
    conv_w = din("conv_w", [1, 3, 5632])
    conv_b = din("conv_b", [1, 5632])
    w_down = din("w_down", [1, 2816, 1024])
    post_ffn_g = din("post_ffn_g", [1, 1024])
    out_d = nc.dram_tensor("out", [16 * 128, 1024], F32, kind="ExternalOutput").ap()
    br_scr = nc.dram_tensor("br_scr", [17, 128, 24 * 128], BF16, kind="Internal").ap()
    x1_scr = nc.dram_tensor("x1_scr", [17, 128, 1024], F32, kind="Internal").ap()
    dbg_out = {}

    def dbg_t(name, shape, dt=F32):
        return nc.dram_tensor("dbg_" + name, list(shape), dt, kind="ExternalOutput").ap()

    with ExitStack() as st:
        S = Sched(nc, st)
        ARN = 95800
        arena = st.enter_context(nc.sbuf_tensor("arena", [128, ARN], BF16))
        PS = [st.enter_context(nc.psum_tensor(f"ps{i}", [128, 512], F32)) for i in range(8)]
        PB = [S.buf(f"ps{i}", excl=True) for i in range(8)]
        top = [0]

        def abf(n):
            a = arena[:, top[0]:top[0] + n]
            top[0] += (n + 31) // 32 * 32
            assert top[0] <= ARN, top[0]
            return a

        def af32(n):
            a = arena[:, top[0]:top[0] + 2 * n].bitcast(F32)
            top[0] += (n + 15) // 16 * 32
            assert top[0] <= ARN, top[0]
            return a

        def ai32(n):
            a = arena[:, top[0]:top[0] + 2 * n].bitcast(I32)
            top[0] += (n + 15) // 16 * 32
            assert top[0] <= ARN, top[0]
            return a

        def mm(out, lhsT, rhs, r, w, start=True, stop=True):
            return S.op("pe", lambda e: e.matmul(out, lhsT=lhsT, rhs=rhs, start=start, stop=stop,
                                                 skip_group_check=True), r, w)

        last_func = [None]

        def act(out, in_, func, r, w, bias=None, scale=None, accum=None):
            if func != last_func[0]:
                last_func[0] = func
                S.op("act", lambda e: e.activation(out=bsc[1][:, 0:1], in_=bsc[3][:, 0:1], func=func), [B_bsc3], [])
            kw = {}
            if bias is not None:
                kw["bias"] = bias
            if scale is not None:
                kw["scale"] = scale
            if accum is not None:
                kw["accum_out"] = accum
            return S.op("act", lambda e: e.activation(out=out, in_=in_, func=func, **kw), r, w)

        def ts(eng, out, in0, s1, s2, op0, op1, r, w):
            if s2 is None:
                return S.op(eng, lambda e: e.tensor_scalar(out=out, in0=in0, scalar1=s1, scalar2=None, op0=op0), r, w)
            return S.op(eng, lambda e: e.tensor_scalar(out=out, in0=in0, scalar1=s1, scalar2=s2, op0=op0, op1=op1), r, w)

        def tt(eng, out, in0, in1, op, r, w):
            return S.op(eng, lambda e: e.tensor_tensor(out=out, in0=in0, in1=in1, op=op), r, w)

        def stt(out, in0, scalar, in1, op0, op1, r, w, accum=None):
            if accum is None:
                return S.op("dve", lambda e: e.scalar_tensor_tensor(out=out, in0=in0, scalar=scalar, in1=in1, op0=op0, op1=op1), r, w)
            return S.op("dve", lambda e: e.scalar_tensor_tensor(out=out, in0=in0, scalar=scalar, in1=in1, op0=op0, op1=op1,
                                                                accum_out=accum), r, w)

        def cp(eng, out, in_, r, w, scale=None):
            if eng == "act":
                return act(out, in_, AF.Copy, r, w, scale=scale)
            if scale is not None:
                return ts(eng, out, in_, scale, None, ALU.mult, None, r, w)
            return S.op(eng, lambda e: e.tensor_copy(out=out, in_=in_), r, w)

        def memset(eng, ap, val, w):
            return S.op(eng, lambda e: e.memset(ap, val), (), w)

        dmas = []

        def dma(out, in_, r, w, slow=False):
            if slow:
                i = S.op("sp", lambda e: e.dma_start(out=out, in_=in_, allow_slow_non_contiguous=True), r, w, dma=True)
            else:
                i = S.op("sp", lambda e: e.dma_start(out=out, in_=in_), r, w, dma=True)
            dmas.append(i)
            return i

        bsc = [af32(16) for _ in range(4)]
        bbuf = {e: S.buf("bar_" + e) for e in S.ENGS}
        bar_scr = nc.dram_tensor("bar_scr", [128, 16], F32, kind="Internal").ap()

        def barrier():
            allb = list(bbuf.values())
            mm(PS[7][:, 0:1], ident_b[:, 0:128], ident_b[:, 0:1], [PB[7]], [bbuf["pe"], PB[7]])
            memset("dve", bsc[0], 0.0, [bbuf["dve"]])
            act(bsc[1], bsc[3], AF.Copy, [B_bsc3], [bbuf["act"]])
            memset("pool", bsc[2], 0.0, [bbuf["pool"]])
            i = S.op("sp", lambda e: e.dma_start(out=bar_scr, in_=bsc[3]), [B_bsc3], [bbuf["sp"]], dma=True)
            for d in dmas:
                if d not in i.deps:
                    i.deps.append(d)
            dmas.clear()
            mm(PS[7][:, 0:1], ident_b[:, 0:128], ident_b[:, 0:1], allb + [PB[7]], [PB[7]])
            S.op("dve", lambda e: e.memset(bsc[0], 0.0), allb, [])
            act(bsc[1], bsc[3], AF.Copy, allb + [B_bsc3], [])
            S.op("pool", lambda e: e.memset(bsc[2], 0.0), allb, [])
            S.op("sp", lambda e: e.dma_start(out=bar_scr, in_=bsc[3]), allb + [B_bsc3], [], dma=True)

        ident_b = abf(128)
        B_ident = S.buf("ident")
        stage = [af32(1024), af32(1024)]
        B_stage = [S.buf("stg0"), S.buf("stg1")]
        stg_i = [0]
        cast_i = [0]
        B_bsc3 = S.buf("bsc3")
        memset("dve", bsc[3], 0.0, [B_bsc3])

        def loadw(dst_fn, src, ncols, scale=None, r_extra=(), w=None, perm_q=False):
            for c0 in range(0, ncols, 1024):
                n = min(1024, ncols - c0)
                k = stg_i[0] % 2
                stg_i[0] += 1
                np_ = src.shape[0]
                sl = stage[k][0:np_, 0:n]
                dma(sl, src[:, c0:c0 + n], [], [B_stage[k]])
                eng = ("pool", "dve")[cast_i[0] % 2]
                cast_i[0] += 1
                if perm_q:
                    for g in range(2):
                        o = dst_fn(c0, n).rearrange("p (c g d) -> p c g d", c=8, g=2)[:, :, g, :]
                        i_ = stage[k][0:np_, g * 512:(g + 1) * 512].rearrange("p (c d) -> p c d", c=8)
                        if scale is not None:
                            ts(eng, o, i_, scale, None, ALU.mult, None, [B_stage[k]] + list(r_extra), w)
                        else:
                            cp(eng, o, i_, [B_stage[k]] + list(r_extra), w)
                else:
                    if scale is not None:
                        ts(eng, dst_fn(c0, n), sl, scale, None, ALU.mult, None, [B_stage[k]] + list(r_extra), w)
                    else:
                        cp(eng, dst_fn(c0, n), sl, [B_stage[k]] + list(r_extra), w)

        def tr(out_ps, in_sb, r, w, start=True):
            return mm(out_ps, in_sb, ident_b[0:in_sb.shape[0], 0:in_sb.shape[0]], list(r) + [B_ident], w)

        dma(stage[0][:, 0:128], ident_d, [], [B_stage[0]])
        cp("dve", ident_b, stage[0][:, 0:128], [B_stage[0]], [B_ident])

        gpre = af32(8)
        gmem = af32(8)
        gffn = af32(8)
        B_small = S.buf("small")
        dma(gpre, pre_mix_g[0].rearrange("(k p) -> p k", p=128), [], [B_small], slow=True)
        dma(gmem, mem_norm_g[0].rearrange("(k p) -> p k", p=128), [], [B_small], slow=True)
        dma(gffn, pre_ffn_g[0].rearrange("(k p) -> p k", p=128), [], [B_small], slow=True)
        tqe = af32(NEXT)
        dma(tqe, tq_ext, [], [B_small])
        wval = af32(NEXT)
        dma(wval, wvalid_d.partition_broadcast(128), [], [B_small])
        hflag = af32(1)
        dma(hflag, hflag_d.partition_broadcast(128), [], [B_small])
        invcnt = af32(NEXT * 4)
        dma(invcnt, invcnt_d, [], [B_small])
        ss2 = af32(4)
        B_ss2 = S.buf("ss2")
        persist_top = top[0]

        def rms_scale(xt, xn, Bx, Bxn, ss, junk, Bss):
            act(junk, xt, AF.Square, [Bx], [Bss], accum=ss)
            ts("dve", ss, ss, 1.0 / 1024, 1e-6, ALU.mult, ALU.add, [Bss], [Bss])
            act(ss, ss, AF.Sqrt, [Bss], [Bss])
            S.op("dve", lambda e: e.reciprocal(out=ss, in_=ss), [Bss], [Bss])
            ts("dve", xn, xt, ss, None, ALU.mult, None, [Bx, Bss], [Bxn])

        def sincos(ang, n, osin, ocos, tmp, Bt, Bo):
            t, kf, g, ki = tmp
            ts("dve", ang, ang, 1.0 / TWO_PI, None, ALU.mult, None, [Bt], [Bt])
            for dst, off in ((osin, 0.0), (ocos, 0.25)):
                ts("dve", t, ang, off, None, ALU.add, None, [Bt], [Bt])
                cp("dve", ki, t, [Bt], [Bt])
                cp("dve", kf, ki, [Bt], [Bt])
                tt("dve", t, t, kf, ALU.subtract, [Bt], [Bt])
                ts("dve", g, t, 0.5, None, ALU.is_gt, None, [Bt], [Bt])
                tt("dve", t, t, g, ALU.subtract, [Bt], [Bt])
                ts("dve", g, t, -0.5, None, ALU.is_lt, None, [Bt], [Bt])
                tt("dve", t, t, g, ALU.add, [Bt], [Bt])
                act(dst, t, AF.Sin, [Bt], [Bo], scale=TWO_PI)

        def rotary(src4, dst4, cs, sn, tmp, rB, wB, Bt):
            a, b = src4.shape[1], src4.shape[2]
            n = a * b * 8
            x1 = src4[:, :, :, 0:8]
            x2 = src4[:, :, :, 8:16]
            csb = cs.unsqueeze(1).unsqueeze(1).to_broadcast([128, a, b, 8])
            snb = sn.unsqueeze(1).unsqueeze(1).to_broadcast([128, a, b, 8])
            t1 = tmp[:, 0:n].rearrange("p (a b d) -> p a b d", a=a, b=b)
            t2 = tmp[:, n:2 * n].rearrange("p (a b d) -> p a b d", a=a, b=b)
            tt("dve", t1, x1, csb, ALU.mult, rB, [Bt])
            tt("dve", t2, x2, snb, ALU.mult, rB, [Bt])
            tt("dve", dst4[:, :, :, 0:8], t1, t2, ALU.subtract, [Bt], wB)
            tt("dve", t1, x2, csb, ALU.mult, rB, [Bt])
            tt("dve", t2, x1, snb, ALU.mult, rB, [Bt])
            tt("dve", dst4[:, :, :, 8:16], t1, t2, ALU.add, [Bt], wB)

        KTs = abf(8192)
        Vs = abf(64 * 2 * 65).rearrange("p (t g d) -> p t g d", t=64, g=2)
        KTw = abf(NEXT * 128)
        Vw = abf(NEXT * 2 * 65).rearrange("p (t g d) -> p t g d", t=NEXT, g=2)
        KcT = abf(512)
        Vc = abf(4 * 2 * 65).rearrange("p (t g d) -> p t g d", t=4, g=2)
        Gb = abf(8192)
        B_KTs, B_Vs, B_KTw, B_Vw, B_KcT, B_Vc, B_G = [S.buf(n) for n in "KTs Vs KTw Vw KcT Vc G".split()]
        cosE = af32(NEXT * 8).rearrange("p (t f) -> p t f", f=8)
        sinE = af32(NEXT * 8).rearrange("p (t f) -> p t f", f=8)
        cos8 = af32(NEXT * 8).rearrange("p (t f) -> p t f", f=8)
        sin8 = af32(NEXT * 8).rearrange("p (t f) -> p t f", f=8)
        B_tabE = S.buf("tabE")
        kv_top = top[0]

        memset("pool", Vs, 1.0, [B_Vs])
        memset("pool", Vw, 1.0, [B_Vw])
        memset("pool", Vc, 0.0, [B_Vc])
        memset("pool", Vc[:, :, :, 64:65], 1.0, [B_Vc])
        loadw(lambda c0, n: Gb[:, c0:c0 + n], G_d, 8192, w=[B_G])

        cosA = af32(NALL * 8).rearrange("p (t f) -> p t f", f=8)
        sinA = af32(NALL * 8).rearrange("p (t f) -> p t f", f=8)
        B_tabA = S.buf("tabA")
        KcRaw = abf(8192)
        VcRaw = abf(8192)
        B_KcRaw, B_VcRaw = S.buf("KcRaw"), S.buf("VcRaw")
        wkvA = abf(8 * 512).rearrange("p (k c) -> p k c", k=8)
        B_wkvA = S.buf("wkvA")
        for k in range(8):
            loadw(lambda c0, n, k=k: wkvA[:, k, c0:c0 + n], w_in[0][k * 128:(k + 1) * 128, 1536:2048], 512,
                  scale=gpre[:, k:k + 1], r_extra=[B_small], w=[B_wkvA])
        mark_tab = top[0]
        posi = ai32(512)
        posf = af32(512)
        invf = af32(8)
        ang = af32(512)
        tmp3 = (af32(512), af32(512), af32(512), ai32(512))
        B_t = S.buf("tabtmp")
        dma(invf, invf_d.partition_broadcast(128), [], [B_t])
        dma(posi[:, 0:NALL], pos_all, [], [B_t])
        cp("dve", posf[:, 0:NALL], posi[:, 0:NALL], [B_t], [B_t])
        tt("dve", ang.rearrange("p (t f) -> p t f", f=8), posf[:, 0:NALL].unsqueeze(2).to_broadcast([128, NALL, 8]),
           invf.unsqueeze(1).to_broadcast([128, NALL, 8]), ALU.mult, [B_t], [B_t])
        sincos(ang, 512, sinA.rearrange("p t f -> p (t f)"), cosA.rearrange("p t f -> p (t f)"),
               tmp3, B_t, B_tabA)
        dma(posi[:, 0:NEXT], pos_ext, [B_t], [B_t])
        cp("dve", posf[:, 0:NEXT], posi[:, 0:NEXT], [B_t], [B_t])
        ne = NEXT * 8
        tt("dve", ang[:, 0:ne].rearrange("p (t f) -> p t f", f=8), posf[:, 0:NEXT].unsqueeze(2).to_broadcast([128, NEXT, 8]),
           invf.unsqueeze(1).to_broadcast([128, NEXT, 8]), ALU.mult, [B_t], [B_t])
        sincos(ang[:, 0:ne], ne, sinE.rearrange("p t f -> p (t f)"), cosE.rearrange("p t f -> p (t f)"),
               tuple(a[:, 0:ne] for a in tmp3), B_t, B_tabE)
        ts("dve", cos8.rearrange("p t f -> p (t f)"), cosE.rearrange("p t f -> p (t f)"), 0.125, None, ALU.mult, None, [B_tabE], [B_tabE])
        ts("dve", sin8.rearrange("p t f -> p (t f)"), sinE.rearrange("p t f -> p (t f)"), 0.125, None, ALU.mult, None, [B_tabE], [B_tabE])
        if dbg != "0":
            barrier()
            top[0] = mark_tab

        if dbg == "0":
            f1 = dma(dbg_t("cosA", [128, NALL * 8]), cosA.rearrange("p t f -> p (t f)"), [B_tabA], [])
            f2 = dma(dbg_t("sinA", [128, NALL * 8]), sinA.rearrange("p t f -> p (t f)"), [B_tabA], [])
            f3 = dma(dbg_t("cos8", [128, NEXT * 8]), cos8.rearrange("p t f -> p (t f)"), [B_tabE], [])
            f4 = dma(dbg_t("Gb", [128, 8192], BF16), Gb, [B_G], [])
            print(S.emit(final_waits=[f1, f2, f3, f4]))
            return nc
        xt2 = [af32(1024), af32(1024)]
        B_xt = [S.buf("xt0"), S.buf("xt1")]
        xn = abf(1024)
        B_xn = S.buf("xn")
        junk = abf(1024)
        ssA = af32(1)
        B_ss = S.buf("ss")
        hT = abf(1024).rearrange("p (k t) -> p k t", k=8)
        B_hT = S.buf("hT")
        kb = abf(512)
        B_kb = S.buf("kb")
        rtmp = af32(2 * 16 * 8)
        B_rt = S.buf("rtmp")

        def norm_T(xt, Bx, pa, pb, hTd, BhT, xnb=None, Bxnb=None):
            if xnb is None:
                xnb, Bxnb = xn, B_xn
            rms_scale(xt, xnb, Bx, Bxnb, ssA, junk, B_ss)
            for k in range(8):
                bank = pa if k < 4 else pb
                tr(PS[bank][:, (k % 4) * 128:(k % 4 + 1) * 128], xnb[:, k * 128:(k + 1) * 128], [Bxnb], [PB[bank]])
            cp("act", hTd[:, 0:4, :], PS[pa][:, :].rearrange("p (k t) -> p k t", k=4), [PB[pa]], [BhT])
            cp("dve", hTd[:, 4:8, :], PS[pb][:, :].rearrange("p (k t) -> p k t", k=4), [PB[pb]], [BhT])

        xnA = [xn, abf(1024)]
        B_xnA = [B_xn, S.buf("xnA1")]
        hTA = [hT, abf(1024).rearrange("p (k t) -> p k t", k=8)]
        B_hTA = [B_hT, S.buf("hTA1")]
        dma(xt2[0], x_all[0:128, :], [], [B_xt[0]])
        dma(xt2[1], x_all[128:256, :], [], [B_xt[1]])
        norm_T(xt2[0], B_xt[0], 0, 1, hTA[0], B_hTA[0], xnA[0], B_xnA[0])
        for T in range(na):
            s = T % 2
            if T + 1 < na:
                norm_T(xt2[1 - s], B_xt[1 - s], 0, 1, hTA[1 - s], B_hTA[1 - s], xnA[1 - s], B_xnA[1 - s])
            if T + 2 < na:
                dma(xt2[s], x_all[(T + 2) * 128:(T + 3) * 128, :], [], [B_xt[s]])
            hT, B_hT = hTA[s], B_hTA[s]
            for k in range(8):
                mm(PS[2][:, 0:512], hT[:, k, :], wkvA[:, k, :], [B_hT, B_wkvA], [PB[2]], start=(k == 0), stop=(k == 7))
            cp("act", kb, PS[2][:, 0:512], [PB[2]], [B_kb])
            v5 = PS[2][:, 0:512].rearrange("p (j2 jj g d) -> p j2 jj g d", j2=2, jj=2, g=2)
            k5 = kb.rearrange("p (j2 jj g d) -> p j2 jj g d", j2=2, jj=2, g=2)
            rotary(v5[:, :, 0, :, :], k5[:, :, 0, :, :], cosA[:, T, :], sinA[:, T, :], rtmp, [PB[2], B_tabA], [B_kb], B_rt)
            cp("pool", Vs[:, T, :, 0:64], kb[:, 384:512].rearrange("p (g d) -> p g d", g=2), [B_kb], [B_Vs])
            tr(PS[3][:, 0:128], kb[:, 0:128], [B_kb], [PB[3]])
            tr(PS[3][:, 128:256], kb[:, 128:256], [B_kb], [PB[3]])
            tr(PS[3][:, 256:384], kb[:, 256:384], [B_kb], [PB[3]])
            cp("act", KcRaw[:, T * 128:(T + 1) * 128], PS[3][:, 0:128], [PB[3]], [B_KcRaw])
            cp("dve", VcRaw[:, T * 128:(T + 1) * 128], PS[3][:, 128:256], [PB[3]], [B_VcRaw])
            cp("act", KTs[:, T * 128:(T + 1) * 128], PS[3][:, 256:384], [PB[3]], [B_KTs])

        w1z = [abf(32 * 256).rearrange("p (l m) -> p l m", l=32) for _ in range(2)]
        B_w1 = S.buf("w1b")
        memset("pool", w1z[0][64:128, :, :], 0.0, [B_w1])
        memset("pool", w1z[1][0:64, :, :], 0.0, [B_w1])
        w2b = abf(2 * 128).rearrange("p (h d) -> p h d", h=2)
        B_w2 = S.buf("w2b")
        peT = abf(32)
        pef = af32(32)
        B_pe = S.buf("pe")
        hid2 = [abf(2 * 512).rearrange("p (h c) -> p h c", h=2) for _ in range(2)]
        B_hid2 = [S.buf("hid0"), S.buf("hid1")]
        cbias = af32(2)
        B_cb = S.buf("cbias")
        for kvi, raw, Braw in (() if skipcmp else ((0, KcRaw, B_KcRaw), (1, VcRaw, B_VcRaw))):
            w1v = cmp_w1[0][kvi].rearrange("(l d) m -> d l m", d=64)
            for half in range(2):
                for l0 in range(0, 32, 4):
                    k = stg_i[0] % 2
                    stg_i[0] += 1
                    sl = stage[k][64 * half:64 * half + 64, 0:1024]
                    dma(sl.rearrange("p (l m) -> p l m", l=4), w1v[:, l0:l0 + 4, :], [], [B_stage[k]])
                    cp(("pool", "dve")[(l0 // 4) % 2], w1z[half][64 * half:64 * half + 64, l0:l0 + 4, :],
                       sl.rearrange("p (l m) -> p l m", l=4), [B_stage[k]], [B_w1])
            k = stg_i[0] % 2
            stg_i[0] += 1
            dma(stage[k][:, 0:128].rearrange("p (h d) -> p h d", h=2), cmp_w2[0][kvi].rearrange("(h p) d -> p h d", p=128),
                [], [B_stage[k]])
            cp("dve", w2b[:, :, 0:64], stage[k][:, 0:128].rearrange("p (h d) -> p h d", h=2), [B_stage[k]], [B_w2])
            cp("dve", w2b[:, :, 64:128], stage[k][:, 0:128].rearrange("p (h d) -> p h d", h=2), [B_stage[k]], [B_w2])
            dma(pef[0:64, :], cmp_pe[0][kvi].rearrange("l d -> d l"), [], [B_pe], slow=True)
            memset("dve", peT[64:128, :], 0.0, [B_pe])
            cp("dve", peT[0:64, :], pef[0:64, :], [B_pe], [B_pe])
            for half in range(2):
                for l in range(32):
                    mm(PS[4][:, half:half + 1], w1z[0][:, l, half * 128:(half + 1) * 128], peT[:, l:l + 1],
                       [B_w1, B_pe], [PB[4]], start=(l == 0 and half == 0), stop=(l == 31))
            cp("dve", cbias, PS[4][:, 0:2], [PB[4]], [B_cb])
            if not NOBAR:
                barrier()
            if dbg == "A" and kvi == 0 and (DUMPX & 1):
                dma(dbg_t("w1b", [128, 8192], BF16), w1z[0].rearrange("p l m -> p (l m)"), [B_w1], [])
                dma(dbg_t("cbias", [128, 2]), cbias, [B_cb], [])
            rawv = raw.rearrange("p (i s) -> p i s", s=16)
            for g in range(2):
                if not NOBAR:
                    barrier()
                hid, B_hid = hid2[g], B_hid2[g]
                pr = slice(64 * g, 64 * g + 64)
                for half in range(2):
                    for l in range(32):
                        rhs = rawv[:, 0:511, l] if l < 16 else rawv[:, 1:512, l - 16]
                        mm(PS[half][:, 0:511], w1z[g][:, l, half * 128:(half + 1) * 128], rhs, [B_w1, Braw], [PB[half]],
                           start=(l == 0), stop=(l == 31))
                    act(hid[:, half, 0:511], PS[half][:, 0:511], AF.Gelu_apprx_tanh, [PB[half], B_cb], [B_hid],
                        bias=cbias[:, half:half + 1])
                if dbg == "A" and kvi == 0 and (DUMPX & 2):
                    dma(dbg_t(f"hid{g}", [128, 1024], BF16), hid.rearrange("p h c -> p (h c)"), [B_hid], [])
                if kvi == 0:
                    for half in range(2):
                        mm(PS[2][:, 0:511], w2b[:, half, :], hid[:, half, 0:511], [B_w2, B_hid], [PB[2]],
                           start=(half == 0), stop=(half == 1))
                    cp("act", KcT[pr, 0:511], PS[2][pr, 0:511], [PB[2]], [B_KcT])
                else:
                    for c in range(4):
                        m = 128 if c < 3 else 127
                        for half in range(2):
                            mm(PS[2][0:m, c * 64:(c + 1) * 64], hid[:, half, c * 128:c * 128 + m], w2b[:, half, 0:64],
                               [B_w2, B_hid], [PB[2]], start=(half == 0 and c == 0), stop=(half == 1))
                    for c in range(4):
                        m = 128 if c < 3 else 127
                        cp("act", Vc[0:m, c, g, 0:64], PS[2][0:m, c * 64:(c + 1) * 64], [PB[2]], [B_Vc])
        memset("dve", KcT[:, 511:512], 0.0, [B_KcT])
        if dbg == "A":
            fl = [dma(dbg_t("KTs", [128, 8192], BF16), KTs, [B_KTs], []),
                  dma(dbg_t("Vs", [128, 64 * 130], BF16), Vs.rearrange("p t g d -> p (t g d)"), [B_Vs], []),
                  dma(dbg_t("KcT", [128, 512], BF16), KcT, [B_KcT], []),
                  dma(dbg_t("Vc", [128, 4 * 130], BF16), Vc.rearrange("p t g d -> p (t g d)"), [B_Vc], []),
                  dma(dbg_t("KcRaw", [128, 8192], BF16), KcRaw, [B_KcRaw], [])]
            print(S.emit(final_waits=fl))
            return nc
        barrier()
        top[0] = kv_top
        wb1 = abf(8 * 2352).rearrange("p (k c) -> p k c", k=8)
        B_wb1 = S.buf("wb1")
        for k in range(8):
            rows = w_in[0][k * 128:(k + 1) * 128, :]
            sc = gpre[:, k:k + 1]
            loadw(lambda c0, n, k=k: wb1[:, k, c0:c0 + n], rows[:, 0:512], 512, scale=sc, r_extra=[B_small], w=[B_wb1])
            loadw(lambda c0, n, k=k: wb1[:, k, 512:1536], rows[:, 512:1536], 1024, scale=sc, r_extra=[B_small], w=[B_wb1], perm_q=True)
            loadw(lambda c0, n, k=k: wb1[:, k, 1536 + c0:1536 + c0 + n], rows[:, 2048:2864], 816, scale=sc, r_extra=[B_small], w=[B_wb1])
        poolw = abf(4 * 128).rearrange("p (g d) -> p g d", g=4)
        B_cst = S.buf("cst")
        k_ = stg_i[0] % 2
        stg_i[0] += 1
        dma(stage[k_][:, 0:512].rearrange("p (g d) -> p g d", g=4), pool_w[0].rearrange("g c d -> c g d"), [], [B_stage[k_]])
        cp("dve", poolw, stage[k_][:, 0:512].rearrange("p (g d) -> p g d", g=4), [B_stage[k_]], [B_cst])
        pscale = af32(4)
        dma(pscale, pool_scale[0].rearrange("(g d) -> d g", d=128), [], [B_cst], slow=True)
        Ab = abf(1024).rearrange("p (g c t) -> p g c t", g=4, c=2)
        k_ = stg_i[0] % 2
        stg_i[0] += 1
        dma(stage[k_][:, 0:1024], Ab_d, [], [B_stage[k_]])
        cp("dve", Ab.rearrange("p g c t -> p (g c t)"), stage[k_][:, 0:1024], [B_stage[k_]], [B_cst])
        Mw3 = abf(384).rearrange("p (j q) -> p j q", j=3)
        k_ = stg_i[0] % 2
        stg_i[0] += 1
        dma(stage[k_][:, 0:256], Mw_d, [], [B_stage[k_]])
        cp("dve", Mw3[:, 0, :], stage[k_][:, 0:128], [B_stage[k_]], [B_cst])
        cp("dve", Mw3[:, 2, :], stage[k_][:, 128:256], [B_stage[k_]], [B_cst])
        memset("dve", Mw3[:, 1, :], 1.0, [B_cst])
        onesb = abf(1)
        memset("dve", onesb, 1.0, [B_cst])
        qrow = af32(128)
        dma(qrow, qrow_d.partition_broadcast(128), [], [B_cst])
        tqst = af32(NEXT)
        dma(tqst, tqst_d.partition_broadcast(128), [], [B_cst])
        tqt = af32(128)
        B_tqt = S.buf("tqt")
        bsrow = af32(128)
        dma(bsrow, bsrow_d.partition_broadcast(128), [], [B_cst])
        cendrow = af32(512)
        dma(cendrow, cendrow_d.partition_broadcast(128), [], [B_cst])
        kidx = af32(64)
        dma(kidx, kidx_d, [], [B_cst])
        cendcol = af32(4)
        dma(cendcol, cendcol_d, [], [B_cst])
        e0big = af32(128)
        dma(e0big, e0_d.partition_broadcast(128), [], [B_cst])

        xt2 = [af32(1024), af32(1024)]
        B_xt = [S.buf("bxt0"), S.buf("bxt1")]
        xn = abf(1024)
        B_xn = S.buf("bxn")
        junk = abf(1024)
        ssA = af32(1)
        B_ss = S.buf("bss")
        brT0 = abf(24 * 128).rearrange("p (k t) -> p k t", k=24)
        brT = [brT0, brT0]
        B_br0 = S.buf("br0")
        B_br = [B_br0, B_br0]
        rtmp = af32(256)
        B_rt = S.buf("brtmp")

        KmT = abf(4 * 256).rearrange("p (h m) -> p h m", h=4)
        Vm = abf(2 * 4 * 128).rearrange("p (c h d) -> p c h d", c=2, h=4)
        kmb = abf(512)
        mark_m = top[0]
        wmem = abf(8 * 1024).rearrange("p (k c) -> p k c", k=8)
        B_wmem = S.buf("wmem")
        for k in range(8):
            loadw(lambda c0, n, k=k: wmem[:, k, c0:c0 + n], w_mem_kv[0][k * 128:(k + 1) * 128, :], 1024,
                  scale=gmem[:, k:k + 1], r_extra=[B_small], w=[B_wmem])
        B_km, B_vm, B_kmb = S.buf("KmT"), S.buf("Vm"), S.buf("kmb")
        for c in range(2):
            dma(xt2[c], mem_d[c * 128:(c + 1) * 128, :], [], [B_xt[c]])
            norm_T(xt2[c], B_xt[c], 0, 1, brT[c][:, 16:24, :], B_br[c])
            for nb in range(2):
                for k in range(8):
                    mm(PS[2 + nb][:, :], brT[c][:, 16 + k, :], wmem[:, k, nb * 512:(nb + 1) * 512], [B_br[c], B_wmem], [PB[2 + nb]],
                       start=(k == 0), stop=(k == 7))
            cp("act", kmb, PS[2][:, :], [PB[2]], [B_kmb], scale=float(128 ** -0.5))
            cp("dve", Vm[:, c, :, :], PS[3][:, :].rearrange("p (h d) -> p h d", h=4), [PB[3]], [B_vm])
            for h in range(4):
                tr(PS[4][:, h * 128:(h + 1) * 128], kmb[:, h * 128:(h + 1) * 128], [B_kmb], [PB[4]])
            cp("act", KmT[:, :, c * 128:(c + 1) * 128], PS[4][:, :].rearrange("p (h m) -> p h m", h=4), [PB[4]], [B_km])

        barrier()
        top[0] = mark_m
        kwb = abf(256)
        B_kwb = S.buf("kwb")
        ub = [abf(512), abf(512)]
        B_ub = [S.buf("ub0"), S.buf("ub1")]
        uf = af32(512)
        B_uf = S.buf("uf")
        pbb = abf(512)
        B_pbb = S.buf("pbb")
        pT = abf(512).rearrange("p (g t) -> p g t", g=4)
        B_pT = S.buf("pT")
        qb = abf(1024)
        B_qb = S.buf("qb")
        qTz = [abf(1024).rearrange("p (k t) -> p k t", k=8) for _ in range(2)]
        B_qT = S.buf("qT")
        memset("pool", qTz[0], 0.0, [B_qT])
        memset("pool", qTz[1], 0.0, [B_qT])
        gn = af32(48)
        B_gn = S.buf("gn")
        qxb = abf(512)
        B_qxb = S.buf("qxb")
        qxT = abf(512).rearrange("p (h t) -> p h t", h=4)
        B_qxT = S.buf("qxT")
        mpT = [abf(512).rearrange("p (h t) -> p h t", h=4) for _ in range(2)]
        B_mpT = [S.buf("mpT0"), S.buf("mpT1")]
        rsm = af32(4)
        B_rsm = S.buf("rsm")
        ymemb = abf(512)
        B_ymem = S.buf("ymem")
        ef0 = af32(512)
        ef = [ef0, ef0]
        B_ef0 = S.buf("ef0")
        B_ef = [B_ef0, B_ef0]
        em = af32(512)
        B_em = S.buf("em")
        ssum = af32(2)
        B_ssum = S.buf("ssum")
        Pb = [af32(516), af32(516)]
        B_Pb = [S.buf("Pb0"), S.buf("Pb1")]
        cmrow = af32(512)
        B_cm = S.buf("cmrow")
        imp = af32(128)
        nd = af32(128)
        itmp = af32(128)
        wk = af32(128)
        mx = af32(16)
        B_imp = S.buf("imp")
        selb = abf(128)
        B_sel = S.buf("sel")
        selT = [abf(128), abf(128)]
        B_selT = [S.buf("selT0"), S.buf("selT1")]
        NSL = 3
        NSU = 6
        ucount = [0]
        mk = [abf(128) for _ in range(NSL)]
        B_mk = [S.buf(f"mk{i}") for i in range(NSL)]
        pTu = [abf(512).rearrange("p (h t) -> p h t", h=4) for _ in range(NSU)]
        B_pTu = [S.buf(f"pTu{i}") for i in range(NSU)]
        ynsa = af32(1024)
        B_yn = S.buf("ynsa")
        ytmp = af32(256)
        B_yt = S.buf("ytmp")
        rs = af32(8)
        B_rs = S.buf("rs")
        ynb = abf(1024)
        B_ynb = S.buf("ynb")
        memset("dve", Pb[0], 0.0, [B_Pb[0]])
        memset("dve", Pb[1], 0.0, [B_Pb[1]])
        nchunk = [0]

        def attend(e, g, br, chunks):
            LOOK = 3
            units = [(n, q) for n in range(len(chunks)) for q in range(2)]
            nu = len(units)
            info = {}

            def stage_scores(u):
                n, q = units[u]
                KT, V, mk_pe, mk_dve = chunks[n]
                if q == 0:
                    cs = nchunk[0]
                    nchunk[0] += 1
                    info[n] = (cs % 2, cs % NSL)
                    if mk_pe is not None:
                        mk_pe(cs % 2)
                bank = 2 + (ucount[0] % 4)
                ub = ucount[0] % NSU
                ucount[0] += 1
                mm(PS[bank][:, :], KT, qTz[g][:, 4 * q:4 * q + 4, :], [B_qT, B_KTs, B_KTw, B_KcT], [PB[bank]])
                return (bank, ub)

            pend = [stage_scores(u) for u in range(min(LOOK, nu))]
            for u, (n, q) in enumerate(units):
                bank, ub = pend.pop(0)
                if u + LOOK < nu:
                    pend.append(stage_scores(u + LOOK))
                KT, V, mk_pe, mk_dve = chunks[n]
                sl, bs = info[n]
                pt = pTu[ub]
                act(pt, PS[bank][:, :].rearrange("p (h t) -> p h t", h=4), AF.Exp, [PB[bank]], [B_pTu[ub]])
                if q == 0:
                    mk_dve(sl, bs)
                mb = mk[bs].unsqueeze(1).to_broadcast([128, 4, 128])
                tt(("dve", "pool")[q], pt, pt, mb, ALU.mult, [B_pTu[ub], B_mk[bs]], [B_pTu[ub]])
                for hh in range(4 * q, 4 * q + 4):
                    mm(PS[q][:, (hh % 4) * 128:(hh % 4) * 128 + 65], pt[:, hh % 4, :], V, [B_pTu[ub], B_Vs, B_Vw, B_Vc], [PB[q]],
                       start=(n == 0 and hh % 4 == 0), stop=(n == len(chunks) - 1))
            gn3 = gn.rearrange("p (h b) -> p h b", b=3)
            for b in range(2):
                Ov = PS[b][:, :].rearrange("p (h d) -> p h d", h=4)
                h0 = 8 * g + 4 * b
                rsb = rs[:, 4 * b:4 * b + 4]
                ts("dve", rsb, Ov[:, :, 64], 1e-30, None, ALU.max, None, [PB[b]], [B_rs])
                S.op("dve", lambda e_, rsb=rsb: e_.reciprocal(out=rsb, in_=rsb), [B_rs], [B_rs])
                tt("dve", rsb, rsb, gn3[:, h0:h0 + 4, br], ALU.mult, [B_rs, B_gn], [B_rs])
                dst = ynsa.rearrange("p (h d) -> p h d", h=16)[:, h0:h0 + 4, :]
                rb = rsb.unsqueeze(2).to_broadcast([128, 4, 64])
                if br == 0:
                    tt("dve", dst, Ov[:, :, 0:64], rb, ALU.mult, [PB[b], B_rs], [B_yn])
                else:
                    yt = ytmp.rearrange("p (h d) -> p h d", h=4)
                    tt("dve", yt, Ov[:, :, 0:64], rb, ALU.mult, [PB[b], B_rs], [B_yt])
                    tt("pool", dst, dst, yt, ALU.add, [B_yn, B_yt], [B_yn])

        dma(xt2[0], x_ext[0:128, :], [], [B_xt[0]])
        for e in range(ntile_b1):
            s = e % 2
            if e + 1 < NEXT:
                dma(xt2[1 - s], x_ext[(e + 1) * 128:(e + 2) * 128, :], [], [B_xt[1 - s]])
            bt = brT[s]
            hTd = bt[:, 16:24, :]
            norm_T(xt2[s], B_xt[s], 0, 1, hTd, B_br[s])
            for k in range(8):
                mm(PS[2][:, 0:256], hTd[:, k, :], wb1[:, k, 1536:1792], [B_br[s], B_wb1], [PB[2]], start=(k == 0), stop=(k == 7))
            cp("act", kwb, PS[2][:, 0:256], [PB[2]], [B_kwb])
            rotary(PS[2][:, 0:128].rearrange("p (a g d) -> p a g d", a=1, g=2), kwb[:, 0:128].rearrange("p (a g d) -> p a g d", a=1, g=2),
                   cosE[:, e, :], sinE[:, e, :], rtmp, [PB[2], B_tabE], [B_kwb], B_rt)
            cp("pool", Vw[:, e, :, 0:64], kwb[:, 128:256].rearrange("p (g d) -> p g d", g=2), [B_kwb], [B_Vw])
            tr(PS[3][:, 0:128], kwb[:, 0:128], [B_kwb], [PB[3]])
            cp("act", KTw[:, e * 128:(e + 1) * 128], PS[3][:, 0:128], [PB[3]], [B_KTw])
            for k in range(8):
                mm(PS[4][:, :], hTd[:, k, :], wb1[:, k, 0:512], [B_br[s], B_wb1], [PB[4]], start=(k == 0), stop=(k == 7))
            cp("act", ub[s], PS[4][:, :], [PB[4]], [B_ub[s]])
            if e < 4:
                continue
            cp("dve", uf, PS[4][:, :], [PB[4]], [B_uf])
            i = e - 4
            for gi in range(4):
                blk = slice(gi * 128, (gi + 1) * 128)
                mm(PS[5][:, blk], Ab[:, gi, 0, :], ub[s][:, blk], [B_cst, B_ub[s]], [PB[5]], start=True, stop=False)
                mm(PS[5][:, blk], Ab[:, gi, 1, :], ub[1 - s][:, blk], [B_cst, B_ub[1 - s]], [PB[5]], start=False, stop=True)
            for gi in range(4):
                blk = slice(gi * 128, (gi + 1) * 128)
                stt(pbb[:, blk], PS[5][:, blk], invcnt[:, e * 4 + gi:e * 4 + gi + 1], uf[:, blk], ALU.mult, ALU.subtract,
                    [PB[5], B_uf, B_small], [B_pbb])
            for gi in range(4):
                blk = slice(gi * 128, (gi + 1) * 128)
                tr(PS[6][:, blk], pbb[:, blk], [B_pbb], [PB[6]])
            cp("act", pT, PS[6][:, :].rearrange("p (g t) -> p g t", g=4), [PB[6]], [B_pT])
            for gi in range(4):
                blk = slice(gi * 128, (gi + 1) * 128)
                mm(PS[5][:, blk], poolw[:, gi, :], pT[:, gi, :], [B_cst, B_pT], [PB[5]])
            for gi in range(4):
                blk = slice(gi * 128, (gi + 1) * 128)
                ts("dve", bt[:, gi, :], PS[5][:, blk], pscale[:, gi:gi + 1], None, ALU.mult, None, [PB[5], B_cst], [B_br[s]])
            for nb in range(2):
                for k in range(8):
                    mm(PS[nb][:, :], hTd[:, k, :], wb1[:, k, 512 + nb * 512:1024 + nb * 512], [B_br[s], B_wb1], [PB[nb]],
                       start=(k == 0), stop=(k == 7))
            for nb in range(2):
                qv = qb[:, nb * 512:(nb + 1) * 512]
                cp("act", qv, PS[nb][:, :], [PB[nb]], [B_qb], scale=0.125)
                rotary(PS[nb][:, :].rearrange("p (c g d) -> p c g d", c=4, g=2), qv.rearrange("p (c g d) -> p c g d", c=4, g=2),
                       cos8[:, e, :], sin8[:, e, :], rtmp, [PB[nb], B_tabE], [B_qb], B_rt)
            for k in range(8):
                tr(PS[2 + k // 4][:, (k % 4) * 128:(k % 4 + 1) * 128], qb[:, k * 128:(k + 1) * 128], [B_qb], [PB[2 + k // 4]])
            for (bk, c0) in ((2, 0), (3, 4)):
                cp("act", qTz[0][0:64, c0:c0 + 4, :], PS[bk][0:64, :].rearrange("p (k t) -> p k t", k=4), [PB[bk]], [B_qT])
                cp("dve", qTz[1][64:128, c0:c0 + 4, :], PS[bk][64:128, :].rearrange("p (k t) -> p k t", k=4), [PB[bk]], [B_qT])
            for k in range(8):
                mm(PS[4][:, 0:48], hTd[:, k, :], wb1[:, k, 1792:1840], [B_br[s], B_wb1], [PB[4]], start=(k == 0), stop=(k == 7))
            act(gn, PS[4][:, 0:48], AF.Sigmoid, [PB[4]], [B_gn])
            for k in range(8):
                mm(PS[5][:, :], hTd[:, k, :], wb1[:, k, 1840:2352], [B_br[s], B_wb1], [PB[5]], start=(k == 0), stop=(k == 7))
            cp("act", qxb, PS[5][:, :], [PB[5]], [B_qxb])
            for h in range(4):
                tr(PS[6][:, h * 128:(h + 1) * 128], qxb[:, h * 128:(h + 1) * 128], [B_qxb], [PB[6]])
            cp("dve", qxT, PS[6][:, :].rearrange("p (h t) -> p h t", h=4), [PB[6]], [B_qxT])
            for c in range(2):
                for h in range(4):
                    mm(PS[7][:, h * 128:(h + 1) * 128], KmT[:, h, c * 128:(c + 1) * 128], qxT[:, h, :], [B_km, B_qxT], [PB[7]])
                act(mpT[c], PS[7][:, :].rearrange("p (h t) -> p h t", h=4), AF.Exp, [PB[7]], [B_mpT[c]])
            for h in range(4):
                for c in range(2):
                    mm(PS[5][:, h * 128:(h + 1) * 128], mpT[c][:, h, :], Vm[:, c, h, :], [B_mpT[c], B_vm], [PB[5]],
                       start=(c == 0), stop=(c == 1))
            for h in range(4):
                for c in range(2):
                    mm(PS[6][:, h:h + 1], mpT[c][:, h, :], onesb[:, 0:1], [B_mpT[c], B_cst], [PB[6]],
                       start=(c == 0), stop=(c == 1))
            S.op("dve", lambda e_: e_.reciprocal(out=rsm, in_=PS[6][:, 0:4]), [PB[6]], [B_rsm])
            tt("dve", ymemb.rearrange("p (h d) -> p h d", h=4), PS[5][:, :].rearrange("p (h d) -> p h d", h=4),
               rsm.unsqueeze(2).to_broadcast([128, 4, 128]), ALU.mult, [PB[5], B_rsm], [B_ymem])
            for h in range(4):
                tr(PS[7][:, h * 128:(h + 1) * 128], ymemb[:, h * 128:(h + 1) * 128], [B_ymem], [PB[7]])
            cp("act", bt[:, 12:16, :], PS[7][:, :].rearrange("p (h t) -> p h t", h=4), [PB[7]], [B_br[s]])
            tqs = tqe[:, e:e + 1]
            ts("dve", cmrow, cendrow, tqs, None, ALU.is_le, None, [B_cst, B_small], [B_cm])
            ts("dve", tqt, qrow, tqst[:, e:e + 1], None, ALU.add, None, [B_cst], [B_tqt])
            for g in range(2):
                pr = slice(64 * g, 64 * g + 64)
                Pv = Pb[g][:, 1:513]
                for c_ in range(8):
                    a = c_ % 2
                    mm(PS[4 + a][:, :], qTz[g][:, c_, :], KcT[:, 0:512], [B_qT, B_KcT], [PB[4 + a]])
                    act(ef[a], PS[4 + a][:, :], AF.Exp, [PB[4 + a]], [B_ef[a]])
                    stt(em, ef[a], 1.0, cmrow, ALU.mult, ALU.mult, [B_ef[a], B_cm], [B_em, B_ssum], accum=ssum[:, 0:1])
                    ts("dve", ssum[:, 1:2], ssum[:, 0:1], 1e-30, None, ALU.max, None, [B_ssum], [B_ssum])
                    S.op("dve", lambda e_: e_.reciprocal(out=ssum[:, 1:2], in_=ssum[:, 1:2]), [B_ssum], [B_ssum])
                    if c_ == 0:
                        ts("dve", Pv, em, ssum[:, 1:2], None, ALU.mult, None, [B_em, B_ssum], [B_Pb[g]])
                    else:
                        stt(Pv, em, ssum[:, 1:2], Pv, ALU.mult, ALU.add, [B_em, B_ssum, B_Pb[g]], [B_Pb[g]])
                S.op("dve", lambda e_, g=g: e_.tensor_reduce(out=imp, in_=Pb[g][:, 0:512].rearrange("p (j s) -> p j s", s=4),
                                                          axis=AX.X, op=ALU.add), [B_Pb[g]], [B_imp])
                tt("dve", imp, imp, Pb[g][:, 4:516].rearrange("p (j s) -> p j s", s=4)[:, :, 0], ALU.add, [B_Pb[g], B_imp], [B_imp])
                ts("dve", nd, bsrow, tqs, None, ALU.subtract, None, [B_cst, B_small, B_imp], [B_imp])
                ts("dve", itmp, nd, -128.0, BIG, ALU.is_gt, ALU.mult, [B_imp], [B_imp])
                tt("dve", imp, imp, itmp, ALU.add, [B_imp], [B_imp])
                ts("dve", itmp, nd, 0.0, -3.0 * BIG, ALU.is_gt, ALU.mult, [B_imp], [B_imp])
                tt("dve", imp, imp, itmp, ALU.add, [B_imp], [B_imp])
                tt("dve", imp, imp, e0big, ALU.add, [B_imp, B_cst], [B_imp])
                S.op("dve", lambda e_: e_.max(out=mx[:, 0:8], in_=imp), [B_imp], [B_imp])
                S.op("dve", lambda e_: e_.match_replace(out=wk, in_to_replace=mx[:, 0:8], in_values=imp, imm_value=-1e30), [B_imp], [B_imp])
                S.op("dve", lambda e_: e_.max(out=mx[:, 8:16], in_=wk), [B_imp], [B_imp])
                ts("dve", selb, imp, mx[:, 15:16], None, ALU.is_ge, None, [B_imp], [B_sel])
                tr(PS[6][:, g * 128:(g + 1) * 128], selb, [B_sel], [PB[6]])
                cp("act", selT[g], PS[6][:, g * 128:(g + 1) * 128], [PB[6]], [B_selT[g]])
            for g in range(2):
                pr = slice(64 * g, 64 * g + 64)

                def mk_cmp(c):
                    return (None, lambda sl, bs: ts("dve", mk[bs], tqt, cendcol[:, c:c + 1], None, ALU.is_ge, None, [B_cst, B_tqt], [B_mk[bs]]))

                def mk_win(j, ee):
                    v = 0 if j == 0 else (2 if j == 4 else 1)
                    return (None, lambda sl, bs: ts("dve", mk[bs], Mw3[:, v, :], wval[:, ee:ee + 1], None, ALU.mult, None, [B_cst, B_small], [B_mk[bs]]))

                def mk_sel(cc, g=g):
                    def f_pe(sl):
                        mm(PS[6 + sl][:, 256:384], Gb[:, cc * 128:(cc + 1) * 128], selT[g], [B_G, B_selT[g]], [PB[6 + sl]])

                    def f_dve(sl, bs):
                        stt(mk[bs], tqt, kidx[:, cc:cc + 1], PS[6 + sl][:, 256:384], ALU.is_ge, ALU.mult,
                            [B_cst, B_tqt, PB[6 + sl]], [B_mk[bs]])
                    return (f_pe, f_dve)

                attend(e, g, 0, [(KcT[:, c * 128:(c + 1) * 128], Vc[:, c, g, :]) + mk_cmp(c) for c in range(4)])
                attend(e, g, 1, [(KTs[:, cc * 128:(cc + 1) * 128], Vs[:, cc, g, :]) + mk_sel(cc) for cc in range(48 + i)])
                attend(e, g, 2, [(KTw[:, (e - 4 + j) * 128:(e - 3 + j) * 128], Vw[:, e - 4 + j, g, :]) + mk_win(j, e - 4 + j)
                                 for j in range(5)])
            cp("act", ynb, ynsa, [B_yn], [B_ynb])
            for k in range(8):
                tr(PS[2 + k // 4][:, (k % 4) * 128:(k % 4 + 1) * 128], ynb[:, k * 128:(k + 1) * 128], [B_ynb], [PB[2 + k // 4]])
            cp("act", bt[:, 4:8, :], PS[2][:, :].rearrange("p (k t) -> p k t", k=4), [PB[2]], [B_br[s]])
            cp("dve", bt[:, 8:12, :], PS[3][:, :].rearrange("p (k t) -> p k t", k=4), [PB[3]], [B_br[s]])
            lastbr = dma(br_scr[i], bt.rearrange("p k t -> p (k t)"), [B_br[s]], [])
            if dbg == "B1":
                lastbr = dma(dbg_t(f"br{i}", [128, 24 * 128], BF16), bt.rearrange("p k t -> p (k t)"), [B_br[s]], [])
                fl1 = [lastbr, dma(dbg_t(f"ynsa{i}", [128, 1024]), ynsa, [B_yn], []),
                       dma(dbg_t(f"qb{i}", [128, 1024], BF16), qb, [B_qb], []),
                       dma(dbg_t(f"gn{i}", [128, 48]), gn, [B_gn], []),
                       dma(dbg_t(f"selT{i}", [128, 128], BF16), selT[1], [B_selT[1]], []),
                       dma(dbg_t(f"Pb{i}", [128, 516]), Pb[1], [B_Pb[1]], [])]
        if dbg == "B1":
            print(S.emit(final_waits=fl1))
            return nc
        barrier()
        top[0] = persist_top
        wg = abf(8 * 3072).rearrange("p (k c) -> p k c", k=8)
        wbp = abf(4 * 1024).rearrange("p (k c) -> p k c", k=4)
        wbn = abf(8 * 1024).rearrange("p (k c) -> p k c", k=8)
        wbx = abf(4 * 1024).rearrange("p (k c) -> p k c", k=4)
        wo = abf(8 * 1024).rearrange("p (k c) -> p k c", k=8)
        B_w2p = S.buf("w_b2")
        for k in range(8):
            loadw(lambda c0, n, k=k: wg[:, k, c0:c0 + n], w_in[0][k * 128:(k + 1) * 128, 2864:5936], 3072,
                  scale=gpre[:, k:k + 1], r_extra=[B_small], w=[B_w2p])
            loadw(lambda c0, n, k=k: wbn[:, k, c0:c0 + n], w_br_nsa[0][k * 128:(k + 1) * 128, :], 1024, w=[B_w2p])
            loadw(lambda c0, n, k=k: wo[:, k, c0:c0 + n], w_out[0][k * 128:(k + 1) * 128, :], 1024, w=[B_w2p])
            if k < 4:
                loadw(lambda c0, n, k=k: wbp[:, k, c0:c0 + n], w_br_pool[0][k * 128:(k + 1) * 128, :], 1024, w=[B_w2p])
                loadw(lambda c0, n, k=k: wbx[:, k, c0:c0 + n], w_br_xa[0][k * 128:(k + 1) * 128, :], 1024, w=[B_w2p])
        gpost = af32(1024)
        B_gp = S.buf("gpost")
        dma(gpost, post_mix_g.partition_broadcast(128), [], [B_gp])
        brT = [abf(24 * 128).rearrange("p (k t) -> p k t", k=24) for _ in range(2)]
        B_br = [S.buf("c_br0"), S.buf("c_br1")]
        xt2 = [af32(1024), af32(1024)]
        B_xt = [S.buf("c_xt0"), S.buf("c_xt1")]
        sg = af32(1024)
        B_sg = S.buf("sg")
        yy = af32(1024)
        B_y = S.buf("yy")
        ytm = af32(1024)
        B_ytm = S.buf("ytm")
        yb = abf(1024)
        B_yb = S.buf("yb")
        yT = abf(1024).rearrange("p (k t) -> p k t", k=8)
        B_yT = S.buf("yT")
        junk = abf(512)
        x1t = [af32(1024), af32(1024)]
        B_x1 = [S.buf("x1t0"), S.buf("x1t1")]

        def post_norm_res(pa, pb, gp, Bg, xres, Bxres, dst, Bdst):
            act(junk[:, 0:512], PS[pa][:, :], AF.Square, [PB[pa]], [B_ss2], accum=ss2[:, 0:1])
            act(junk[:, 0:512], PS[pb][:, :], AF.Square, [PB[pb]], [B_ss2], accum=ss2[:, 1:2])
            tt("dve", ss2[:, 2:3], ss2[:, 0:1], ss2[:, 1:2], ALU.add, [B_ss2], [B_ss2])
            ts("dve", ss2[:, 2:3], ss2[:, 2:3], 1.0 / 1024, 1e-6, ALU.mult, ALU.add, [B_ss2], [B_ss2])
            act(ss2[:, 2:3], ss2[:, 2:3], AF.Sqrt, [B_ss2], [B_ss2])
            S.op("dve", lambda e_: e_.reciprocal(out=ss2[:, 3:4], in_=ss2[:, 2:3]), [B_ss2], [B_ss2])
            for nb, bank in enumerate((pa, pb)):
                blk = slice(nb * 512, (nb + 1) * 512)
                stt(dst[:, blk], PS[bank][:, :], ss2[:, 3:4], gp[:, blk], ALU.mult, ALU.mult, [PB[bank], B_ss2, Bg], [Bdst])
                tt("pool", dst[:, blk], dst[:, blk], xres[:, blk], ALU.add, [Bdst, Bxres], [Bdst])

        dma(brT[0].rearrange("p k t -> p (k t)"), br_scr[0], [], [B_br[0]])
        dma(xt2[0], x_ext[4 * 128:5 * 128, :], [], [B_xt[0]])
        for i in range(17):
            s = i % 2
            e = i + 4
            if i + 1 < 17:
                dma(brT[1 - s].rearrange("p k t -> p (k t)"), br_scr[i + 1], [], [B_br[1 - s]])
                dma(xt2[1 - s], x_ext[(e + 1) * 128:(e + 2) * 128, :], [], [B_xt[1 - s]])
            bt = brT[s]
            for br in range(3):
                for nb in range(2):
                    for k in range(8):
                        mm(PS[nb][:, :], bt[:, 16 + k, :], wg[:, k, br * 1024 + nb * 512:br * 1024 + (nb + 1) * 512],
                           [B_br[s], B_w2p], [PB[nb]], start=(k == 0), stop=(k == 7))
                wsel, off, nk = ((wbp, 0, 4), (wbn, 4, 8), (wbx, 12, 4))[br]
                for nb in range(2):
                    for k in range(nk):
                        mm(PS[2 + nb][:, :], bt[:, off + k, :], wsel[:, k, nb * 512:(nb + 1) * 512],
                           [B_br[s], B_w2p], [PB[2 + nb]], start=(k == 0), stop=(k == nk - 1))
                for nb in range(2):
                    blk = slice(nb * 512, (nb + 1) * 512)
                    act(sg[:, blk], PS[nb][:, :], AF.Sigmoid, [PB[nb]], [B_sg])
                    if br == 0:
                        tt("dve", yy[:, blk], sg[:, blk], PS[2 + nb][:, :], ALU.mult, [B_sg, PB[2 + nb]], [B_y])
                    else:
                        tt("dve", ytm[:, blk], sg[:, blk], PS[2 + nb][:, :], ALU.mult, [B_sg, PB[2 + nb]], [B_ytm])
                        tt("pool", yy[:, blk], yy[:, blk], ytm[:, blk], ALU.add, [B_y, B_ytm], [B_y])
            cp("act", yb, yy, [B_y], [B_yb])
            for k in range(8):
                tr(PS[4 + k // 4][:, (k % 4) * 128:(k % 4 + 1) * 128], yb[:, k * 128:(k + 1) * 128], [B_yb], [PB[4 + k // 4]])
            cp("act", yT[:, 0:4, :], PS[4][:, :].rearrange("p (k t) -> p k t", k=4), [PB[4]], [B_yT])
            cp("dve", yT[:, 4:8, :], PS[5][:, :].rearrange("p (k t) -> p k t", k=4), [PB[5]], [B_yT])
            for nb in range(2):
                for k in range(8):
                    mm(PS[6 + nb][:, :], yT[:, k, :], wo[:, k, nb * 512:(nb + 1) * 512], [B_yT, B_w2p], [PB[6 + nb]],
                       start=(k == 0), stop=(k == 7))
            post_norm_res(6, 7, gpost, B_gp, xt2[s], B_xt[s], x1t[s], B_x1[s])
            lx = dma(x1_scr[i], x1t[s], [B_x1[s]], [])
            if dbg == "B2":
                lx = dma(dbg_t(f"x1_{i}", [128, 1024]), x1t[s], [B_x1[s]], [])
        if dbg == "B2":
            print(S.emit(final_waits=[lx]))
            return nc
        barrier()
        top[0] = persist_top

        wup = abf(8 * 5632).rearrange("p (k c) -> p k c", k=8)
        wdn = abf(22 * 1024).rearrange("p (k c) -> p k c", k=22)
        B_w3 = S.buf("w_c")
        for k in range(8):
            loadw(lambda c0, n, k=k: wup[:, k, c0:c0 + n], w_up[0][k * 128:(k + 1) * 128, :], 5632,
                  scale=gffn[:, k:k + 1], r_extra=[B_small], w=[B_w3])
        for k in range(22):
            loadw(lambda c0, n, k=k: wdn[:, k, c0:c0 + n], w_down[0][k * 128:(k + 1) * 128, :], 1024, w=[B_w3])
        convp = af32(44 * 4).rearrange("p (j c) -> p j c", c=4)
        B_cv = S.buf("convp")
        for kk in range(3):
            dma(convp[:, :, kk], conv_w[0][kk].rearrange("(j p) -> p j", p=128), [], [B_cv], slow=True)
        dma(convp[:, :, 3], conv_b[0].rearrange("(j p) -> p j", p=128), [], [B_cv], slow=True)
        gpost2 = af32(1024)
        B_gp2 = S.buf("gpost2")
        dma(gpost2, post_ffn_g.partition_broadcast(128), [], [B_gp2])
        xt2 = [af32(1024), af32(1024)]
        B_xt = [S.buf("d_xt0"), S.buf("d_xt1")]
        xn = abf(1024)
        B_xn = S.buf("d_xn")
        junk = abf(1024)
        ssA = af32(2)
        B_ss = S.buf("d_ss")
        h2Ts = [abf(8 * 130).rearrange("p (k t) -> p k t", k=8) for _ in range(2)]
        B_h2s = [S.buf("h2T0"), S.buf("h2T1")]
        xnC = [xn, abf(1024)]
        B_xnC = [B_xn, S.buf("xnC1")]
        aT = abf(22 * 128).rearrange("p (k t) -> p k t", k=22)
        B_aT = S.buf("aT")
        cg = [af32(128), af32(128)]
        cv = [af32(128), af32(128)]
        gl = [af32(128), af32(128)]
        B_cg = [S.buf("cg0"), S.buf("cg1")]
        B_cvb = [S.buf("cv0"), S.buf("cv1")]
        B_gl = [S.buf("gl0"), S.buf("gl1")]
        ot = [af32(1024), af32(1024)]
        B_ot = [S.buf("ot0"), S.buf("ot1")]
        xt3 = [xt2[0], xt2[1], af32(1024)]
        B_xt3 = [B_xt[0], B_xt[1], S.buf("d_xt2")]
        fins = []

        def c_stage1(i):
            s_ = i % 2
            x3 = i % 3
            rms_scale(xt3[x3], xnC[s_], B_xt3[x3], B_xnC[s_], ssA[:, 0:1], junk, B_ss)
            for k in range(8):
                bank = k // 4
                tr(PS[bank][:, (k % 4) * 128:(k % 4 + 1) * 128], xnC[s_][:, k * 128:(k + 1) * 128], [B_xnC[s_]], [PB[bank]])
            cp("act", h2Ts[s_][:, 0:4, 2:130], PS[0][:, :].rearrange("p (k t) -> p k t", k=4), [PB[0]], [B_h2s[s_]])
            cp("dve", h2Ts[s_][:, 4:8, 2:130], PS[1][:, :].rearrange("p (k t) -> p k t", k=4), [PB[1]], [B_h2s[s_]])
            if i == 1:
                ts("pool", h2Ts[s_][:, :, 0:2], h2Ts[1 - s_][:, :, 128:130], hflag[:, 0:1], None, ALU.mult, None,
                   [B_h2s[1 - s_], B_small, B_h2s[s_]], [B_h2s[s_]])
            elif i > 1:
                cp("pool", h2Ts[s_][:, :, 0:2], h2Ts[1 - s_][:, :, 128:130], [B_h2s[1 - s_], B_h2s[s_]], [B_h2s[s_]])

        dma(xt3[0], x1_scr[0], [], [B_xt3[0]])
        dma(xt3[1], x1_scr[1], [], [B_xt3[1]])
        dma(xt3[2], x1_scr[2], [], [B_xt3[2]])
        c_stage1(0)
        for i in range(17):
            s = i % 2
            if i + 1 < 17:
                c_stage1(i + 1)
            if i == 0:
                dma(xt3[0], x1_scr[3], [], [B_xt3[0]])
                continue
            h2T, B_h2 = h2Ts[s], B_h2s[s]
            for j in range(22):
                a = j % 2
                bg, bv = 2 + 2 * a, 3 + 2 * a
                for k in range(8):
                    mm(PS[bg][:, 0:130], wup[:, k, j * 128:(j + 1) * 128], h2T[:, k, :], [B_w3, B_h2], [PB[bg]],
                       start=(k == 0), stop=(k == 7))
                for k in range(8):
                    mm(PS[bv][:, 0:130], wup[:, k, (22 + j) * 128:(23 + j) * 128], h2T[:, k, :], [B_w3, B_h2], [PB[bv]],
                       start=(k == 0), stop=(k == 7))
                for (bank, dstc, Bd, jj) in ((bg, cg[a], B_cg[a], j), (bv, cv[a], B_cvb[a], 22 + j)):
                    act(dstc, PS[bank][:, 2:130], AF.Identity, [PB[bank], B_cv], [Bd], bias=convp[:, jj, 3:4], scale=convp[:, jj, 2:3])
                    stt(dstc, PS[bank][:, 1:129], convp[:, jj, 1:2], dstc, ALU.mult, ALU.add, [PB[bank], B_cv, Bd], [Bd])
                    stt(dstc, PS[bank][:, 0:128], convp[:, jj, 0:1], dstc, ALU.mult, ALU.add, [PB[bank], B_cv, Bd], [Bd])
                act(gl[a], cg[a], AF.Gelu_apprx_tanh, [B_cg[a]], [B_gl[a]])
                tt("pool", aT[:, j, :], gl[a], cv[a], ALU.mult, [B_gl[a], B_cvb[a]], [B_aT])
            for nb in range(2):
                for j in range(22):
                    mm(PS[6 + nb][:, :], aT[:, j, :], wdn[:, j, nb * 512:(nb + 1) * 512], [B_aT, B_w3], [PB[6 + nb]],
                       start=(j == 0), stop=(j == 21))
            post_norm_res(6, 7, gpost2, B_gp2, xt3[i % 3], B_xt3[i % 3], ot[s], B_ot[s])
            fins.append(dma(out_d[(i - 1) * 128:i * 128, :], ot[s], [B_ot[s]], []))
            if i + 3 < 17:
                dma(xt3[i % 3], x1_scr[i + 3], [], [B_xt3[i % 3]])
        stats = S.emit(final_waits=fins)
        print("emit stats", stats, flush=True)
    return nc


def _consts():
    c = {}
    c["ident"] = np.eye(128, dtype=np.float32)
    k = np.arange(8192)
    c["Gm"] = (k[None, :] // 64 == np.arange(128)[:, None]).astype(np.float32)
    c["invf"] = (np.float32(500000.0) ** (-np.arange(8, dtype=np.float32) * np.float32(2.0 / 16))).astype(np.float32).reshape(1, 8)
    c["bsrow"] = (64.0 * np.arange(128, dtype=np.float32)).reshape(1, 128)
    ce = (16.0 * np.arange(512, dtype=np.float32) + 31.0)
    ce[511] = 1e9
    c["cendrow"] = ce.reshape(1, 512)
    c["kidx"] = (128.0 * np.arange(64)[None, :] + np.arange(128)[:, None]).astype(np.float32)
    c["cendcol"] = np.ascontiguousarray(ce.reshape(4, 128).T)
    p = np.arange(128)[:, None]
    q = np.arange(128)[None, :]
    c["Mw"] = np.concatenate([(q < p), (q >= p)], axis=1).astype(np.float32)
    e0 = np.zeros((1, 128), np.float32)
    e0[0, 0] = BIG
    c["e0big"] = e0
    A = np.zeros((128, 4, 2, 128), np.float32)
    for gi, w in enumerate((2, 4, 8, 16)):
        tp = np.arange(128)[:, None]
        t = np.arange(128)[None, :]
        A[:, gi, 0, :] = ((t - tp >= 0) & (t - tp < w))
        A[:, gi, 1, :] = ((t + 128 - tp >= 0) & (t + 128 - tp < w))
    c["Aband"] = A.reshape(128, 1024)
    c["qrow"] = np.arange(128, dtype=np.float32).reshape(1, 128)
    return c


_PROG = {}


def kernel(**inputs):
    x = np.asarray(inputs["x"], dtype=np.float32)
    mem = np.asarray(inputs["mem"], dtype=np.float32)
    positions = np.asarray(inputs["positions"]).astype(np.int32)
    if inputs.get("_return_maps"):
        nc = None
    else:
        if "nc" not in _PROG:
            _PROG["nc"] = build()
        nc = _PROG["nc"]
    consts = _consts()
    wnames = ["pre_mix_g", "w_in", "pool_w", "pool_scale", "cmp_pe", "cmp_w1", "cmp_w2", "mem_norm_g", "w_mem_kv",
              "w_br_pool", "w_br_nsa", "w_br_xa", "w_out", "post_mix_g", "pre_ffn_g", "w_up", "conv_w", "conv_b",
              "w_down", "post_ffn_g"]
    shared = {n: np.ascontiguousarray(np.asarray(inputs[n], dtype=np.float32)) for n in wnames}
    in_maps = []
    for core in range(8):
        b, r = core // 4, core % 4
        m = dict(shared)
        m.update(consts)
        m["x_all"] = np.ascontiguousarray(x[b])
        m["mem"] = np.ascontiguousarray(mem[b])
        xe = np.zeros((NEXT * 128, 1024), np.float32)
        pe = np.zeros((NEXT, 128), np.int32)
        tq = np.zeros((NEXT, 128), np.float32)
        wv = np.zeros((1, NEXT), np.float32)
        ic = np.ones((128, NEXT, 4), np.float32)
        tst = np.zeros((1, NEXT), np.float32)
        for e in range(NEXT):
            ge = 16 * r - 5 + e
            tq[e] = ge * 128 + np.arange(128)
            tst[0, e] = ge * 128
            if ge >= 0:
                xe[e * 128:(e + 1) * 128] = x[b, ge * 128:(ge + 1) * 128]
                pe[e] = positions[b, ge * 128:(ge + 1) * 128]
                wv[0, e] = 1.0
                t = ge * 128 + np.arange(128)
                for gi, w in enumerate((2, 4, 8, 16)):
                    ic[:, e, gi] = 1.0 / np.minimum(t + 1, w).astype(np.float32)
        m["x_ext"] = xe
        m["pos_ext"] = np.ascontiguousarray(pe.T)
        m["pos_all"] = np.ascontiguousarray(positions[b].reshape(NALL, 128).T)
        m["tq_ext"] = np.ascontiguousarray(tq.T)
        m["tqstart"] = tst
        m["wvalid"] = wv
        m["invcnt"] = np.ascontiguousarray(ic.reshape(128, NEXT * 4))
        m["hflag"] = np.array([[0.0 if r == 0 else 1.0]], np.float32)
        in_maps.append(m)
    if inputs.get("_return_maps"):
        return in_maps
    res = run_bass_kernel_spmd(nc, in_maps, core_ids=list(range(8)))
    out = np.zeros((2, 8192, 1024), np.float32)
    for core in range(8):
        b, r = core // 4, core % 4
        out[b, r * 2048:(r + 1) * 2048] = res.results[core]["out"]
    return out
```

```python
import numpy as np
from contextlib import ExitStack
import concourse.bass as bass
import concourse.mybir as mybir
from concourse.bass_utils import run_bass_kernel_spmd

F32 = mybir.dt.float32
BF16 = mybir.dt.bfloat16
I32 = mybir.dt.int32
AF = mybir.ActivationFunctionType
ALU = mybir.AluOpType
AX = mybir.AxisListType


import sys as _sys


def _where():
    f = _sys._getframe(2)
    out = []
    while f is not None and len(out) < 4:
        if f.f_code.co_name != "<lambda>":
            out.append(f.f_lineno)
        f = f.f_back
    return out


class Buf:
    __slots__ = ("name", "writers", "readers", "excl", "last")

    def __init__(self, name, excl=False):
        self.name = name
        self.writers = []
        self.readers = []
        self.excl = excl
        self.last = {}


class Ins:
    __slots__ = ("eng", "fn", "deps", "idx", "flag", "tok", "dma", "pre", "where")

    def __init__(self, eng, fn, dma):
        self.eng = eng
        self.fn = fn
        self.deps = []
        self.flag = False
        self.tok = None
        self.dma = dma
        self.pre = None


class Sched:
    ENGS = ("pe", "dve", "act", "pool", "sp")
    EPOCH = 8000
    NDMA = 24

    def __init__(self, nc, stack):
        self.nc = nc
        self.stack = stack
        self.q = {e: [] for e in self.ENGS}
        self.nbuf = 0

    def buf(self, name=None, excl=False):
        self.nbuf += 1
        return Buf(name or f"b{self.nbuf}", excl)

    def op(self, eng, fn, r=(), w=(), dma=False):
        ins = Ins(eng, fn, dma)
        ins.where = _where()
        deps = []
        for b in r:
            deps.extend(b.writers)
        for b in w:
            deps.extend(b.readers)
        for b in list(r) + list(w):
            if b.excl:
                for en, li in b.last.items():
                    if en != eng:
                        deps.append(li)
                b.last[eng] = ins
        for b in w:
            if b.readers or (b in r):
                b.writers = [ins]
                b.readers = []
            else:
                b.writers.append(ins)
                if len(b.writers) > 48:
                    b.writers = b.writers[-48:]
        for b in r:
            if b not in w:
                b.readers.append(ins)
                if len(b.readers) > 48:
                    b.readers = b.readers[-48:]
        seen = set()
        for d in deps:
            if d is ins or id(d) in seen:
                continue
            if d.eng == "pe" and eng == "pe" and not d.dma:
                continue
            seen.add(id(d))
            ins.deps.append(d)
            d.flag = True
        self.q[eng].append(ins)
        return ins

    def emit(self, final_waits=()):
        nc = self.nc
        stack = self.stack
        sems = {}
        for e in self.ENGS:
            n = 0
            for ins in self.q[e]:
                if ins.dma:
                    continue
                if ins.flag:
                    n += 1
                    ins.idx = n
            nep = (n + self.EPOCH - 1) // self.EPOCH
            sems[e] = [stack.enter_context(nc.semaphore(f"s_{e}_{k}")) for k in range(max(nep, 1))]
        dsems = [stack.enter_context(nc.semaphore(f"s_dma_{k}")) for k in range(self.NDMA)]
        duse = [0] * self.NDMA
        dma_engs = [e for e in self.ENGS if any(i.dma for i in self.q[e])]
        share = {}
        if dma_engs:
            per = self.NDMA // len(dma_engs)
            for k, e in enumerate(dma_engs):
                share[e] = list(range(k * per, (k + 1) * per))
        for e in dma_engs:
            j = 0
            for ins in self.q[e]:
                if not ins.dma:
                    continue
                s = share[e][j % len(share[e])]
                j += 1
                prev = duse[s]
                duse[s] += 1
                ins.tok = (dsems[s], 16 * duse[s])
                ins.pre = (dsems[s], 16 * prev) if prev > 0 else None
        for e in self.ENGS:
            for ins in self.q[e]:
                if ins.dma or not ins.flag:
                    continue
                k = (ins.idx - 1) // self.EPOCH
                ins.tok = (sems[e][k], (ins.idx - 1) % self.EPOCH + 1)
        engobj = {"pe": "tensor", "dve": "vector", "act": "scalar", "pool": "gpsimd", "sp": "sync"}
        stats = {}
        with nc.Block() as block:
            for e in self.ENGS:
                lst = self.q[e]

                def body(eng, lst=lst, e=e):
                    waited = {}
                    nw = 0

                    def wait(tok):
                        nonlocal nw
                        sem, val = tok
                        key = id(sem)
                        if waited.get(key, 0) >= val:
                            return
                        waited[key] = val
                        eng.wait_ge(sem, val)
                        nw += 1

                    for ins in lst:
                        if ins.pre is not None:
                            wait(ins.pre)
                        for d in ins.deps:
                            wait(d.tok)
                        try:
                            bi = ins.fn(eng)
                        except BaseException:
                            print("FAILED op recorded at lines", ins.where, flush=True)
                            raise
                        if ins.dma:
                            bi.then_inc(ins.tok[0], 16)
                        elif ins.flag:
                            bi.then_inc(ins.tok[0], 1)
                    if e == "sp":
                        for fw in final_waits:
                            wait(fw.tok)
                    stats[e] = (len(lst), nw)

                getattr(block, engobj[e])(body)
        return stats


import os as _os
DUMPX = int(_os.environ.get('DUMPX', '0'))
NOBAR = int(_os.environ.get('NOBAR', '0'))
NEXT = 21
NALL = 64
BIG = 100.0
TWO_PI = float(2 * np.pi)


def build(dbg=None, ntile_b1=NEXT, na=NALL, skipcmp=False):
    nc = bass.Bass("TRN2", target_bir_lowering=False)

    def din(name, shape, dt=F32):
        return nc.dram_tensor(name, list(shape), dt, kind="ExternalInput").ap()

    x_all = din("x_all", [8192, 1024])
    x_ext = din("x_ext", [NEXT * 128, 1024])
    pos_all = din("pos_all", [128, NALL], I32)
    pos_ext = din("pos_ext", [128, NEXT], I32)
    tq_ext = din("tq_ext", [128, NEXT])
    qrow_d = din("qrow", [1, 128])
    tqst_d = din("tqstart", [1, NEXT])
    wvalid_d = din("wvalid", [1, NEXT])
    invcnt_d = din("invcnt", [128, NEXT * 4])
    hflag_d = din("hflag", [1, 1])
    mem_d = din("mem", [256, 1024])
    ident_d = din("ident", [128, 128])
    G_d = din("Gm", [128, 8192])
    invf_d = din("invf", [1, 8])
    bsrow_d = din("bsrow", [1, 128])
    cendrow_d = din("cendrow", [1, 512])
    kidx_d = din("kidx", [128, 64])
    cendcol_d = din("cendcol", [128, 4])
    Mw_d = din("Mw", [128, 256])
    e0_d = din("e0big", [1, 128])
    Ab_d = din("Aband", [128, 1024])
    pre_mix_g = din("pre_mix_g", [1, 1024])
    w_in = din("w_in", [1, 1024, 5936])
    pool_w = din("pool_w", [1, 4, 128, 128])
    pool_scale = din("pool_scale", [1, 512])
    cmp_pe = din("cmp_pe", [1, 2, 32, 64])
    cmp_w1 = din("cmp_w1", [1, 2, 2048, 256])
    cmp_w2 = din("cmp_w2", [1, 2, 256, 64])
    mem_norm_g = din("mem_norm_g", [1, 1024])
    w_mem_kv = din("w_mem_kv", [1, 1024, 1024])
    w_br_pool = din("w_br_pool", [1, 512, 1024])
    w_br_nsa = din("w_br_nsa", [1, 1024, 1024])
    w_br_xa = din("w_br_xa", [1, 512, 1024])
    w_out = din("w_out", [1, 1024, 1024])
    post_mix_g = din("post_mix_g", [1, 1024])
    pre_ffn_g = din("pre_ffn_g", [1, 1024])
    w_up = din("w_up", [1, 1024, 5632])
    conv_w = din("conv_w", [1, 3, 5632])
    conv_b = din("conv_b", [1, 5632])
    w_down = din("w_down", [1, 2816, 1024])
    post_ffn_g = din("post_ffn_g", [1, 1024])
    out_d = nc.dram_tensor("out", [16 * 128, 1024], F32, kind="ExternalOutput").ap()
    br_scr = nc.dram_tensor("br_scr", [17, 128, 24 * 128], BF16, kind="Internal").ap()
    x1_scr = nc.dram_tensor("x1_scr", [17, 128, 1024], F32, kind="Internal").ap()
    dbg_out = {}

    def dbg_t(name, shape, dt=F32):
        return nc.dram_tensor("dbg_" + name, list(shape), dt, kind="ExternalOutput").ap()

    with ExitStack() as st:
        S = Sched(nc, st)
        ARN = 95800
        arena = st.enter_context(nc.sbuf_tensor("arena", [128, ARN], BF16))
        PS = [st.enter_context(nc.psum_tensor(f"ps{i}", [128, 512], F32)) for i in range(8)]
        PB = [S.buf(f"ps{i}", excl=True) for i in range(8)]
        top = [0]

        def abf(n):
            a = arena[:, top[0]:top[0] + n]
            top[0] += (n + 31) // 32 * 32
            assert top[0] <= ARN, top[0]
            return a

        def af32(n):
            a = arena[:, top[0]:top[0] + 2 * n].bitcast(F32)
            top[0] += (n + 15) // 16 * 32
            assert top[0] <= ARN, top[0]
            return a

        def ai32(n):
            a = arena[:, top[0]:top[0] + 2 * n].bitcast(I32)
            top[0] += (n + 15) // 16 * 32
            assert top[0] <= ARN, top[0]
            return a

        def mm(out, lhsT, rhs, r, w, start=True, stop=True):
            return S.op("pe", lambda e: e.matmul(out, lhsT=lhsT, rhs=rhs, start=start, stop=stop,
                                                 skip_group_check=True), r, w)

        last_func = [None]

        def act(out, in_, func, r, w, bias=None, scale=None, accum=None):
            if func != last_func[0]:
                last_func[0] = func
                S.op("act", lambda e: e.activation(out=bsc[1][:, 0:1], in_=bsc[3][:, 0:1], func=func), [B_bsc3], [])
            kw = {}
            if bias is not None:
                kw["bias"] = bias
            if scale is not None:
                kw["scale"] = scale
            if accum is not None:
                kw["accum_out"] = accum
            return S.op("act", lambda e: e.activation(out=out, in_=in_, func=func, **kw), r, w)

        def ts(eng, out, in0, s1, s2, op0, op1, r, w):
            if s2 is None:
                return S.op(eng, lambda e: e.tensor_scalar(out=out, in0=in0, scalar1=s1, scalar2=None, op0=op0), r, w)
            return S.op(eng, lambda e: e.tensor_scalar(out=out, in0=in0, scalar1=s1, scalar2=s2, op0=op0, op1=op1), r, w)

        def tt(eng, out, in0, in1, op, r, w):
            return S.op(eng, lambda e: e.tensor_tensor(out=out, in0=in0, in1=in1, op=op), r, w)

        def stt(out, in0, scalar, in1, op0, op1, r, w, accum=None):
            if accum is None:
                return S.op("dve", lambda e: e.scalar_tensor_tensor(out=out, in0=in0, scalar=scalar, in1=in1, op0=op0, op1=op1), r, w)
            return S.op("dve", lambda e: e.scalar_tensor_tensor(out=out, in0=in0, scalar=scalar, in1=in1, op0=op0, op1=op1,
                                                                accum_out=accum), r, w)

        def cp(eng, out, in_, r, w, scale=None):
            if eng == "act":
                return act(out, in_, AF.Copy, r, w, scale=scale)
            if scale is not None:
                return ts(eng, out, in_, scale, None, ALU.mult, None, r, w)
            return S.op(eng, lambda e: e.tensor_copy(out=out, in_=in_), r, w)

        def memset(eng, ap, val, w):
            return S.op(eng, lambda e: e.memset(ap, val), (), w)

        dmas = []

        def dma(out, in_, r, w, slow=False):
            if slow:
                i = S.op("sp", lambda e: e.dma_start(out=out, in_=in_, allow_slow_non_contiguous=True), r, w, dma=True)
            else:
                i = S.op("sp", lambda e: e.dma_start(out=out, in_=in_), r, w, dma=True)
            dmas.append(i)
            return i

        bsc = [af32(16) for _ in range(4)]
        bbuf = {e: S.buf("bar_" + e) for e in S.ENGS}
        bar_scr = nc.dram_tensor("bar_scr", [128, 16], F32, kind="Internal").ap()

        def barrier():
            allb = list(bbuf.values())
            mm(PS[7][:, 0:1], ident_b[:, 0:128], ident_b[:, 0:1], [PB[7]], [bbuf["pe"], PB[7]])
            memset("dve", bsc[0], 0.0, [bbuf["dve"]])
            act(bsc[1], bsc[3], AF.Copy, [B_bsc3], [bbuf["act"]])
            memset("pool", bsc[2], 0.0, [bbuf["pool"]])
            i = S.op("sp", lambda e: e.dma_start(out=bar_scr, in_=bsc[3]), [B_bsc3], [bbuf["sp"]], dma=True)
            for d in dmas:
                if d not in i.deps:
                    i.deps.append(d)
            dmas.clear()
            mm(PS[7][:, 0:1], ident_b[:, 0:128], ident_b[:, 0:1], allb + [PB[7]], [PB[7]])
            S.op("dve", lambda e: e.memset(bsc[0], 0.0), allb, [])
            act(bsc[1], bsc[3], AF.Copy, allb + [B_bsc3], [])
            S.op("pool", lambda e: e.memset(bsc[2], 0.0), allb, [])
            S.op("sp", lambda e: e.dma_start(out=bar_scr, in_=bsc[3]), allb + [B_bsc3], [], dma=True)

        ident_b = abf(128)
        B_ident = S.buf("ident")
        stage = [af32(1024), af32(1024)]
        B_stage = [S.buf("stg0"), S.buf("stg1")]
        stg_i = [0]
        cast_i = [0]
        B_bsc3 = S.buf("bsc3")
        memset("dve", bsc[3], 0.0, [B_bsc3])

        def loadw(dst_fn, src, ncols, scale=None, r_extra=(), w=None, perm_q=False):
            for c0 in range(0, ncols, 1024):
                n = min(1024, ncols - c0)
                k = stg_i[0] % 2
                stg_i[0] += 1
                np_ = src.shape[0]
                sl = stage[k][0:np_, 0:n]
                dma(sl, src[:, c0:c0 + n], [], [B_stage[k]])
                eng = ("pool", "dve")[cast_i[0] % 2]
                cast_i[0] += 1
                if perm_q:
                    for g in range(2):
                        o = dst_fn(c0, n).rearrange("p (c g d) -> p c g d", c=8, g=2)[:, :, g, :]
                        i_ = stage[k][0:np_, g * 512:(g + 1) * 512].rearrange("p (c d) -> p c d", c=8)
                        if scale is not None:
                            ts(eng, o, i_, scale, None, ALU.mult, None, [B_stage[k]] + list(r_extra), w)
                        else:
                            cp(eng, o, i_, [B_stage[k]] + list(r_extra), w)
                else:
                    if scale is not None:
                        ts(eng, dst_fn(c0, n), sl, scale, None, ALU.mult, None, [B_stage[k]] + list(r_extra), w)
                    else:
                        cp(eng, dst_fn(c0, n), sl, [B_stage[k]] + list(r_extra), w)

        def tr(out_ps, in_sb, r, w, start=True):
            return mm(out_ps, in_sb, ident_b[0:in_sb.shape[0], 0:in_sb.shape[0]], list(r) + [B_ident], w)

        dma(stage[0][:, 0:128], ident_d, [], [B_stage[0]])
        cp("dve", ident_b, stage[0][:, 0:128], [B_stage[0]], [B_ident])

        gpre = af32(8)
        gmem = af32(8)
        gffn = af32(8)
        B_small = S.buf("small")
        dma(gpre, pre_mix_g[0].rearrange("(k p) -> p k", p=128), [], [B_small], slow=True)
        dma(gmem, mem_norm_g[0].rearrange("(k p) -> p k", p=128), [], [B_small], slow=True)
        dma(gffn, pre_ffn_g[0].rearrange("(k p) -> p k", p=128), [], [B_small], slow=True)
        tqe = af32(NEXT)
        dma(tqe, tq_ext, [], [B_small])
        wval = af32(NEXT)
        dma(wval, wvalid_d.partition_broadcast(128), [], [B_small])
        hflag = af32(1)
        dma(hflag, hflag_d.partition_broadcast(128), [], [B_small])
        invcnt = af32(NEXT * 4)
        dma(invcnt, invcnt_d, [], [B_small])
        ss2 = af32(4)
        B_ss2 = S.buf("ss2")
        persist_top = top[0]

        def rms_scale(xt, xn, Bx, Bxn, ss, junk, Bss):
            act(junk, xt, AF.Square, [Bx], [Bss], accum=ss)
            ts("dve", ss, ss, 1.0 / 1024, 1e-6, ALU.mult, ALU.add, [Bss], [Bss])
            act(ss, ss, AF.Sqrt, [Bss], [Bss])
            S.op("dve", lambda e: e.reciprocal(out=ss, in_=ss), [Bss], [Bss])
            ts("dve", xn, xt, ss, None, ALU.mult, None, [Bx, Bss], [Bxn])

        def sincos(ang, n, osin, ocos, tmp, Bt, Bo):
            t, kf, g, ki = tmp
            ts("dve", ang, ang, 1.0 / TWO_PI, None, ALU.mult, None, [Bt], [Bt])
            for dst, off in ((osin, 0.0), (ocos, 0.25)):
                ts("dve", t, ang, off, None, ALU.add, None, [Bt], [Bt])
                cp("dve", ki, t, [Bt], [Bt])
                cp("dve", kf, ki, [Bt], [Bt])
                tt("dve", t, t, kf, ALU.subtract, [Bt], [Bt])
                ts("dve", g, t, 0.5, None, ALU.is_gt, None, [Bt], [Bt])
                tt("dve", t, t, g, ALU.subtract, [Bt], [Bt])
                ts("dve", g, t, -0.5, None, ALU.is_lt, None, [Bt], [Bt])
                tt("dve", t, t, g, ALU.add, [Bt], [Bt])
                act(dst, t, AF.Sin, [Bt], [Bo], scale=TWO_PI)

        def rotary(src4, dst4, cs, sn, tmp, rB, wB, Bt):
            a, b = src4.shape[1], src4.shape[2]
            n = a * b * 8
            x1 = src4[:, :, :, 0:8]
            x2 = src4[:, :, :, 8:16]
            csb = cs.unsqueeze(1).unsqueeze(1).to_broadcast([128, a, b, 8])
            snb = sn.unsqueeze(1).unsqueeze(1).to_broadcast([128, a, b, 8])
            t1 = tmp[:, 0:n].rearrange("p (a b d) -> p a b d", a=a, b=b)
            t2 = tmp[:, n:2 * n].rearrange("p (a b d) -> p a b d", a=a, b=b)
            tt("dve", t1, x1, csb, ALU.mult, rB, [Bt])
            tt("dve", t2, x2, snb, ALU.mult, rB, [Bt])
            tt("dve", dst4[:, :, :, 0:8], t1, t2, ALU.subtract, [Bt], wB)
            tt("dve", t1, x2, csb, ALU.mult, rB, [Bt])
            tt("dve", t2, x1, snb, ALU.mult, rB, [Bt])
            tt("dve", dst4[:, :, :, 8:16], t1, t2, ALU.add, [Bt], wB)

        KTs = abf(8192)
        Vs = abf(64 * 2 * 65).rearrange("p (t g d) -> p t g d", t=64, g=2)
        KTw = abf(NEXT * 128)
        Vw = abf(NEXT * 2 * 65).rearrange("p (t g d) -> p t g d", t=NEXT, g=2)
        KcT = abf(512)
        Vc = abf(4 * 2 * 65).rearrange("p (t g d) -> p t g d", t=4, g=2)
        Gb = abf(8192)
        B_KTs, B_Vs, B_KTw, B_Vw, B_KcT, B_Vc, B_G = [S.buf(n) for n in "KTs Vs KTw Vw KcT Vc G".split()]
        cosE = af32(NEXT * 8).rearrange("p (t f) -> p t f", f=8)
        sinE = af32(NEXT * 8).rearrange("p (t f) -> p t f", f=8)
        cos8 = af32(NEXT * 8).rearrange("p (t f) -> p t f", f=8)
        sin8 = af32(NEXT * 8).rearrange("p (t f) -> p t f", f=8)
        B_tabE = S.buf("tabE")
        kv_top = top[0]

        memset("pool", Vs, 1.0, [B_Vs])
        memset("pool", Vw, 1.0, [B_Vw])
        memset("pool", Vc, 0.0, [B_Vc])
        memset("pool", Vc[:, :, :, 64:65], 1.0, [B_Vc])
        loadw(lambda c0, n: Gb[:, c0:c0 + n], G_d, 8192, w=[B_G])

        cosA = af32(NALL * 8).rearrange("p (t f) -> p t f", f=8)
        sinA = af32(NALL * 8).rearrange("p (t f) -> p t f", f=8)
        B_tabA = S.buf("tabA")
        KcRaw = abf(8192)
        VcRaw = abf(8192)
        B_KcRaw, B_VcRaw = S.buf("KcRaw"), S.buf("VcRaw")
        wkvA = abf(8 * 512).rearrange("p (k c) -> p k c", k=8)
        B_wkvA = S.buf("wkvA")
        for k in range(8):
            loadw(lambda c0, n, k=k: wkvA[:, k, c0:c0 + n], w_in[0][k * 128:(k + 1) * 128, 1536:2048], 512,
                  scale=gpre[:, k:k + 1], r_extra=[B_small], w=[B_wkvA])
        mark_tab = top[0]
        posi = ai32(512)
        posf = af32(512)
        invf = af32(8)
        ang = af32(512)
        tmp3 = (af32(512), af32(512), af32(512), ai32(512))
        B_t = S.buf("tabtmp")
        dma(invf, invf_d.partition_broadcast(128), [], [B_t])
        dma(posi[:, 0:NALL], pos_all, [], [B_t])
        cp("dve", posf[:, 0:NALL], posi[:, 0:NALL], [B_t], [B_t])
        tt("dve", ang.rearrange("p (t f) -> p t f", f=8), posf[:, 0:NALL].unsqueeze(2).to_broadcast([128, NALL, 8]),
           invf.unsqueeze(1).to_broadcast([128, NALL, 8]), ALU.mult, [B_t], [B_t])
        sincos(ang, 512, sinA.rearrange("p t f -> p (t f)"), cosA.rearrange("p t f -> p (t f)"),
               tmp3, B_t, B_tabA)
        dma(posi[:, 0:NEXT], pos_ext, [B_t], [B_t])
        cp("dve", posf[:, 0:NEXT], posi[:, 0:NEXT], [B_t], [B_t])
        ne = NEXT * 8
        tt("dve", ang[:, 0:ne].rearrange("p (t f) -> p t f", f=8), posf[:, 0:NEXT].unsqueeze(2).to_broadcast([128, NEXT, 8]),
           invf.unsqueeze(1).to_broadcast([128, NEXT, 8]), ALU.mult, [B_t], [B_t])
        sincos(ang[:, 0:ne], ne, sinE.rearrange("p t f -> p (t f)"), cosE.rearrange("p t f -> p (t f)"),
               tuple(a[:, 0:ne] for a in tmp3), B_t, B_tabE)
        ts("dve", cos8.rearrange("p t f -> p (t f)"), cosE.rearrange("p t f -> p (t f)"), 0.125, None, ALU.mult, None, [B_tabE], [B_tabE])
        ts("dve", sin8.rearrange("p t f -> p (t f)"), sinE.rearrange("p t f -> p (t f)"), 0.125, None, ALU.mult, None, [B_tabE], [B_tabE])
        if dbg != "0":
            barrier()
            top[0] = mark_tab

        if dbg == "0":
            f1 = dma(dbg_t("cosA", [128, NALL * 8]), cosA.rearrange("p t f -> p (t f)"), [B_tabA], [])
            f2 = dma(dbg_t("sinA", [128, NALL * 8]), sinA.rearrange("p t f -> p (t f)"), [B_tabA], [])
            f3 = dma(dbg_t("cos8", [128, NEXT * 8]), cos8.rearrange("p t f -> p (t f)"), [B_tabE], [])
            f4 = dma(dbg_t("Gb", [128, 8192], BF16), Gb, [B_G], [])
            print(S.emit(final_waits=[f1, f2, f3, f4]))
            return nc
        xt2 = [af32(1024), af32(1024)]
        B_xt = [S.buf("xt0"), S.buf("xt1")]
        xn = abf(1024)
        B_xn = S.buf("xn")
        junk = abf(1024)
        ssA = af32(1)
        B_ss = S.buf("ss")
        hT = abf(1024).rearrange("p (k t) -> p k t", k=8)
        B_hT = S.buf("hT")
        kb = abf(512)
        B_kb = S.buf("kb")
        rtmp = af32(2 * 16 * 8)
        B_rt = S.buf("rtmp")

        def norm_T(xt, Bx, pa, pb, hTd, BhT, xnb=None, Bxnb=None):
            if xnb is None:
                xnb, Bxnb = xn, B_xn
            rms_scale(xt, xnb, Bx, Bxnb, ssA, junk, B_ss)
            for k in range(8):
                bank = pa if k < 4 else pb
                tr(PS[bank][:, (k % 4) * 128:(k % 4 + 1) * 128], xnb[:, k * 128:(k + 1) * 128], [Bxnb], [PB[bank]])
            cp("act", hTd[:, 0:4, :], PS[pa][:, :].rearrange("p (k t) -> p k t", k=4), [PB[pa]], [BhT])
            cp("dve", hTd[:, 4:8, :], PS[pb][:, :].rearrange("p (k t) -> p k t", k=4), [PB[pb]], [BhT])

        xnA = [xn, abf(1024)]
        B_xnA = [B_xn, S.buf("xnA1")]
        hTA = [hT, abf(1024).rearrange("p (k t) -> p k t", k=8)]
        B_hTA = [B_hT, S.buf("hTA1")]
        dma(xt2[0], x_all[0:128, :], [], [B_xt[0]])
        dma(xt2[1], x_all[128:256, :], [], [B_xt[1]])
        norm_T(xt2[0], B_xt[0], 0, 1, hTA[0], B_hTA[0], xnA[0], B_xnA[0])
        for T in range(na):
            s = T % 2
            if T + 1 < na:
                norm_T(xt2[1 - s], B_xt[1 - s], 0, 1, hTA[1 - s], B_hTA[1 - s], xnA[1 - s], B_xnA[1 - s])
            if T + 2 < na:
                dma(xt2[s], x_all[(T + 2) * 128:(T + 3) * 128, :], [], [B_xt[s]])
            hT, B_hT = hTA[s], B_hTA[s]
            for k in range(8):
                mm(PS[2][:, 0:512], hT[:, k, :], wkvA[:, k, :], [B_hT, B_wkvA], [PB[2]], start=(k == 0), stop=(k == 7))
            cp("act", kb, PS[2][:, 0:512], [PB[2]], [B_kb])
            v5 = PS[2][:, 0:512].rearrange("p (j2 jj g d) -> p j2 jj g d", j2=2, jj=2, g=2)
            k5 = kb.rearrange("p (j2 jj g d) -> p j2 jj g d", j2=2, jj=2, g=2)
            rotary(v5[:, :, 0, :, :], k5[:, :, 0, :, :], cosA[:, T, :], sinA[:, T, :], rtmp, [PB[2], B_tabA], [B_kb], B_rt)
            cp("pool", Vs[:, T, :, 0:64], kb[:, 384:512].rearrange("p (g d) -> p g d", g=2), [B_kb], [B_Vs])
            tr(PS[3][:, 0:128], kb[:, 0:128], [B_kb], [PB[3]])
            tr(PS[3][:, 128:256], kb[:, 128:256], [B_kb], [PB[3]])
            tr(PS[3][:, 256:384], kb[:, 256:384], [B_kb], [PB[3]])
            cp("act", KcRaw[:, T * 128:(T + 1) * 128], PS[3][:, 0:128], [PB[3]], [B_KcRaw])
            cp("dve", VcRaw[:, T * 128:(T + 1) * 128], PS[3][:, 128:256], [PB[3]], [B_VcRaw])
            cp("act", KTs[:, T * 128:(T + 1) * 128], PS[3][:, 256:384], [PB[3]], [B_KTs])

        w1z = [abf(32 * 256).rearrange("p (l m) -> p l m", l=32) for _ in range(2)]
        B_w1 = S.buf("w1b")
        memset("pool", w1z[0][64:128, :, :], 0.0, [B_w1])
        memset("pool", w1z[1][0:64, :, :], 0.0, [B_w1])
        w2b = abf(2 * 128).rearrange("p (h d) -> p h d", h=2)
        B_w2 = S.buf("w2b")
        peT = abf(32)
        pef = af32(32)
        B_pe = S.buf("pe")
        hid2 = [abf(2 * 512).rearrange("p (h c) -> p h c", h=2) for _ in range(2)]
        B_hid2 = [S.buf("hid0"), S.buf("hid1")]
        cbias = af32(2)
        B_cb = S.buf("cbias")
        for kvi, raw, Braw in (() if skipcmp else ((0, KcRaw, B_KcRaw), (1, VcRaw, B_VcRaw))):
            w1v = cmp_w1[0][kvi].rearrange("(l d) m -> d l m", d=64)
            for half in range(2):
                for l0 in range(0, 32, 4):
                    k = stg_i[0] % 2
                    stg_i[0] += 1
                    sl = stage[k][64 * half:64 * half + 64, 0:1024]
                    dma(sl.rearrange("p (l m) -> p l m", l=4), w1v[:, l0:l0 + 4, :], [], [B_stage[k]])
                    cp(("pool", "dve")[(l0 // 4) % 2], w1z[half][64 * half:64 * half + 64, l0:l0 + 4, :],
                       sl.rearrange("p (l m) -> p l m", l=4), [B_stage[k]], [B_w1])
            k = stg_i[0] % 2
            stg_i[0] += 1
            dma(stage[k][:, 0:128].rearrange("p (h d) -> p h d", h=2), cmp_w2[0][kvi].rearrange("(h p) d -> p h d", p=128),
                [], [B_stage[k]])
            cp("dve", w2b[:, :, 0:64], stage[k][:, 0:128].rearrange("p (h d) -> p h d", h=2), [B_stage[k]], [B_w2])
            cp("dve", w2b[:, :, 64:128], stage[k][:, 0:128].rearrange("p (h d) -> p h d", h=2), [B_stage[k]], [B_w2])
            dma(pef[0:64, :], cmp_pe[0][kvi].rearrange("l d -> d l"), [], [B_pe], slow=True)
            memset("dve", peT[64:128, :], 0.0, [B_pe])
            cp("dve", peT[0:64, :], pef[0:64, :], [B_pe], [B_pe])
            for half in range(2):
                for l in range(32):
                    mm(PS[4][:, half:half + 1], w1z[0][:, l, half * 128:(half + 1) * 128], peT[:, l:l + 1],
                       [B_w1, B_pe], [PB[4]], start=(l == 0 and half == 0), stop=(l == 31))
            cp("dve", cbias, PS[4][:, 0:2], [PB[4]], [B_cb])
            if not NOBAR:
                barrier()
            if dbg == "A" and kvi == 0 and (DUMPX & 1):
                dma(dbg_t("w1b", [128, 8192], BF16), w1z[0].rearrange("p l m -> p (l m)"), [B_w1], [])
                dma(dbg_t("cbias", [128, 2]), cbias, [B_cb], [])
            rawv = raw.rearrange("p (i s) -> p i s", s=16)
            for g in range(2):
                if not NOBAR:
                    barrier()
                hid, B_hid = hid2[g], B_hid2[g]
                pr = slice(64 * g, 64 * g + 64)
                for half in range(2):
                    for l in range(32):
                        rhs = rawv[:, 0:511, l] if l < 16 else rawv[:, 1:512, l - 16]
                        mm(PS[half][:, 0:511], w1z[g][:, l, half * 128:(half + 1) * 128], rhs, [B_w1, Braw], [PB[half]],
                           start=(l == 0), stop=(l == 31))
                    act(hid[:, half, 0:511], PS[half][:, 0:511], AF.Gelu_apprx_tanh, [PB[half], B_cb], [B_hid],
                        bias=cbias[:, half:half + 1])
                if dbg == "A" and kvi == 0 and (DUMPX & 2):
                    dma(dbg_t(f"hid{g}", [128, 1024], BF16), hid.rearrange("p h c -> p (h c)"), [B_hid], [])
                if kvi == 0:
                    for half in range(2):
                        mm(PS[2][:, 0:511], w2b[:, half, :], hid[:, half, 0:511], [B_w2, B_hid], [PB[2]],
                           start=(half == 0), stop=(half == 1))
                    cp("act", KcT[pr, 0:511], PS[2][pr, 0:511], [PB[2]], [B_KcT])
                else:
                    for c in range(4):
                        m = 128 if c < 3 else 127
                        for half in range(2):
                            mm(PS[2][0:m, c * 64:(c + 1) * 64], hid[:, half, c * 128:c * 128 + m], w2b[:, half, 0:64],
                               [B_w2, B_hid], [PB[2]], start=(half == 0 and c == 0), stop=(half == 1))
                    for c in range(4):
                        m = 128 if c < 3 else 127
                        cp("act", Vc[0:m, c, g, 0:64], PS[2][0:m, c * 64:(c + 1) * 64], [PB[2]], [B_Vc])
        memset("dve", KcT[:, 511:512], 0.0, [B_KcT])
        if dbg == "A":
            fl = [dma(dbg_t("KTs", [128, 8192], BF16), KTs, [B_KTs], []),
                  dma(dbg_t("Vs", [128, 64 * 130], BF16), Vs.rearrange("p t g d -> p (t g d)"), [B_Vs], []),
                  dma(dbg_t("KcT", [128, 512], BF16), KcT, [B_KcT], []),
                  dma(dbg_t("Vc", [128, 4 * 130], BF16), Vc.rearrange("p t g d -> p (t g d)"), [B_Vc], []),
                  dma(dbg_t("KcRaw", [128, 8192], BF16), KcRaw, [B_KcRaw], [])]
            print(S.emit(final_waits=fl))
            return nc
        barrier()
        top[0] = kv_top
        wb1 = abf(8 * 2352).rearrange("p (k c) -> p k c", k=8)
        B_wb1 = S.buf("wb1")
        for k in range(8):
            rows = w_in[0][k * 128:(k + 1) * 128, :]
            sc = gpre[:, k:k + 1]
            loadw(lambda c0, n, k=k: wb1[:, k, c0:c0 + n], rows[:, 0:512], 512, scale=sc, r_extra=[B_small], w=[B_wb1])
            loadw(lambda c0, n, k=k: wb1[:, k, 512:1536], rows[:, 512:1536], 1024, scale=sc, r_extra=[B_small], w=[B_wb1], perm_q=True)
            loadw(lambda c0, n, k=k: wb1[:, k, 1536 + c0:1536 + c0 + n], rows[:, 2048:2864], 816, scale=sc, r_extra=[B_small], w=[B_wb1])
        poolw = abf(4 * 128).rearrange("p (g d) -> p g d", g=4)
        B_cst = S.buf("cst")
        k_ = stg_i[0] % 2
        stg_i[0] += 1
        dma(stage[k_][:, 0:512].rearrange("p (g d) -> p g d", g=4), pool_w[0].rearrange("g c d -> c g d"), [], [B_stage[k_]])
        cp("dve", poolw, stage[k_][:, 0:512].rearrange("p (g d) -> p g d", g=4), [B_stage[k_]], [B_cst])
        pscale = af32(4)
        dma(pscale, pool_scale[0].rearrange("(g d) -> d g", d=128), [], [B_cst], slow=True)
        Ab = abf(1024).rearrange("p (g c t) -> p g c t", g=4, c=2)
        k_ = stg_i[0] % 2
        stg_i[0] += 1
        dma(stage[k_][:, 0:1024], Ab_d, [], [B_stage[k_]])
        cp("dve", Ab.rearrange("p g c t -> p (g c t)"), stage[k_][:, 0:1024], [B_stage[k_]], [B_cst])
        Mw3 = abf(384).rearrange("p (j q) -> p j q", j=3)
        k_ = stg_i[0] % 2
        stg_i[0] += 1
        dma(stage[k_][:, 0:256], Mw_d, [], [B_stage[k_]])
        cp("dve", Mw3[:, 0, :], stage[k_][:, 0:128], [B_stage[k_]], [B_cst])
        cp("dve", Mw3[:, 2, :], stage[k_][:, 128:256], [B_stage[k_]], [B_cst])
        memset("dve", Mw3[:, 1, :], 1.0, [B_cst])
        onesb = abf(1)
        memset("dve", onesb, 1.0, [B_cst])
        qrow = af32(128)
        dma(qrow, qrow_d.partition_broadcast(128), [], [B_cst])
        tqst = af32(NEXT)
        dma(tqst, tqst_d.partition_broadcast(128), [], [B_cst])
        tqt = af32(128)
        B_tqt = S.buf("tqt")
        bsrow = af32(128)
        dma(bsrow, bsrow_d.partition_broadcast(128), [], [B_cst])
        cendrow = af32(512)
        dma(cendrow, cendrow_d.partition_broadcast(128), [], [B_cst])
        kidx = af32(64)
        dma(kidx, kidx_d, [], [B_cst])
        cendcol = af32(4)
        dma(cendcol, cendcol_d, [], [B_cst])
        e0big = af32(128)
        dma(e0big, e0_d.partition_broadcast(128), [], [B_cst])

        xt2 = [af32(1024), af32(1024)]
        B_xt = [S.buf("bxt0"), S.buf("bxt1")]
        xn = abf(1024)
        B_xn = S.buf("bxn")
        junk = abf(1024)
        ssA = af32(1)
        B_ss = S.buf("bss")
        brT0 = abf(24 * 128).rearrange("p (k t) -> p k t", k=24)
        brT = [brT0, brT0]
        B_br0 = S.buf("br0")
        B_br = [B_br0, B_br0]
        rtmp = af32(256)
        B_rt = S.buf("brtmp")

        KmT = abf(4 * 256).rearrange("p (h m) -> p h m", h=4)
        Vm = abf(2 * 4 * 128).rearrange("p (c h d) -> p c h d", c=2, h=4)
        kmb = abf(512)
        mark_m = top[0]
        wmem = abf(8 * 1024).rearrange("p (k c) -> p k c", k=8)
        B_wmem = S.buf("wmem")
        for k in range(8):
            loadw(lambda c0, n, k=k: wmem[:, k, c0:c0 + n], w_mem_kv[0][k * 128:(k + 1) * 128, :], 1024,
                  scale=gmem[:, k:k + 1], r_extra=[B_small], w=[B_wmem])
        B_km, B_vm, B_kmb = S.buf("KmT"), S.buf("Vm"), S.buf("kmb")
        for c in range(2):
            dma(xt2[c], mem_d[c * 128:(c + 1) * 128, :], [], [B_xt[c]])
            norm_T(xt2[c], B_xt[c], 0, 1, brT[c][:, 16:24, :], B_br[c])
            for nb in range(2):
                for k in range(8):
                    mm(PS[2 + nb][:, :], brT[c][:, 16 + k, :], wmem[:, k, nb * 512:(nb + 1) * 512], [B_br[c], B_wmem], [PB[2 + nb]],
                       start=(k == 0), stop=(k == 7))
            cp("act", kmb, PS[2][:, :], [PB[2]], [B_kmb], scale=float(128 ** -0.5))
            cp("dve", Vm[:, c, :, :], PS[3][:, :].rearrange("p (h d) -> p h d", h=4), [PB[3]], [B_vm])
            for h in range(4):
                tr(PS[4][:, h * 128:(h + 1) * 128], kmb[:, h * 128:(h + 1) * 128], [B_kmb], [PB[4]])
            cp("act", KmT[:, :, c * 128:(c + 1) * 128], PS[4][:, :].rearrange("p (h m) -> p h m", h=4), [PB[4]], [B_km])

        barrier()
        top[0] = mark_m
        kwb = abf(256)
        B_kwb = S.buf("kwb")
        ub = [abf(512), abf(512)]
        B_ub = [S.buf("ub0"), S.buf("ub1")]
        uf = af32(512)
        B_uf = S.buf("uf")
        pbb = abf(512)
        B_pbb = S.buf("pbb")
        pT = abf(512).rearrange("p (g t) -> p g t", g=4)
        B_pT = S.buf("pT")
        qb = abf(1024)
        B_qb = S.buf("qb")
        qTz = [abf(1024).rearrange("p (k t) -> p k t", k=8) for _ in range(2)]
        B_qT = S.buf("qT")
        memset("pool", qTz[0], 0.0, [B_qT])
        memset("pool", qTz[1], 0.0, [B_qT])
        gn = af32(48)
        B_gn = S.buf("gn")
        qxb = abf(512)
        B_qxb = S.buf("qxb")
        qxT = abf(512).rearrange("p (h t) -> p h t", h=4)
        B_qxT = S.buf("qxT")
        mpT = [abf(512).rearrange("p (h t) -> p h t", h=4) for _ in range(2)]
        B_mpT = [S.buf("mpT0"), S.buf("mpT1")]
        rsm = af32(4)
        B_rsm = S.buf("rsm")
        ymemb = abf(512)
        B_ymem = S.buf("ymem")
        ef = [af32(512), af32(512)]
        B_ef = [S.buf("ef0"), S.buf("ef1")]
        ssum2 = [af32(2), af32(2)]
        B_ssum2 = [S.buf("ssum0"), S.buf("ssum1")]
        Pb = [af32(516), af32(516)]
        B_Pb = [S.buf("Pb0"), S.buf("Pb1")]
        cmneg = abf(512)
        B_cm = S.buf("cmneg")
        imp = af32(128)
        nd = af32(128)
        itmp = af32(128)
        wk = af32(128)
        mx = af32(16)
        B_imp = S.buf("imp")
        selb = abf(128)
        B_sel = S.buf("sel")
        selT = [abf(128), abf(128)]
        B_selT = [S.buf("selT0"), S.buf("selT1")]
        NSL = 3
        NSU = 6
        ucount = [0]
        mk = [abf(128) for _ in range(NSL)]
        B_mk = [S.buf(f"mk{i}") for i in range(NSL)]
        pTu = [abf(512).rearrange("p (h t) -> p h t", h=4) for _ in range(NSU)]
        B_pTu = [S.buf(f"pTu{i}") for i in range(NSU)]
        ynsa = af32(1024)
        B_yn = S.buf("ynsa")
        ytmp = af32(256)
        B_yt = S.buf("ytmp")
        rs = af32(8)
        B_rs = S.buf("rs")
        ynb = abf(1024)
        B_ynb = S.buf("ynb")
        memset("dve", Pb[0], 0.0, [B_Pb[0]])
        memset("dve", Pb[1], 0.0, [B_Pb[1]])
        nchunk = [0]

        def attend(e, g, br, chunks):
            LOOK = 3
            units = [(n, q) for n in range(len(chunks)) for q in range(2)]
            nu = len(units)
            info = {}

            def stage_scores(u):
                n, q = units[u]
                KT, V, mk_pe, mk_dve = chunks[n]
                if q == 0:
                    cs = nchunk[0]
                    nchunk[0] += 1
                    info[n] = (cs % 2, cs % NSL)
                    if mk_pe is not None:
                        mk_pe(cs % 2)
                bank = 2 + (ucount[0] % 4)
                ub = ucount[0] % NSU
                ucount[0] += 1
                mm(PS[bank][:, :], KT, qTz[g][:, 4 * q:4 * q + 4, :], [B_qT, B_KTs, B_KTw, B_KcT], [PB[bank]])
                return (bank, ub)

            pend = [stage_scores(u) for u in range(min(LOOK, nu))]
            for u, (n, q) in enumerate(units):
                bank, ub = pend.pop(0)
                if u + LOOK < nu:
                    pend.append(stage_scores(u + LOOK))
                KT, V, mk_pe, mk_dve = chunks[n]
                sl, bs = info[n]
                pt = pTu[ub]
                act(pt, PS[bank][:, :].rearrange("p (h t) -> p h t", h=4), AF.Exp, [PB[bank]], [B_pTu[ub]])
                if q == 0:
                    mk_dve(sl, bs)
                mb = mk[bs].unsqueeze(1).to_broadcast([128, 4, 128])
                tt(("dve", "pool")[q], pt, pt, mb, ALU.mult, [B_pTu[ub], B_mk[bs]], [B_pTu[ub]])
                for hh in range(4 * q, 4 * q + 4):
                    mm(PS[q][:, (hh % 4) * 128:(hh % 4) * 128 + 65], pt[:, hh % 4, :], V, [B_pTu[ub], B_Vs, B_Vw, B_Vc], [PB[q]],
                       start=(n == 0 and hh % 4 == 0), stop=(n == len(chunks) - 1))
            gn3 = gn.rearrange("p (h b) -> p h b", b=3)
            for b in range(2):
                Ov = PS[b][:, :].rearrange("p (h d) -> p h d", h=4)
                h0 = 8 * g + 4 * b
                rsb = rs[:, 4 * b:4 * b + 4]
                ts("dve", rsb, Ov[:, :, 64], 1e-30, None, ALU.max, None, [PB[b]], [B_rs])
                S.op("dve", lambda e_, rsb=rsb: e_.reciprocal(out=rsb, in_=rsb), [B_rs], [B_rs])
                tt("dve", rsb, rsb, gn3[:, h0:h0 + 4, br], ALU.mult, [B_rs, B_gn], [B_rs])
                dst = ynsa.rearrange("p (h d) -> p h d", h=16)[:, h0:h0 + 4, :]
                rb = rsb.unsqueeze(2).to_broadcast([128, 4, 64])
                if br == 0:
                    tt("dve", dst, Ov[:, :, 0:64], rb, ALU.mult, [PB[b], B_rs], [B_yn])
                else:
                    yt = ytmp.rearrange("p (h d) -> p h d", h=4)
                    tt("dve", yt, Ov[:, :, 0:64], rb, ALU.mult, [PB[b], B_rs], [B_yt])
                    tt("pool", dst, dst, yt, ALU.add, [B_yn, B_yt], [B_yn])

        dma(xt2[0], x_ext[0:128, :], [], [B_xt[0]])
        for e in range(ntile_b1):
            s = e % 2
            if e + 1 < NEXT:
                dma(xt2[1 - s], x_ext[(e + 1) * 128:(e + 2) * 128, :], [], [B_xt[1 - s]])
            bt = brT[s]
            hTd = bt[:, 16:24, :]
            norm_T(xt2[s], B_xt[s], 0, 1, hTd, B_br[s])
            for k in range(8):
                mm(PS[2][:, 0:256], hTd[:, k, :], wb1[:, k, 1536:1792], [B_br[s], B_wb1], [PB[2]], start=(k == 0), stop=(k == 7))
            cp("act", kwb, PS[2][:, 0:256], [PB[2]], [B_kwb])
            rotary(PS[2][:, 0:128].rearrange("p (a g d) -> p a g d", a=1, g=2), kwb[:, 0:128].rearrange("p (a g d) -> p a g d", a=1, g=2),
                   cosE[:, e, :], sinE[:, e, :], rtmp, [PB[2], B_tabE], [B_kwb], B_rt)
            cp("pool", Vw[:, e, :, 0:64], kwb[:, 128:256].rearrange("p (g d) -> p g d", g=2), [B_kwb], [B_Vw])
            tr(PS[3][:, 0:128], kwb[:, 0:128], [B_kwb], [PB[3]])
            cp("act", KTw[:, e * 128:(e + 1) * 128], PS[3][:, 0:128], [PB[3]], [B_KTw])
            for k in range(8):
                mm(PS[4][:, :], hTd[:, k, :], wb1[:, k, 0:512], [B_br[s], B_wb1], [PB[4]], start=(k == 0), stop=(k == 7))
            cp("act", ub[s], PS[4][:, :], [PB[4]], [B_ub[s]])
            if e < 4:
                continue
            cp("dve", uf, PS[4][:, :], [PB[4]], [B_uf])
            i = e - 4
            for gi in range(4):
                blk = slice(gi * 128, (gi + 1) * 128)
                mm(PS[5][:, blk], Ab[:, gi, 0, :], ub[s][:, blk], [B_cst, B_ub[s]], [PB[5]], start=True, stop=False)
                mm(PS[5][:, blk], Ab[:, gi, 1, :], ub[1 - s][:, blk], [B_cst, B_ub[1 - s]], [PB[5]], start=False, stop=True)
            for gi in range(4):
                blk = slice(gi * 128, (gi + 1) * 128)
                stt(pbb[:, blk], PS[5][:, blk], invcnt[:, e * 4 + gi:e * 4 + gi + 1], uf[:, blk], ALU.mult, ALU.subtract,
                    [PB[5], B_uf, B_small], [B_pbb])
            for gi in range(4):
                blk = slice(gi * 128, (gi + 1) * 128)
                tr(PS[6][:, blk], pbb[:, blk], [B_pbb], [PB[6]])
            cp("act", pT, PS[6][:, :].rearrange("p (g t) -> p g t", g=4), [PB[6]], [B_pT])
            for gi in range(4):
                blk = slice(gi * 128, (gi + 1) * 128)
                mm(PS[5][:, blk], poolw[:, gi, :], pT[:, gi, :], [B_cst, B_pT], [PB[5]])
            for gi in range(4):
                blk = slice(gi * 128, (gi + 1) * 128)
                ts("dve", bt[:, gi, :], PS[5][:, blk], pscale[:, gi:gi + 1], None, ALU.mult, None, [PB[5], B_cst], [B_br[s]])
            for nb in range(2):
                for k in range(8):
                    mm(PS[nb][:, :], hTd[:, k, :], wb1[:, k, 512 + nb * 512:1024 + nb * 512], [B_br[s], B_wb1], [PB[nb]],
                       start=(k == 0), stop=(k == 7))
            for nb in range(2):
                qv = qb[:, nb * 512:(nb + 1) * 512]
                cp("act", qv, PS[nb][:, :], [PB[nb]], [B_qb], scale=0.125)
                rotary(PS[nb][:, :].rearrange("p (c g d) -> p c g d", c=4, g=2), qv.rearrange("p (c g d) -> p c g d", c=4, g=2),
                       cos8[:, e, :], sin8[:, e, :], rtmp, [PB[nb], B_tabE], [B_qb], B_rt)
            for k in range(8):
                tr(PS[2 + k // 4][:, (k % 4) * 128:(k % 4 + 1) * 128], qb[:, k * 128:(k + 1) * 128], [B_qb], [PB[2 + k // 4]])
            for (bk, c0) in ((2, 0), (3, 4)):
                cp("act", qTz[0][0:64, c0:c0 + 4, :], PS[bk][0:64, :].rearrange("p (k t) -> p k t", k=4), [PB[bk]], [B_qT])
                cp("dve", qTz[1][64:128, c0:c0 + 4, :], PS[bk][64:128, :].rearrange("p (k t) -> p k t", k=4), [PB[bk]], [B_qT])
            for k in range(8):
                mm(PS[4][:, 0:48], hTd[:, k, :], wb1[:, k, 1792:1840], [B_br[s], B_wb1], [PB[4]], start=(k == 0), stop=(k == 7))
            act(gn, PS[4][:, 0:48], AF.Sigmoid, [PB[4]], [B_gn])
            for k in range(8):
                mm(PS[5][:, :], hTd[:, k, :], wb1[:, k, 1840:2352], [B_br[s], B_wb1], [PB[5]], start=(k == 0), stop=(k == 7))
            cp("act", qxb, PS[5][:, :], [PB[5]], [B_qxb])
            for h in range(4):
                tr(PS[6][:, h * 128:(h + 1) * 128], qxb[:, h * 128:(h + 1) * 128], [B_qxb], [PB[6]])
            cp("dve", qxT, PS[6][:, :].rearrange("p (h t) -> p h t", h=4), [PB[6]], [B_qxT])
            for c in range(2):
                for h in range(4):
                    mm(PS[7][:, h * 128:(h + 1) * 128], KmT[:, h, c * 128:(c + 1) * 128], qxT[:, h, :], [B_km, B_qxT], [PB[7]])
                act(mpT[c], PS[7][:, :].rearrange("p (h t) -> p h t", h=4), AF.Exp, [PB[7]], [B_mpT[c]])
            for h in range(4):
                for c in range(2):
                    mm(PS[5][:, h * 128:(h + 1) * 128], mpT[c][:, h, :], Vm[:, c, h, :], [B_mpT[c], B_vm], [PB[5]],
                       start=(c == 0), stop=(c == 1))
            for h in range(4):
                for c in range(2):
                    mm(PS[6][:, h:h + 1], mpT[c][:, h, :], onesb[:, 0:1], [B_mpT[c], B_cst], [PB[6]],
                       start=(c == 0), stop=(c == 1))
            S.op("dve", lambda e_: e_.reciprocal(out=rsm, in_=PS[6][:, 0:4]), [PB[6]], [B_rsm])
            tt("dve", ymemb.rearrange("p (h d) -> p h d", h=4), PS[5][:, :].rearrange("p (h d) -> p h d", h=4),
               rsm.unsqueeze(2).to_broadcast([128, 4, 128]), ALU.mult, [PB[5], B_rsm], [B_ymem])
            for h in range(4):
                tr(PS[7][:, h * 128:(h + 1) * 128], ymemb[:, h * 128:(h + 1) * 128], [B_ymem], [PB[7]])
            cp("act", bt[:, 12:16, :], PS[7][:, :].rearrange("p (h t) -> p h t", h=4), [PB[7]], [B_br[s]])
            tqs = tqe[:, e:e + 1]
            ts("dve", cmneg, cendrow, tqs, -29952.0, ALU.is_gt, ALU.mult, [B_cst, B_small], [B_cm])
            ts("dve", tqt, qrow, tqst[:, e:e + 1], None, ALU.add, None, [B_cst], [B_tqt])
            for g in range(2):
                pr = slice(64 * g, 64 * g + 64)
                Pv = Pb[g][:, 1:513]
                for c_ in range(8):
                    a = c_ % 2
                    ss_, Bs_ = ssum2[a], B_ssum2[a]
                    mm(PS[4 + a][:, :], qTz[g][:, c_, :], KcT[:, 0:512], [B_qT, B_KcT], [PB[4 + a]], start=True, stop=False)
                    mm(PS[4 + a][:, :], ident_b, cmneg, [B_ident, B_cm], [PB[4 + a]], start=False, stop=True)
                    act(ef[a], PS[4 + a][:, :], AF.Exp, [PB[4 + a]], [B_ef[a], Bs_], accum=ss_[:, 0:1])
                    ts("dve", ss_[:, 1:2], ss_[:, 0:1], 1e-30, None, ALU.max, None, [Bs_], [Bs_])
                    S.op("dve", lambda e_, ss_=ss_: e_.reciprocal(out=ss_[:, 1:2], in_=ss_[:, 1:2]), [Bs_], [Bs_])
                    if c_ == 0:
                        ts("dve", Pv, ef[a], ss_[:, 1:2], None, ALU.mult, None, [B_ef[a], Bs_], [B_Pb[g]])
                    else:
                        stt(Pv, ef[a], ss_[:, 1:2], Pv, ALU.mult, ALU.add, [B_ef[a], Bs_, B_Pb[g]], [B_Pb[g]])
                S.op("dve", lambda e_, g=g: e_.tensor_reduce(out=imp, in_=Pb[g][:, 0:512].rearrange("p (j s) -> p j s", s=4),
                                                          axis=AX.X, op=ALU.add), [B_Pb[g]], [B_imp])
                tt("dve", imp, imp, Pb[g][:, 4:516].rearrange("p (j s) -> p j s", s=4)[:, :, 0], ALU.add, [B_Pb[g], B_imp], [B_imp])
                ts("dve", nd, bsrow, tqs, None, ALU.subtract, None, [B_cst, B_small, B_imp], [B_imp])
                ts("dve", itmp, nd, -128.0, BIG, ALU.is_gt, ALU.mult, [B_imp], [B_imp])
                tt("dve", imp, imp, itmp, ALU.add, [B_imp], [B_imp])
                ts("dve", itmp, nd, 0.0, -3.0 * BIG, ALU.is_gt, ALU.mult, [B_imp], [B_imp])
                tt("dve", imp, imp, itmp, ALU.add, [B_imp], [B_imp])
                tt("dve", imp, imp, e0big, ALU.add, [B_imp, B_cst], [B_imp])
                S.op("dve", lambda e_: e_.max(out=mx[:, 0:8], in_=imp), [B_imp], [B_imp])
                S.op("dve", lambda e_: e_.match_replace(out=wk, in_to_replace=mx[:, 0:8], in_values=imp, imm_value=-1e30), [B_imp], [B_imp])
                S.op("dve", lambda e_: e_.max(out=mx[:, 8:16], in_=wk), [B_imp], [B_imp])
                ts("dve", selb, imp, mx[:, 15:16], None, ALU.is_ge, None, [B_imp], [B_sel])
                tr(PS[6][:, g * 128:(g + 1) * 128], selb, [B_sel], [PB[6]])
                cp("act", selT[g], PS[6][:, g * 128:(g + 1) * 128], [PB[6]], [B_selT[g]])
            for g in range(2):
                pr = slice(64 * g, 64 * g + 64)

                def mk_cmp(c):
                    return (None, lambda sl, bs: ts("dve", mk[bs], tqt, cendcol[:, c:c + 1], None, ALU.is_ge, None, [B_cst, B_tqt], [B_mk[bs]]))

                def mk_win(j, ee):
                    v = 0 if j == 0 else (2 if j == 4 else 1)
                    return (None, lambda sl, bs: ts("dve", mk[bs], Mw3[:, v, :], wval[:, ee:ee + 1], None, ALU.mult, None, [B_cst, B_small], [B_mk[bs]]))

                def mk_sel(cc, g=g):
                    def f_pe(sl):
                        mm(PS[6 + sl][:, 256:384], Gb[:, cc * 128:(cc + 1) * 128], selT[g], [B_G, B_selT[g]], [PB[6 + sl]])

                    def f_dve(sl, bs):
                        stt(mk[bs], tqt, kidx[:, cc:cc + 1], PS[6 + sl][:, 256:384], ALU.is_ge, ALU.mult,
                            [B_cst, B_tqt, PB[6 + sl]], [B_mk[bs]])
                    return (f_pe, f_dve)

                attend(e, g, 0, [(KcT[:, c * 128:(c + 1) * 128], Vc[:, c, g, :]) + mk_cmp(c) for c in range(4)])
                attend(e, g, 1, [(KTs[:, cc * 128:(cc + 1) * 128], Vs[:, cc, g, :]) + mk_sel(cc) for cc in range(48 + i)])
                attend(e, g, 2, [(KTw[:, (e - 4 + j) * 128:(e - 3 + j) * 128], Vw[:, e - 4 + j, g, :]) + mk_win(j, e - 4 + j)
                                 for j in range(5)])
            cp("act", ynb, ynsa, [B_yn], [B_ynb])
            for k in range(8):
                tr(PS[2 + k // 4][:, (k % 4) * 128:(k % 4 + 1) * 128], ynb[:, k * 128:(k + 1) * 128], [B_ynb], [PB[2 + k // 4]])
            cp("act", bt[:, 4:8, :], PS[2][:, :].rearrange("p (k t) -> p k t", k=4), [PB[2]], [B_br[s]])
            cp("dve", bt[:, 8:12, :], PS[3][:, :].rearrange("p (k t) -> p k t", k=4), [PB[3]], [B_br[s]])
            lastbr = dma(br_scr[i], bt.rearrange("p k t -> p (k t)"), [B_br[s]], [])
            if dbg == "B1":
                lastbr = dma(dbg_t(f"br{i}", [128, 24 * 128], BF16), bt.rearrange("p k t -> p (k t)"), [B_br[s]], [])
                fl1 = [lastbr, dma(dbg_t(f"ynsa{i}", [128, 1024]), ynsa, [B_yn], []),
                       dma(dbg_t(f"qb{i}", [128, 1024], BF16), qb, [B_qb], []),
                       dma(dbg_t(f"gn{i}", [128, 48]), gn, [B_gn], []),
                       dma(dbg_t(f"selT{i}", [128, 128], BF16), selT[1], [B_selT[1]], []),
                       dma(dbg_t(f"Pb{i}", [128, 516]), Pb[1], [B_Pb[1]], [])]
        if dbg == "B1":
            print(S.emit(final_waits=fl1))
            return nc
        barrier()
        top[0] = persist_top
        wg = abf(8 * 3072).rearrange("p (k c) -> p k c", k=8)
        wbp = abf(4 * 1024).rearrange("p (k c) -> p k c", k=4)
        wbn = abf(8 * 1024).rearrange("p (k c) -> p k c", k=8)
        wbx = abf(4 * 1024).rearrange("p (k c) -> p k c", k=4)
        wo = abf(8 * 1024).rearrange("p (k c) -> p k c", k=8)
        B_w2p = S.buf("w_b2")
        for k in range(8):
            loadw(lambda c0, n, k=k: wg[:, k, c0:c0 + n], w_in[0][k * 128:(k + 1) * 128, 2864:5936], 3072,
                  scale=gpre[:, k:k + 1], r_extra=[B_small], w=[B_w2p])
            loadw(lambda c0, n, k=k: wbn[:, k, c0:c0 + n], w_br_nsa[0][k * 128:(k + 1) * 128, :], 1024, w=[B_w2p])
            loadw(lambda c0, n, k=k: wo[:, k, c0:c0 + n], w_out[0][k * 128:(k + 1) * 128, :], 1024, w=[B_w2p])
            if k < 4:
                loadw(lambda c0, n, k=k: wbp[:, k, c0:c0 + n], w_br_pool[0][k * 128:(k + 1) * 128, :], 1024, w=[B_w2p])
                loadw(lambda c0, n, k=k: wbx[:, k, c0:c0 + n], w_br_xa[0][k * 128:(k + 1) * 128, :], 1024, w=[B_w2p])
        gpost = af32(1024)
        B_gp = S.buf("gpost")
        dma(gpost, post_mix_g.partition_broadcast(128), [], [B_gp])
        brT = [abf(24 * 128).rearrange("p (k t) -> p k t", k=24) for _ in range(2)]
        B_br = [S.buf("c_br0"), S.buf("c_br1")]
        xt2 = [af32(1024), af32(1024)]
        B_xt = [S.buf("c_xt0"), S.buf("c_xt1")]
        sg = af32(1024)
        B_sg = S.buf("sg")
        yy = af32(1024)
        B_y = S.buf("yy")
        ytm = af32(1024)
        B_ytm = S.buf("ytm")
        yb = abf(1024)
        B_yb = S.buf("yb")
        yT = abf(1024).rearrange("p (k t) -> p k t", k=8)
        B_yT = S.buf("yT")
        junk = abf(512)
        x1t = [af32(1024), af32(1024)]
        B_x1 = [S.buf("x1t0"), S.buf("x1t1")]

        def post_norm_res(pa, pb, gp, Bg, xres, Bxres, dst, Bdst):
            act(junk[:, 0:512], PS[pa][:, :], AF.Square, [PB[pa]], [B_ss2], accum=ss2[:, 0:1])
            act(junk[:, 0:512], PS[pb][:, :], AF.Square, [PB[pb]], [B_ss2], accum=ss2[:, 1:2])
            tt("dve", ss2[:, 2:3], ss2[:, 0:1], ss2[:, 1:2], ALU.add, [B_ss2], [B_ss2])
            ts("dve", ss2[:, 2:3], ss2[:, 2:3], 1.0 / 1024, 1e-6, ALU.mult, ALU.add, [B_ss2], [B_ss2])
            act(ss2[:, 2:3], ss2[:, 2:3], AF.Sqrt, [B_ss2], [B_ss2])
            S.op("dve", lambda e_: e_.reciprocal(out=ss2[:, 3:4], in_=ss2[:, 2:3]), [B_ss2], [B_ss2])
            for nb, bank in enumerate((pa, pb)):
                blk = slice(nb * 512, (nb + 1) * 512)
                stt(dst[:, blk], PS[bank][:, :], ss2[:, 3:4], gp[:, blk], ALU.mult, ALU.mult, [PB[bank], B_ss2, Bg], [Bdst])
                tt("pool", dst[:, blk], dst[:, blk], xres[:, blk], ALU.add, [Bdst, Bxres], [Bdst])

        dma(brT[0].rearrange("p k t -> p (k t)"), br_scr[0], [], [B_br[0]])
        dma(xt2[0], x_ext[4 * 128:5 * 128, :], [], [B_xt[0]])
        for i in range(17):
            s = i % 2
            e = i + 4
            if i + 1 < 17:
                dma(brT[1 - s].rearrange("p k t -> p (k t)"), br_scr[i + 1], [], [B_br[1 - s]])
                dma(xt2[1 - s], x_ext[(e + 1) * 128:(e + 2) * 128, :], [], [B_xt[1 - s]])
            bt = brT[s]
            for br in range(3):
                for nb in range(2):
                    for k in range(8):
                        mm(PS[nb][:, :], bt[:, 16 + k, :], wg[:, k, br * 1024 + nb * 512:br * 1024 + (nb + 1) * 512],
                           [B_br[s], B_w2p], [PB[nb]], start=(k == 0), stop=(k == 7))
                wsel, off, nk = ((wbp, 0, 4), (wbn, 4, 8), (wbx, 12, 4))[br]
                for nb in range(2):
                    for k in range(nk):
                        mm(PS[2 + nb][:, :], bt[:, off + k, :], wsel[:, k, nb * 512:(nb + 1) * 512],
                           [B_br[s], B_w2p], [PB[2 + nb]], start=(k == 0), stop=(k == nk - 1))
                for nb in range(2):
                    blk = slice(nb * 512, (nb + 1) * 512)
                    act(sg[:, blk], PS[nb][:, :], AF.Sigmoid, [PB[nb]], [B_sg])
                    if br == 0:
                        tt("dve", yy[:, blk], sg[:, blk], PS[2 + nb][:, :], ALU.mult, [B_sg, PB[2 + nb]], [B_y])
                    else:
                        tt("dve", ytm[:, blk], sg[:, blk], PS[2 + nb][:, :], ALU.mult, [B_sg, PB[2 + nb]], [B_ytm])
                        tt("pool", yy[:, blk], yy[:, blk], ytm[:, blk], ALU.add, [B_y, B_ytm], [B_y])
            cp("act", yb, yy, [B_y], [B_yb])
            for k in range(8):
                tr(PS[4 + k // 4][:, (k % 4) * 128:(k % 4 + 1) * 128], yb[:, k * 128:(k + 1) * 128], [B_yb], [PB[4 + k // 4]])
            cp("act", yT[:, 0:4, :], PS[4][:, :].rearrange("p (k t) -> p k t", k=4), [PB[4]], [B_yT])
            cp("dve", yT[:, 4:8, :], PS[5][:, :].rearrange("p (k t) -> p k t", k=4), [PB[5]], [B_yT])
            for nb in range(2):
                for k in range(8):
                    mm(PS[6 + nb][:, :], yT[:, k, :], wo[:, k, nb * 512:(nb + 1) * 512], [B_yT, B_w2p], [PB[6 + nb]],
                       start=(k == 0), stop=(k == 7))
            post_norm_res(6, 7, gpost, B_gp, xt2[s], B_xt[s], x1t[s], B_x1[s])
            lx = dma(x1_scr[i], x1t[s], [B_x1[s]], [])
            if dbg == "B2":
                lx = dma(dbg_t(f"x1_{i}", [128, 1024]), x1t[s], [B_x1[s]], [])
        if dbg == "B2":
            print(S.emit(final_waits=[lx]))
            return nc
        barrier()
        top[0] = persist_top

        wup = abf(8 * 5632).rearrange("p (k c) -> p k c", k=8)
        wdn = abf(22 * 1024).rearrange("p (k c) -> p k c", k=22)
        B_w3 = S.buf("w_c")
        for k in range(8):
            loadw(lambda c0, n, k=k: wup[:, k, c0:c0 + n], w_up[0][k * 128:(k + 1) * 128, :], 5632,
                  scale=gffn[:, k:k + 1], r_extra=[B_small], w=[B_w3])
        for k in range(22):
            loadw(lambda c0, n, k=k: wdn[:, k, c0:c0 + n], w_down[0][k * 128:(k + 1) * 128, :], 1024, w=[B_w3])
        convp = af32(44 * 4).rearrange("p (j c) -> p j c", c=4)
        B_cv = S.buf("convp")
        for kk in range(3):
            dma(convp[:, :, kk], conv_w[0][kk].rearrange("(j p) -> p j", p=128), [], [B_cv], slow=True)
        dma(convp[:, :, 3], conv_b[0].rearrange("(j p) -> p j", p=128), [], [B_cv], slow=True)
        gpost2 = af32(1024)
        B_gp2 = S.buf("gpost2")
        dma(gpost2, post_ffn_g.partition_broadcast(128), [], [B_gp2])
        xt2 = [af32(1024), af32(1024)]
        B_xt = [S.buf("d_xt0"), S.buf("d_xt1")]
        xn = abf(1024)
        B_xn = S.buf("d_xn")
        junk = abf(1024)
        ssA = af32(2)
        B_ss = S.buf("d_ss")
        h2Ts = [abf(8 * 130).rearrange("p (k t) -> p k t", k=8) for _ in range(2)]
        B_h2s = [S.buf("h2T0"), S.buf("h2T1")]
        xnC = [xn, abf(1024)]
        B_xnC = [B_xn, S.buf("xnC1")]
        aT = abf(22 * 128).rearrange("p (k t) -> p k t", k=22)
        B_aT = S.buf("aT")
        cg = [af32(128), af32(128)]
        cv = [af32(128), af32(128)]
        gl = [af32(128), af32(128)]
        B_cg = [S.buf("cg0"), S.buf("cg1")]
        B_cvb = [S.buf("cv0"), S.buf("cv1")]
        B_gl = [S.buf("gl0"), S.buf("gl1")]
        ot = [af32(1024), af32(1024)]
        B_ot = [S.buf("ot0"), S.buf("ot1")]
        xt3 = [xt2[0], xt2[1], af32(1024)]
        B_xt3 = [B_xt[0], B_xt[1], S.buf("d_xt2")]
        fins = []

        def c_stage1(i):
            s_ = i % 2
            x3 = i % 3
            rms_scale(xt3[x3], xnC[s_], B_xt3[x3], B_xnC[s_], ssA[:, 0:1], junk, B_ss)
            for k in range(8):
                bank = k // 4
                tr(PS[bank][:, (k % 4) * 128:(k % 4 + 1) * 128], xnC[s_][:, k * 128:(k + 1) * 128], [B_xnC[s_]], [PB[bank]])
            cp("act", h2Ts[s_][:, 0:4, 2:130], PS[0][:, :].rearrange("p (k t) -> p k t", k=4), [PB[0]], [B_h2s[s_]])
            cp("dve", h2Ts[s_][:, 4:8, 2:130], PS[1][:, :].rearrange("p (k t) -> p k t", k=4), [PB[1]], [B_h2s[s_]])
            if i == 1:
                ts("pool", h2Ts[s_][:, :, 0:2], h2Ts[1 - s_][:, :, 128:130], hflag[:, 0:1], None, ALU.mult, None,
                   [B_h2s[1 - s_], B_small, B_h2s[s_]], [B_h2s[s_]])
            elif i > 1:
                cp("pool", h2Ts[s_][:, :, 0:2], h2Ts[1 - s_][:, :, 128:130], [B_h2s[1 - s_], B_h2s[s_]], [B_h2s[s_]])

        dma(xt3[0], x1_scr[0], [], [B_xt3[0]])
        dma(xt3[1], x1_scr[1], [], [B_xt3[1]])
        dma(xt3[2], x1_scr[2], [], [B_xt3[2]])
        c_stage1(0)
        for i in range(17):
            s = i % 2
            if i + 1 < 17:
                c_stage1(i + 1)
            if i == 0:
                dma(xt3[0], x1_scr[3], [], [B_xt3[0]])
                continue
            h2T, B_h2 = h2Ts[s], B_h2s[s]
            for j in range(22):
                a = j % 2
                bg, bv = 2 + 2 * a, 3 + 2 * a
                for k in range(8):
                    mm(PS[bg][:, 0:130], wup[:, k, j * 128:(j + 1) * 128], h2T[:, k, :], [B_w3, B_h2], [PB[bg]],
                       start=(k == 0), stop=(k == 7))
                for k in range(8):
                    mm(PS[bv][:, 0:130], wup[:, k, (22 + j) * 128:(23 + j) * 128], h2T[:, k, :], [B_w3, B_h2], [PB[bv]],
                       start=(k == 0), stop=(k == 7))
                for (bank, dstc, Bd, jj) in ((bg, cg[a], B_cg[a], j), (bv, cv[a], B_cvb[a], 22 + j)):
                    act(dstc, PS[bank][:, 2:130], AF.Identity, [PB[bank], B_cv], [Bd], bias=convp[:, jj, 3:4], scale=convp[:, jj, 2:3])
                    stt(dstc, PS[bank][:, 1:129], convp[:, jj, 1:2], dstc, ALU.mult, ALU.add, [PB[bank], B_cv, Bd], [Bd])
                    stt(dstc, PS[bank][:, 0:128], convp[:, jj, 0:1], dstc, ALU.mult, ALU.add, [PB[bank], B_cv, Bd], [Bd])
                act(gl[a], cg[a], AF.Gelu_apprx_tanh, [B_cg[a]], [B_gl[a]])
                tt("pool", aT[:, j, :], gl[a], cv[a], ALU.mult, [B_gl[a], B_cvb[a]], [B_aT])
            for nb in range(2):
                for j in range(22):
                    mm(PS[6 + nb][:, :], aT[:, j, :], wdn[:, j, nb * 512:(nb + 1) * 512], [B_aT, B_w3], [PB[6 + nb]],
                       start=(j == 0), stop=(j == 21))
            post_norm_res(6, 7, gpost2, B_gp2, xt3[i % 3], B_xt3[i % 3], ot[s], B_ot[s])
            fins.append(dma(out_d[(i - 1) * 128:i * 128, :], ot[s], [B_ot[s]], []))
            if i + 3 < 17:
                dma(xt3[i % 3], x1_scr[i + 3], [], [B_xt3[i % 3]])
        stats = S.emit(final_waits=fins)
        print("emit stats", stats, flush=True)
    return nc


def _consts():
    c = {}
    c["ident"] = np.eye(128, dtype=np.float32)
    k = np.arange(8192)
    c["Gm"] = (k[None, :] // 64 == np.arange(128)[:, None]).astype(np.float32)
    c["invf"] = (np.float32(500000.0) ** (-np.arange(8, dtype=np.float32) * np.float32(2.0 / 16))).astype(np.float32).reshape(1, 8)
    c["bsrow"] = (64.0 * np.arange(128, dtype=np.float32)).reshape(1, 128)
    ce = (16.0 * np.arange(512, dtype=np.float32) + 31.0)
    ce[511] = 1e9
    c["cendrow"] = ce.reshape(1, 512)
    c["kidx"] = (128.0 * np.arange(64)[None, :] + np.arange(128)[:, None]).astype(np.float32)
    c["cendcol"] = np.ascontiguousarray(ce.reshape(4, 128).T)
    p = np.arange(128)[:, None]
    q = np.arange(128)[None, :]
    c["Mw"] = np.concatenate([(q < p), (q >= p)], axis=1).astype(np.float32)
    e0 = np.zeros((1, 128), np.float32)
    e0[0, 0] = BIG
    c["e0big"] = e0
    A = np.zeros((128, 4, 2, 128), np.float32)
    for gi, w in enumerate((2, 4, 8, 16)):
        tp = np.arange(128)[:, None]
        t = np.arange(128)[None, :]
        A[:, gi, 0, :] = ((t - tp >= 0) & (t - tp < w))
        A[:, gi, 1, :] = ((t + 128 - tp >= 0) & (t + 128 - tp < w))
    c["Aband"] = A.reshape(128, 1024)
    c["qrow"] = np.arange(128, dtype=np.float32).reshape(1, 128)
    return c


_PROG = {}


def kernel(**inputs):
    x = np.asarray(inputs["x"], dtype=np.float32)
    mem = np.asarray(inputs["mem"], dtype=np.float32)
    positions = np.asarray(inputs["positions"]).astype(np.int32)
    if inputs.get("_return_maps"):
        nc = None
    else:
        if "nc" not in _PROG:
            _PROG["nc"] = build()
        nc = _PROG["nc"]
    consts = _consts()
    wnames = ["pre_mix_g", "w_in", "pool_w", "pool_scale", "cmp_pe", "cmp_w1", "cmp_w2", "mem_norm_g", "w_mem_kv",
              "w_br_pool", "w_br_nsa", "w_br_xa", "w_out", "post_mix_g", "pre_ffn_g", "w_up", "conv_w", "conv_b",
              "w_down", "post_ffn_g"]
    shared = {n: np.ascontiguousarray(np.asarray(inputs[n], dtype=np.float32)) for n in wnames}
    in_maps = []
    for core in range(8):
        b, r = core // 4, core % 4
        m = dict(shared)
        m.update(consts)
        m["x_all"] = np.ascontiguousarray(x[b])
        m["mem"] = np.ascontiguousarray(mem[b])
        xe = np.zeros((NEXT * 128, 1024), np.float32)
        pe = np.zeros((NEXT, 128), np.int32)
        tq = np.zeros((NEXT, 128), np.float32)
        wv = np.zeros((1, NEXT), np.float32)
        ic = np.ones((128, NEXT, 4), np.float32)
        tst = np.zeros((1, NEXT), np.float32)
        for e in range(NEXT):
            ge = 16 * r - 5 + e
            tq[e] = ge * 128 + np.arange(128)
            tst[0, e] = ge * 128
            if ge >= 0:
                xe[e * 128:(e + 1) * 128] = x[b, ge * 128:(ge + 1) * 128]
                pe[e] = positions[b, ge * 128:(ge + 1) * 128]
                wv[0, e] = 1.0
                t = ge * 128 + np.arange(128)
                for gi, w in enumerate((2, 4, 8, 16)):
                    ic[:, e, gi] = 1.0 / np.minimum(t + 1, w).astype(np.float32)
        m["x_ext"] = xe
        m["pos_ext"] = np.ascontiguousarray(pe.T)
        m["pos_all"] = np.ascontiguousarray(positions[b].reshape(NALL, 128).T)
        m["tq_ext"] = np.ascontiguousarray(tq.T)
        m["tqstart"] = tst
        m["wvalid"] = wv
        m["invcnt"] = np.ascontiguousarray(ic.reshape(128, NEXT * 4))
        m["hflag"] = np.array([[0.0 if r == 0 else 1.0]], np.float32)
        in_maps.append(m)
    if inputs.get("_return_maps"):
        return in_maps
    res = run_bass_kernel_spmd(nc, in_maps, core_ids=list(range(8)))
    out = np.zeros((2, 8192, 1024), np.float32)
    for core in range(8):
        b, r = core // 4, core % 4
        out[b, r * 2048:(r + 1) * 2048] = res.results[core]["out"]
    return out
```

```python
import numpy as np
from contextlib import ExitStack
import concourse.bass as bass
import concourse.mybir as mybir
from concourse.bass_utils import run_bass_kernel_spmd

F32 = mybir.dt.float32
BF16 = mybir.dt.bfloat16
I32 = mybir.dt.int32
AF = mybir.ActivationFunctionType
ALU = mybir.AluOpType
AX = mybir.AxisListType


import sys as _sys


def _where():
    f = _sys._getframe(2)
    out = []
    while f is not None and len(out) < 4:
        if f.f_code.co_name != "<lambda>":
            out.append(f.f_lineno)
        f = f.f_back
    return out


class Buf:
    __slots__ = ("name", "writers", "readers", "excl", "last")

    def __init__(self, name, excl=False):
        self.name = name
        self.writers = []
        self.readers = []
        self.excl = excl
        self.last = {}


class Ins:
    __slots__ = ("eng", "fn", "deps", "idx", "flag", "tok", "dma", "pre", "where")

    def __init__(self, eng, fn, dma):
        self.eng = eng
        self.fn = fn
        self.deps = []
        self.flag = False
        self.tok = None
        self.dma = dma
        self.pre = None


class Sched:
    ENGS = ("pe", "dve", "act", "pool", "sp")
    EPOCH = 8000
    NDMA = 24

    def __init__(self, nc, stack):
        self.nc = nc
        self.stack = stack
        self.q = {e: [] for e in self.ENGS}
        self.nbuf = 0

    def buf(self, name=None, excl=False):
        self.nbuf += 1
        return Buf(name or f"b{self.nbuf}", excl)

    def op(self, eng, fn, r=(), w=(), dma=False):
        ins = Ins(eng, fn, dma)
        ins.where = _where()
        deps = []
        for b in r:
            deps.extend(b.writers)
        for b in w:
            deps.extend(b.readers)
        for b in list(r) + list(w):
            if b.excl:
                for en, li in b.last.items():
                    if en != eng:
                        deps.append(li)
                b.last[eng] = ins
        for b in w:
            if b.readers or (b in r):
                b.writers = [ins]
                b.readers = []
            else:
                b.writers.append(ins)
                if len(b.writers) > 48:
                    b.writers = b.writers[-48:]
        for b in r:
            if b not in w:
                b.readers.append(ins)
                if len(b.readers) > 48:
                    b.readers = b.readers[-48:]
        seen = set()
        for d in deps:
            if d is ins or id(d) in seen:
                continue
            if d.eng == "pe" and eng == "pe" and not d.dma:
                continue
            seen.add(id(d))
            ins.deps.append(d)
            d.flag = True
        self.q[eng].append(ins)
        return ins

    def emit(self, final_waits=()):
        nc = self.nc
        stack = self.stack
        sems = {}
        for e in self.ENGS:
            n = 0
            for ins in self.q[e]:
                if ins.dma:
                    continue
                if ins.flag:
                    n += 1
                    ins.idx = n
            nep = (n + self.EPOCH - 1) // self.EPOCH
            sems[e] = [stack.enter_context(nc.semaphore(f"s_{e}_{k}")) for k in range(max(nep, 1))]
        dsems = [stack.enter_context(nc.semaphore(f"s_dma_{k}")) for k in range(self.NDMA)]
        duse = [0] * self.NDMA
        dma_engs = [e for e in self.ENGS if any(i.dma for i in self.q[e])]
        share = {}
        if dma_engs:
            per = self.NDMA // len(dma_engs)
            for k, e in enumerate(dma_engs):
                share[e] = list(range(k * per, (k + 1) * per))
        for e in dma_engs:
            j = 0
            for ins in self.q[e]:
                if not ins.dma:
                    continue
                s = share[e][j % len(share[e])]
                j += 1
                prev = duse[s]
                duse[s] += 1
                ins.tok = (dsems[s], 16 * duse[s])
                ins.pre = (dsems[s], 16 * prev) if prev > 0 else None
        for e in self.ENGS:
            for ins in self.q[e]:
                if ins.dma or not ins.flag:
                    continue
                k = (ins.idx - 1) // self.EPOCH
                ins.tok = (sems[e][k], (ins.idx - 1) % self.EPOCH + 1)
        engobj = {"pe": "tensor", "dve": "vector", "act": "scalar", "pool": "gpsimd", "sp": "sync"}
        stats = {}
        with nc.Block() as block:
            for e in self.ENGS:
                lst = self.q[e]

                def body(eng, lst=lst, e=e):
                    waited = {}
                    nw = 0

                    def wait(tok):
                        nonlocal nw
                        sem, val = tok
                        key = id(sem)
                        if waited.get(key, 0) >= val:
                            return
                        waited[key] = val
                        eng.wait_ge(sem, val)
                        nw += 1

                    for ins in lst:
                        if ins.pre is not None:
                            wait(ins.pre)
                        for d in ins.deps:
                            wait(d.tok)
                        try:
                            bi = ins.fn(eng)
                        except BaseException:
                            print("FAILED op recorded at lines", ins.where, flush=True)
                            raise
                        if ins.dma:
                            bi.then_inc(ins.tok[0], 16)
                        elif ins.flag:
                            bi.then_inc(ins.tok[0], 1)
                    if e == "sp":
                        for fw in final_waits:
                            wait(fw.tok)
                    stats[e] = (len(lst), nw)

                getattr(block, engobj[e])(body)
        return stats


import os as _os
DUMPX = int(_os.environ.get('DUMPX', '0'))
NOBAR = int(_os.environ.get('NOBAR', '0'))
NEXT = 21
NALL = 64
BIG = 100.0
TWO_PI = float(2 * np.pi)


def build(dbg=None, ntile_b1=NEXT, na=NALL, skipcmp=False):
    nc = bass.Bass("TRN2", target_bir_lowering=False)

    def din(name, shape, dt=F32):
        return nc.dram_tensor(name, list(shape), dt, kind="ExternalInput").ap()

    x_all = din("x_all", [8192, 1024])
    x_ext = din("x_ext", [NEXT * 128, 1024])
    pos_all = din("pos_all", [128, NALL], I32)
    pos_ext = din("pos_ext", [128, NEXT], I32)
    tq_ext = din("tq_ext", [128, NEXT])
    qrow_d = din("qrow", [1, 128])
    tqst_d = din("tqstart", [1, NEXT])
    wvalid_d = din("wvalid", [1, NEXT])
    invcnt_d = din("invcnt", [128, NEXT * 4])
    hflag_d = din("hflag", [1, 1])
    mem_d = din("mem", [256, 1024])
    ident_d = din("ident", [128, 128])
    G_d = din("Gm", [128, 8192])
    invf_d = din("invf", [1, 8])
    bsrow_d = din("bsrow", [1, 128])
    cendrow_d = din("cendrow", [1, 512])
    kidx_d = din("kidx", [128, 64])
    cendcol_d = din("cendcol", [128, 4])
    Mw_d = din("Mw", [128, 256])
    e0_d = din("e0big", [1, 128])
    Ab_d = din("Aband", [128, 1024])
    pre_mix_g = din("pre_mix_g", [1, 1024])
    w_in = din("w_in", [1, 1024, 5936])
    pool_w = din("pool_w", [1, 4, 128, 128])
    pool_scale = din("pool_scale", [1, 512])
    cmp_pe = din("cmp_pe", [1, 2, 32, 64])
    cmp_w1 = din("cmp_w1", [1, 2, 2048, 256])
    cmp_w2 = din("cmp_w2", [1, 2, 256, 64])
    mem_norm_g = din("mem_norm_g", [1, 1024])
    w_mem_kv = din("w_mem_kv", [1, 1024, 1024])
    w_br_pool = din("w_br_pool", [1, 512, 1024])
    w_br_nsa = din("w_br_nsa", [1, 1024, 1024])
    w_br_xa = din("w_br_xa", [1, 512, 1024])
    w_out = din("w_out", [1, 1024, 1024])
    post_mix_g = din("post_mix_g", [1, 1024])
    pre_ffn_g = din("pre_ffn_g", [1, 1024])
    w_up = din("w_up", [1, 1024, 5632])
    conv_w = din("conv_w", [1, 3, 5632])
    conv_b = din("conv_b", [1, 5632])
    w_down = din("w_down", [1, 2816, 1024])
    post_ffn_g = din("post_ffn_g", [1, 1024])
    out_d = nc.dram_tensor("out", [16 * 128, 1024], F32, kind="ExternalOutput").ap()
    br_scr = nc.dram_tensor("br_scr", [17, 128, 24 * 128], BF16, kind="Internal").ap()
    x1_scr = nc.dram_tensor("x1_scr", [17, 128, 1024], F32, kind="Internal").ap()
    dbg_out = {}

    def dbg_t(name, shape, dt=F32):
        return nc.dram_tensor("dbg_" + name, list(shape), dt, kind="ExternalOutput").ap()

    with ExitStack() as st:
        S = Sched(nc, st)
        ARN = 95800
        arena = st.enter_context(nc.sbuf_tensor("arena", [128, ARN], BF16))
        PS = [st.enter_context(nc.psum_tensor(f"ps{i}", [128, 512], F32)) for i in range(8)]
        PB = [S.buf(f"ps{i}", excl=True) for i in range(8)]
        top = [0]

        def abf(n):
            a = arena[:, top[0]:top[0] + n]
            top[0] += (n + 31) // 32 * 32
            assert top[0] <= ARN, top[0]
            return a

        def af32(n):
            a = arena[:, top[0]:top[0] + 2 * n].bitcast(F32)
            top[0] += (n + 15) // 16 * 32
            assert top[0] <= ARN, top[0]
            return a

        def ai32(n):
            a = arena[:, top[0]:top[0] + 2 * n].bitcast(I32)
            top[0] += (n + 15) // 16 * 32
            assert top[0] <= ARN, top[0]
            return a

        def mm(out, lhsT, rhs, r, w, start=True, stop=True):
            return S.op("pe", lambda e: e.matmul(out, lhsT=lhsT, rhs=rhs, start=start, stop=stop,
                                                 skip_group_check=True), r, w)

        last_func = [None]

        def act(out, in_, func, r, w, bias=None, scale=None, accum=None):
            if func != last_func[0]:
                last_func[0] = func
                S.op("act", lambda e: e.activation(out=bsc[1][:, 0:1], in_=bsc[3][:, 0:1], func=func), [B_bsc3], [])
            kw = {}
            if bias is not None:
                kw["bias"] = bias
            if scale is not None:
                kw["scale"] = scale
            if accum is not None:
                kw["accum_out"] = accum
            return S.op("act", lambda e: e.activation(out=out, in_=in_, func=func, **kw), r, w)

        def ts(eng, out, in0, s1, s2, op0, op1, r, w):
            if s2 is None:
                return S.op(eng, lambda e: e.tensor_scalar(out=out, in0=in0, scalar1=s1, scalar2=None, op0=op0), r, w)
            return S.op(eng, lambda e: e.tensor_scalar(out=out, in0=in0, scalar1=s1, scalar2=s2, op0=op0, op1=op1), r, w)

        def tt(eng, out, in0, in1, op, r, w):
            return S.op(eng, lambda e: e.tensor_tensor(out=out, in0=in0, in1=in1, op=op), r, w)

        def stt(out, in0, scalar, in1, op0, op1, r, w, accum=None):
            if accum is None:
                return S.op("dve", lambda e: e.scalar_tensor_tensor(out=out, in0=in0, scalar=scalar, in1=in1, op0=op0, op1=op1), r, w)
            return S.op("dve", lambda e: e.scalar_tensor_tensor(out=out, in0=in0, scalar=scalar, in1=in1, op0=op0, op1=op1,
                                                                accum_out=accum), r, w)

        def cp(eng, out, in_, r, w, scale=None):
            if eng == "act":
                return act(out, in_, AF.Copy, r, w, scale=scale)
            if scale is not None:
                return ts(eng, out, in_, scale, None, ALU.mult, None, r, w)
            return S.op(eng, lambda e: e.tensor_copy(out=out, in_=in_), r, w)

        def memset(eng, ap, val, w):
            return S.op(eng, lambda e: e.memset(ap, val), (), w)

        dmas = []

        def dma(out, in_, r, w, slow=False):
            if slow:
                i = S.op("sp", lambda e: e.dma_start(out=out, in_=in_, allow_slow_non_contiguous=True), r, w, dma=True)
            else:
                i = S.op("sp", lambda e: e.dma_start(out=out, in_=in_), r, w, dma=True)
            dmas.append(i)
            return i

        bsc = [af32(16) for _ in range(4)]
        bbuf = {e: S.buf("bar_" + e) for e in S.ENGS}
        bar_scr = nc.dram_tensor("bar_scr", [128, 16], F32, kind="Internal").ap()

        def barrier():
            allb = list(bbuf.values())
            mm(PS[7][:, 0:1], ident_b[:, 0:128], ident_b[:, 0:1], [PB[7], B_ident], [bbuf["pe"], PB[7]])
            memset("dve", bsc[0], 0.0, [bbuf["dve"]])
            act(bsc[1], bsc[3], AF.Copy, [B_bsc3], [bbuf["act"]])
            memset("pool", bsc[2], 0.0, [bbuf["pool"]])
            i = S.op("sp", lambda e: e.dma_start(out=bar_scr, in_=bsc[3]), [B_bsc3], [bbuf["sp"]], dma=True)
            for d in dmas:
                if d not in i.deps:
                    i.deps.append(d)
            dmas.clear()
            mm(PS[7][:, 0:1], ident_b[:, 0:128], ident_b[:, 0:1], allb + [PB[7], B_ident], [PB[7]])
            S.op("dve", lambda e: e.memset(bsc[0], 0.0), allb, [])
            act(bsc[1], bsc[3], AF.Copy, allb + [B_bsc3], [])
            S.op("pool", lambda e: e.memset(bsc[2], 0.0), allb, [])
            S.op("sp", lambda e: e.dma_start(out=bar_scr, in_=bsc[3]), allb + [B_bsc3], [], dma=True)

        ident_b = abf(128)
        B_ident = S.buf("ident")
        stage = [af32(1024), af32(1024)]
        B_stage = [S.buf("stg0"), S.buf("stg1")]
        stg_i = [0]
        cast_i = [0]
        B_bsc3 = S.buf("bsc3")
        memset("dve", bsc[3], 0.0, [B_bsc3])

        def loadw(dst_fn, src, ncols, scale=None, r_extra=(), w=None, perm_q=False):
            for c0 in range(0, ncols, 1024):
                n = min(1024, ncols - c0)
                k = stg_i[0] % 2
                stg_i[0] += 1
                np_ = src.shape[0]
                sl = stage[k][0:np_, 0:n]
                dma(sl, src[:, c0:c0 + n], [], [B_stage[k]])
                eng = ("pool", "dve")[cast_i[0] % 2]
                cast_i[0] += 1
                if perm_q:
                    for g in range(2):
                        o = dst_fn(c0, n).rearrange("p (c g d) -> p c g d", c=8, g=2)[:, :, g, :]
                        i_ = stage[k][0:np_, g * 512:(g + 1) * 512].rearrange("p (c d) -> p c d", c=8)
                        if scale is not None:
                            ts(eng, o, i_, scale, None, ALU.mult, None, [B_stage[k]] + list(r_extra), w)
                        else:
                            cp(eng, o, i_, [B_stage[k]] + list(r_extra), w)
                else:
                    if scale is not None:
                        ts(eng, dst_fn(c0, n), sl, scale, None, ALU.mult, None, [B_stage[k]] + list(r_extra), w)
                    else:
                        cp(eng, dst_fn(c0, n), sl, [B_stage[k]] + list(r_extra), w)

        def tr(out_ps, in_sb, r, w, start=True):
            return mm(out_ps, in_sb, ident_b[0:in_sb.shape[0], 0:in_sb.shape[0]], list(r) + [B_ident], w)

        dma(stage[0][:, 0:128], ident_d, [], [B_stage[0]])
        cp("dve", ident_b, stage[0][:, 0:128], [B_stage[0]], [B_ident])

        gpre = af32(8)
        gmem = af32(8)
        gffn = af32(8)
        B_small = S.buf("small")
        dma(gpre, pre_mix_g[0].rearrange("(k p) -> p k", p=128), [], [B_small], slow=True)
        dma(gmem, mem_norm_g[0].rearrange("(k p) -> p k", p=128), [], [B_small], slow=True)
        dma(gffn, pre_ffn_g[0].rearrange("(k p) -> p k", p=128), [], [B_small], slow=True)
        tqe = af32(NEXT)
        dma(tqe, tq_ext, [], [B_small])
        wval = af32(NEXT)
        dma(wval, wvalid_d.partition_broadcast(128), [], [B_small])
        hflag = af32(1)
        dma(hflag, hflag_d.partition_broadcast(128), [], [B_small])
        invcnt = af32(NEXT * 4)
        dma(invcnt, invcnt_d, [], [B_small])
        ss2 = af32(4)
        B_ss2 = S.buf("ss2")
        persist_top = top[0]

        def rms_scale(xt, xn, Bx, Bxn, ss, junk, Bss):
            act(junk, xt, AF.Square, [Bx], [Bss], accum=ss)
            ts("dve", ss, ss, 1.0 / 1024, 1e-6, ALU.mult, ALU.add, [Bss], [Bss])
            act(ss, ss, AF.Sqrt, [Bss], [Bss])
            S.op("dve", lambda e: e.reciprocal(out=ss, in_=ss), [Bss], [Bss])
            ts("dve", xn, xt, ss, None, ALU.mult, None, [Bx, Bss], [Bxn])

        def sincos(ang, n, osin, ocos, tmp, Bt, Bo):
            t, kf, g, ki = tmp
            ts("dve", ang, ang, 1.0 / TWO_PI, None, ALU.mult, None, [Bt], [Bt])
            for dst, off in ((osin, 0.0), (ocos, 0.25)):
                ts("dve", t, ang, off, None, ALU.add, None, [Bt], [Bt])
                cp("dve", ki, t, [Bt], [Bt])
                cp("dve", kf, ki, [Bt], [Bt])
                tt("dve", t, t, kf, ALU.subtract, [Bt], [Bt])
                ts("dve", g, t, 0.5, None, ALU.is_gt, None, [Bt], [Bt])
                tt("dve", t, t, g, ALU.subtract, [Bt], [Bt])
                ts("dve", g, t, -0.5, None, ALU.is_lt, None, [Bt], [Bt])
                tt("dve", t, t, g, ALU.add, [Bt], [Bt])
                act(dst, t, AF.Sin, [Bt], [Bo], scale=TWO_PI)

        def rotary(src4, dst4, cs, sn, tmp, rB, wB, Bt):
            a, b = src4.shape[1], src4.shape[2]
            n = a * b * 8
            x1 = src4[:, :, :, 0:8]
            x2 = src4[:, :, :, 8:16]
            csb = cs.unsqueeze(1).unsqueeze(1).to_broadcast([128, a, b, 8])
            snb = sn.unsqueeze(1).unsqueeze(1).to_broadcast([128, a, b, 8])
            t1 = tmp[:, 0:n].rearrange("p (a b d) -> p a b d", a=a, b=b)
            t2 = tmp[:, n:2 * n].rearrange("p (a b d) -> p a b d", a=a, b=b)
            tt("dve", t1, x1, csb, ALU.mult, rB, [Bt])
            tt("dve", t2, x2, snb, ALU.mult, rB, [Bt])
            tt("dve", dst4[:, :, :, 0:8], t1, t2, ALU.subtract, [Bt], wB)
            tt("dve", t1, x2, csb, ALU.mult, rB, [Bt])
            tt("dve", t2, x1, snb, ALU.mult, rB, [Bt])
            tt("dve", dst4[:, :, :, 8:16], t1, t2, ALU.add, [Bt], wB)

        KTs = abf(8192)
        Vs = abf(64 * 2 * 65).rearrange("p (t g d) -> p t g d", t=64, g=2)
        KTw = abf(NEXT * 128)
        Vw = abf(NEXT * 2 * 65).rearrange("p (t g d) -> p t g d", t=NEXT, g=2)
        KcT = abf(512)
        Vc = abf(4 * 2 * 65).rearrange("p (t g d) -> p t g d", t=4, g=2)
        Gb = abf(8192)
        B_KTs, B_Vs, B_KTw, B_Vw, B_KcT, B_Vc, B_G = [S.buf(n) for n in "KTs Vs KTw Vw KcT Vc G".split()]
        cosE = af32(NEXT * 8).rearrange("p (t f) -> p t f", f=8)
        sinE = af32(NEXT * 8).rearrange("p (t f) -> p t f", f=8)
        cos8 = af32(NEXT * 8).rearrange("p (t f) -> p t f", f=8)
        sin8 = af32(NEXT * 8).rearrange("p (t f) -> p t f", f=8)
        B_tabE = S.buf("tabE")
        kv_top = top[0]

        memset("pool", Vs, 1.0, [B_Vs])
        memset("pool", Vw, 1.0, [B_Vw])
        memset("pool", Vc, 0.0, [B_Vc])
        memset("pool", Vc[:, :, :, 64:65], 1.0, [B_Vc])
        loadw(lambda c0, n: Gb[:, c0:c0 + n], G_d, 8192, w=[B_G])

        cosA = af32(NALL * 8).rearrange("p (t f) -> p t f", f=8)
        sinA = af32(NALL * 8).rearrange("p (t f) -> p t f", f=8)
        B_tabA = S.buf("tabA")
        KcRaw = abf(8192)
        VcRaw = abf(8192)
        B_KcRaw, B_VcRaw = S.buf("KcRaw"), S.buf("VcRaw")
        wkvA = abf(8 * 512).rearrange("p (k c) -> p k c", k=8)
        B_wkvA = S.buf("wkvA")
        for k in range(8):
            loadw(lambda c0, n, k=k: wkvA[:, k, c0:c0 + n], w_in[0][k * 128:(k + 1) * 128, 1536:2048], 512,
                  scale=gpre[:, k:k + 1], r_extra=[B_small], w=[B_wkvA])
        mark_tab = top[0]
        posi = ai32(512)
        posf = af32(512)
        invf = af32(8)
        ang = af32(512)
        tmp3 = (af32(512), af32(512), af32(512), ai32(512))
        B_t = S.buf("tabtmp")
        dma(invf, invf_d.partition_broadcast(128), [], [B_t])
        dma(posi[:, 0:NALL], pos_all, [], [B_t])
        cp("dve", posf[:, 0:NALL], posi[:, 0:NALL], [B_t], [B_t])
        tt("dve", ang.rearrange("p (t f) -> p t f", f=8), posf[:, 0:NALL].unsqueeze(2).to_broadcast([128, NALL, 8]),
           invf.unsqueeze(1).to_broadcast([128, NALL, 8]), ALU.mult, [B_t], [B_t])
        sincos(ang, 512, sinA.rearrange("p t f -> p (t f)"), cosA.rearrange("p t f -> p (t f)"),
               tmp3, B_t, B_tabA)
        dma(posi[:, 0:NEXT], pos_ext, [B_t], [B_t])
        cp("dve", posf[:, 0:NEXT], posi[:, 0:NEXT], [B_t], [B_t])
        ne = NEXT * 8
        tt("dve", ang[:, 0:ne].rearrange("p (t f) -> p t f", f=8), posf[:, 0:NEXT].unsqueeze(2).to_broadcast([128, NEXT, 8]),
           invf.unsqueeze(1).to_broadcast([128, NEXT, 8]), ALU.mult, [B_t], [B_t])
        sincos(ang[:, 0:ne], ne, sinE.rearrange("p t f -> p (t f)"), cosE.rearrange("p t f -> p (t f)"),
               tuple(a[:, 0:ne] for a in tmp3), B_t, B_tabE)
        ts("dve", cos8.rearrange("p t f -> p (t f)"), cosE.rearrange("p t f -> p (t f)"), 0.125, None, ALU.mult, None, [B_tabE], [B_tabE])
        ts("dve", sin8.rearrange("p t f -> p (t f)"), sinE.rearrange("p t f -> p (t f)"), 0.125, None, ALU.mult, None, [B_tabE], [B_tabE])
        if dbg != "0":
            barrier()
            top[0] = mark_tab

        if dbg == "0":
            f1 = dma(dbg_t("cosA", [128, NALL * 8]), cosA.rearrange("p t f -> p (t f)"), [B_tabA], [])
            f2 = dma(dbg_t("sinA", [128, NALL * 8]), sinA.rearrange("p t f -> p (t f)"), [B_tabA], [])
            f3 = dma(dbg_t("cos8", [128, NEXT * 8]), cos8.rearrange("p t f -> p (t f)"), [B_tabE], [])
            f4 = dma(dbg_t("Gb", [128, 8192], BF16), Gb, [B_G], [])
            print(S.emit(final_waits=[f1, f2, f3, f4]))
            return nc
        xt2 = [af32(1024), af32(1024)]
        B_xt = [S.buf("xt0"), S.buf("xt1")]
        xn = abf(1024)
        B_xn = S.buf("xn")
        junk = abf(1024)
        ssA = af32(1)
        B_ss = S.buf("ss")
        hT = abf(1024).rearrange("p (k t) -> p k t", k=8)
        B_hT = S.buf("hT")
        kb = abf(512)
        B_kb = S.buf("kb")
        rtmp = af32(2 * 16 * 8)
        B_rt = S.buf("rtmp")

        def norm_T(xt, Bx, pa, pb, hTd, BhT, xnb=None, Bxnb=None):
            if xnb is None:
                xnb, Bxnb = xn, B_xn
            rms_scale(xt, xnb, Bx, Bxnb, ssA, junk, B_ss)
            for k in range(8):
                bank = pa if k < 4 else pb
                tr(PS[bank][:, (k % 4) * 128:(k % 4 + 1) * 128], xnb[:, k * 128:(k + 1) * 128], [Bxnb], [PB[bank]])
            cp("act", hTd[:, 0:4, :], PS[pa][:, :].rearrange("p (k t) -> p k t", k=4), [PB[pa]], [BhT])
            cp("dve", hTd[:, 4:8, :], PS[pb][:, :].rearrange("p (k t) -> p k t", k=4), [PB[pb]], [BhT])

        xnA = [xn, abf(1024)]
        B_xnA = [B_xn, S.buf("xnA1")]
        hTA = [hT, abf(1024).rearrange("p (k t) -> p k t", k=8)]
        B_hTA = [B_hT, S.buf("hTA1")]
        dma(xt2[0], x_all[0:128, :], [], [B_xt[0]])
        dma(xt2[1], x_all[128:256, :], [], [B_xt[1]])
        norm_T(xt2[0], B_xt[0], 0, 1, hTA[0], B_hTA[0], xnA[0], B_xnA[0])
        for T in range(na):
            s = T % 2
            if T + 1 < na:
                norm_T(xt2[1 - s], B_xt[1 - s], 0, 1, hTA[1 - s], B_hTA[1 - s], xnA[1 - s], B_xnA[1 - s])
            if T + 2 < na:
                dma(xt2[s], x_all[(T + 2) * 128:(T + 3) * 128, :], [], [B_xt[s]])
            hT, B_hT = hTA[s], B_hTA[s]
            for k in range(8):
                mm(PS[2][:, 0:512], hT[:, k, :], wkvA[:, k, :], [B_hT, B_wkvA], [PB[2]], start=(k == 0), stop=(k == 7))
            cp("act", kb, PS[2][:, 0:512], [PB[2]], [B_kb])
            v5 = PS[2][:, 0:512].rearrange("p (j2 jj g d) -> p j2 jj g d", j2=2, jj=2, g=2)
            k5 = kb.rearrange("p (j2 jj g d) -> p j2 jj g d", j2=2, jj=2, g=2)
            rotary(v5[:, :, 0, :, :], k5[:, :, 0, :, :], cosA[:, T, :], sinA[:, T, :], rtmp, [PB[2], B_tabA], [B_kb], B_rt)
            cp("pool", Vs[:, T, :, 0:64], kb[:, 384:512].rearrange("p (g d) -> p g d", g=2), [B_kb], [B_Vs])
            tr(PS[3][:, 0:128], kb[:, 0:128], [B_kb], [PB[3]])
            tr(PS[3][:, 128:256], kb[:, 128:256], [B_kb], [PB[3]])
            tr(PS[3][:, 256:384], kb[:, 256:384], [B_kb], [PB[3]])
            cp("act", KcRaw[:, T * 128:(T + 1) * 128], PS[3][:, 0:128], [PB[3]], [B_KcRaw])
            cp("dve", VcRaw[:, T * 128:(T + 1) * 128], PS[3][:, 128:256], [PB[3]], [B_VcRaw])
            cp("act", KTs[:, T * 128:(T + 1) * 128], PS[3][:, 256:384], [PB[3]], [B_KTs])

        w1z = [abf(32 * 256).rearrange("p (l m) -> p l m", l=32) for _ in range(2)]
        B_w1 = S.buf("w1b")
        memset("pool", w1z[0][64:128, :, :], 0.0, [B_w1])
        memset("pool", w1z[1][0:64, :, :], 0.0, [B_w1])
        w2b = abf(2 * 128).rearrange("p (h d) -> p h d", h=2)
        B_w2 = S.buf("w2b")
        peT = abf(32)
        pef = af32(32)
        B_pe = S.buf("pe")
        hid2 = [abf(2 * 512).rearrange("p (h c) -> p h c", h=2) for _ in range(2)]
        B_hid2 = [S.buf("hid0"), S.buf("hid1")]
        cbias = af32(2)
        B_cb = S.buf("cbias")
        for kvi, raw, Braw in (() if skipcmp else ((0, KcRaw, B_KcRaw), (1, VcRaw, B_VcRaw))):
            w1v = cmp_w1[0][kvi].rearrange("(l d) m -> d l m", d=64)
            for half in range(2):
                for l0 in range(0, 32, 4):
                    k = stg_i[0] % 2
                    stg_i[0] += 1
                    sl = stage[k][64 * half:64 * half + 64, 0:1024]
                    dma(sl.rearrange("p (l m) -> p l m", l=4), w1v[:, l0:l0 + 4, :], [], [B_stage[k]])
                    cp(("pool", "dve")[(l0 // 4) % 2], w1z[half][64 * half:64 * half + 64, l0:l0 + 4, :],
                       sl.rearrange("p (l m) -> p l m", l=4), [B_stage[k]], [B_w1])
            k = stg_i[0] % 2
            stg_i[0] += 1
            dma(stage[k][:, 0:128].rearrange("p (h d) -> p h d", h=2), cmp_w2[0][kvi].rearrange("(h p) d -> p h d", p=128),
                [], [B_stage[k]])
            cp("dve", w2b[:, :, 0:64], stage[k][:, 0:128].rearrange("p (h d) -> p h d", h=2), [B_stage[k]], [B_w2])
            cp("dve", w2b[:, :, 64:128], stage[k][:, 0:128].rearrange("p (h d) -> p h d", h=2), [B_stage[k]], [B_w2])
            dma(pef[0:64, :], cmp_pe[0][kvi].rearrange("l d -> d l"), [], [B_pe], slow=True)
            memset("dve", peT[64:128, :], 0.0, [B_pe])
            cp("dve", peT[0:64, :], pef[0:64, :], [B_pe], [B_pe])
            for half in range(2):
                for l in range(32):
                    mm(PS[4][:, half:half + 1], w1z[0][:, l, half * 128:(half + 1) * 128], peT[:, l:l + 1],
                       [B_w1, B_pe], [PB[4]], start=(l == 0 and half == 0), stop=(l == 31))
            cp("dve", cbias, PS[4][:, 0:2], [PB[4]], [B_cb])
            if not NOBAR:
                barrier()
            if dbg == "A" and kvi == 0 and (DUMPX & 1):
                dma(dbg_t("w1b", [128, 8192], BF16), w1z[0].rearrange("p l m -> p (l m)"), [B_w1], [])
                dma(dbg_t("cbias", [128, 2]), cbias, [B_cb], [])
            rawv = raw.rearrange("p (i s) -> p i s", s=16)
            for g in range(2):
                if not NOBAR:
                    barrier()
                hid, B_hid = hid2[g], B_hid2[g]
                pr = slice(64 * g, 64 * g + 64)
                for half in range(2):
                    for l in range(32):
                        rhs = rawv[:, 0:511, l] if l < 16 else rawv[:, 1:512, l - 16]
                        mm(PS[half][:, 0:511], w1z[g][:, l, half * 128:(half + 1) * 128], rhs, [B_w1, Braw], [PB[half]],
                           start=(l == 0), stop=(l == 31))
                    act(hid[:, half, 0:511], PS[half][:, 0:511], AF.Gelu_apprx_tanh, [PB[half], B_cb], [B_hid],
                        bias=cbias[:, half:half + 1])
                if dbg == "A" and kvi == 0 and (DUMPX & 2):
                    dma(dbg_t(f"hid{g}", [128, 1024], BF16), hid.rearrange("p h c -> p (h c)"), [B_hid], [])
                if kvi == 0:
                    for half in range(2):
                        mm(PS[2][:, 0:511], w2b[:, half, :], hid[:, half, 0:511], [B_w2, B_hid], [PB[2]],
                           start=(half == 0), stop=(half == 1))
                    cp("act", KcT[pr, 0:511], PS[2][pr, 0:511], [PB[2]], [B_KcT])
                else:
                    for c in range(4):
                        m = 128 if c < 3 else 127
                        for half in range(2):
                            mm(PS[2][0:m, c * 64:(c + 1) * 64], hid[:, half, c * 128:c * 128 + m], w2b[:, half, 0:64],
                               [B_w2, B_hid], [PB[2]], start=(half == 0 and c == 0), stop=(half == 1))
                    for c in range(4):
                        m = 128 if c < 3 else 127
                        cp("act", Vc[0:m, c, g, 0:64], PS[2][0:m, c * 64:(c + 1) * 64], [PB[2]], [B_Vc])
        memset("dve", KcT[:, 511:512], 0.0, [B_KcT])
        if dbg == "A":
            fl = [dma(dbg_t("KTs", [128, 8192], BF16), KTs, [B_KTs], []),
                  dma(dbg_t("Vs", [128, 64 * 130], BF16), Vs.rearrange("p t g d -> p (t g d)"), [B_Vs], []),
                  dma(dbg_t("KcT", [128, 512], BF16), KcT, [B_KcT], []),
                  dma(dbg_t("Vc", [128, 4 * 130], BF16), Vc.rearrange("p t g d -> p (t g d)"), [B_Vc], []),
                  dma(dbg_t("KcRaw", [128, 8192], BF16), KcRaw, [B_KcRaw], [])]
            print(S.emit(final_waits=fl))
            return nc
        barrier()
        top[0] = kv_top
        wb1 = abf(8 * 2352).rearrange("p (k c) -> p k c", k=8)
        B_wb1 = S.buf("wb1")
        for k in range(8):
            rows = w_in[0][k * 128:(k + 1) * 128, :]
            sc = gpre[:, k:k + 1]
            loadw(lambda c0, n, k=k: wb1[:, k, c0:c0 + n], rows[:, 0:512], 512, scale=sc, r_extra=[B_small], w=[B_wb1])
            loadw(lambda c0, n, k=k: wb1[:, k, 512:1536], rows[:, 512:1536], 1024, scale=sc, r_extra=[B_small], w=[B_wb1], perm_q=True)
            loadw(lambda c0, n, k=k: wb1[:, k, 1536 + c0:1536 + c0 + n], rows[:, 2048:2864], 816, scale=sc, r_extra=[B_small], w=[B_wb1])
        poolw = abf(4 * 128).rearrange("p (g d) -> p g d", g=4)
        B_cst = S.buf("cst")
        k_ = stg_i[0] % 2
        stg_i[0] += 1
        dma(stage[k_][:, 0:512].rearrange("p (g d) -> p g d", g=4), pool_w[0].rearrange("g c d -> c g d"), [], [B_stage[k_]])
        cp("dve", poolw, stage[k_][:, 0:512].rearrange("p (g d) -> p g d", g=4), [B_stage[k_]], [B_cst])
        pscale = af32(4)
        dma(pscale, pool_scale[0].rearrange("(g d) -> d g", d=128), [], [B_cst], slow=True)
        Ab = abf(1024).rearrange("p (g c t) -> p g c t", g=4, c=2)
        k_ = stg_i[0] % 2
        stg_i[0] += 1
        dma(stage[k_][:, 0:1024], Ab_d, [], [B_stage[k_]])
        cp("dve", Ab.rearrange("p g c t -> p (g c t)"), stage[k_][:, 0:1024], [B_stage[k_]], [B_cst])
        Mw3 = abf(384).rearrange("p (j q) -> p j q", j=3)
        k_ = stg_i[0] % 2
        stg_i[0] += 1
        dma(stage[k_][:, 0:256], Mw_d, [], [B_stage[k_]])
        cp("dve", Mw3[:, 0, :], stage[k_][:, 0:128], [B_stage[k_]], [B_cst])
        cp("dve", Mw3[:, 2, :], stage[k_][:, 128:256], [B_stage[k_]], [B_cst])
        memset("dve", Mw3[:, 1, :], 1.0, [B_cst])
        onesb = abf(1)
        memset("dve", onesb, 1.0, [B_cst])
        qrow = af32(128)
        dma(qrow, qrow_d.partition_broadcast(128), [], [B_cst])
        tqst = af32(NEXT)
        dma(tqst, tqst_d.partition_broadcast(128), [], [B_cst])
        tqt = af32(128)
        B_tqt = S.buf("tqt")
        bsrow = af32(128)
        dma(bsrow, bsrow_d.partition_broadcast(128), [], [B_cst])
        cendrow = af32(512)
        dma(cendrow, cendrow_d.partition_broadcast(128), [], [B_cst])
        kidx = af32(64)
        dma(kidx, kidx_d, [], [B_cst])
        cendcol = af32(4)
        dma(cendcol, cendcol_d, [], [B_cst])
        e0big = af32(128)
        dma(e0big, e0_d.partition_broadcast(128), [], [B_cst])

        xt2 = [af32(1024), af32(1024)]
        B_xt = [S.buf("bxt0"), S.buf("bxt1")]
        xn = abf(1024)
        B_xn = S.buf("bxn")
        junk = abf(1024)
        ssA = af32(1)
        B_ss = S.buf("bss")
        brT0 = abf(24 * 128).rearrange("p (k t) -> p k t", k=24)
        brT = [brT0, brT0]
        B_br0 = S.buf("br0")
        B_br = [B_br0, B_br0]
        rtmp = af32(256)
        B_rt = S.buf("brtmp")

        KmT = abf(4 * 256).rearrange("p (h m) -> p h m", h=4)
        Vm = abf(2 * 4 * 128).rearrange("p (c h d) -> p c h d", c=2, h=4)
        kmb = abf(512)
        mark_m = top[0]
        wmem = abf(8 * 1024).rearrange("p (k c) -> p k c", k=8)
        B_wmem = S.buf("wmem")
        for k in range(8):
            loadw(lambda c0, n, k=k: wmem[:, k, c0:c0 + n], w_mem_kv[0][k * 128:(k + 1) * 128, :], 1024,
                  scale=gmem[:, k:k + 1], r_extra=[B_small], w=[B_wmem])
        B_km, B_vm, B_kmb = S.buf("KmT"), S.buf("Vm"), S.buf("kmb")
        for c in range(2):
            dma(xt2[c], mem_d[c * 128:(c + 1) * 128, :], [], [B_xt[c]])
            norm_T(xt2[c], B_xt[c], 0, 1, brT[c][:, 16:24, :], B_br[c])
            for nb in range(2):
                for k in range(8):
                    mm(PS[2 + nb][:, :], brT[c][:, 16 + k, :], wmem[:, k, nb * 512:(nb + 1) * 512], [B_br[c], B_wmem], [PB[2 + nb]],
                       start=(k == 0), stop=(k == 7))
            cp("act", kmb, PS[2][:, :], [PB[2]], [B_kmb], scale=float(128 ** -0.5))
            cp("dve", Vm[:, c, :, :], PS[3][:, :].rearrange("p (h d) -> p h d", h=4), [PB[3]], [B_vm])
            for h in range(4):
                tr(PS[4][:, h * 128:(h + 1) * 128], kmb[:, h * 128:(h + 1) * 128], [B_kmb], [PB[4]])
            cp("act", KmT[:, :, c * 128:(c + 1) * 128], PS[4][:, :].rearrange("p (h m) -> p h m", h=4), [PB[4]], [B_km])

        barrier()
        top[0] = mark_m
        kwb = abf(256)
        B_kwb = S.buf("kwb")
        ub = [abf(512), abf(512)]
        B_ub = [S.buf("ub0"), S.buf("ub1")]
        uf = af32(512)
        B_uf = S.buf("uf")
        pbb = abf(512)
        B_pbb = S.buf("pbb")
        pT = abf(512).rearrange("p (g t) -> p g t", g=4)
        B_pT = S.buf("pT")
        qb = abf(1024)
        B_qb = S.buf("qb")
        qTz = [abf(1024).rearrange("p (k t) -> p k t", k=8) for _ in range(2)]
        B_qT = S.buf("qT")
        memset("pool", qTz[0], 0.0, [B_qT])
        memset("pool", qTz[1], 0.0, [B_qT])
        gn = af32(48)
        B_gn = S.buf("gn")
        qxb = abf(512)
        B_qxb = S.buf("qxb")
        qxT = abf(512).rearrange("p (h t) -> p h t", h=4)
        B_qxT = S.buf("qxT")
        mpT = [abf(512).rearrange("p (h t) -> p h t", h=4) for _ in range(2)]
        B_mpT = [S.buf("mpT0"), S.buf("mpT1")]
        rsm = af32(4)
        B_rsm = S.buf("rsm")
        ymemb = abf(512)
        B_ymem = S.buf("ymem")
        ef = [af32(512), af32(512)]
        B_ef = [S.buf("ef0"), S.buf("ef1")]
        ssum2 = [af32(2), af32(2)]
        B_ssum2 = [S.buf("ssum0"), S.buf("ssum1")]
        Pb = [af32(516), af32(516)]
        B_Pb = [S.buf("Pb0"), S.buf("Pb1")]
        cmneg = abf(512)
        B_cm = S.buf("cmneg")
        imp = af32(128)
        nd = af32(128)
        itmp = af32(128)
        wk = af32(128)
        mx = af32(16)
        B_imp = S.buf("imp")
        selb = abf(128)
        B_sel = S.buf("sel")
        selT = [abf(128), abf(128)]
        B_selT = [S.buf("selT0"), S.buf("selT1")]
        NSL = 3
        NSU = 6
        ucount = [0]
        mk = [abf(128) for _ in range(NSL)]
        B_mk = [S.buf(f"mk{i}") for i in range(NSL)]
        pTu = [abf(512).rearrange("p (h t) -> p h t", h=4) for _ in range(NSU)]
        B_pTu = [S.buf(f"pTu{i}") for i in range(NSU)]
        ynsa = af32(1024)
        B_yn = S.buf("ynsa")
        ytmp = af32(256)
        B_yt = S.buf("ytmp")
        rs = af32(8)
        B_rs = S.buf("rs")
        ynb = abf(1024)
        B_ynb = S.buf("ynb")
        memset("dve", Pb[0], 0.0, [B_Pb[0]])
        memset("dve", Pb[1], 0.0, [B_Pb[1]])
        nchunk = [0]

        def attend(e, g, br, chunks):
            LOOK = 3
            units = [(n, q) for n in range(len(chunks)) for q in range(2)]
            nu = len(units)
            info = {}

            def stage_scores(u):
                n, q = units[u]
                KT, V, mk_pe, mk_dve = chunks[n]
                if q == 0:
                    cs = nchunk[0]
                    nchunk[0] += 1
                    info[n] = (cs % 2, cs % NSL)
                    if mk_pe is not None:
                        mk_pe(cs % 2)
                bank = 2 + (ucount[0] % 4)
                ub = ucount[0] % NSU
                ucount[0] += 1
                mm(PS[bank][:, :], KT, qTz[g][:, 4 * q:4 * q + 4, :], [B_qT, B_KTs, B_KTw, B_KcT], [PB[bank]])
                return (bank, ub)

            pend = [stage_scores(u) for u in range(min(LOOK, nu))]
            for u, (n, q) in enumerate(units):
                bank, ub = pend.pop(0)
                if u + LOOK < nu:
                    pend.append(stage_scores(u + LOOK))
                KT, V, mk_pe, mk_dve = chunks[n]
                sl, bs = info[n]
                pt = pTu[ub]
                act(pt, PS[bank][:, :].rearrange("p (h t) -> p h t", h=4), AF.Exp, [PB[bank]], [B_pTu[ub]])
                if q == 0:
                    mk_dve(sl, bs)
                mb = mk[bs].unsqueeze(1).to_broadcast([128, 4, 128])
                tt(("dve", "pool")[q], pt, pt, mb, ALU.mult, [B_pTu[ub], B_mk[bs]], [B_pTu[ub]])
                for hh in range(4 * q, 4 * q + 4):
                    mm(PS[q][:, (hh % 4) * 128:(hh % 4) * 128 + 65], pt[:, hh % 4, :], V, [B_pTu[ub], B_Vs, B_Vw, B_Vc], [PB[q]],
                       start=(n == 0 and hh % 4 == 0), stop=(n == len(chunks) - 1))
            gn3 = gn.rearrange("p (h b) -> p h b", b=3)
            for b in range(2):
                Ov = PS[b][:, :].rearrange("p (h d) -> p h d", h=4)
                h0 = 8 * g + 4 * b
                rsb = rs[:, 4 * b:4 * b + 4]
                ts("dve", rsb, Ov[:, :, 64], 1e-30, None, ALU.max, None, [PB[b]], [B_rs])
                S.op("dve", lambda e_, rsb=rsb: e_.reciprocal(out=rsb, in_=rsb), [B_rs], [B_rs])
                tt("dve", rsb, rsb, gn3[:, h0:h0 + 4, br], ALU.mult, [B_rs, B_gn], [B_rs])
                dst = ynsa.rearrange("p (h d) -> p h d", h=16)[:, h0:h0 + 4, :]
                rb = rsb.unsqueeze(2).to_broadcast([128, 4, 64])
                if br == 0:
                    tt("dve", dst, Ov[:, :, 0:64], rb, ALU.mult, [PB[b], B_rs], [B_yn])
                else:
                    yt = ytmp.rearrange("p (h d) -> p h d", h=4)
                    tt("dve", yt, Ov[:, :, 0:64], rb, ALU.mult, [PB[b], B_rs], [B_yt])
                    tt("pool", dst, dst, yt, ALU.add, [B_yn, B_yt], [B_yn])

        dma(xt2[0], x_ext[0:128, :], [], [B_xt[0]])
        for e in range(ntile_b1):
            s = e % 2
            if e + 1 < NEXT:
                dma(xt2[1 - s], x_ext[(e + 1) * 128:(e + 2) * 128, :], [], [B_xt[1 - s]])
            bt = brT[s]
            hTd = bt[:, 16:24, :]
            norm_T(xt2[s], B_xt[s], 0, 1, hTd, B_br[s])
            for k in range(8):
                mm(PS[2][:, 0:256], hTd[:, k, :], wb1[:, k, 1536:1792], [B_br[s], B_wb1], [PB[2]], start=(k == 0), stop=(k == 7))
            cp("act", kwb, PS[2][:, 0:256], [PB[2]], [B_kwb])
            rotary(PS[2][:, 0:128].rearrange("p (a g d) -> p a g d", a=1, g=2), kwb[:, 0:128].rearrange("p (a g d) -> p a g d", a=1, g=2),
                   cosE[:, e, :], sinE[:, e, :], rtmp, [PB[2], B_tabE], [B_kwb], B_rt)
            cp("pool", Vw[:, e, :, 0:64], kwb[:, 128:256].rearrange("p (g d) -> p g d", g=2), [B_kwb], [B_Vw])
            tr(PS[3][:, 0:128], kwb[:, 0:128], [B_kwb], [PB[3]])
            cp("act", KTw[:, e * 128:(e + 1) * 128], PS[3][:, 0:128], [PB[3]], [B_KTw])
            for k in range(8):
                mm(PS[4][:, :], hTd[:, k, :], wb1[:, k, 0:512], [B_br[s], B_wb1], [PB[4]], start=(k == 0), stop=(k == 7))
            cp("act", ub[s], PS[4][:, :], [PB[4]], [B_ub[s]])
            if e < 4:
                continue
            cp("dve", uf, PS[4][:, :], [PB[4]], [B_uf])
            i = e - 4
            for gi in range(4):
                blk = slice(gi * 128, (gi + 1) * 128)
                mm(PS[5][:, blk], Ab[:, gi, 0, :], ub[s][:, blk], [B_cst, B_ub[s]], [PB[5]], start=True, stop=False)
                mm(PS[5][:, blk], Ab[:, gi, 1, :], ub[1 - s][:, blk], [B_cst, B_ub[1 - s]], [PB[5]], start=False, stop=True)
            for gi in range(4):
                blk = slice(gi * 128, (gi + 1) * 128)
                stt(pbb[:, blk], PS[5][:, blk], invcnt[:, e * 4 + gi:e * 4 + gi + 1], uf[:, blk], ALU.mult, ALU.subtract,
                    [PB[5], B_uf, B_small], [B_pbb])
            for gi in range(4):
                blk = slice(gi * 128, (gi + 1) * 128)
                tr(PS[6][:, blk], pbb[:, blk], [B_pbb], [PB[6]])
            cp("act", pT, PS[6][:, :].rearrange("p (g t) -> p g t", g=4), [PB[6]], [B_pT])
            for gi in range(4):
                blk = slice(gi * 128, (gi + 1) * 128)
                mm(PS[5][:, blk], poolw[:, gi, :], pT[:, gi, :], [B_cst, B_pT], [PB[5]])
            for gi in range(4):
                blk = slice(gi * 128, (gi + 1) * 128)
                ts("dve", bt[:, gi, :], PS[5][:, blk], pscale[:, gi:gi + 1], None, ALU.mult, None, [PB[5], B_cst], [B_br[s]])
            for nb in range(2):
                for k in range(8):
                    mm(PS[nb][:, :], hTd[:, k, :], wb1[:, k, 512 + nb * 512:1024 + nb * 512], [B_br[s], B_wb1], [PB[nb]],
                       start=(k == 0), stop=(k == 7))
            for nb in range(2):
                qv = qb[:, nb * 512:(nb + 1) * 512]
                cp("act", qv, PS[nb][:, :], [PB[nb]], [B_qb], scale=0.125)
                rotary(PS[nb][:, :].rearrange("p (c g d) -> p c g d", c=4, g=2), qv.rearrange("p (c g d) -> p c g d", c=4, g=2),
                       cos8[:, e, :], sin8[:, e, :], rtmp, [PB[nb], B_tabE], [B_qb], B_rt)
            for k in range(8):
                tr(PS[2 + k // 4][:, (k % 4) * 128:(k % 4 + 1) * 128], qb[:, k * 128:(k + 1) * 128], [B_qb], [PB[2 + k // 4]])
            for (bk, c0) in ((2, 0), (3, 4)):
                cp("act", qTz[0][0:64, c0:c0 + 4, :], PS[bk][0:64, :].rearrange("p (k t) -> p k t", k=4), [PB[bk]], [B_qT])
                cp("dve", qTz[1][64:128, c0:c0 + 4, :], PS[bk][64:128, :].rearrange("p (k t) -> p k t", k=4), [PB[bk]], [B_qT])
            for k in range(8):
                mm(PS[4][:, 0:48], hTd[:, k, :], wb1[:, k, 1792:1840], [B_br[s], B_wb1], [PB[4]], start=(k == 0), stop=(k == 7))
            act(gn, PS[4][:, 0:48], AF.Sigmoid, [PB[4]], [B_gn])
            for k in range(8):
                mm(PS[5][:, :], hTd[:, k, :], wb1[:, k, 1840:2352], [B_br[s], B_wb1], [PB[5]], start=(k == 0), stop=(k == 7))
            cp("act", qxb, PS[5][:, :], [PB[5]], [B_qxb])
            for h in range(4):
                tr(PS[6][:, h * 128:(h + 1) * 128], qxb[:, h * 128:(h + 1) * 128], [B_qxb], [PB[6]])
            cp("dve", qxT, PS[6][:, :].rearrange("p (h t) -> p h t", h=4), [PB[6]], [B_qxT])
            for c in range(2):
                for h in range(4):
                    mm(PS[7][:, h * 128:(h + 1) * 128], KmT[:, h, c * 128:(c + 1) * 128], qxT[:, h, :], [B_km, B_qxT], [PB[7]])
                act(mpT[c], PS[7][:, :].rearrange("p (h t) -> p h t", h=4), AF.Exp, [PB[7]], [B_mpT[c]])
            for h in range(4):
                for c in range(2):
                    mm(PS[5][:, h * 128:(h + 1) * 128], mpT[c][:, h, :], Vm[:, c, h, :], [B_mpT[c], B_vm], [PB[5]],
                       start=(c == 0), stop=(c == 1))
            for h in range(4):
                for c in range(2):
                    mm(PS[6][:, h:h + 1], mpT[c][:, h, :], onesb[:, 0:1], [B_mpT[c], B_cst], [PB[6]],
                       start=(c == 0), stop=(c == 1))
            S.op("dve", lambda e_: e_.reciprocal(out=rsm, in_=PS[6][:, 0:4]), [PB[6]], [B_rsm])
            tt("dve", ymemb.rearrange("p (h d) -> p h d", h=4), PS[5][:, :].rearrange("p (h d) -> p h d", h=4),
               rsm.unsqueeze(2).to_broadcast([128, 4, 128]), ALU.mult, [PB[5], B_rsm], [B_ymem])
            for h in range(4):
                tr(PS[7][:, h * 128:(h + 1) * 128], ymemb[:, h * 128:(h + 1) * 128], [B_ymem], [PB[7]])
            cp("act", bt[:, 12:16, :], PS[7][:, :].rearrange("p (h t) -> p h t", h=4), [PB[7]], [B_br[s]])
            tqs = tqe[:, e:e + 1]
            ts("dve", cmneg, cendrow, tqs, -29952.0, ALU.is_gt, ALU.mult, [B_cst, B_small], [B_cm])
            ts("dve", tqt, qrow, tqst[:, e:e + 1], None, ALU.add, None, [B_cst], [B_tqt])
            for g in range(2):
                pr = slice(64 * g, 64 * g + 64)
                Pv = Pb[g][:, 1:513]
                for c_ in range(8):
                    a = c_ % 2
                    ss_, Bs_ = ssum2[a], B_ssum2[a]
                    mm(PS[4 + a][:, :], qTz[g][:, c_, :], KcT[:, 0:512], [B_qT, B_KcT], [PB[4 + a]], start=True, stop=False)
                    mm(PS[4 + a][:, :], ident_b, cmneg, [B_ident, B_cm], [PB[4 + a]], start=False, stop=True)
                    act(ef[a], PS[4 + a][:, :], AF.Exp, [PB[4 + a]], [B_ef[a], Bs_], accum=ss_[:, 0:1])
                    ts("dve", ss_[:, 1:2], ss_[:, 0:1], 1e-30, None, ALU.max, None, [Bs_], [Bs_])
                    S.op("dve", lambda e_, ss_=ss_: e_.reciprocal(out=ss_[:, 1:2], in_=ss_[:, 1:2]), [Bs_], [Bs_])
                    if c_ == 0:
                        ts("dve", Pv, ef[a], ss_[:, 1:2], None, ALU.mult, None, [B_ef[a], Bs_], [B_Pb[g]])
                    else:
                        stt(Pv, ef[a], ss_[:, 1:2], Pv, ALU.mult, ALU.add, [B_ef[a], Bs_, B_Pb[g]], [B_Pb[g]])
                S.op("dve", lambda e_, g=g: e_.tensor_reduce(out=imp, in_=Pb[g][:, 0:512].rearrange("p (j s) -> p j s", s=4),
                                                          axis=AX.X, op=ALU.add), [B_Pb[g]], [B_imp])
                tt("dve", imp, imp, Pb[g][:, 4:516].rearrange("p (j s) -> p j s", s=4)[:, :, 0], ALU.add, [B_Pb[g], B_imp], [B_imp])
                ts("dve", nd, bsrow, tqs, None, ALU.subtract, None, [B_cst, B_small, B_imp], [B_imp])
                ts("dve", itmp, nd, -128.0, BIG, ALU.is_gt, ALU.mult, [B_imp], [B_imp])
                tt("dve", imp, imp, itmp, ALU.add, [B_imp], [B_imp])
                ts("dve", itmp, nd, 0.0, -3.0 * BIG, ALU.is_gt, ALU.mult, [B_imp], [B_imp])
                tt("dve", imp, imp, itmp, ALU.add, [B_imp], [B_imp])
                tt("dve", imp, imp, e0big, ALU.add, [B_imp, B_cst], [B_imp])
                S.op("dve", lambda e_: e_.max(out=mx[:, 0:8], in_=imp), [B_imp], [B_imp])
                S.op("dve", lambda e_: e_.match_replace(out=wk, in_to_replace=mx[:, 0:8], in_values=imp, imm_value=-1e30), [B_imp], [B_imp])
                S.op("dve", lambda e_: e_.max(out=mx[:, 8:16], in_=wk), [B_imp], [B_imp])
                ts("dve", selb, imp, mx[:, 15:16], None, ALU.is_ge, None, [B_imp], [B_sel])
                tr(PS[6][:, g * 128:(g + 1) * 128], selb, [B_sel], [PB[6]])
                cp("act", selT[g], PS[6][:, g * 128:(g + 1) * 128], [PB[6]], [B_selT[g]])
            for g in range(2):
                pr = slice(64 * g, 64 * g + 64)

                def mk_cmp(c):
                    return (None, lambda sl, bs: ts("dve", mk[bs], tqt, cendcol[:, c:c + 1], None, ALU.is_ge, None, [B_cst, B_tqt], [B_mk[bs]]))

                def mk_win(j, ee):
                    v = 0 if j == 0 else (2 if j == 4 else 1)
                    return (None, lambda sl, bs: ts("dve", mk[bs], Mw3[:, v, :], wval[:, ee:ee + 1], None, ALU.mult, None, [B_cst, B_small], [B_mk[bs]]))

                def mk_sel(cc, g=g):
                    def f_pe(sl):
                        mm(PS[6 + sl][:, 256:384], Gb[:, cc * 128:(cc + 1) * 128], selT[g], [B_G, B_selT[g]], [PB[6 + sl]])

                    def f_dve(sl, bs):
                        stt(mk[bs], tqt, kidx[:, cc:cc + 1], PS[6 + sl][:, 256:384], ALU.is_ge, ALU.mult,
                            [B_cst, B_tqt, PB[6 + sl]], [B_mk[bs]])
                    return (f_pe, f_dve)

                attend(e, g, 0, [(KcT[:, c * 128:(c + 1) * 128], Vc[:, c, g, :]) + mk_cmp(c) for c in range(4)])
                attend(e, g, 1, [(KTs[:, cc * 128:(cc + 1) * 128], Vs[:, cc, g, :]) + mk_sel(cc) for cc in range(48 + i)])
                attend(e, g, 2, [(KTw[:, (e - 4 + j) * 128:(e - 3 + j) * 128], Vw[:, e - 4 + j, g, :]) + mk_win(j, e - 4 + j)
                                 for j in range(5)])
            cp("act", ynb, ynsa, [B_yn], [B_ynb])
            for k in range(8):
                tr(PS[2 + k // 4][:, (k % 4) * 128:(k % 4 + 1) * 128], ynb[:, k * 128:(k + 1) * 128], [B_ynb], [PB[2 + k // 4]])
            cp("act", bt[:, 4:8, :], PS[2][:, :].rearrange("p (k t) -> p k t", k=4), [PB[2]], [B_br[s]])
            cp("dve", bt[:, 8:12, :], PS[3][:, :].rearrange("p (k t) -> p k t", k=4), [PB[3]], [B_br[s]])
            lastbr = dma(br_scr[i], bt.rearrange("p k t -> p (k t)"), [B_br[s]], [])
            if dbg == "B1":
                lastbr = dma(dbg_t(f"br{i}", [128, 24 * 128], BF16), bt.rearrange("p k t -> p (k t)"), [B_br[s]], [])
                fl1 = [lastbr, dma(dbg_t(f"ynsa{i}", [128, 1024]), ynsa, [B_yn], []),
                       dma(dbg_t(f"qb{i}", [128, 1024], BF16), qb, [B_qb], []),
                       dma(dbg_t(f"gn{i}", [128, 48]), gn, [B_gn], []),
                       dma(dbg_t(f"selT{i}", [128, 128], BF16), selT[1], [B_selT[1]], []),
                       dma(dbg_t(f"Pb{i}", [128, 516]), Pb[1], [B_Pb[1]], [])]
        if dbg == "B1":
            print(S.emit(final_waits=fl1))
            return nc
        barrier()
        top[0] = persist_top
        wg = abf(8 * 3072).rearrange("p (k c) -> p k c", k=8)
        wbp = abf(4 * 1024).rearrange("p (k c) -> p k c", k=4)
        wbn = abf(8 * 1024).rearrange("p (k c) -> p k c", k=8)
        wbx = abf(4 * 1024).rearrange("p (k c) -> p k c", k=4)
        wo = abf(8 * 1024).rearrange("p (k c) -> p k c", k=8)
        B_w2p = S.buf("w_b2")
        for k in range(8):
            loadw(lambda c0, n, k=k: wg[:, k, c0:c0 + n], w_in[0][k * 128:(k + 1) * 128, 2864:5936], 3072,
                  scale=gpre[:, k:k + 1], r_extra=[B_small], w=[B_w2p])
            loadw(lambda c0, n, k=k: wbn[:, k, c0:c0 + n], w_br_nsa[0][k * 128:(k + 1) * 128, :], 1024, w=[B_w2p])
            loadw(lambda c0, n, k=k: wo[:, k, c0:c0 + n], w_out[0][k * 128:(k + 1) * 128, :], 1024, w=[B_w2p])
            if k < 4:
                loadw(lambda c0, n, k=k: wbp[:, k, c0:c0 + n], w_br_pool[0][k * 128:(k + 1) * 128, :], 1024, w=[B_w2p])
                loadw(lambda c0, n, k=k: wbx[:, k, c0:c0 + n], w_br_xa[0][k * 128:(k + 1) * 128, :], 1024, w=[B_w2p])
        gpost = af32(1024)
        B_gp = S.buf("gpost")
        dma(gpost, post_mix_g.partition_broadcast(128), [], [B_gp])
        brT = [abf(24 * 128).rearrange("p (k t) -> p k t", k=24) for _ in range(2)]
        B_br = [S.buf("c_br0"), S.buf("c_br1")]
        xt2 = [af32(1024), af32(1024)]
        B_xt = [S.buf("c_xt0"), S.buf("c_xt1")]
        sg = af32(1024)
        B_sg = S.buf("sg")
        yy = af32(1024)
        B_y = S.buf("yy")
        ytm = af32(1024)
        B_ytm = S.buf("ytm")
        yb = abf(1024)
        B_yb = S.buf("yb")
        yT = abf(1024).rearrange("p (k t) -> p k t", k=8)
        B_yT = S.buf("yT")
        junk = abf(512)
        x1t = [af32(1024), af32(1024)]
        B_x1 = [S.buf("x1t0"), S.buf("x1t1")]

        def post_norm_res(pa, pb, gp, Bg, xres, Bxres, dst, Bdst):
            act(junk[:, 0:512], PS[pa][:, :], AF.Square, [PB[pa]], [B_ss2], accum=ss2[:, 0:1])
            act(junk[:, 0:512], PS[pb][:, :], AF.Square, [PB[pb]], [B_ss2], accum=ss2[:, 1:2])
            tt("dve", ss2[:, 2:3], ss2[:, 0:1], ss2[:, 1:2], ALU.add, [B_ss2], [B_ss2])
            ts("dve", ss2[:, 2:3], ss2[:, 2:3], 1.0 / 1024, 1e-6, ALU.mult, ALU.add, [B_ss2], [B_ss2])
            act(ss2[:, 2:3], ss2[:, 2:3], AF.Sqrt, [B_ss2], [B_ss2])
            S.op("dve", lambda e_: e_.reciprocal(out=ss2[:, 3:4], in_=ss2[:, 2:3]), [B_ss2], [B_ss2])
            for nb, bank in enumerate((pa, pb)):
                blk = slice(nb * 512, (nb + 1) * 512)
                stt(dst[:, blk], PS[bank][:, :], ss2[:, 3:4], gp[:, blk], ALU.mult, ALU.mult, [PB[bank], B_ss2, Bg], [Bdst])
                tt("pool", dst[:, blk], dst[:, blk], xres[:, blk], ALU.add, [Bdst, Bxres], [Bdst])

        dma(brT[0].rearrange("p k t -> p (k t)"), br_scr[0], [], [B_br[0]])
        dma(xt2[0], x_ext[4 * 128:5 * 128, :], [], [B_xt[0]])
        for i in range(17):
            s = i % 2
            e = i + 4
            if i + 1 < 17:
                dma(brT[1 - s].rearrange("p k t -> p (k t)"), br_scr[i + 1], [], [B_br[1 - s]])
                dma(xt2[1 - s], x_ext[(e + 1) * 128:(e + 2) * 128, :], [], [B_xt[1 - s]])
            bt = brT[s]
            for br in range(3):
                for nb in range(2):
                    for k in range(8):
                        mm(PS[nb][:, :], bt[:, 16 + k, :], wg[:, k, br * 1024 + nb * 512:br * 1024 + (nb + 1) * 512],
                           [B_br[s], B_w2p], [PB[nb]], start=(k == 0), stop=(k == 7))
                wsel, off, nk = ((wbp, 0, 4), (wbn, 4, 8), (wbx, 12, 4))[br]
                for nb in range(2):
                    for k in range(nk):
                        mm(PS[2 + nb][:, :], bt[:, off + k, :], wsel[:, k, nb * 512:(nb + 1) * 512],
                           [B_br[s], B_w2p], [PB[2 + nb]], start=(k == 0), stop=(k == nk - 1))
                for nb in range(2):
                    blk = slice(nb * 512, (nb + 1) * 512)
                    act(sg[:, blk], PS[nb][:, :], AF.Sigmoid, [PB[nb]], [B_sg])
                    if br == 0:
                        tt("dve", yy[:, blk], sg[:, blk], PS[2 + nb][:, :], ALU.mult, [B_sg, PB[2 + nb]], [B_y])
                    else:
                        tt("dve", ytm[:, blk], sg[:, blk], PS[2 + nb][:, :], ALU.mult, [B_sg, PB[2 + nb]], [B_ytm])
                        tt("pool", yy[:, blk], yy[:, blk], ytm[:, blk], ALU.add, [B_y, B_ytm], [B_y])
            cp("act", yb, yy, [B_y], [B_yb])
            for k in range(8):
                tr(PS[4 + k // 4][:, (k % 4) * 128:(k % 4 + 1) * 128], yb[:, k * 128:(k + 1) * 128], [B_yb], [PB[4 + k // 4]])
            cp("act", yT[:, 0:4, :], PS[4][:, :].rearrange("p (k t) -> p k t", k=4), [PB[4]], [B_yT])
            cp("dve", yT[:, 4:8, :], PS[5][:, :].rearrange("p (k t) -> p k t", k=4), [PB[5]], [B_yT])
            for nb in range(2):
                for k in range(8):
                    mm(PS[6 + nb][:, :], yT[:, k, :], wo[:, k, nb * 512:(nb + 1) * 512], [B_yT, B_w2p], [PB[6 + nb]],
                       start=(k == 0), stop=(k == 7))
            post_norm_res(6, 7, gpost, B_gp, xt2[s], B_xt[s], x1t[s], B_x1[s])
            lx = dma(x1_scr[i], x1t[s], [B_x1[s]], [])
            if dbg == "B2":
                lx = dma(dbg_t(f"x1_{i}", [128, 1024]), x1t[s], [B_x1[s]], [])
        if dbg == "B2":
            print(S.emit(final_waits=[lx]))
            return nc
        barrier()
        top[0] = persist_top

        wup = abf(8 * 5632).rearrange("p (k c) -> p k c", k=8)
        wdn = abf(22 * 1024).rearrange("p (k c) -> p k c", k=22)
        B_w3 = S.buf("w_c")
        for k in range(8):
            loadw(lambda c0, n, k=k: wup[:, k, c0:c0 + n], w_up[0][k * 128:(k + 1) * 128, :], 5632,
                  scale=gffn[:, k:k + 1], r_extra=[B_small], w=[B_w3])
        for k in range(22):
            loadw(lambda c0, n, k=k: wdn[:, k, c0:c0 + n], w_down[0][k * 128:(k + 1) * 128, :], 1024, w=[B_w3])
        convp = af32(44 * 4).rearrange("p (j c) -> p j c", c=4)
        B_cv = S.buf("convp")
        for kk in range(3):
            dma(convp[:, :, kk], conv_w[0][kk].rearrange("(j p) -> p j", p=128), [], [B_cv], slow=True)
        dma(convp[:, :, 3], conv_b[0].rearrange("(j p) -> p j", p=128), [], [B_cv], slow=True)
        gpost2 = af32(1024)
        B_gp2 = S.buf("gpost2")
        dma(gpost2, post_ffn_g.partition_broadcast(128), [], [B_gp2])
        xt2 = [af32(1024), af32(1024)]
        B_xt = [S.buf("d_xt0"), S.buf("d_xt1")]
        xn = abf(1024)
        B_xn = S.buf("d_xn")
        junk = abf(1024)
        ssA = af32(2)
        B_ss = S.buf("d_ss")
        h2Ts = [abf(8 * 130).rearrange("p (k t) -> p k t", k=8) for _ in range(2)]
        B_h2s = [S.buf("h2T0"), S.buf("h2T1")]
        xnC = [xn, abf(1024)]
        B_xnC = [B_xn, S.buf("xnC1")]
        aT = abf(22 * 128).rearrange("p (k t) -> p k t", k=22)
        B_aT = S.buf("aT")
        cg = [af32(128), af32(128)]
        cv = [af32(128), af32(128)]
        gl = [af32(128), af32(128)]
        B_cg = [S.buf("cg0"), S.buf("cg1")]
        B_cvb = [S.buf("cv0"), S.buf("cv1")]
        B_gl = [S.buf("gl0"), S.buf("gl1")]
        ot = [af32(1024), af32(1024)]
        B_ot = [S.buf("ot0"), S.buf("ot1")]
        xt3 = [xt2[0], xt2[1], af32(1024)]
        B_xt3 = [B_xt[0], B_xt[1], S.buf("d_xt2")]
        fins = []

        def c_stage1(i):
            s_ = i % 2
            x3 = i % 3
            rms_scale(xt3[x3], xnC[s_], B_xt3[x3], B_xnC[s_], ssA[:, 0:1], junk, B_ss)
            for k in range(8):
                bank = k // 4
                tr(PS[bank][:, (k % 4) * 128:(k % 4 + 1) * 128], xnC[s_][:, k * 128:(k + 1) * 128], [B_xnC[s_]], [PB[bank]])
            cp("act", h2Ts[s_][:, 0:4, 2:130], PS[0][:, :].rearrange("p (k t) -> p k t", k=4), [PB[0]], [B_h2s[s_]])
            cp("dve", h2Ts[s_][:, 4:8, 2:130], PS[1][:, :].rearrange("p (k t) -> p k t", k=4), [PB[1]], [B_h2s[s_]])
            if i == 1:
                ts("pool", h2Ts[s_][:, :, 0:2], h2Ts[1 - s_][:, :, 128:130], hflag[:, 0:1], None, ALU.mult, None,
                   [B_h2s[1 - s_], B_small, B_h2s[s_]], [B_h2s[s_]])
            elif i > 1:
                cp("pool", h2Ts[s_][:, :, 0:2], h2Ts[1 - s_][:, :, 128:130], [B_h2s[1 - s_], B_h2s[s_]], [B_h2s[s_]])

        dma(xt3[0], x1_scr[0], [], [B_xt3[0]])
        dma(xt3[1], x1_scr[1], [], [B_xt3[1]])
        dma(xt3[2], x1_scr[2], [], [B_xt3[2]])
        c_stage1(0)
        for i in range(17):
            s = i % 2
            if i + 1 < 17:
                c_stage1(i + 1)
            if i == 0:
                dma(xt3[0], x1_scr[3], [], [B_xt3[0]])
                continue
            h2T, B_h2 = h2Ts[s], B_h2s[s]
            for j in range(22):
                a = j % 2
                bg, bv = 2 + 2 * a, 3 + 2 * a
                for k in range(8):
                    mm(PS[bg][:, 0:130], wup[:, k, j * 128:(j + 1) * 128], h2T[:, k, :], [B_w3, B_h2], [PB[bg]],
                       start=(k == 0), stop=(k == 7))
                for k in range(8):
                    mm(PS[bv][:, 0:130], wup[:, k, (22 + j) * 128:(23 + j) * 128], h2T[:, k, :], [B_w3, B_h2], [PB[bv]],
                       start=(k == 0), stop=(k == 7))
                for (bank, dstc, Bd, jj) in ((bg, cg[a], B_cg[a], j), (bv, cv[a], B_cvb[a], 22 + j)):
                    act(dstc, PS[bank][:, 2:130], AF.Identity, [PB[bank], B_cv], [Bd], bias=convp[:, jj, 3:4], scale=convp[:, jj, 2:3])
                    stt(dstc, PS[bank][:, 1:129], convp[:, jj, 1:2], dstc, ALU.mult, ALU.add, [PB[bank], B_cv, Bd], [Bd])
                    stt(dstc, PS[bank][:, 0:128], convp[:, jj, 0:1], dstc, ALU.mult, ALU.add, [PB[bank], B_cv, Bd], [Bd])
                act(gl[a], cg[a], AF.Gelu_apprx_tanh, [B_cg[a]], [B_gl[a]])
                tt("pool", aT[:, j, :], gl[a], cv[a], ALU.mult, [B_gl[a], B_cvb[a]], [B_aT])
            for nb in range(2):
                for j in range(22):
                    mm(PS[6 + nb][:, :], aT[:, j, :], wdn[:, j, nb * 512:(nb + 1) * 512], [B_aT, B_w3], [PB[6 + nb]],
                       start=(j == 0), stop=(j == 21))
            post_norm_res(6, 7, gpost2, B_gp2, xt3[i % 3], B_xt3[i % 3], ot[s], B_ot[s])
            fins.append(dma(out_d[(i - 1) * 128:i * 128, :], ot[s], [B_ot[s]], []))
            if i + 3 < 17:
                dma(xt3[i % 3], x1_scr[i + 3], [], [B_xt3[i % 3]])
        stats = S.emit(final_waits=fins)
        print("emit stats", stats, flush=True)
    return nc


def _consts():
    c = {}
    c["ident"] = np.eye(128, dtype=np.float32)
    k = np.arange(8192)
    c["Gm"] = (k[None, :] // 64 == np.arange(128)[:, None]).astype(np.float32)
    c["invf"] = (np.float32(500000.0) ** (-np.arange(8, dtype=np.float32) * np.float32(2.0 / 16))).astype(np.float32).reshape(1, 8)
    c["bsrow"] = (64.0 * np.arange(128, dtype=np.float32)).reshape(1, 128)
    ce = (16.0 * np.arange(512, dtype=np.float32) + 31.0)
    ce[511] = 1e9
    c["cendrow"] = ce.reshape(1, 512)
    c["kidx"] = (128.0 * np.arange(64)[None, :] + np.arange(128)[:, None]).astype(np.float32)
    c["cendcol"] = np.ascontiguousarray(ce.reshape(4, 128).T)
    p = np.arange(128)[:, None]
    q = np.arange(128)[None, :]
    c["Mw"] = np.concatenate([(q < p), (q >= p)], axis=1).astype(np.float32)
    e0 = np.zeros((1, 128), np.float32)
    e0[0, 0] = BIG
    c["e0big"] = e0
    A = np.zeros((128, 4, 2, 128), np.float32)
    for gi, w in enumerate((2, 4, 8, 16)):
        tp = np.arange(128)[:, None]
        t = np.arange(128)[None, :]
        A[:, gi, 0, :] = ((t - tp >= 0) & (t - tp < w))
        A[:, gi, 1, :] = ((t + 128 - tp >= 0) & (t + 128 - tp < w))
    c["Aband"] = A.reshape(128, 1024)
    c["qrow"] = np.arange(128, dtype=np.float32).reshape(1, 128)
    return c


_PROG = {}


def kernel(**inputs):
    x = np.asarray(inputs["x"], dtype=np.float32)
    mem = np.asarray(inputs["mem"], dtype=np.float32)
    positions = np.asarray(inputs["positions"]).astype(np.int32)
    if inputs.get("_return_maps"):
        nc = None
    else:
        if "nc" not in _PROG:
            _PROG["nc"] = build()
        nc = _PROG["nc"]
    consts = _consts()
    wnames = ["pre_mix_g", "w_in", "pool_w", "pool_scale", "cmp_pe", "cmp_w1", "cmp_w2", "mem_norm_g", "w_mem_kv",
              "w_br_pool", "w_br_nsa", "w_br_xa", "w_out", "post_mix_g", "pre_ffn_g", "w_up", "conv_w", "conv_b",
              "w_down", "post_ffn_g"]
    shared = {n: np.ascontiguousarray(np.asarray(inputs[n], dtype=np.float32)) for n in wnames}
    in_maps = []
    for core in range(8):
        b, r = core // 4, core % 4
        m = dict(shared)
        m.update(consts)
        m["x_all"] = np.ascontiguousarray(x[b])
        m["mem"] = np.ascontiguousarray(mem[b])
        xe = np.zeros((NEXT * 128, 1024), np.float32)
        pe = np.zeros((NEXT, 128), np.int32)
        tq = np.zeros((NEXT, 128), np.float32)
        wv = np.zeros((1, NEXT), np.float32)
        ic = np.ones((128, NEXT, 4), np.float32)
        tst = np.zeros((1, NEXT), np.float32)
        for e in range(NEXT):
            ge = 16 * r - 5 + e
            tq[e] = ge * 128 + np.arange(128)
            tst[0, e] = ge * 128
            if ge >= 0:
                xe[e * 128:(e + 1) * 128] = x[b, ge * 128:(ge + 1) * 128]
                pe[e] = positions[b, ge * 128:(ge + 1) * 128]
                wv[0, e] = 1.0
                t = ge * 128 + np.arange(128)
                for gi, w in enumerate((2, 4, 8, 16)):
                    ic[:, e, gi] = 1.0 / np.minimum(t + 1, w).astype(np.float32)
        m["x_ext"] = xe
        m["pos_ext"] = np.ascontiguousarray(pe.T)
        m["pos_all"] = np.ascontiguousarray(positions[b].reshape(NALL, 128).T)
        m["tq_ext"] = np.ascontiguousarray(tq.T)
        m["tqstart"] = tst
        m["wvalid"] = wv
        m["invcnt"] = np.ascontiguousarray(ic.reshape(128, NEXT * 4))
        m["hflag"] = np.array([[0.0 if r == 0 else 1.0]], np.float32)
        in_maps.append(m)
    if inputs.get("_return_maps"):
        return in_maps
    res = run_bass_kernel_spmd(nc, in_maps, core_ids=list(range(8)))
    out = np.zeros((2, 8192, 1024), np.float32)
    for core in range(8):
        b, r = core // 4, core % 4
        out[b, r * 2048:(r + 1) * 2048] = res.results[core]["out"]
    return out
```

```python
import numpy as np
from contextlib import ExitStack
import concourse.bass as bass
import concourse.mybir as mybir
from concourse.bass_utils import run_bass_kernel_spmd

F32 = mybir.dt.float32
BF16 = mybir.dt.bfloat16
I32 = mybir.dt.int32
AF = mybir.ActivationFunctionType
ALU = mybir.AluOpType
AX = mybir.AxisListType


import sys as _sys


def _where():
    f = _sys._getframe(2)
    out = []
    while f is not None and len(out) < 4:
        if f.f_code.co_name != "<lambda>":
            out.append(f.f_lineno)
        f = f.f_back
    return out


class Buf:
    __slots__ = ("name", "writers", "readers", "excl", "last")

    def __init__(self, name, excl=False):
        self.name = name
        self.writers = []
        self.readers = []
        self.excl = excl
        self.last = {}


class Ins:
    __slots__ = ("eng", "fn", "deps", "idx", "flag", "tok", "dma", "pre", "where")

    def __init__(self, eng, fn, dma):
        self.eng = eng
        self.fn = fn
        self.deps = []
        self.flag = False
        self.tok = None
        self.dma = dma
        self.pre = None


class Sched:
    ENGS = ("pe", "dve", "act", "pool", "sp")
    EPOCH = 8000
    NDMA = 24

    def __init__(self, nc, stack):
        self.nc = nc
        self.stack = stack
        self.q = {e: [] for e in self.ENGS}
        self.nbuf = 0

    def buf(self, name=None, excl=False):
        self.nbuf += 1
        return Buf(name or f"b{self.nbuf}", excl)

    def op(self, eng, fn, r=(), w=(), dma=False):
        ins = Ins(eng, fn, dma)
        ins.where = _where()
        deps = []
        for b in r:
            deps.extend(b.writers)
        for b in w:
            deps.extend(b.readers)
        for b in list(r) + list(w):
            if b.excl:
                for en, li in b.last.items():
                    if en != eng:
                        deps.append(li)
                b.last[eng] = ins
        for b in w:
            if b.readers or (b in r):
                b.writers = [ins]
                b.readers = []
            else:
                b.writers.append(ins)
                if len(b.writers) > 48:
                    b.writers = b.writers[-48:]
        for b in r:
            if b not in w:
                b.readers.append(ins)
                if len(b.readers) > 48:
                    b.readers = b.readers[-48:]
        seen = set()
        for d in deps:
            if d is ins or id(d) in seen:
                continue
            if d.eng == "pe" and eng == "pe" and not d.dma:
                continue
            seen.add(id(d))
            ins.deps.append(d)
            d.flag = True
        self.q[eng].append(ins)
        return ins

    def emit(self, final_waits=()):
        nc = self.nc
        stack = self.stack
        sems = {}
        for e in self.ENGS:
            n = 0
            for ins in self.q[e]:
                if ins.dma:
                    continue
                if ins.flag:
                    n += 1
                    ins.idx = n
            nep = (n + self.EPOCH - 1) // self.EPOCH
            sems[e] = [stack.enter_context(nc.semaphore(f"s_{e}_{k}")) for k in range(max(nep, 1))]
        dsems = [stack.enter_context(nc.semaphore(f"s_dma_{k}")) for k in range(self.NDMA)]
        duse = [0] * self.NDMA
        dma_engs = [e for e in self.ENGS if any(i.dma for i in self.q[e])]
        share = {}
        if dma_engs:
            per = self.NDMA // len(dma_engs)
            for k, e in enumerate(dma_engs):
                share[e] = list(range(k * per, (k + 1) * per))
        for e in dma_engs:
            j = 0
            for ins in self.q[e]:
                if not ins.dma:
                    continue
                s = share[e][j % len(share[e])]
                j += 1
                prev = duse[s]
                duse[s] += 1
                ins.tok = (dsems[s], 16 * duse[s])
                ins.pre = (dsems[s], 16 * prev) if prev > 0 else None
        for e in self.ENGS:
            for ins in self.q[e]:
                if ins.dma or not ins.flag:
                    continue
                k = (ins.idx - 1) // self.EPOCH
                ins.tok = (sems[e][k], (ins.idx - 1) % self.EPOCH + 1)
        engobj = {"pe": "tensor", "dve": "vector", "act": "scalar", "pool": "gpsimd", "sp": "sync"}
        stats = {}
        with nc.Block() as block:
            for e in self.ENGS:
                lst = self.q[e]

                def body(eng, lst=lst, e=e):
                    waited = {}
                    nw = 0

                    def wait(tok):
                        nonlocal nw
                        sem, val = tok
                        key = id(sem)
                        if waited.get(key, 0) >= val:
                            return
                        waited[key] = val
                        eng.wait_ge(sem, val)
                        nw += 1

                    for ins in lst:
                        if ins.pre is not None:
                            wait(ins.pre)
                        for d in ins.deps:
                            wait(d.tok)
                        try:
                            bi = ins.fn(eng)
                        except BaseException:
                            print("FAILED op recorded at lines", ins.where, flush=True)
                            raise
                        if ins.dma:
                            bi.then_inc(ins.tok[0], 16)
                        elif ins.flag:
                            bi.then_inc(ins.tok[0], 1)
                    if e == "sp":
                        for fw in final_waits:
                            wait(fw.tok)
                    stats[e] = (len(lst), nw)

                getattr(block, engobj[e])(body)
        return stats


import os as _os
DUMPX = int(_os.environ.get('DUMPX', '0'))
NOBAR = int(_os.environ.get('NOBAR', '0'))
NEXT = 21
NALL = 64
BIG = 100.0
TWO_PI = float(2 * np.pi)


def build(dbg=None, ntile_b1=NEXT, na=NALL, skipcmp=False):
    nc = bass.Bass("TRN2", target_bir_lowering=False)

    def din(name, shape, dt=F32):
        return nc.dram_tensor(name, list(shape), dt, kind="ExternalInput").ap()

    x_all = din("x_all", [8192, 1024])
    x_ext = din("x_ext", [NEXT * 128, 1024])
    pos_all = din("pos_all", [128, NALL], I32)
    pos_ext = din("pos_ext", [128, NEXT], I32)
    tq_ext = din("tq_ext", [128, NEXT])
    qrow_d = din("qrow", [1, 128])
    tqst_d = din("tqstart", [1, NEXT])
    wvalid_d = din("wvalid", [1, NEXT])
    invcnt_d = din("invcnt", [128, NEXT * 4])
    hflag_d = din("hflag", [1, 1])
    mem_d = din("mem", [256, 1024])
    ident_d = din("ident", [128, 128])
    G_d = din("Gm", [128, 8192])
    invf_d = din("invf", [1, 8])
    bsrow_d = din("bsrow", [1, 128])
    cendrow_d = din("cendrow", [1, 512])
    kidx_d = din("kidx", [128, 64])
    cendcol_d = din("cendcol", [128, 4])
    Mw_d = din("Mw", [128, 256])
    e0_d = din("e0big", [1, 128])
    Ab_d = din("Aband", [128, 1024])
    pre_mix_g = din("pre_mix_g", [1, 1024])
    w_in = din("w_in", [1, 1024, 5936])
    pool_w = din("pool_w", [1, 4, 128, 128])
    pool_scale = din("pool_scale", [1, 512])
    cmp_pe = din("cmp_pe", [1, 2, 32, 64])
    cmp_w1 = din("cmp_w1", [1, 2, 2048, 256])
    cmp_w2 = din("cmp_w2", [1, 2, 256, 64])
    mem_norm_g = din("mem_norm_g", [1, 1024])
    w_mem_kv = din("w_mem_kv", [1, 1024, 1024])
    w_br_pool = din("w_br_pool", [1, 512, 1024])
    w_br_nsa = din("w_br_nsa", [1, 1024, 1024])
    w_br_xa = din("w_br_xa", [1, 512, 1024])
    w_out = din("w_out", [1, 1024, 1024])
    post_mix_g = din("post_mix_g", [1, 1024])
    pre_ffn_g = din("pre_ffn_g", [1, 1024])
    w_up = din("w_up", [1, 1024, 5632])
    conv_w = din("conv_w", [1, 3, 5632])
    conv_b = din("conv_b", [1, 5632])
    w_down = din("w_down", [1, 2816, 1024])
    post_ffn_g = din("post_ffn_g", [1, 1024])
    out_d = nc.dram_tensor("out", [16 * 128, 1024], F32, kind="ExternalOutput").ap()
    br_scr = nc.dram_tensor("br_scr", [17, 128, 24 * 128], BF16, kind="Internal").ap()
    x1_scr = nc.dram_tensor("x1_scr", [17, 128, 1024], F32, kind="Internal").ap()
    dbg_out = {}

    def dbg_t(name, shape, dt=F32):
        return nc.dram_tensor("dbg_" + name, list(shape), dt, kind="ExternalOutput").ap()

    with ExitStack() as st:
        S = Sched(nc, st)
        ARN = 95800
        arena = st.enter_context(nc.sbuf_tensor("arena", [128, ARN], BF16))
        PS = [st.enter_context(nc.psum_tensor(f"ps{i}", [128, 512], F32)) for i in range(8)]
        PB = [S.buf(f"ps{i}", excl=True) for i in range(8)]
        top = [0]

        def abf(n):
            a = arena[:, top[0]:top[0] + n]
            top[0] += (n + 31) // 32 * 32
            assert top[0] <= ARN, top[0]
            return a

        def af32(n):
            a = arena[:, top[0]:top[0] + 2 * n].bitcast(F32)
            top[0] += (n + 15) // 16 * 32
            assert top[0] <= ARN, top[0]
            return a

        def ai32(n):
            a = arena[:, top[0]:top[0] + 2 * n].bitcast(I32)
            top[0] += (n + 15) // 16 * 32
            assert top[0] <= ARN, top[0]
            return a

        def mm(out, lhsT, rhs, r, w, start=True, stop=True):
            return S.op("pe", lambda e: e.matmul(out, lhsT=lhsT, rhs=rhs, start=start, stop=stop,
                                                 skip_group_check=True), r, w)

        last_func = [None]

        def act(out, in_, func, r, w, bias=None, scale=None, accum=None):
            if func != last_func[0]:
                last_func[0] = func
                S.op("act", lambda e: e.activation(out=bsc[1][:, 0:1], in_=bsc[3][:, 0:1], func=func), [B_bsc3], [])
            kw = {}
            if bias is not None:
                kw["bias"] = bias
            if scale is not None:
                kw["scale"] = scale
            if accum is not None:
                kw["accum_out"] = accum
            return S.op("act", lambda e: e.activation(out=out, in_=in_, func=func, **kw), r, w)

        def ts(eng, out, in0, s1, s2, op0, op1, r, w):
            if s2 is None:
                return S.op(eng, lambda e: e.tensor_scalar(out=out, in0=in0, scalar1=s1, scalar2=None, op0=op0), r, w)
            return S.op(eng, lambda e: e.tensor_scalar(out=out, in0=in0, scalar1=s1, scalar2=s2, op0=op0, op1=op1), r, w)

        def tt(eng, out, in0, in1, op, r, w):
            return S.op(eng, lambda e: e.tensor_tensor(out=out, in0=in0, in1=in1, op=op), r, w)

        def stt(out, in0, scalar, in1, op0, op1, r, w, accum=None):
            if accum is None:
                return S.op("dve", lambda e: e.scalar_tensor_tensor(out=out, in0=in0, scalar=scalar, in1=in1, op0=op0, op1=op1), r, w)
            return S.op("dve", lambda e: e.scalar_tensor_tensor(out=out, in0=in0, scalar=scalar, in1=in1, op0=op0, op1=op1,
                                                                accum_out=accum), r, w)

        def cp(eng, out, in_, r, w, scale=None):
            if eng == "act":
                return act(out, in_, AF.Copy, r, w, scale=scale)
            if scale is not None:
                return ts(eng, out, in_, scale, None, ALU.mult, None, r, w)
            return S.op(eng, lambda e: e.tensor_copy(out=out, in_=in_), r, w)

        def memset(eng, ap, val, w):
            return S.op(eng, lambda e: e.memset(ap, val), (), w)

        dmas = []

        def dma(out, in_, r, w, slow=False):
            if slow:
                i = S.op("sp", lambda e: e.dma_start(out=out, in_=in_, allow_slow_non_contiguous=True), r, w, dma=True)
            else:
                i = S.op("sp", lambda e: e.dma_start(out=out, in_=in_), r, w, dma=True)
            dmas.append(i)
            return i

        bsc = [af32(16) for _ in range(4)]
        bbuf = {e: S.buf("bar_" + e) for e in S.ENGS}
        bar_scr = nc.dram_tensor("bar_scr", [128, 16], F32, kind="Internal").ap()

        def barrier():
            allb = list(bbuf.values())
            mm(PS[7][:, 0:1], ident_b[:, 0:128], ident_b[:, 0:1], [PB[7], B_ident], [bbuf["pe"], PB[7]])
            memset("dve", bsc[0], 0.0, [bbuf["dve"]])
            act(bsc[1], bsc[3], AF.Copy, [B_bsc3], [bbuf["act"]])
            memset("pool", bsc[2], 0.0, [bbuf["pool"]])
            i = S.op("sp", lambda e: e.dma_start(out=bar_scr, in_=bsc[3]), [B_bsc3], [bbuf["sp"]], dma=True)
            for d in dmas:
                if d not in i.deps:
                    i.deps.append(d)
            dmas.clear()
            mm(PS[7][:, 0:1], ident_b[:, 0:128], ident_b[:, 0:1], allb + [PB[7], B_ident], [PB[7]])
            S.op("dve", lambda e: e.memset(bsc[0], 0.0), allb, [])
            act(bsc[1], bsc[3], AF.Copy, allb + [B_bsc3], [])
            S.op("pool", lambda e: e.memset(bsc[2], 0.0), allb, [])
            S.op("sp", lambda e: e.dma_start(out=bar_scr, in_=bsc[3]), allb + [B_bsc3], [], dma=True)

        ident_b = abf(128)
        B_ident = S.buf("ident")
        stage = [af32(1024), af32(1024)]
        B_stage = [S.buf("stg0"), S.buf("stg1")]
        stg_i = [0]
        cast_i = [0]
        B_bsc3 = S.buf("bsc3")
        memset("dve", bsc[3], 0.0, [B_bsc3])

        def loadw(dst_fn, src, ncols, scale=None, r_extra=(), w=None, perm_q=False):
            for c0 in range(0, ncols, 1024):
                n = min(1024, ncols - c0)
                k = stg_i[0] % 2
                stg_i[0] += 1
                np_ = src.shape[0]
                sl = stage[k][0:np_, 0:n]
                dma(sl, src[:, c0:c0 + n], [], [B_stage[k]])
                eng = ("pool", "dve")[cast_i[0] % 2]
                cast_i[0] += 1
                if perm_q:
                    for g in range(2):
                        o = dst_fn(c0, n).rearrange("p (c g d) -> p c g d", c=8, g=2)[:, :, g, :]
                        i_ = stage[k][0:np_, g * 512:(g + 1) * 512].rearrange("p (c d) -> p c d", c=8)
                        if scale is not None:
                            ts(eng, o, i_, scale, None, ALU.mult, None, [B_stage[k]] + list(r_extra), w)
                        else:
                            cp(eng, o, i_, [B_stage[k]] + list(r_extra), w)
                else:
                    if scale is not None:
                        ts(eng, dst_fn(c0, n), sl, scale, None, ALU.mult, None, [B_stage[k]] + list(r_extra), w)
                    else:
                        cp(eng, dst_fn(c0, n), sl, [B_stage[k]] + list(r_extra), w)

        def tr(out_ps, in_sb, r, w, start=True):
            return mm(out_ps, in_sb, ident_b[0:in_sb.shape[0], 0:in_sb.shape[0]], list(r) + [B_ident], w)

        dma(stage[0][:, 0:128], ident_d, [], [B_stage[0]])
        cp("dve", ident_b, stage[0][:, 0:128], [B_stage[0]], [B_ident])

        gpre = af32(8)
        gmem = af32(8)
        gffn = af32(8)
        B_small = S.buf("small")
        dma(gpre, pre_mix_g[0].rearrange("(k p) -> p k", p=128), [], [B_small], slow=True)
        dma(gmem, mem_norm_g[0].rearrange("(k p) -> p k", p=128), [], [B_small], slow=True)
        dma(gffn, pre_ffn_g[0].rearrange("(k p) -> p k", p=128), [], [B_small], slow=True)
        tqe = af32(NEXT)
        dma(tqe, tq_ext, [], [B_small])
        wval = af32(NEXT)
        dma(wval, wvalid_d.partition_broadcast(128), [], [B_small])
        hflag = af32(1)
        dma(hflag, hflag_d.partition_broadcast(128), [], [B_small])
        invcnt = af32(NEXT * 4)
        dma(invcnt, invcnt_d, [], [B_small])
        ss2 = af32(4)
        B_ss2 = S.buf("ss2")
        persist_top = top[0]

        def rms_scale(xt, xn, Bx, Bxn, ss, junk, Bss):
            act(junk, xt, AF.Square, [Bx], [Bss], accum=ss)
            ts("dve", ss, ss, 1.0 / 1024, 1e-6, ALU.mult, ALU.add, [Bss], [Bss])
            act(ss, ss, AF.Sqrt, [Bss], [Bss])
            S.op("dve", lambda e: e.reciprocal(out=ss, in_=ss), [Bss], [Bss])
            ts("dve", xn, xt, ss, None, ALU.mult, None, [Bx, Bss], [Bxn])

        def sincos(ang, n, osin, ocos, tmp, Bt, Bo):
            t, kf, g, ki = tmp
            ts("dve", ang, ang, 1.0 / TWO_PI, None, ALU.mult, None, [Bt], [Bt])
            for dst, off in ((osin, 0.0), (ocos, 0.25)):
                ts("dve", t, ang, off, None, ALU.add, None, [Bt], [Bt])
                cp("dve", ki, t, [Bt], [Bt])
                cp("dve", kf, ki, [Bt], [Bt])
                tt("dve", t, t, kf, ALU.subtract, [Bt], [Bt])
                ts("dve", g, t, 0.5, None, ALU.is_gt, None, [Bt], [Bt])
                tt("dve", t, t, g, ALU.subtract, [Bt], [Bt])
                ts("dve", g, t, -0.5, None, ALU.is_lt, None, [Bt], [Bt])
                tt("dve", t, t, g, ALU.add, [Bt], [Bt])
                act(dst, t, AF.Sin, [Bt], [Bo], scale=TWO_PI)

        def rotary(src4, dst4, cs, sn, tmp, rB, wB, Bt):
            a, b = src4.shape[1], src4.shape[2]
            n = a * b * 8
            x1 = src4[:, :, :, 0:8]
            x2 = src4[:, :, :, 8:16]
            csb = cs.unsqueeze(1).unsqueeze(1).to_broadcast([128, a, b, 8])
            snb = sn.unsqueeze(1).unsqueeze(1).to_broadcast([128, a, b, 8])
            t1 = tmp[:, 0:n].rearrange("p (a b d) -> p a b d", a=a, b=b)
            t2 = tmp[:, n:2 * n].rearrange("p (a b d) -> p a b d", a=a, b=b)
            tt("dve", t1, x1, csb, ALU.mult, rB, [Bt])
            tt("dve", t2, x2, snb, ALU.mult, rB, [Bt])
            tt("dve", dst4[:, :, :, 0:8], t1, t2, ALU.subtract, [Bt], wB)
            tt("dve", t1, x2, csb, ALU.mult, rB, [Bt])
            tt("dve", t2, x1, snb, ALU.mult, rB, [Bt])
            tt("dve", dst4[:, :, :, 8:16], t1, t2, ALU.add, [Bt], wB)

        KTs = abf(8192)
        Vs = abf(64 * 2 * 65).rearrange("p (t g d) -> p t g d", t=64, g=2)
        KTw = abf(NEXT * 128)
        Vw = abf(NEXT * 2 * 65).rearrange("p (t g d) -> p t g d", t=NEXT, g=2)
        KcT = abf(512)
        Vc = abf(4 * 2 * 65).rearrange("p (t g d) -> p t g d", t=4, g=2)
        Gb = abf(8192)
        B_KTs, B_Vs, B_KTw, B_Vw, B_KcT, B_Vc, B_G = [S.buf(n) for n in "KTs Vs KTw Vw KcT Vc G".split()]
        cosE = af32(NEXT * 8).rearrange("p (t f) -> p t f", f=8)
        sinE = af32(NEXT * 8).rearrange("p (t f) -> p t f", f=8)
        cos8 = af32(NEXT * 8).rearrange("p (t f) -> p t f", f=8)
        sin8 = af32(NEXT * 8).rearrange("p (t f) -> p t f", f=8)
        B_tabE = S.buf("tabE")
        kv_top = top[0]

        memset("pool", Vs, 1.0, [B_Vs])
        memset("pool", Vw, 1.0, [B_Vw])
        memset("pool", Vc, 0.0, [B_Vc])
        memset("pool", Vc[:, :, :, 64:65], 1.0, [B_Vc])
        loadw(lambda c0, n: Gb[:, c0:c0 + n], G_d, 8192, w=[B_G])

        cosA = af32(NALL * 8).rearrange("p (t f) -> p t f", f=8)
        sinA = af32(NALL * 8).rearrange("p (t f) -> p t f", f=8)
        B_tabA = S.buf("tabA")
        KcRaw = abf(8192)
        VcRaw = abf(8192)
        B_KcRaw, B_VcRaw = S.buf("KcRaw"), S.buf("VcRaw")
        wkvA = abf(8 * 512).rearrange("p (k c) -> p k c", k=8)
        B_wkvA = S.buf("wkvA")
        for k in range(8):
            loadw(lambda c0, n, k=k: wkvA[:, k, c0:c0 + n], w_in[0][k * 128:(k + 1) * 128, 1536:2048], 512,
                  scale=gpre[:, k:k + 1], r_extra=[B_small], w=[B_wkvA])
        mark_tab = top[0]
        posi = ai32(512)
        posf = af32(512)
        invf = af32(8)
        ang = af32(512)
        tmp3 = (af32(512), af32(512), af32(512), ai32(512))
        B_t = S.buf("tabtmp")
        dma(invf, invf_d.partition_broadcast(128), [], [B_t])
        dma(posi[:, 0:NALL], pos_all, [], [B_t])
        cp("dve", posf[:, 0:NALL], posi[:, 0:NALL], [B_t], [B_t])
        tt("dve", ang.rearrange("p (t f) -> p t f", f=8), posf[:, 0:NALL].unsqueeze(2).to_broadcast([128, NALL, 8]),
           invf.unsqueeze(1).to_broadcast([128, NALL, 8]), ALU.mult, [B_t], [B_t])
        sincos(ang, 512, sinA.rearrange("p t f -> p (t f)"), cosA.rearrange("p t f -> p (t f)"),
               tmp3, B_t, B_tabA)
        dma(posi[:, 0:NEXT], pos_ext, [B_t], [B_t])
        cp("dve", posf[:, 0:NEXT], posi[:, 0:NEXT], [B_t], [B_t])
        ne = NEXT * 8
        tt("dve", ang[:, 0:ne].rearrange("p (t f) -> p t f", f=8), posf[:, 0:NEXT].unsqueeze(2).to_broadcast([128, NEXT, 8]),
           invf.unsqueeze(1).to_broadcast([128, NEXT, 8]), ALU.mult, [B_t], [B_t])
        sincos(ang[:, 0:ne], ne, sinE.rearrange("p t f -> p (t f)"), cosE.rearrange("p t f -> p (t f)"),
               tuple(a[:, 0:ne] for a in tmp3), B_t, B_tabE)
        ts("dve", cos8.rearrange("p t f -> p (t f)"), cosE.rearrange("p t f -> p (t f)"), 0.125, None, ALU.mult, None, [B_tabE], [B_tabE])
        ts("dve", sin8.rearrange("p t f -> p (t f)"), sinE.rearrange("p t f -> p (t f)"), 0.125, None, ALU.mult, None, [B_tabE], [B_tabE])
        if dbg != "0":
            barrier()
            top[0] = mark_tab

        if dbg == "0":
            f1 = dma(dbg_t("cosA", [128, NALL * 8]), cosA.rearrange("p t f -> p (t f)"), [B_tabA], [])
            f2 = dma(dbg_t("sinA", [128, NALL * 8]), sinA.rearrange("p t f -> p (t f)"), [B_tabA], [])
            f3 = dma(dbg_t("cos8", [128, NEXT * 8]), cos8.rearrange("p t f -> p (t f)"), [B_tabE], [])
            f4 = dma(dbg_t("Gb", [128, 8192], BF16), Gb, [B_G], [])
            print(S.emit(final_waits=[f1, f2, f3, f4]))
            return nc
        xt2 = [af32(1024), af32(1024)]
        B_xt = [S.buf("xt0"), S.buf("xt1")]
        xn = abf(1024)
        B_xn = S.buf("xn")
        junk = abf(1024)
        ssA = af32(1)
        B_ss = S.buf("ss")
        hT = abf(1024).rearrange("p (k t) -> p k t", k=8)
        B_hT = S.buf("hT")
        kb = abf(512)
        B_kb = S.buf("kb")
        rtmp = af32(2 * 16 * 8)
        B_rt = S.buf("rtmp")

        def norm_T(xt, Bx, pa, pb, hTd, BhT, xnb=None, Bxnb=None):
            if xnb is None:
                xnb, Bxnb = xn, B_xn
            rms_scale(xt, xnb, Bx, Bxnb, ssA, junk, B_ss)
            for k in range(8):
                bank = pa if k < 4 else pb
                tr(PS[bank][:, (k % 4) * 128:(k % 4 + 1) * 128], xnb[:, k * 128:(k + 1) * 128], [Bxnb], [PB[bank]])
            cp("act", hTd[:, 0:4, :], PS[pa][:, :].rearrange("p (k t) -> p k t", k=4), [PB[pa]], [BhT])
            cp("dve", hTd[:, 4:8, :], PS[pb][:, :].rearrange("p (k t) -> p k t", k=4), [PB[pb]], [BhT])

        xnA = [xn, abf(1024)]
        B_xnA = [B_xn, S.buf("xnA1")]
        hTA = [hT, abf(1024).rearrange("p (k t) -> p k t", k=8)]
        B_hTA = [B_hT, S.buf("hTA1")]
        dma(xt2[0], x_all[0:128, :], [], [B_xt[0]])
        dma(xt2[1], x_all[128:256, :], [], [B_xt[1]])
        norm_T(xt2[0], B_xt[0], 0, 1, hTA[0], B_hTA[0], xnA[0], B_xnA[0])
        for T in range(na):
            s = T % 2
            if T + 1 < na:
                norm_T(xt2[1 - s], B_xt[1 - s], 0, 1, hTA[1 - s], B_hTA[1 - s], xnA[1 - s], B_xnA[1 - s])
            if T + 2 < na:
                dma(xt2[s], x_all[(T + 2) * 128:(T + 3) * 128, :], [], [B_xt[s]])
            hT, B_hT = hTA[s], B_hTA[s]
            for k in range(8):
                mm(PS[2][:, 0:512], hT[:, k, :], wkvA[:, k, :], [B_hT, B_wkvA], [PB[2]], start=(k == 0), stop=(k == 7))
            cp("act", kb, PS[2][:, 0:512], [PB[2]], [B_kb])
            v5 = PS[2][:, 0:512].rearrange("p (j2 jj g d) -> p j2 jj g d", j2=2, jj=2, g=2)
            k5 = kb.rearrange("p (j2 jj g d) -> p j2 jj g d", j2=2, jj=2, g=2)
            rotary(v5[:, :, 0, :, :], k5[:, :, 0, :, :], cosA[:, T, :], sinA[:, T, :], rtmp, [PB[2], B_tabA], [B_kb], B_rt)
            cp("pool", Vs[:, T, :, 0:64], kb[:, 384:512].rearrange("p (g d) -> p g d", g=2), [B_kb], [B_Vs])
            tr(PS[3][:, 0:128], kb[:, 0:128], [B_kb], [PB[3]])
            tr(PS[3][:, 128:256], kb[:, 128:256], [B_kb], [PB[3]])
            tr(PS[3][:, 256:384], kb[:, 256:384], [B_kb], [PB[3]])
            cp("act", KcRaw[:, T * 128:(T + 1) * 128], PS[3][:, 0:128], [PB[3]], [B_KcRaw])
            cp("dve", VcRaw[:, T * 128:(T + 1) * 128], PS[3][:, 128:256], [PB[3]], [B_VcRaw])
            cp("act", KTs[:, T * 128:(T + 1) * 128], PS[3][:, 256:384], [PB[3]], [B_KTs])

        w1z = [abf(32 * 256).rearrange("p (l m) -> p l m", l=32) for _ in range(2)]
        B_w1 = S.buf("w1b")
        memset("pool", w1z[0][64:128, :, :], 0.0, [B_w1])
        memset("pool", w1z[1][0:64, :, :], 0.0, [B_w1])
        w2b = abf(2 * 128).rearrange("p (h d) -> p h d", h=2)
        B_w2 = S.buf("w2b")
        peT = abf(32)
        pef = af32(32)
        B_pe = S.buf("pe")
        hid2 = [abf(2 * 512).rearrange("p (h c) -> p h c", h=2) for _ in range(2)]
        B_hid2 = [S.buf("hid0"), S.buf("hid1")]
        cbias = af32(2)
        B_cb = S.buf("cbias")
        for kvi, raw, Braw in (() if skipcmp else ((0, KcRaw, B_KcRaw), (1, VcRaw, B_VcRaw))):
            w1v = cmp_w1[0][kvi].rearrange("(l d) m -> d l m", d=64)
            for half in range(2):
                for l0 in range(0, 32, 4):
                    k = stg_i[0] % 2
                    stg_i[0] += 1
                    sl = stage[k][64 * half:64 * half + 64, 0:1024]
                    dma(sl.rearrange("p (l m) -> p l m", l=4), w1v[:, l0:l0 + 4, :], [], [B_stage[k]])
                    cp(("pool", "dve")[(l0 // 4) % 2], w1z[half][64 * half:64 * half + 64, l0:l0 + 4, :],
                       sl.rearrange("p (l m) -> p l m", l=4), [B_stage[k]], [B_w1])
            k = stg_i[0] % 2
            stg_i[0] += 1
            dma(stage[k][:, 0:128].rearrange("p (h d) -> p h d", h=2), cmp_w2[0][kvi].rearrange("(h p) d -> p h d", p=128),
                [], [B_stage[k]])
            cp("dve", w2b[:, :, 0:64], stage[k][:, 0:128].rearrange("p (h d) -> p h d", h=2), [B_stage[k]], [B_w2])
            cp("dve", w2b[:, :, 64:128], stage[k][:, 0:128].rearrange("p (h d) -> p h d", h=2), [B_stage[k]], [B_w2])
            dma(pef[0:64, :], cmp_pe[0][kvi].rearrange("l d -> d l"), [], [B_pe], slow=True)
            memset("dve", peT[64:128, :], 0.0, [B_pe])
            cp("dve", peT[0:64, :], pef[0:64, :], [B_pe], [B_pe])
            for half in range(2):
                for l in range(32):
                    mm(PS[4][:, half:half + 1], w1z[0][:, l, half * 128:(half + 1) * 128], peT[:, l:l + 1],
                       [B_w1, B_pe], [PB[4]], start=(l == 0 and half == 0), stop=(l == 31))
            cp("dve", cbias, PS[4][:, 0:2], [PB[4]], [B_cb])
            if not NOBAR:
                barrier()
            if dbg == "A" and kvi == 0 and (DUMPX & 1):
                dma(dbg_t("w1b", [128, 8192], BF16), w1z[0].rearrange("p l m -> p (l m)"), [B_w1], [])
                dma(dbg_t("cbias", [128, 2]), cbias, [B_cb], [])
            rawv = raw.rearrange("p (i s) -> p i s", s=16)
            for g in range(2):
                if not NOBAR:
                    barrier()
                hid, B_hid = hid2[g], B_hid2[g]
                pr = slice(64 * g, 64 * g + 64)
                for half in range(2):
                    for l in range(32):
                        rhs = rawv[:, 0:511, l] if l < 16 else rawv[:, 1:512, l - 16]
                        mm(PS[half][:, 0:511], w1z[g][:, l, half * 128:(half + 1) * 128], rhs, [B_w1, Braw], [PB[half]],
                           start=(l == 0), stop=(l == 31))
                    act(hid[:, half, 0:511], PS[half][:, 0:511], AF.Gelu_apprx_tanh, [PB[half], B_cb], [B_hid],
                        bias=cbias[:, half:half + 1])
                if dbg == "A" and kvi == 0 and (DUMPX & 2):
                    dma(dbg_t(f"hid{g}", [128, 1024], BF16), hid.rearrange("p h c -> p (h c)"), [B_hid], [])
                if kvi == 0:
                    for half in range(2):
                        mm(PS[2][:, 0:511], w2b[:, half, :], hid[:, half, 0:511], [B_w2, B_hid], [PB[2]],
                           start=(half == 0), stop=(half == 1))
                    cp("act", KcT[pr, 0:511], PS[2][pr, 0:511], [PB[2]], [B_KcT])
                else:
                    for c in range(4):
                        m = 128 if c < 3 else 127
                        for half in range(2):
                            mm(PS[2][0:m, c * 64:(c + 1) * 64], hid[:, half, c * 128:c * 128 + m], w2b[:, half, 0:64],
                               [B_w2, B_hid], [PB[2]], start=(half == 0 and c == 0), stop=(half == 1))
                    for c in range(4):
                        m = 128 if c < 3 else 127
                        cp("act", Vc[0:m, c, g, 0:64], PS[2][0:m, c * 64:(c + 1) * 64], [PB[2]], [B_Vc])
        memset("dve", KcT[:, 511:512], 0.0, [B_KcT])
        if dbg == "A":
            fl = [dma(dbg_t("KTs", [128, 8192], BF16), KTs, [B_KTs], []),
                  dma(dbg_t("Vs", [128, 64 * 130], BF16), Vs.rearrange("p t g d -> p (t g d)"), [B_Vs], []),
                  dma(dbg_t("KcT", [128, 512], BF16), KcT, [B_KcT], []),
                  dma(dbg_t("Vc", [128, 4 * 130], BF16), Vc.rearrange("p t g d -> p (t g d)"), [B_Vc], []),
                  dma(dbg_t("KcRaw", [128, 8192], BF16), KcRaw, [B_KcRaw], [])]
            print(S.emit(final_waits=fl))
            return nc
        barrier()
        top[0] = kv_top
        wb1 = abf(8 * 2352).rearrange("p (k c) -> p k c", k=8)
        B_wb1 = S.buf("wb1")
        for k in range(8):
            rows = w_in[0][k * 128:(k + 1) * 128, :]
            sc = gpre[:, k:k + 1]
            loadw(lambda c0, n, k=k: wb1[:, k, c0:c0 + n], rows[:, 0:512], 512, scale=sc, r_extra=[B_small], w=[B_wb1])
            loadw(lambda c0, n, k=k: wb1[:, k, 512:1536], rows[:, 512:1536], 1024, scale=sc, r_extra=[B_small], w=[B_wb1], perm_q=True)
            loadw(lambda c0, n, k=k: wb1[:, k, 1536 + c0:1536 + c0 + n], rows[:, 2048:2864], 816, scale=sc, r_extra=[B_small], w=[B_wb1])
        poolw = abf(4 * 128).rearrange("p (g d) -> p g d", g=4)
        B_cst = S.buf("cst")
        k_ = stg_i[0] % 2
        stg_i[0] += 1
        dma(stage[k_][:, 0:512].rearrange("p (g d) -> p g d", g=4), pool_w[0].rearrange("g c d -> c g d"), [], [B_stage[k_]])
        cp("dve", poolw, stage[k_][:, 0:512].rearrange("p (g d) -> p g d", g=4), [B_stage[k_]], [B_cst])
        pscale = af32(4)
        dma(pscale, pool_scale[0].rearrange("(g d) -> d g", d=128), [], [B_cst], slow=True)
        Ab = abf(1024).rearrange("p (g c t) -> p g c t", g=4, c=2)
        k_ = stg_i[0] % 2
        stg_i[0] += 1
        dma(stage[k_][:, 0:1024], Ab_d, [], [B_stage[k_]])
        cp("dve", Ab.rearrange("p g c t -> p (g c t)"), stage[k_][:, 0:1024], [B_stage[k_]], [B_cst])
        Mw3 = abf(384).rearrange("p (j q) -> p j q", j=3)
        k_ = stg_i[0] % 2
        stg_i[0] += 1
        dma(stage[k_][:, 0:256], Mw_d, [], [B_stage[k_]])
        cp("dve", Mw3[:, 0, :], stage[k_][:, 0:128], [B_stage[k_]], [B_cst])
        cp("dve", Mw3[:, 2, :], stage[k_][:, 128:256], [B_stage[k_]], [B_cst])
        memset("dve", Mw3[:, 1, :], 1.0, [B_cst])
        onesb = abf(1)
        memset("dve", onesb, 1.0, [B_cst])
        qrow = af32(128)
        dma(qrow, qrow_d.partition_broadcast(128), [], [B_cst])
        tqst = af32(NEXT)
        dma(tqst, tqst_d.partition_broadcast(128), [], [B_cst])
        tqt = af32(128)
        B_tqt = S.buf("tqt")
        bsrow = af32(128)
        dma(bsrow, bsrow_d.partition_broadcast(128), [], [B_cst])
        cendrow = af32(512)
        dma(cendrow, cendrow_d.partition_broadcast(128), [], [B_cst])
        kidx = af32(64)
        dma(kidx, kidx_d, [], [B_cst])
        cendcol = af32(4)
        dma(cendcol, cendcol_d, [], [B_cst])
        e0big = af32(128)
        dma(e0big, e0_d.partition_broadcast(128), [], [B_cst])

        xt2 = [af32(1024), af32(1024)]
        B_xt = [S.buf("bxt0"), S.buf("bxt1")]
        xn = abf(1024)
        B_xn = S.buf("bxn")
        junk = abf(1024)
        ssA = af32(1)
        B_ss = S.buf("bss")
        brT0 = abf(24 * 128).rearrange("p (k t) -> p k t", k=24)
        brT = [brT0, brT0]
        B_br0 = S.buf("br0")
        B_br = [B_br0, B_br0]
        rtmp = af32(256)
        B_rt = S.buf("brtmp")

        KmT = abf(4 * 256).rearrange("p (h m) -> p h m", h=4)
        Vm = abf(2 * 4 * 128).rearrange("p (c h d) -> p c h d", c=2, h=4)
        kmb = abf(512)
        mark_m = top[0]
        wmem = abf(8 * 1024).rearrange("p (k c) -> p k c", k=8)
        B_wmem = S.buf("wmem")
        for k in range(8):
            loadw(lambda c0, n, k=k: wmem[:, k, c0:c0 + n], w_mem_kv[0][k * 128:(k + 1) * 128, :], 1024,
                  scale=gmem[:, k:k + 1], r_extra=[B_small], w=[B_wmem])
        B_km, B_vm, B_kmb = S.buf("KmT"), S.buf("Vm"), S.buf("kmb")
        for c in range(2):
            dma(xt2[c], mem_d[c * 128:(c + 1) * 128, :], [], [B_xt[c]])
            norm_T(xt2[c], B_xt[c], 0, 1, brT[c][:, 16:24, :], B_br[c])
            for nb in range(2):
                for k in range(8):
                    mm(PS[2 + nb][:, :], brT[c][:, 16 + k, :], wmem[:, k, nb * 512:(nb + 1) * 512], [B_br[c], B_wmem], [PB[2 + nb]],
                       start=(k == 0), stop=(k == 7))
            cp("act", kmb, PS[2][:, :], [PB[2]], [B_kmb], scale=float(128 ** -0.5))
            cp("dve", Vm[:, c, :, :], PS[3][:, :].rearrange("p (h d) -> p h d", h=4), [PB[3]], [B_vm])
            for h in range(4):
                tr(PS[4][:, h * 128:(h + 1) * 128], kmb[:, h * 128:(h + 1) * 128], [B_kmb], [PB[4]])
            cp("act", KmT[:, :, c * 128:(c + 1) * 128], PS[4][:, :].rearrange("p (h m) -> p h m", h=4), [PB[4]], [B_km])

        barrier()
        top[0] = mark_m
        kwb = abf(256)
        B_kwb = S.buf("kwb")
        ub = [abf(512), abf(512)]
        B_ub = [S.buf("ub0"), S.buf("ub1")]
        uf = af32(512)
        B_uf = S.buf("uf")
        pbb = abf(512)
        B_pbb = S.buf("pbb")
        pT = abf(512).rearrange("p (g t) -> p g t", g=4)
        B_pT = S.buf("pT")
        qb = abf(1024)
        B_qb = S.buf("qb")
        qTz = [abf(1024).rearrange("p (k t) -> p k t", k=8) for _ in range(2)]
        B_qT = S.buf("qT")
        memset("pool", qTz[0], 0.0, [B_qT])
        memset("pool", qTz[1], 0.0, [B_qT])
        gn = af32(48)
        B_gn = S.buf("gn")
        qxb = abf(512)
        B_qxb = S.buf("qxb")
        qxT = abf(512).rearrange("p (h t) -> p h t", h=4)
        B_qxT = S.buf("qxT")
        mpT = [abf(512).rearrange("p (h t) -> p h t", h=4) for _ in range(2)]
        B_mpT = [S.buf("mpT0"), S.buf("mpT1")]
        rsm = af32(4)
        B_rsm = S.buf("rsm")
        ymemb = abf(512)
        B_ymem = S.buf("ymem")
        ef = [af32(512), af32(512)]
        B_ef = [S.buf("ef0"), S.buf("ef1")]
        ssum2 = [af32(2), af32(2)]
        B_ssum2 = [S.buf("ssum0"), S.buf("ssum1")]
        Pb = [af32(516), af32(516)]
        B_Pb = [S.buf("Pb0"), S.buf("Pb1")]
        cmneg = abf(512)
        B_cm = S.buf("cmneg")
        imp = af32(128)
        nd = af32(128)
        itmp = af32(128)
        wk = af32(128)
        mx = af32(16)
        B_imp = S.buf("imp")
        selb = abf(128)
        B_sel = S.buf("sel")
        selT = [abf(128), abf(128)]
        B_selT = [S.buf("selT0"), S.buf("selT1")]
        NSL = 3
        NSU = 6
        ucount = [0]
        mk = [abf(128) for _ in range(NSL)]
        B_mk = [S.buf(f"mk{i}") for i in range(NSL)]
        pTu = [abf(512).rearrange("p (h t) -> p h t", h=4) for _ in range(NSU)]
        B_pTu = [S.buf(f"pTu{i}") for i in range(NSU)]
        ynsa = af32(1024)
        B_yn = S.buf("ynsa")
        ytmp = af32(256)
        B_yt = S.buf("ytmp")
        rs = af32(8)
        B_rs = S.buf("rs")
        ynb = abf(1024)
        B_ynb = S.buf("ynb")
        memset("dve", Pb[0], 0.0, [B_Pb[0]])
        memset("dve", Pb[1], 0.0, [B_Pb[1]])
        nchunk = [0]

        def attend(e, g, br, chunks):
            LOOK = 3
            units = [(n, q) for n in range(len(chunks)) for q in range(2)]
            nu = len(units)
            info = {}

            def stage_scores(u):
                n, q = units[u]
                KT, V, mk_pe, mk_dve = chunks[n]
                if q == 0:
                    cs = nchunk[0]
                    nchunk[0] += 1
                    info[n] = (cs % 2, cs % NSL)
                    if mk_pe is not None:
                        mk_pe(cs % 2)
                bank = 2 + (ucount[0] % 4)
                ub = ucount[0] % NSU
                ucount[0] += 1
                mm(PS[bank][:, :], KT, qTz[g][:, 4 * q:4 * q + 4, :], [B_qT, B_KTs, B_KTw, B_KcT], [PB[bank]])
                return (bank, ub)

            pend = [stage_scores(u) for u in range(min(LOOK, nu))]
            for u, (n, q) in enumerate(units):
                bank, ub = pend.pop(0)
                if u + LOOK < nu:
                    pend.append(stage_scores(u + LOOK))
                KT, V, mk_pe, mk_dve = chunks[n]
                sl, bs = info[n]
                pt = pTu[ub]
                act(pt, PS[bank][:, :].rearrange("p (h t) -> p h t", h=4), AF.Exp, [PB[bank]], [B_pTu[ub]])
                if mk_dve is not None:
                    if q == 0:
                        mk_dve(sl, bs)
                    mb = mk[bs].unsqueeze(1).to_broadcast([128, 4, 128])
                    tt(("dve", "pool")[q], pt, pt, mb, ALU.mult, [B_pTu[ub], B_mk[bs]], [B_pTu[ub]])
                for hh in range(4 * q, 4 * q + 4):
                    mm(PS[q][:, (hh % 4) * 128:(hh % 4) * 128 + 65], pt[:, hh % 4, :], V, [B_pTu[ub], B_Vs, B_Vw, B_Vc], [PB[q]],
                       start=(n == 0 and hh % 4 == 0), stop=(n == len(chunks) - 1))
            gn3 = gn.rearrange("p (h b) -> p h b", b=3)
            for b in range(2):
                Ov = PS[b][:, :].rearrange("p (h d) -> p h d", h=4)
                h0 = 8 * g + 4 * b
                rsb = rs[:, 4 * b:4 * b + 4]
                ts("dve", rsb, Ov[:, :, 64], 1e-30, None, ALU.max, None, [PB[b]], [B_rs])
                S.op("dve", lambda e_, rsb=rsb: e_.reciprocal(out=rsb, in_=rsb), [B_rs], [B_rs])
                tt("dve", rsb, rsb, gn3[:, h0:h0 + 4, br], ALU.mult, [B_rs, B_gn], [B_rs])
                dst = ynsa.rearrange("p (h d) -> p h d", h=16)[:, h0:h0 + 4, :]
                rb = rsb.unsqueeze(2).to_broadcast([128, 4, 64])
                if br == 0:
                    tt("dve", dst, Ov[:, :, 0:64], rb, ALU.mult, [PB[b], B_rs], [B_yn])
                else:
                    yt = ytmp.rearrange("p (h d) -> p h d", h=4)
                    tt("dve", yt, Ov[:, :, 0:64], rb, ALU.mult, [PB[b], B_rs], [B_yt])
                    tt("pool", dst, dst, yt, ALU.add, [B_yn, B_yt], [B_yn])

        dma(xt2[0], x_ext[0:128, :], [], [B_xt[0]])
        for e in range(ntile_b1):
            s = e % 2
            if e + 1 < NEXT:
                dma(xt2[1 - s], x_ext[(e + 1) * 128:(e + 2) * 128, :], [], [B_xt[1 - s]])
            bt = brT[s]
            hTd = bt[:, 16:24, :]
            norm_T(xt2[s], B_xt[s], 0, 1, hTd, B_br[s])
            for k in range(8):
                mm(PS[2][:, 0:256], hTd[:, k, :], wb1[:, k, 1536:1792], [B_br[s], B_wb1], [PB[2]], start=(k == 0), stop=(k == 7))
            cp("act", kwb, PS[2][:, 0:256], [PB[2]], [B_kwb])
            rotary(PS[2][:, 0:128].rearrange("p (a g d) -> p a g d", a=1, g=2), kwb[:, 0:128].rearrange("p (a g d) -> p a g d", a=1, g=2),
                   cosE[:, e, :], sinE[:, e, :], rtmp, [PB[2], B_tabE], [B_kwb], B_rt)
            ts("pool", Vw[:, e, :, 0:64], kwb[:, 128:256].rearrange("p (g d) -> p g d", g=2), wval[:, e:e + 1], None, ALU.mult, None,
               [B_kwb, B_small], [B_Vw])
            for g_ in range(2):
                cp("pool", Vw[:, e, g_, 64:65], wval[:, e:e + 1], [B_small], [B_Vw])
            tr(PS[3][:, 0:128], kwb[:, 0:128], [B_kwb], [PB[3]])
            cp("act", KTw[:, e * 128:(e + 1) * 128], PS[3][:, 0:128], [PB[3]], [B_KTw])
            for k in range(8):
                mm(PS[4][:, :], hTd[:, k, :], wb1[:, k, 0:512], [B_br[s], B_wb1], [PB[4]], start=(k == 0), stop=(k == 7))
            cp("act", ub[s], PS[4][:, :], [PB[4]], [B_ub[s]])
            if e < 4:
                continue
            cp("dve", uf, PS[4][:, :], [PB[4]], [B_uf])
            i = e - 4
            for gi in range(4):
                blk = slice(gi * 128, (gi + 1) * 128)
                mm(PS[5][:, blk], Ab[:, gi, 0, :], ub[s][:, blk], [B_cst, B_ub[s]], [PB[5]], start=True, stop=False)
                mm(PS[5][:, blk], Ab[:, gi, 1, :], ub[1 - s][:, blk], [B_cst, B_ub[1 - s]], [PB[5]], start=False, stop=True)
            for gi in range(4):
                blk = slice(gi * 128, (gi + 1) * 128)
                stt(pbb[:, blk], PS[5][:, blk], invcnt[:, e * 4 + gi:e * 4 + gi + 1], uf[:, blk], ALU.mult, ALU.subtract,
                    [PB[5], B_uf, B_small], [B_pbb])
            for gi in range(4):
                blk = slice(gi * 128, (gi + 1) * 128)
                tr(PS[6][:, blk], pbb[:, blk], [B_pbb], [PB[6]])
            cp("act", pT, PS[6][:, :].rearrange("p (g t) -> p g t", g=4), [PB[6]], [B_pT])
            for gi in range(4):
                blk = slice(gi * 128, (gi + 1) * 128)
                mm(PS[5][:, blk], poolw[:, gi, :], pT[:, gi, :], [B_cst, B_pT], [PB[5]])
            for gi in range(4):
                blk = slice(gi * 128, (gi + 1) * 128)
                ts("dve", bt[:, gi, :], PS[5][:, blk], pscale[:, gi:gi + 1], None, ALU.mult, None, [PB[5], B_cst], [B_br[s]])
            for nb in range(2):
                for k in range(8):
                    mm(PS[nb][:, :], hTd[:, k, :], wb1[:, k, 512 + nb * 512:1024 + nb * 512], [B_br[s], B_wb1], [PB[nb]],
                       start=(k == 0), stop=(k == 7))
            for nb in range(2):
                qv = qb[:, nb * 512:(nb + 1) * 512]
                cp("act", qv, PS[nb][:, :], [PB[nb]], [B_qb], scale=0.125)
                rotary(PS[nb][:, :].rearrange("p (c g d) -> p c g d", c=4, g=2), qv.rearrange("p (c g d) -> p c g d", c=4, g=2),
                       cos8[:, e, :], sin8[:, e, :], rtmp, [PB[nb], B_tabE], [B_qb], B_rt)
            for k in range(8):
                tr(PS[2 + k // 4][:, (k % 4) * 128:(k % 4 + 1) * 128], qb[:, k * 128:(k + 1) * 128], [B_qb], [PB[2 + k // 4]])
            for (bk, c0) in ((2, 0), (3, 4)):
                cp("act", qTz[0][0:64, c0:c0 + 4, :], PS[bk][0:64, :].rearrange("p (k t) -> p k t", k=4), [PB[bk]], [B_qT])
                cp("dve", qTz[1][64:128, c0:c0 + 4, :], PS[bk][64:128, :].rearrange("p (k t) -> p k t", k=4), [PB[bk]], [B_qT])
            for k in range(8):
                mm(PS[4][:, 0:48], hTd[:, k, :], wb1[:, k, 1792:1840], [B_br[s], B_wb1], [PB[4]], start=(k == 0), stop=(k == 7))
            act(gn, PS[4][:, 0:48], AF.Sigmoid, [PB[4]], [B_gn])
            for k in range(8):
                mm(PS[5][:, :], hTd[:, k, :], wb1[:, k, 1840:2352], [B_br[s], B_wb1], [PB[5]], start=(k == 0), stop=(k == 7))
            cp("act", qxb, PS[5][:, :], [PB[5]], [B_qxb])
            for h in range(4):
                tr(PS[6][:, h * 128:(h + 1) * 128], qxb[:, h * 128:(h + 1) * 128], [B_qxb], [PB[6]])
            cp("dve", qxT, PS[6][:, :].rearrange("p (h t) -> p h t", h=4), [PB[6]], [B_qxT])
            for c in range(2):
                for h in range(4):
                    mm(PS[7][:, h * 128:(h + 1) * 128], KmT[:, h, c * 128:(c + 1) * 128], qxT[:, h, :], [B_km, B_qxT], [PB[7]])
                act(mpT[c], PS[7][:, :].rearrange("p (h t) -> p h t", h=4), AF.Exp, [PB[7]], [B_mpT[c]])
            for h in range(4):
                for c in range(2):
                    mm(PS[5][:, h * 128:(h + 1) * 128], mpT[c][:, h, :], Vm[:, c, h, :], [B_mpT[c], B_vm], [PB[5]],
                       start=(c == 0), stop=(c == 1))
            for h in range(4):
                for c in range(2):
                    mm(PS[6][:, h:h + 1], mpT[c][:, h, :], onesb[:, 0:1], [B_mpT[c], B_cst], [PB[6]],
                       start=(c == 0), stop=(c == 1))
            S.op("dve", lambda e_: e_.reciprocal(out=rsm, in_=PS[6][:, 0:4]), [PB[6]], [B_rsm])
            tt("dve", ymemb.rearrange("p (h d) -> p h d", h=4), PS[5][:, :].rearrange("p (h d) -> p h d", h=4),
               rsm.unsqueeze(2).to_broadcast([128, 4, 128]), ALU.mult, [PB[5], B_rsm], [B_ymem])
            for h in range(4):
                tr(PS[7][:, h * 128:(h + 1) * 128], ymemb[:, h * 128:(h + 1) * 128], [B_ymem], [PB[7]])
            cp("act", bt[:, 12:16, :], PS[7][:, :].rearrange("p (h t) -> p h t", h=4), [PB[7]], [B_br[s]])
            tqs = tqe[:, e:e + 1]
            ts("dve", cmneg, cendrow, tqs, -29952.0, ALU.is_gt, ALU.mult, [B_cst, B_small], [B_cm])
            ts("dve", tqt, qrow, tqst[:, e:e + 1], None, ALU.add, None, [B_cst], [B_tqt])
            for g in range(2):
                pr = slice(64 * g, 64 * g + 64)
                Pv = Pb[g][:, 1:513]
                for c_ in range(8):
                    a = c_ % 2
                    ss_, Bs_ = ssum2[a], B_ssum2[a]
                    mm(PS[4 + a][:, :], qTz[g][:, c_, :], KcT[:, 0:512], [B_qT, B_KcT], [PB[4 + a]], start=True, stop=False)
                    mm(PS[4 + a][:, :], ident_b, cmneg, [B_ident, B_cm], [PB[4 + a]], start=False, stop=True)
                    act(ef[a], PS[4 + a][:, :], AF.Exp, [PB[4 + a]], [B_ef[a], Bs_], accum=ss_[:, 0:1])
                    ts("dve", ss_[:, 1:2], ss_[:, 0:1], 1e-30, None, ALU.max, None, [Bs_], [Bs_])
                    S.op("dve", lambda e_, ss_=ss_: e_.reciprocal(out=ss_[:, 1:2], in_=ss_[:, 1:2]), [Bs_], [Bs_])
                    if c_ == 0:
                        ts("dve", Pv, ef[a], ss_[:, 1:2], None, ALU.mult, None, [B_ef[a], Bs_], [B_Pb[g]])
                    else:
                        stt(Pv, ef[a], ss_[:, 1:2], Pv, ALU.mult, ALU.add, [B_ef[a], Bs_, B_Pb[g]], [B_Pb[g]])
                S.op("dve", lambda e_, g=g: e_.tensor_reduce(out=imp, in_=Pb[g][:, 0:512].rearrange("p (j s) -> p j s", s=4),
                                                          axis=AX.X, op=ALU.add), [B_Pb[g]], [B_imp])
                tt("dve", imp, imp, Pb[g][:, 4:516].rearrange("p (j s) -> p j s", s=4)[:, :, 0], ALU.add, [B_Pb[g], B_imp], [B_imp])
                ts("dve", nd, bsrow, tqs, None, ALU.subtract, None, [B_cst, B_small, B_imp], [B_imp])
                ts("dve", itmp, nd, -128.0, BIG, ALU.is_gt, ALU.mult, [B_imp], [B_imp])
                tt("dve", imp, imp, itmp, ALU.add, [B_imp], [B_imp])
                ts("dve", itmp, nd, 0.0, -3.0 * BIG, ALU.is_gt, ALU.mult, [B_imp], [B_imp])
                tt("dve", imp, imp, itmp, ALU.add, [B_imp], [B_imp])
                tt("dve", imp, imp, e0big, ALU.add, [B_imp, B_cst], [B_imp])
                S.op("dve", lambda e_: e_.max(out=mx[:, 0:8], in_=imp), [B_imp], [B_imp])
                S.op("dve", lambda e_: e_.match_replace(out=wk, in_to_replace=mx[:, 0:8], in_values=imp, imm_value=-1e30), [B_imp], [B_imp])
                S.op("dve", lambda e_: e_.max(out=mx[:, 8:16], in_=wk), [B_imp], [B_imp])
                ts("dve", selb, imp, mx[:, 15:16], None, ALU.is_ge, None, [B_imp], [B_sel])
                tr(PS[6][:, g * 128:(g + 1) * 128], selb, [B_sel], [PB[6]])
                cp("act", selT[g], PS[6][:, g * 128:(g + 1) * 128], [PB[6]], [B_selT[g]])
            for g in range(2):
                pr = slice(64 * g, 64 * g + 64)

                def mk_cmp(c):
                    return (None, lambda sl, bs: ts("dve", mk[bs], tqt, cendcol[:, c:c + 1], None, ALU.is_ge, None, [B_cst, B_tqt], [B_mk[bs]]))

                def mk_win(j, ee):
                    if 1 <= j <= 3:
                        return (None, None)
                    v = 0 if j == 0 else (2 if j == 4 else 1)
                    return (None, lambda sl, bs: ts("dve", mk[bs], Mw3[:, v, :], wval[:, ee:ee + 1], None, ALU.mult, None, [B_cst, B_small], [B_mk[bs]]))

                def mk_sel(cc, g=g):
                    def f_pe(sl):
                        mm(PS[6 + sl][:, 256:384], Gb[:, cc * 128:(cc + 1) * 128], selT[g], [B_G, B_selT[g]], [PB[6 + sl]])

                    def f_dve(sl, bs):
                        stt(mk[bs], tqt, kidx[:, cc:cc + 1], PS[6 + sl][:, 256:384], ALU.is_ge, ALU.mult,
                            [B_cst, B_tqt, PB[6 + sl]], [B_mk[bs]])
                    return (f_pe, f_dve)

                attend(e, g, 0, [(KcT[:, c * 128:(c + 1) * 128], Vc[:, c, g, :]) + mk_cmp(c) for c in range(4)])
                attend(e, g, 1, [(KTs[:, cc * 128:(cc + 1) * 128], Vs[:, cc, g, :]) + mk_sel(cc) for cc in range(48 + i)])
                attend(e, g, 2, [(KTw[:, (e - 4 + j) * 128:(e - 3 + j) * 128], Vw[:, e - 4 + j, g, :]) + mk_win(j, e - 4 + j)
                                 for j in range(5)])
            cp("act", ynb, ynsa, [B_yn], [B_ynb])
            for k in range(8):
                tr(PS[2 + k // 4][:, (k % 4) * 128:(k % 4 + 1) * 128], ynb[:, k * 128:(k + 1) * 128], [B_ynb], [PB[2 + k // 4]])
            cp("act", bt[:, 4:8, :], PS[2][:, :].rearrange("p (k t) -> p k t", k=4), [PB[2]], [B_br[s]])
            cp("dve", bt[:, 8:12, :], PS[3][:, :].rearrange("p (k t) -> p k t", k=4), [PB[3]], [B_br[s]])
            lastbr = dma(br_scr[i], bt.rearrange("p k t -> p (k t)"), [B_br[s]], [])
            if dbg == "B1":
                lastbr = dma(dbg_t(f"br{i}", [128, 24 * 128], BF16), bt.rearrange("p k t -> p (k t)"), [B_br[s]], [])
                fl1 = [lastbr, dma(dbg_t(f"ynsa{i}", [128, 1024]), ynsa, [B_yn], []),
                       dma(dbg_t(f"qb{i}", [128, 1024], BF16), qb, [B_qb], []),
                       dma(dbg_t(f"gn{i}", [128, 48]), gn, [B_gn], []),
                       dma(dbg_t(f"selT{i}", [128, 128], BF16), selT[1], [B_selT[1]], []),
                       dma(dbg_t(f"Pb{i}", [128, 516]), Pb[1], [B_Pb[1]], [])]
        if dbg == "B1":
            print(S.emit(final_waits=fl1))
            return nc
        barrier()
        top[0] = persist_top
        wg = abf(8 * 3072).rearrange("p (k c) -> p k c", k=8)
        wbp = abf(4 * 1024).rearrange("p (k c) -> p k c", k=4)
        wbn = abf(8 * 1024).rearrange("p (k c) -> p k c", k=8)
        wbx = abf(4 * 1024).rearrange("p (k c) -> p k c", k=4)
        wo = abf(8 * 1024).rearrange("p (k c) -> p k c", k=8)
        B_w2p = S.buf("w_b2")
        for k in range(8):
            loadw(lambda c0, n, k=k: wg[:, k, c0:c0 + n], w_in[0][k * 128:(k + 1) * 128, 2864:5936], 3072,
                  scale=gpre[:, k:k + 1], r_extra=[B_small], w=[B_w2p])
            loadw(lambda c0, n, k=k: wbn[:, k, c0:c0 + n], w_br_nsa[0][k * 128:(k + 1) * 128, :], 1024, w=[B_w2p])
            loadw(lambda c0, n, k=k: wo[:, k, c0:c0 + n], w_out[0][k * 128:(k + 1) * 128, :], 1024, w=[B_w2p])
            if k < 4:
                loadw(lambda c0, n, k=k: wbp[:, k, c0:c0 + n], w_br_pool[0][k * 128:(k + 1) * 128, :], 1024, w=[B_w2p])
                loadw(lambda c0, n, k=k: wbx[:, k, c0:c0 + n], w_br_xa[0][k * 128:(k + 1) * 128, :], 1024, w=[B_w2p])
        gpost = af32(1024)
        B_gp = S.buf("gpost")
        dma(gpost, post_mix_g.partition_broadcast(128), [], [B_gp])
        brT = [abf(24 * 128).rearrange("p (k t) -> p k t", k=24) for _ in range(2)]
        B_br = [S.buf("c_br0"), S.buf("c_br1")]
        xt2 = [af32(1024), af32(1024)]
        B_xt = [S.buf("c_xt0"), S.buf("c_xt1")]
        sg = af32(1024)
        B_sg = S.buf("sg")
        yy = af32(1024)
        B_y = S.buf("yy")
        ytm = af32(1024)
        B_ytm = S.buf("ytm")
        yb = abf(1024)
        B_yb = S.buf("yb")
        yT = abf(1024).rearrange("p (k t) -> p k t", k=8)
        B_yT = S.buf("yT")
        junk = abf(512)
        x1t = [af32(1024), af32(1024)]
        B_x1 = [S.buf("x1t0"), S.buf("x1t1")]

        def post_norm_res(pa, pb, gp, Bg, xres, Bxres, dst, Bdst):
            act(junk[:, 0:512], PS[pa][:, :], AF.Square, [PB[pa]], [B_ss2], accum=ss2[:, 0:1])
            act(junk[:, 0:512], PS[pb][:, :], AF.Square, [PB[pb]], [B_ss2], accum=ss2[:, 1:2])
            tt("dve", ss2[:, 2:3], ss2[:, 0:1], ss2[:, 1:2], ALU.add, [B_ss2], [B_ss2])
            ts("dve", ss2[:, 2:3], ss2[:, 2:3], 1.0 / 1024, 1e-6, ALU.mult, ALU.add, [B_ss2], [B_ss2])
            act(ss2[:, 2:3], ss2[:, 2:3], AF.Sqrt, [B_ss2], [B_ss2])
            S.op("dve", lambda e_: e_.reciprocal(out=ss2[:, 3:4], in_=ss2[:, 2:3]), [B_ss2], [B_ss2])
            for nb, bank in enumerate((pa, pb)):
                blk = slice(nb * 512, (nb + 1) * 512)
                stt(dst[:, blk], PS[bank][:, :], ss2[:, 3:4], gp[:, blk], ALU.mult, ALU.mult, [PB[bank], B_ss2, Bg], [Bdst])
                tt("pool", dst[:, blk], dst[:, blk], xres[:, blk], ALU.add, [Bdst, Bxres], [Bdst])

        dma(brT[0].rearrange("p k t -> p (k t)"), br_scr[0], [], [B_br[0]])
        dma(xt2[0], x_ext[4 * 128:5 * 128, :], [], [B_xt[0]])
        for i in range(17):
            s = i % 2
            e = i + 4
            if i + 1 < 17:
                dma(brT[1 - s].rearrange("p k t -> p (k t)"), br_scr[i + 1], [], [B_br[1 - s]])
                dma(xt2[1 - s], x_ext[(e + 1) * 128:(e + 2) * 128, :], [], [B_xt[1 - s]])
            bt = brT[s]
            for br in range(3):
                for nb in range(2):
                    for k in range(8):
                        mm(PS[nb][:, :], bt[:, 16 + k, :], wg[:, k, br * 1024 + nb * 512:br * 1024 + (nb + 1) * 512],
                           [B_br[s], B_w2p], [PB[nb]], start=(k == 0), stop=(k == 7))
                wsel, off, nk = ((wbp, 0, 4), (wbn, 4, 8), (wbx, 12, 4))[br]
                for nb in range(2):
                    for k in range(nk):
                        mm(PS[2 + nb][:, :], bt[:, off + k, :], wsel[:, k, nb * 512:(nb + 1) * 512],
                           [B_br[s], B_w2p], [PB[2 + nb]], start=(k == 0), stop=(k == nk - 1))
                for nb in range(2):
                    blk = slice(nb * 512, (nb + 1) * 512)
                    act(sg[:, blk], PS[nb][:, :], AF.Sigmoid, [PB[nb]], [B_sg])
                    if br == 0:
                        tt("dve", yy[:, blk], sg[:, blk], PS[2 + nb][:, :], ALU.mult, [B_sg, PB[2 + nb]], [B_y])
                    else:
                        tt("dve", ytm[:, blk], sg[:, blk], PS[2 + nb][:, :], ALU.mult, [B_sg, PB[2 + nb]], [B_ytm])
                        tt("pool", yy[:, blk], yy[:, blk], ytm[:, blk], ALU.add, [B_y, B_ytm], [B_y])
            cp("act", yb, yy, [B_y], [B_yb])
            for k in range(8):
                tr(PS[4 + k // 4][:, (k % 4) * 128:(k % 4 + 1) * 128], yb[:, k * 128:(k + 1) * 128], [B_yb], [PB[4 + k // 4]])
            cp("act", yT[:, 0:4, :], PS[4][:, :].rearrange("p (k t) -> p k t", k=4), [PB[4]], [B_yT])
            cp("dve", yT[:, 4:8, :], PS[5][:, :].rearrange("p (k t) -> p k t", k=4), [PB[5]], [B_yT])
            for nb in range(2):
                for k in range(8):
                    mm(PS[6 + nb][:, :], yT[:, k, :], wo[:, k, nb * 512:(nb + 1) * 512], [B_yT, B_w2p], [PB[6 + nb]],
                       start=(k == 0), stop=(k == 7))
            post_norm_res(6, 7, gpost, B_gp, xt2[s], B_xt[s], x1t[s], B_x1[s])
            lx = dma(x1_scr[i], x1t[s], [B_x1[s]], [])
            if dbg == "B2":
                lx = dma(dbg_t(f"x1_{i}", [128, 1024]), x1t[s], [B_x1[s]], [])
        if dbg == "B2":
            print(S.emit(final_waits=[lx]))
            return nc
        barrier()
        top[0] = persist_top

        wup = abf(8 * 5632).rearrange("p (k c) -> p k c", k=8)
        wdn = abf(22 * 1024).rearrange("p (k c) -> p k c", k=22)
        B_w3 = S.buf("w_c")
        for k in range(8):
            loadw(lambda c0, n, k=k: wup[:, k, c0:c0 + n], w_up[0][k * 128:(k + 1) * 128, :], 5632,
                  scale=gffn[:, k:k + 1], r_extra=[B_small], w=[B_w3])
        for k in range(22):
            loadw(lambda c0, n, k=k: wdn[:, k, c0:c0 + n], w_down[0][k * 128:(k + 1) * 128, :], 1024, w=[B_w3])
        convp = af32(44 * 4).rearrange("p (j c) -> p j c", c=4)
        B_cv = S.buf("convp")
        for kk in range(3):
            dma(convp[:, :, kk], conv_w[0][kk].rearrange("(j p) -> p j", p=128), [], [B_cv], slow=True)
        dma(convp[:, :, 3], conv_b[0].rearrange("(j p) -> p j", p=128), [], [B_cv], slow=True)
        gpost2 = af32(1024)
        B_gp2 = S.buf("gpost2")
        dma(gpost2, post_ffn_g.partition_broadcast(128), [], [B_gp2])
        xt2 = [af32(1024), af32(1024)]
        B_xt = [S.buf("d_xt0"), S.buf("d_xt1")]
        xn = abf(1024)
        B_xn = S.buf("d_xn")
        junk = abf(1024)
        ssA = af32(2)
        B_ss = S.buf("d_ss")
        h2Ts = [abf(8 * 130).rearrange("p (k t) -> p k t", k=8) for _ in range(2)]
        B_h2s = [S.buf("h2T0"), S.buf("h2T1")]
        xnC = [xn, abf(1024)]
        B_xnC = [B_xn, S.buf("xnC1")]
        aT = abf(22 * 128).rearrange("p (k t) -> p k t", k=22)
        B_aT = S.buf("aT")
        cg = [af32(128), af32(128)]
        cv = [af32(128), af32(128)]
        gl = [af32(128), af32(128)]
        B_cg = [S.buf("cg0"), S.buf("cg1")]
        B_cvb = [S.buf("cv0"), S.buf("cv1")]
        B_gl = [S.buf("gl0"), S.buf("gl1")]
        ot = [af32(1024), af32(1024)]
        B_ot = [S.buf("ot0"), S.buf("ot1")]
        xt3 = [xt2[0], xt2[1], af32(1024)]
        B_xt3 = [B_xt[0], B_xt[1], S.buf("d_xt2")]
        fins = []

        def c_stage1(i):
            s_ = i % 2
            x3 = i % 3
            rms_scale(xt3[x3], xnC[s_], B_xt3[x3], B_xnC[s_], ssA[:, 0:1], junk, B_ss)
            for k in range(8):
                bank = k // 4
                tr(PS[bank][:, (k % 4) * 128:(k % 4 + 1) * 128], xnC[s_][:, k * 128:(k + 1) * 128], [B_xnC[s_]], [PB[bank]])
            cp("act", h2Ts[s_][:, 0:4, 2:130], PS[0][:, :].rearrange("p (k t) -> p k t", k=4), [PB[0]], [B_h2s[s_]])
            cp("dve", h2Ts[s_][:, 4:8, 2:130], PS[1][:, :].rearrange("p (k t) -> p k t", k=4), [PB[1]], [B_h2s[s_]])
            if i == 1:
                ts("pool", h2Ts[s_][:, :, 0:2], h2Ts[1 - s_][:, :, 128:130], hflag[:, 0:1], None, ALU.mult, None,
                   [B_h2s[1 - s_], B_small, B_h2s[s_]], [B_h2s[s_]])
            elif i > 1:
                cp("pool", h2Ts[s_][:, :, 0:2], h2Ts[1 - s_][:, :, 128:130], [B_h2s[1 - s_], B_h2s[s_]], [B_h2s[s_]])

        dma(xt3[0], x1_scr[0], [], [B_xt3[0]])
        dma(xt3[1], x1_scr[1], [], [B_xt3[1]])
        dma(xt3[2], x1_scr[2], [], [B_xt3[2]])
        c_stage1(0)
        for i in range(17):
            s = i % 2
            if i + 1 < 17:
                c_stage1(i + 1)
            if i == 0:
                dma(xt3[0], x1_scr[3], [], [B_xt3[0]])
                continue
            h2T, B_h2 = h2Ts[s], B_h2s[s]
            for j in range(22):
                a = j % 2
                bg, bv = 2 + 2 * a, 3 + 2 * a
                for k in range(8):
                    mm(PS[bg][:, 0:130], wup[:, k, j * 128:(j + 1) * 128], h2T[:, k, :], [B_w3, B_h2], [PB[bg]],
                       start=(k == 0), stop=(k == 7))
                for k in range(8):
                    mm(PS[bv][:, 0:130], wup[:, k, (22 + j) * 128:(23 + j) * 128], h2T[:, k, :], [B_w3, B_h2], [PB[bv]],
                       start=(k == 0), stop=(k == 7))
                for (bank, dstc, Bd, jj) in ((bg, cg[a], B_cg[a], j), (bv, cv[a], B_cvb[a], 22 + j)):
                    act(dstc, PS[bank][:, 2:130], AF.Identity, [PB[bank], B_cv], [Bd], bias=convp[:, jj, 3:4], scale=convp[:, jj, 2:3])
                    stt(dstc, PS[bank][:, 1:129], convp[:, jj, 1:2], dstc, ALU.mult, ALU.add, [PB[bank], B_cv, Bd], [Bd])
                    stt(dstc, PS[bank][:, 0:128], convp[:, jj, 0:1], dstc, ALU.mult, ALU.add, [PB[bank], B_cv, Bd], [Bd])
                act(gl[a], cg[a], AF.Gelu_apprx_tanh, [B_cg[a]], [B_gl[a]])
                tt("pool", aT[:, j, :], gl[a], cv[a], ALU.mult, [B_gl[a], B_cvb[a]], [B_aT])
            for nb in range(2):
                for j in range(22):
                    mm(PS[6 + nb][:, :], aT[:, j, :], wdn[:, j, nb * 512:(nb + 1) * 512], [B_aT, B_w3], [PB[6 + nb]],
                       start=(j == 0), stop=(j == 21))
            post_norm_res(6, 7, gpost2, B_gp2, xt3[i % 3], B_xt3[i % 3], ot[s], B_ot[s])
            fins.append(dma(out_d[(i - 1) * 128:i * 128, :], ot[s], [B_ot[s]], []))
            if i + 3 < 17:
                dma(xt3[i % 3], x1_scr[i + 3], [], [B_xt3[i % 3]])
        stats = S.emit(final_waits=fins)
        print("emit stats", stats, flush=True)
    return nc


def _consts():
    c = {}
    c["ident"] = np.eye(128, dtype=np.float32)
    k = np.arange(8192)
    c["Gm"] = (k[None, :] // 64 == np.arange(128)[:, None]).astype(np.float32)
    c["invf"] = (np.float32(500000.0) ** (-np.arange(8, dtype=np.float32) * np.float32(2.0 / 16))).astype(np.float32).reshape(1, 8)
    c["bsrow"] = (64.0 * np.arange(128, dtype=np.float32)).reshape(1, 128)
    ce = (16.0 * np.arange(512, dtype=np.float32) + 31.0)
    ce[511] = 1e9
    c["cendrow"] = ce.reshape(1, 512)
    c["kidx"] = (128.0 * np.arange(64)[None, :] + np.arange(128)[:, None]).astype(np.float32)
    c["cendcol"] = np.ascontiguousarray(ce.reshape(4, 128).T)
    p = np.arange(128)[:, None]
    q = np.arange(128)[None, :]
    c["Mw"] = np.concatenate([(q < p), (q >= p)], axis=1).astype(np.float32)
    e0 = np.zeros((1, 128), np.float32)
    e0[0, 0] = BIG
    c["e0big"] = e0
    A = np.zeros((128, 4, 2, 128), np.float32)
    for gi, w in enumerate((2, 4, 8, 16)):
        tp = np.arange(128)[:, None]
        t = np.arange(128)[None, :]
        A[:, gi, 0, :] = ((t - tp >= 0) & (t - tp < w))
        A[:, gi, 1, :] = ((t + 128 - tp >= 0) & (t + 128 - tp < w))
    c["Aband"] = A.reshape(128, 1024)
    c["qrow"] = np.arange(128, dtype=np.float32).reshape(1, 128)
    return c


_PROG = {}


def kernel(**inputs):
    x = np.asarray(inputs["x"], dtype=np.float32)
    mem = np.asarray(inputs["mem"], dtype=np.float32)
    positions = np.asarray(inputs["positions"]).astype(np.int32)
    if inputs.get("_return_maps"):
        nc = None
    else:
        if "nc" not in _PROG:
            _PROG["nc"] = build()
        nc = _PROG["nc"]
    consts = _consts()
    wnames = ["pre_mix_g", "w_in", "pool_w", "pool_scale", "cmp_pe", "cmp_w1", "cmp_w2", "mem_norm_g", "w_mem_kv",
              "w_br_pool", "w_br_nsa", "w_br_xa", "w_out", "post_mix_g", "pre_ffn_g", "w_up", "conv_w", "conv_b",
              "w_down", "post_ffn_g"]
    shared = {n: np.ascontiguousarray(np.asarray(inputs[n], dtype=np.float32)) for n in wnames}
    in_maps = []
    for core in range(8):
        b, r = core // 4, core % 4
        m = dict(shared)
        m.update(consts)
        m["x_all"] = np.ascontiguousarray(x[b])
        m["mem"] = np.ascontiguousarray(mem[b])
        xe = np.zeros((NEXT * 128, 1024), np.float32)
        pe = np.zeros((NEXT, 128), np.int32)
        tq = np.zeros((NEXT, 128), np.float32)
        wv = np.zeros((1, NEXT), np.float32)
        ic = np.ones((128, NEXT, 4), np.float32)
        tst = np.zeros((1, NEXT), np.float32)
        for e in range(NEXT):
            ge = 16 * r - 5 + e
            tq[e] = ge * 128 + np.arange(128)
            tst[0, e] = ge * 128
            if ge >= 0:
                xe[e * 128:(e + 1) * 128] = x[b, ge * 128:(ge + 1) * 128]
                pe[e] = positions[b, ge * 128:(ge + 1) * 128]
                wv[0, e] = 1.0
                t = ge * 128 + np.arange(128)
                for gi, w in enumerate((2, 4, 8, 16)):
                    ic[:, e, gi] = 1.0 / np.minimum(t + 1, w).astype(np.float32)
        m["x_ext"] = xe
        m["pos_ext"] = np.ascontiguousarray(pe.T)
        m["pos_all"] = np.ascontiguousarray(positions[b].reshape(NALL, 128).T)
        m["tq_ext"] = np.ascontiguousarray(tq.T)
        m["tqstart"] = tst
        m["wvalid"] = wv
        m["invcnt"] = np.ascontiguousarray(ic.reshape(128, NEXT * 4))
        m["hflag"] = np.array([[0.0 if r == 0 else 1.0]], np.float32)
        in_maps.append(m)
    if inputs.get("_return_maps"):
        return in_maps
    res = run_bass_kernel_spmd(nc, in_maps, core_ids=list(range(8)))
    out = np.zeros((2, 8192, 1024), np.float32)
    for core in range(8):
        b, r = core // 4, core % 4
        out[b, r * 2048:(r + 1) * 2048] = res.results[core]["out"]
    return out
```

```python
import numpy as np
from contextlib import ExitStack
import concourse.bass as bass
import concourse.mybir as mybir
from concourse.bass_utils import run_bass_kernel_spmd

F32 = mybir.dt.float32
BF16 = mybir.dt.bfloat16
I32 = mybir.dt.int32
AF = mybir.ActivationFunctionType
ALU = mybir.AluOpType
AX = mybir.AxisListType


import sys as _sys


def _where():
    f = _sys._getframe(2)
    out = []
    while f is not None and len(out) < 4:
        if f.f_code.co_name != "<lambda>":
            out.append(f.f_lineno)
        f = f.f_back
    return out


class Buf:
    __slots__ = ("name", "writers", "readers", "excl", "last")

    def __init__(self, name, excl=False):
        self.name = name
        self.writers = []
        self.readers = []
        self.excl = excl
        self.last = {}


class Ins:
    __slots__ = ("eng", "fn", "deps", "idx", "flag", "tok", "dma", "pre", "where")

    def __init__(self, eng, fn, dma):
        self.eng = eng
        self.fn = fn
        self.deps = []
        self.flag = False
        self.tok = None
        self.dma = dma
        self.pre = None


class Sched:
    ENGS = ("pe", "dve", "act", "pool", "sp")
    EPOCH = 8000
    NDMA = 24

    def __init__(self, nc, stack):
        self.nc = nc
        self.stack = stack
        self.q = {e: [] for e in self.ENGS}
        self.nbuf = 0

    def buf(self, name=None, excl=False):
        self.nbuf += 1
        return Buf(name or f"b{self.nbuf}", excl)

    def op(self, eng, fn, r=(), w=(), dma=False):
        ins = Ins(eng, fn, dma)
        ins.where = _where()
        deps = []
        for b in r:
            deps.extend(b.writers)
        for b in w:
            deps.extend(b.readers)
        for b in list(r) + list(w):
            if b.excl:
                for en, li in b.last.items():
                    if en != eng:
                        deps.append(li)
                b.last[eng] = ins
        for b in w:
            if b.readers or (b in r):
                b.writers = [ins]
                b.readers = []
            else:
                b.writers.append(ins)
                if len(b.writers) > 48:
                    b.writers = b.writers[-48:]
        for b in r:
            if b not in w:
                b.readers.append(ins)
                if len(b.readers) > 48:
                    b.readers = b.readers[-48:]
        seen = set()
        for d in deps:
            if d is ins or id(d) in seen:
                continue
            if d.eng == "pe" and eng == "pe" and not d.dma:
                continue
            seen.add(id(d))
            ins.deps.append(d)
            d.flag = True
        self.q[eng].append(ins)
        return ins

    def emit(self, final_waits=()):
        nc = self.nc
        stack = self.stack
        sems = {}
        for e in self.ENGS:
            n = 0
            for ins in self.q[e]:
                if ins.dma:
                    continue
                if ins.flag:
                    n += 1
                    ins.idx = n
            nep = (n + self.EPOCH - 1) // self.EPOCH
            sems[e] = [stack.enter_context(nc.semaphore(f"s_{e}_{k}")) for k in range(max(nep, 1))]
        dsems = [stack.enter_context(nc.semaphore(f"s_dma_{k}")) for k in range(self.NDMA)]
        duse = [0] * self.NDMA
        dma_engs = [e for e in self.ENGS if any(i.dma for i in self.q[e])]
        share = {}
        if dma_engs:
            per = self.NDMA // len(dma_engs)
            for k, e in enumerate(dma_engs):
                share[e] = list(range(k * per, (k + 1) * per))
        for e in dma_engs:
            j = 0
            for ins in self.q[e]:
                if not ins.dma:
                    continue
                s = share[e][j % len(share[e])]
                j += 1
                prev = duse[s]
                duse[s] += 1
                ins.tok = (dsems[s], 16 * duse[s])
                ins.pre = (dsems[s], 16 * prev) if prev > 0 else None
        for e in self.ENGS:
            for ins in self.q[e]:
                if ins.dma or not ins.flag:
                    continue
                k = (ins.idx - 1) // self.EPOCH
                ins.tok = (sems[e][k], (ins.idx - 1) % self.EPOCH + 1)
        engobj = {"pe": "tensor", "dve": "vector", "act": "scalar", "pool": "gpsimd", "sp": "sync"}
        stats = {}
        with nc.Block() as block:
            for e in self.ENGS:
                lst = self.q[e]

                def body(eng, lst=lst, e=e):
                    waited = {}
                    nw = 0

                    def wait(tok):
                        nonlocal nw
                        sem, val = tok
                        key = id(sem)
                        if waited.get(key, 0) >= val:
                            return
                        waited[key] = val
                        eng.wait_ge(sem, val)
                        nw += 1

                    for ins in lst:
                        if ins.pre is not None:
                            wait(ins.pre)
                        for d in ins.deps:
                            wait(d.tok)
                        try:
                            bi = ins.fn(eng)
                        except BaseException:
                            print("FAILED op recorded at lines", ins.where, flush=True)
                            raise
                        if ins.dma:
                            bi.then_inc(ins.tok[0], 16)
                        elif ins.flag:
                            bi.then_inc(ins.tok[0], 1)
                    if e == "sp":
                        for fw in final_waits:
                            wait(fw.tok)
                    stats[e] = (len(lst), nw)

                getattr(block, engobj[e])(body)
        return stats


import os as _os
DUMPX = int(_os.environ.get('DUMPX', '0'))
NOBAR = int(_os.environ.get('NOBAR', '0'))
NEXT = 21
NALL = 64
BIG = 100.0
TWO_PI = float(2 * np.pi)


def build(dbg=None, ntile_b1=NEXT, na=NALL, skipcmp=False):
    nc = bass.Bass("TRN2", target_bir_lowering=False)

    def din(name, shape, dt=F32):
        return nc.dram_tensor(name, list(shape), dt, kind="ExternalInput").ap()

    x_all = din("x_all", [8192, 1024])
    x_ext = din("x_ext", [NEXT * 128, 1024])
    pos_all = din("pos_all", [128, NALL], I32)
    pos_ext = din("pos_ext", [128, NEXT], I32)
    tq_ext = din("tq_ext", [128, NEXT])
    qrow_d = din("qrow", [1, 128])
    tqst_d = din("tqstart", [1, NEXT])
    wvalid_d = din("wvalid", [1, NEXT])
    invcnt_d = din("invcnt", [128, NEXT * 4])
    hflag_d = din("hflag", [1, 1])
    mem_d = din("mem", [256, 1024])
    ident_d = din("ident", [128, 128])
    G_d = din("Gm", [128, 8192])
    invf_d = din("invf", [1, 8])
    bsrow_d = din("bsrow", [1, 128])
    cendrow_d = din("cendrow", [1, 512])
    kidx_d = din("kidx", [128, 64])
    cendcol_d = din("cendcol", [128, 4])
    Mw_d = din("Mw", [128, 256])
    e0_d = din("e0big", [1, 128])
    Ab_d = din("Aband", [128, 1024])
    pre_mix_g = din("pre_mix_g", [1, 1024])
    w_in = din("w_in", [1, 1024, 5936])
    pool_w = din("pool_w", [1, 4, 128, 128])
    pool_scale = din("pool_scale", [1, 512])
    cmp_pe = din("cmp_pe", [1, 2, 32, 64])
    cmp_w1 = din("cmp_w1", [1, 2, 2048, 256])
    cmp_w2 = din("cmp_w2", [1, 2, 256, 64])
    mem_norm_g = din("mem_norm_g", [1, 1024])
    w_mem_kv = din("w_mem_kv", [1, 1024, 1024])
    w_br_pool = din("w_br_pool", [1, 512, 1024])
    w_br_nsa = din("w_br_nsa", [1, 1024, 1024])
    w_br_xa = din("w_br_xa", [1, 512, 1024])
    w_out = din("w_out", [1, 1024, 1024])
    post_mix_g = din("post_mix_g", [1, 1024])
    pre_ffn_g = din("pre_ffn_g", [1, 1024])
    w_up = din("w_up", [1, 1024, 5632])
    conv_w = din("conv_w", [1, 3, 5632])
    conv_b = din("conv_b", [1, 5632])
    w_down = din("w_down", [1, 2816, 1024])
    post_ffn_g = din("post_ffn_g", [1, 1024])
    out_d = nc.dram_tensor("out", [16 * 128, 1024], F32, kind="ExternalOutput").ap()
    br_scr = nc.dram_tensor("br_scr", [17, 128, 24 * 128], BF16, kind="Internal").ap()
    x1_scr = nc.dram_tensor("x1_scr", [17, 128, 1024], F32, kind="Internal").ap()
    dbg_out = {}

    def dbg_t(name, shape, dt=F32):
        return nc.dram_tensor("dbg_" + name, list(shape), dt, kind="ExternalOutput").ap()

    with ExitStack() as st:
        S = Sched(nc, st)
        ARN = 95800
        arena = st.enter_context(nc.sbuf_tensor("arena", [128, ARN], BF16))
        PS = [st.enter_context(nc.psum_tensor(f"ps{i}", [128, 512], F32)) for i in range(8)]
        PB = [S.buf(f"ps{i}", excl=True) for i in range(8)]
        top = [0]

        def abf(n):
            a = arena[:, top[0]:top[0] + n]
            top[0] += (n + 31) // 32 * 32
            assert top[0] <= ARN, top[0]
            return a

        def af32(n):
            a = arena[:, top[0]:top[0] + 2 * n].bitcast(F32)
            top[0] += (n + 15) // 16 * 32
            assert top[0] <= ARN, top[0]
            return a

        def ai32(n):
            a = arena[:, top[0]:top[0] + 2 * n].bitcast(I32)
            top[0] += (n + 15) // 16 * 32
            assert top[0] <= ARN, top[0]
            return a

        def mm(out, lhsT, rhs, r, w, start=True, stop=True):
            return S.op("pe", lambda e: e.matmul(out, lhsT=lhsT, rhs=rhs, start=start, stop=stop,
                                                 skip_group_check=True), r, w)

        last_func = [None]

        def act(out, in_, func, r, w, bias=None, scale=None, accum=None):
            if func != last_func[0]:
                last_func[0] = func
                S.op("act", lambda e: e.activation(out=bsc[1][:, 0:1], in_=bsc[3][:, 0:1], func=func), [B_bsc3], [])
            kw = {}
            if bias is not None:
                kw["bias"] = bias
            if scale is not None:
                kw["scale"] = scale
            if accum is not None:
                kw["accum_out"] = accum
            return S.op("act", lambda e: e.activation(out=out, in_=in_, func=func, **kw), r, w)

        def ts(eng, out, in0, s1, s2, op0, op1, r, w):
            if s2 is None:
                return S.op(eng, lambda e: e.tensor_scalar(out=out, in0=in0, scalar1=s1, scalar2=None, op0=op0), r, w)
            return S.op(eng, lambda e: e.tensor_scalar(out=out, in0=in0, scalar1=s1, scalar2=s2, op0=op0, op1=op1), r, w)

        def tt(eng, out, in0, in1, op, r, w):
            return S.op(eng, lambda e: e.tensor_tensor(out=out, in0=in0, in1=in1, op=op), r, w)

        def stt(out, in0, scalar, in1, op0, op1, r, w, accum=None):
            if accum is None:
                return S.op("dve", lambda e: e.scalar_tensor_tensor(out=out, in0=in0, scalar=scalar, in1=in1, op0=op0, op1=op1), r, w)
            return S.op("dve", lambda e: e.scalar_tensor_tensor(out=out, in0=in0, scalar=scalar, in1=in1, op0=op0, op1=op1,
                                                                accum_out=accum), r, w)

        def cp(eng, out, in_, r, w, scale=None):
            if eng == "act":
                return act(out, in_, AF.Copy, r, w, scale=scale)
            if scale is not None:
                return ts(eng, out, in_, scale, None, ALU.mult, None, r, w)
            return S.op(eng, lambda e: e.tensor_copy(out=out, in_=in_), r, w)

        def memset(eng, ap, val, w):
            return S.op(eng, lambda e: e.memset(ap, val), (), w)

        dmas = []

        def dma(out, in_, r, w, slow=False):
            if slow:
                i = S.op("sp", lambda e: e.dma_start(out=out, in_=in_, allow_slow_non_contiguous=True), r, w, dma=True)
            else:
                i = S.op("sp", lambda e: e.dma_start(out=out, in_=in_), r, w, dma=True)
            dmas.append(i)
            return i

        bsc = [af32(16) for _ in range(4)]
        bbuf = {e: S.buf("bar_" + e) for e in S.ENGS}
        bar_scr = nc.dram_tensor("bar_scr", [128, 16], F32, kind="Internal").ap()

        def barrier():
            allb = list(bbuf.values())
            mm(PS[7][:, 0:1], ident_b[:, 0:128], ident_b[:, 0:1], [PB[7], B_ident], [bbuf["pe"], PB[7]])
            memset("dve", bsc[0], 0.0, [bbuf["dve"]])
            act(bsc[1], bsc[3], AF.Copy, [B_bsc3], [bbuf["act"]])
            memset("pool", bsc[2], 0.0, [bbuf["pool"]])
            i = S.op("sp", lambda e: e.dma_start(out=bar_scr, in_=bsc[3]), [B_bsc3], [bbuf["sp"]], dma=True)
            for d in dmas:
                if d not in i.deps:
                    i.deps.append(d)
            dmas.clear()
            mm(PS[7][:, 0:1], ident_b[:, 0:128], ident_b[:, 0:1], allb + [PB[7], B_ident], [PB[7]])
            S.op("dve", lambda e: e.memset(bsc[0], 0.0), allb, [])
            act(bsc[1], bsc[3], AF.Copy, allb + [B_bsc3], [])
            S.op("pool", lambda e: e.memset(bsc[2], 0.0), allb, [])
            S.op("sp", lambda e: e.dma_start(out=bar_scr, in_=bsc[3]), allb + [B_bsc3], [], dma=True)

        ident_b = abf(128)
        B_ident = S.buf("ident")
        stage = [af32(1024), af32(1024)]
        B_stage = [S.buf("stg0"), S.buf("stg1")]
        stg_i = [0]
        cast_i = [0]
        B_bsc3 = S.buf("bsc3")
        memset("dve", bsc[3], 0.0, [B_bsc3])

        def loadw(dst_fn, src, ncols, scale=None, r_extra=(), w=None, perm_q=False):
            for c0 in range(0, ncols, 1024):
                n = min(1024, ncols - c0)
                k = stg_i[0] % 2
                stg_i[0] += 1
                np_ = src.shape[0]
                sl = stage[k][0:np_, 0:n]
                dma(sl, src[:, c0:c0 + n], [], [B_stage[k]])
                eng = ("pool", "dve")[cast_i[0] % 2]
                cast_i[0] += 1
                if perm_q:
                    for g in range(2):
                        o = dst_fn(c0, n).rearrange("p (c g d) -> p c g d", c=8, g=2)[:, :, g, :]
                        i_ = stage[k][0:np_, g * 512:(g + 1) * 512].rearrange("p (c d) -> p c d", c=8)
                        if scale is not None:
                            ts(eng, o, i_, scale, None, ALU.mult, None, [B_stage[k]] + list(r_extra), w)
                        else:
                            cp(eng, o, i_, [B_stage[k]] + list(r_extra), w)
                else:
                    if scale is not None:
                        ts(eng, dst_fn(c0, n), sl, scale, None, ALU.mult, None, [B_stage[k]] + list(r_extra), w)
                    else:
                        cp(eng, dst_fn(c0, n), sl, [B_stage[k]] + list(r_extra), w)

        def tr(out_ps, in_sb, r, w, start=True):
            return mm(out_ps, in_sb, ident_b[0:in_sb.shape[0], 0:in_sb.shape[0]], list(r) + [B_ident], w)

        dma(stage[0][:, 0:128], ident_d, [], [B_stage[0]])
        cp("dve", ident_b, stage[0][:, 0:128], [B_stage[0]], [B_ident])

        gpre = af32(8)
        gmem = af32(8)
        gffn = af32(8)
        B_small = S.buf("small")
        dma(gpre, pre_mix_g[0].rearrange("(k p) -> p k", p=128), [], [B_small], slow=True)
        dma(gmem, mem_norm_g[0].rearrange("(k p) -> p k", p=128), [], [B_small], slow=True)
        dma(gffn, pre_ffn_g[0].rearrange("(k p) -> p k", p=128), [], [B_small], slow=True)
        tqe = af32(NEXT)
        dma(tqe, tq_ext, [], [B_small])
        wval = af32(NEXT)
        dma(wval, wvalid_d.partition_broadcast(128), [], [B_small])
        hflag = af32(1)
        dma(hflag, hflag_d.partition_broadcast(128), [], [B_small])
        invcnt = af32(NEXT * 4)
        dma(invcnt, invcnt_d, [], [B_small])
        ss2 = af32(4)
        B_ss2 = S.buf("ss2")
        persist_top = top[0]

        def rms_scale(xt, xn, Bx, Bxn, ss, junk, Bss):
            act(junk, xt, AF.Square, [Bx], [Bss], accum=ss)
            ts("dve", ss, ss, 1.0 / 1024, 1e-6, ALU.mult, ALU.add, [Bss], [Bss])
            act(ss, ss, AF.Sqrt, [Bss], [Bss])
            S.op("dve", lambda e: e.reciprocal(out=ss, in_=ss), [Bss], [Bss])
            ts("dve", xn, xt, ss, None, ALU.mult, None, [Bx, Bss], [Bxn])

        def sincos(ang, n, osin, ocos, tmp, Bt, Bo):
            t, kf, g, ki = tmp
            ts("dve", ang, ang, 1.0 / TWO_PI, None, ALU.mult, None, [Bt], [Bt])
            for dst, off in ((osin, 0.0), (ocos, 0.25)):
                ts("dve", t, ang, off, None, ALU.add, None, [Bt], [Bt])
                cp("dve", ki, t, [Bt], [Bt])
                cp("dve", kf, ki, [Bt], [Bt])
                tt("dve", t, t, kf, ALU.subtract, [Bt], [Bt])
                ts("dve", g, t, 0.5, None, ALU.is_gt, None, [Bt], [Bt])
                tt("dve", t, t, g, ALU.subtract, [Bt], [Bt])
                ts("dve", g, t, -0.5, None, ALU.is_lt, None, [Bt], [Bt])
                tt("dve", t, t, g, ALU.add, [Bt], [Bt])
                act(dst, t, AF.Sin, [Bt], [Bo], scale=TWO_PI)

        def rotary(src4, dst4, cs, sn, tmp, rB, wB, Bt):
            a, b = src4.shape[1], src4.shape[2]
            n = a * b * 8
            x1 = src4[:, :, :, 0:8]
            x2 = src4[:, :, :, 8:16]
            csb = cs.unsqueeze(1).unsqueeze(1).to_broadcast([128, a, b, 8])
            snb = sn.unsqueeze(1).unsqueeze(1).to_broadcast([128, a, b, 8])
            t1 = tmp[:, 0:n].rearrange("p (a b d) -> p a b d", a=a, b=b)
            t2 = tmp[:, n:2 * n].rearrange("p (a b d) -> p a b d", a=a, b=b)
            tt("dve", t1, x1, csb, ALU.mult, rB, [Bt])
            tt("dve", t2, x2, snb, ALU.mult, rB, [Bt])
            tt("dve", dst4[:, :, :, 0:8], t1, t2, ALU.subtract, [Bt], wB)
            tt("dve", t1, x2, csb, ALU.mult, rB, [Bt])
            tt("dve", t2, x1, snb, ALU.mult, rB, [Bt])
            tt("dve", dst4[:, :, :, 8:16], t1, t2, ALU.add, [Bt], wB)

        KTs = abf(8192)
        Vs = abf(64 * 2 * 65).rearrange("p (t g d) -> p t g d", t=64, g=2)
        KTw = abf(NEXT * 128)
        Vw = abf(NEXT * 2 * 65).rearrange("p (t g d) -> p t g d", t=NEXT, g=2)
        KcT = abf(512)
        Vc = abf(4 * 2 * 65).rearrange("p (t g d) -> p t g d", t=4, g=2)
        Gb = abf(8192)
        B_KTs, B_Vs, B_KTw, B_Vw, B_KcT, B_Vc, B_G = [S.buf(n) for n in "KTs Vs KTw Vw KcT Vc G".split()]
        cosE = af32(NEXT * 8).rearrange("p (t f) -> p t f", f=8)
        sinE = af32(NEXT * 8).rearrange("p (t f) -> p t f", f=8)
        cos8 = af32(NEXT * 8).rearrange("p (t f) -> p t f", f=8)
        sin8 = af32(NEXT * 8).rearrange("p (t f) -> p t f", f=8)
        B_tabE = S.buf("tabE")
        kv_top = top[0]

        memset("pool", Vs, 1.0, [B_Vs])
        memset("pool", Vw, 1.0, [B_Vw])
        memset("pool", Vc, 0.0, [B_Vc])
        memset("pool", Vc[:, :, :, 64:65], 1.0, [B_Vc])
        loadw(lambda c0, n: Gb[:, c0:c0 + n], G_d, 8192, w=[B_G])

        cosA = af32(NALL * 8).rearrange("p (t f) -> p t f", f=8)
        sinA = af32(NALL * 8).rearrange("p (t f) -> p t f", f=8)
        B_tabA = S.buf("tabA")
        KcRaw = abf(8192)
        VcRaw = abf(8192)
        B_KcRaw, B_VcRaw = S.buf("KcRaw"), S.buf("VcRaw")
        wkvA = abf(8 * 512).rearrange("p (k c) -> p k c", k=8)
        B_wkvA = S.buf("wkvA")
        for k in range(8):
            loadw(lambda c0, n, k=k: wkvA[:, k, c0:c0 + n], w_in[0][k * 128:(k + 1) * 128, 1536:2048], 512,
                  scale=gpre[:, k:k + 1], r_extra=[B_small], w=[B_wkvA])
        mark_tab = top[0]
        posi = ai32(512)
        posf = af32(512)
        invf = af32(8)
        ang = af32(512)
        tmp3 = (af32(512), af32(512), af32(512), ai32(512))
        B_t = S.buf("tabtmp")
        dma(invf, invf_d.partition_broadcast(128), [], [B_t])
        dma(posi[:, 0:NALL], pos_all, [], [B_t])
        cp("dve", posf[:, 0:NALL], posi[:, 0:NALL], [B_t], [B_t])
        tt("dve", ang.rearrange("p (t f) -> p t f", f=8), posf[:, 0:NALL].unsqueeze(2).to_broadcast([128, NALL, 8]),
           invf.unsqueeze(1).to_broadcast([128, NALL, 8]), ALU.mult, [B_t], [B_t])
        sincos(ang, 512, sinA.rearrange("p t f -> p (t f)"), cosA.rearrange("p t f -> p (t f)"),
               tmp3, B_t, B_tabA)
        dma(posi[:, 0:NEXT], pos_ext, [B_t], [B_t])
        cp("dve", posf[:, 0:NEXT], posi[:, 0:NEXT], [B_t], [B_t])
        ne = NEXT * 8
        tt("dve", ang[:, 0:ne].rearrange("p (t f) -> p t f", f=8), posf[:, 0:NEXT].unsqueeze(2).to_broadcast([128, NEXT, 8]),
           invf.unsqueeze(1).to_broadcast([128, NEXT, 8]), ALU.mult, [B_t], [B_t])
        sincos(ang[:, 0:ne], ne, sinE.rearrange("p t f -> p (t f)"), cosE.rearrange("p t f -> p (t f)"),
               tuple(a[:, 0:ne] for a in tmp3), B_t, B_tabE)
        ts("dve", cos8.rearrange("p t f -> p (t f)"), cosE.rearrange("p t f -> p (t f)"), 0.125, None, ALU.mult, None, [B_tabE], [B_tabE])
        ts("dve", sin8.rearrange("p t f -> p (t f)"), sinE.rearrange("p t f -> p (t f)"), 0.125, None, ALU.mult, None, [B_tabE], [B_tabE])
        if dbg != "0":
            barrier()
            top[0] = mark_tab

        if dbg == "0":
            f1 = dma(dbg_t("cosA", [128, NALL * 8]), cosA.rearrange("p t f -> p (t f)"), [B_tabA], [])
            f2 = dma(dbg_t("sinA", [128, NALL * 8]), sinA.rearrange("p t f -> p (t f)"), [B_tabA], [])
            f3 = dma(dbg_t("cos8", [128, NEXT * 8]), cos8.rearrange("p t f -> p (t f)"), [B_tabE], [])
            f4 = dma(dbg_t("Gb", [128, 8192], BF16), Gb, [B_G], [])
            print(S.emit(final_waits=[f1, f2, f3, f4]))
            return nc
        xt2 = [af32(1024), af32(1024)]
        B_xt = [S.buf("xt0"), S.buf("xt1")]
        xn = abf(1024)
        B_xn = S.buf("xn")
        junk = abf(1024)
        ssA = af32(1)
        B_ss = S.buf("ss")
        hT = abf(1024).rearrange("p (k t) -> p k t", k=8)
        B_hT = S.buf("hT")
        kb = abf(512)
        B_kb = S.buf("kb")
        rtmp = af32(2 * 16 * 8)
        B_rt = S.buf("rtmp")

        def norm_T(xt, Bx, pa, pb, hTd, BhT, xnb=None, Bxnb=None):
            if xnb is None:
                xnb, Bxnb = xn, B_xn
            rms_scale(xt, xnb, Bx, Bxnb, ssA, junk, B_ss)
            for k in range(8):
                bank = pa if k < 4 else pb
                tr(PS[bank][:, (k % 4) * 128:(k % 4 + 1) * 128], xnb[:, k * 128:(k + 1) * 128], [Bxnb], [PB[bank]])
            cp("act", hTd[:, 0:4, :], PS[pa][:, :].rearrange("p (k t) -> p k t", k=4), [PB[pa]], [BhT])
            cp("dve", hTd[:, 4:8, :], PS[pb][:, :].rearrange("p (k t) -> p k t", k=4), [PB[pb]], [BhT])

        xnA = [xn, abf(1024)]
        B_xnA = [B_xn, S.buf("xnA1")]
        hTA = [hT, abf(1024).rearrange("p (k t) -> p k t", k=8)]
        B_hTA = [B_hT, S.buf("hTA1")]
        dma(xt2[0], x_all[0:128, :], [], [B_xt[0]])
        dma(xt2[1], x_all[128:256, :], [], [B_xt[1]])
        norm_T(xt2[0], B_xt[0], 0, 1, hTA[0], B_hTA[0], xnA[0], B_xnA[0])
        for T in range(na):
            s = T % 2
            if T + 1 < na:
                norm_T(xt2[1 - s], B_xt[1 - s], 0, 1, hTA[1 - s], B_hTA[1 - s], xnA[1 - s], B_xnA[1 - s])
            if T + 2 < na:
                dma(xt2[s], x_all[(T + 2) * 128:(T + 3) * 128, :], [], [B_xt[s]])
            hT, B_hT = hTA[s], B_hTA[s]
            for k in range(8):
                mm(PS[2][:, 0:512], hT[:, k, :], wkvA[:, k, :], [B_hT, B_wkvA], [PB[2]], start=(k == 0), stop=(k == 7))
            cp("act", kb, PS[2][:, 0:512], [PB[2]], [B_kb])
            v5 = PS[2][:, 0:512].rearrange("p (j2 jj g d) -> p j2 jj g d", j2=2, jj=2, g=2)
            k5 = kb.rearrange("p (j2 jj g d) -> p j2 jj g d", j2=2, jj=2, g=2)
            rotary(v5[:, :, 0, :, :], k5[:, :, 0, :, :], cosA[:, T, :], sinA[:, T, :], rtmp, [PB[2], B_tabA], [B_kb], B_rt)
            cp("pool", Vs[:, T, :, 0:64], kb[:, 384:512].rearrange("p (g d) -> p g d", g=2), [B_kb], [B_Vs])
            tr(PS[3][:, 0:128], kb[:, 0:128], [B_kb], [PB[3]])
            tr(PS[3][:, 128:256], kb[:, 128:256], [B_kb], [PB[3]])
            tr(PS[3][:, 256:384], kb[:, 256:384], [B_kb], [PB[3]])
            cp("act", KcRaw[:, T * 128:(T + 1) * 128], PS[3][:, 0:128], [PB[3]], [B_KcRaw])
            cp("dve", VcRaw[:, T * 128:(T + 1) * 128], PS[3][:, 128:256], [PB[3]], [B_VcRaw])
            cp("act", KTs[:, T * 128:(T + 1) * 128], PS[3][:, 256:384], [PB[3]], [B_KTs])

        w1z = [abf(32 * 256).rearrange("p (l m) -> p l m", l=32) for _ in range(2)]
        B_w1 = S.buf("w1b")
        memset("pool", w1z[0][64:128, :, :], 0.0, [B_w1])
        memset("pool", w1z[1][0:64, :, :], 0.0, [B_w1])
        w2b = abf(2 * 128).rearrange("p (h d) -> p h d", h=2)
        B_w2 = S.buf("w2b")
        peT = abf(32)
        pef = af32(32)
        B_pe = S.buf("pe")
        hid2 = [abf(2 * 512).rearrange("p (h c) -> p h c", h=2) for _ in range(2)]
        B_hid2 = [S.buf("hid0"), S.buf("hid1")]
        cbias = af32(2)
        B_cb = S.buf("cbias")
        for kvi, raw, Braw in (() if skipcmp else ((0, KcRaw, B_KcRaw), (1, VcRaw, B_VcRaw))):
            w1v = cmp_w1[0][kvi].rearrange("(l d) m -> d l m", d=64)
            for half in range(2):
                for l0 in range(0, 32, 4):
                    k = stg_i[0] % 2
                    stg_i[0] += 1
                    sl = stage[k][64 * half:64 * half + 64, 0:1024]
                    dma(sl.rearrange("p (l m) -> p l m", l=4), w1v[:, l0:l0 + 4, :], [], [B_stage[k]])
                    cp(("pool", "dve")[(l0 // 4) % 2], w1z[half][64 * half:64 * half + 64, l0:l0 + 4, :],
                       sl.rearrange("p (l m) -> p l m", l=4), [B_stage[k]], [B_w1])
            k = stg_i[0] % 2
            stg_i[0] += 1
            dma(stage[k][:, 0:128].rearrange("p (h d) -> p h d", h=2), cmp_w2[0][kvi].rearrange("(h p) d -> p h d", p=128),
                [], [B_stage[k]])
            cp("dve", w2b[:, :, 0:64], stage[k][:, 0:128].rearrange("p (h d) -> p h d", h=2), [B_stage[k]], [B_w2])
            cp("dve", w2b[:, :, 64:128], stage[k][:, 0:128].rearrange("p (h d) -> p h d", h=2), [B_stage[k]], [B_w2])
            dma(pef[0:64, :], cmp_pe[0][kvi].rearrange("l d -> d l"), [], [B_pe], slow=True)
            memset("dve", peT[64:128, :], 0.0, [B_pe])
            cp("dve", peT[0:64, :], pef[0:64, :], [B_pe], [B_pe])
            for half in range(2):
                for l in range(32):
                    mm(PS[4][:, half:half + 1], w1z[0][:, l, half * 128:(half + 1) * 128], peT[:, l:l + 1],
                       [B_w1, B_pe], [PB[4]], start=(l == 0 and half == 0), stop=(l == 31))
            cp("dve", cbias, PS[4][:, 0:2], [PB[4]], [B_cb])
            if not NOBAR:
                barrier()
            if dbg == "A" and kvi == 0 and (DUMPX & 1):
                dma(dbg_t("w1b", [128, 8192], BF16), w1z[0].rearrange("p l m -> p (l m)"), [B_w1], [])
                dma(dbg_t("cbias", [128, 2]), cbias, [B_cb], [])
            rawv = raw.rearrange("p (i s) -> p i s", s=16)
            for g in range(2):
                if not NOBAR:
                    barrier()
                hid, B_hid = hid2[g], B_hid2[g]
                pr = slice(64 * g, 64 * g + 64)
                for half in range(2):
                    for l in range(32):
                        rhs = rawv[:, 0:511, l] if l < 16 else rawv[:, 1:512, l - 16]
                        mm(PS[half][:, 0:511], w1z[g][:, l, half * 128:(half + 1) * 128], rhs, [B_w1, Braw], [PB[half]],
                           start=(l == 0), stop=(l == 31))
                    act(hid[:, half, 0:511], PS[half][:, 0:511], AF.Gelu_apprx_tanh, [PB[half], B_cb], [B_hid],
                        bias=cbias[:, half:half + 1])
                if dbg == "A" and kvi == 0 and (DUMPX & 2):
                    dma(dbg_t(f"hid{g}", [128, 1024], BF16), hid.rearrange("p h c -> p (h c)"), [B_hid], [])
                if kvi == 0:
                    for half in range(2):
                        mm(PS[2][:, 0:511], w2b[:, half, :], hid[:, half, 0:511], [B_w2, B_hid], [PB[2]],
                           start=(half == 0), stop=(half == 1))
                    cp("act", KcT[pr, 0:511], PS[2][pr, 0:511], [PB[2]], [B_KcT])
                else:
                    for c in range(4):
                        m = 128 if c < 3 else 127
                        for half in range(2):
                            mm(PS[2][0:m, c * 64:(c + 1) * 64], hid[:, half, c * 128:c * 128 + m], w2b[:, half, 0:64],
                               [B_w2, B_hid], [PB[2]], start=(half == 0 and c == 0), stop=(half == 1))
                    for c in range(4):
                        m = 128 if c < 3 else 127
                        cp("act", Vc[0:m, c, g, 0:64], PS[2][0:m, c * 64:(c + 1) * 64], [PB[2]], [B_Vc])
        memset("dve", KcT[:, 511:512], 0.0, [B_KcT])
        if dbg == "A":
            fl = [dma(dbg_t("KTs", [128, 8192], BF16), KTs, [B_KTs], []),
                  dma(dbg_t("Vs", [128, 64 * 130], BF16), Vs.rearrange("p t g d -> p (t g d)"), [B_Vs], []),
                  dma(dbg_t("KcT", [128, 512], BF16), KcT, [B_KcT], []),
                  dma(dbg_t("Vc", [128, 4 * 130], BF16), Vc.rearrange("p t g d -> p (t g d)"), [B_Vc], []),
                  dma(dbg_t("KcRaw", [128, 8192], BF16), KcRaw, [B_KcRaw], [])]
            print(S.emit(final_waits=fl))
            return nc
        barrier()
        top[0] = kv_top
        wb1 = abf(8 * 2352).rearrange("p (k c) -> p k c", k=8)
        B_wb1 = S.buf("wb1")
        for k in range(8):
            rows = w_in[0][k * 128:(k + 1) * 128, :]
            sc = gpre[:, k:k + 1]
            loadw(lambda c0, n, k=k: wb1[:, k, c0:c0 + n], rows[:, 0:512], 512, scale=sc, r_extra=[B_small], w=[B_wb1])
            loadw(lambda c0, n, k=k: wb1[:, k, 512:1536], rows[:, 512:1536], 1024, scale=sc, r_extra=[B_small], w=[B_wb1], perm_q=True)
            loadw(lambda c0, n, k=k: wb1[:, k, 1536 + c0:1536 + c0 + n], rows[:, 2048:2864], 816, scale=sc, r_extra=[B_small], w=[B_wb1])
        poolw = abf(4 * 128).rearrange("p (g d) -> p g d", g=4)
        B_cst = S.buf("cst")
        k_ = stg_i[0] % 2
        stg_i[0] += 1
        dma(stage[k_][:, 0:512].rearrange("p (g d) -> p g d", g=4), pool_w[0].rearrange("g c d -> c g d"), [], [B_stage[k_]])
        cp("dve", poolw, stage[k_][:, 0:512].rearrange("p (g d) -> p g d", g=4), [B_stage[k_]], [B_cst])
        pscale = af32(4)
        dma(pscale, pool_scale[0].rearrange("(g d) -> d g", d=128), [], [B_cst], slow=True)
        Ab = abf(1024).rearrange("p (g c t) -> p g c t", g=4, c=2)
        k_ = stg_i[0] % 2
        stg_i[0] += 1
        dma(stage[k_][:, 0:1024], Ab_d, [], [B_stage[k_]])
        cp("dve", Ab.rearrange("p g c t -> p (g c t)"), stage[k_][:, 0:1024], [B_stage[k_]], [B_cst])
        Mw3 = abf(384).rearrange("p (j q) -> p j q", j=3)
        k_ = stg_i[0] % 2
        stg_i[0] += 1
        dma(stage[k_][:, 0:256], Mw_d, [], [B_stage[k_]])
        cp("dve", Mw3[:, 0, :], stage[k_][:, 0:128], [B_stage[k_]], [B_cst])
        cp("dve", Mw3[:, 2, :], stage[k_][:, 128:256], [B_stage[k_]], [B_cst])
        memset("dve", Mw3[:, 1, :], 1.0, [B_cst])
        onesb = abf(1)
        memset("dve", onesb, 1.0, [B_cst])
        qrow = af32(128)
        dma(qrow, qrow_d.partition_broadcast(128), [], [B_cst])
        tqst = af32(NEXT)
        dma(tqst, tqst_d.partition_broadcast(128), [], [B_cst])
        tqt = af32(128)
        B_tqt = S.buf("tqt")
        bsrow = af32(128)
        dma(bsrow, bsrow_d.partition_broadcast(128), [], [B_cst])
        cendrow = af32(512)
        dma(cendrow, cendrow_d.partition_broadcast(128), [], [B_cst])
        kidx = af32(64)
        dma(kidx, kidx_d, [], [B_cst])
        cendcol = af32(4)
        dma(cendcol, cendcol_d, [], [B_cst])
        e0big = af32(128)
        dma(e0big, e0_d.partition_broadcast(128), [], [B_cst])

        xt2 = [af32(1024), af32(1024)]
        B_xt = [S.buf("bxt0"), S.buf("bxt1")]
        xn = abf(1024)
        B_xn = S.buf("bxn")
        junk = abf(1024)
        ssA = af32(1)
        B_ss = S.buf("bss")
        brT0 = abf(24 * 128).rearrange("p (k t) -> p k t", k=24)
        brT = [brT0, brT0]
        B_br0 = S.buf("br0")
        B_br = [B_br0, B_br0]
        rtmp = af32(256)
        B_rt = S.buf("brtmp")

        KmT = abf(4 * 256).rearrange("p (h m) -> p h m", h=4)
        Vm = abf(2 * 4 * 128).rearrange("p (c h d) -> p c h d", c=2, h=4)
        kmb = abf(512)
        mark_m = top[0]
        wmem = abf(8 * 1024).rearrange("p (k c) -> p k c", k=8)
        B_wmem = S.buf("wmem")
        for k in range(8):
            loadw(lambda c0, n, k=k: wmem[:, k, c0:c0 + n], w_mem_kv[0][k * 128:(k + 1) * 128, :], 1024,
                  scale=gmem[:, k:k + 1], r_extra=[B_small], w=[B_wmem])
        B_km, B_vm, B_kmb = S.buf("KmT"), S.buf("Vm"), S.buf("kmb")
        for c in range(2):
            dma(xt2[c], mem_d[c * 128:(c + 1) * 128, :], [], [B_xt[c]])
            norm_T(xt2[c], B_xt[c], 0, 1, brT[c][:, 16:24, :], B_br[c])
            for nb in range(2):
                for k in range(8):
                    mm(PS[2 + nb][:, :], brT[c][:, 16 + k, :], wmem[:, k, nb * 512:(nb + 1) * 512], [B_br[c], B_wmem], [PB[2 + nb]],
                       start=(k == 0), stop=(k == 7))
            cp("act", kmb, PS[2][:, :], [PB[2]], [B_kmb], scale=float(128 ** -0.5))
            cp("dve", Vm[:, c, :, :], PS[3][:, :].rearrange("p (h d) -> p h d", h=4), [PB[3]], [B_vm])
            for h in range(4):
                tr(PS[4][:, h * 128:(h + 1) * 128], kmb[:, h * 128:(h + 1) * 128], [B_kmb], [PB[4]])
            cp("act", KmT[:, :, c * 128:(c + 1) * 128], PS[4][:, :].rearrange("p (h m) -> p h m", h=4), [PB[4]], [B_km])

        barrier()
        top[0] = mark_m
        kwb = abf(256)
        B_kwb = S.buf("kwb")
        ub = [abf(512), abf(512)]
        B_ub = [S.buf("ub0"), S.buf("ub1")]
        uf = af32(512)
        B_uf = S.buf("uf")
        pbb = abf(512)
        B_pbb = S.buf("pbb")
        pT = abf(512).rearrange("p (g t) -> p g t", g=4)
        B_pT = S.buf("pT")
        qb = abf(1024)
        B_qb = S.buf("qb")
        qTz = [abf(1024).rearrange("p (k t) -> p k t", k=8) for _ in range(2)]
        B_qT = S.buf("qT")
        memset("pool", qTz[0], 0.0, [B_qT])
        memset("pool", qTz[1], 0.0, [B_qT])
        gn = af32(48)
        B_gn = S.buf("gn")
        qxb = abf(512)
        B_qxb = S.buf("qxb")
        qxT = abf(512).rearrange("p (h t) -> p h t", h=4)
        B_qxT = S.buf("qxT")
        mpT = [abf(512).rearrange("p (h t) -> p h t", h=4) for _ in range(2)]
        B_mpT = [S.buf("mpT0"), S.buf("mpT1")]
        rsm = af32(4)
        B_rsm = S.buf("rsm")
        ymemb = abf(512)
        B_ymem = S.buf("ymem")
        ef = [af32(512), af32(512)]
        B_ef = [S.buf("ef0"), S.buf("ef1")]
        ssum2 = [af32(2), af32(2)]
        B_ssum2 = [S.buf("ssum0"), S.buf("ssum1")]
        Pb = [af32(516), af32(516)]
        B_Pb = [S.buf("Pb0"), S.buf("Pb1")]
        cmneg = abf(512)
        B_cm = S.buf("cmneg")
        imp = af32(128)
        nd = af32(128)
        itmp = af32(128)
        wk = af32(128)
        mx = af32(16)
        B_imp = S.buf("imp")
        selb = abf(128)
        B_sel = S.buf("sel")
        selT = [abf(128), abf(128)]
        B_selT = [S.buf("selT0"), S.buf("selT1")]
        NSL = 3
        NSU = 6
        ucount = [0]
        mk = [abf(128) for _ in range(NSL)]
        B_mk = [S.buf(f"mk{i}") for i in range(NSL)]
        pTu = [abf(512).rearrange("p (h t) -> p h t", h=4) for _ in range(NSU)]
        B_pTu = [S.buf(f"pTu{i}") for i in range(NSU)]
        ynsa = af32(1024)
        B_yn = S.buf("ynsa")
        ytmp = af32(256)
        B_yt = S.buf("ytmp")
        rs = af32(8)
        B_rs = S.buf("rs")
        ynb = abf(1024)
        B_ynb = S.buf("ynb")
        memset("dve", Pb[0], 0.0, [B_Pb[0]])
        memset("dve", Pb[1], 0.0, [B_Pb[1]])
        nchunk = [0]

        def attend(e, g, br, chunks):
            LOOK = 3
            units = [(n, q) for n in range(len(chunks)) for q in range(2)]
            nu = len(units)
            info = {}

            def stage_scores(u):
                n, q = units[u]
                KT, V, mk_pe, mk_dve = chunks[n]
                if q == 0:
                    cs = nchunk[0]
                    nchunk[0] += 1
                    info[n] = (cs % 2, cs % NSL)
                    if mk_pe is not None:
                        mk_pe(cs % 2)
                bank = 2 + (ucount[0] % 4)
                ub = ucount[0] % NSU
                ucount[0] += 1
                mm(PS[bank][:, :], KT, qTz[g][:, 4 * q:4 * q + 4, :], [B_qT, B_KTs, B_KTw, B_KcT], [PB[bank]])
                return (bank, ub)

            pend = [stage_scores(u) for u in range(min(LOOK, nu))]
            for u, (n, q) in enumerate(units):
                bank, ub = pend.pop(0)
                if u + LOOK < nu:
                    pend.append(stage_scores(u + LOOK))
                KT, V, mk_pe, mk_dve = chunks[n]
                sl, bs = info[n]
                pt = pTu[ub]
                act(pt, PS[bank][:, :].rearrange("p (h t) -> p h t", h=4), AF.Exp, [PB[bank]], [B_pTu[ub]])
                if mk_dve is not None:
                    if q == 0:
                        mk_dve(sl, bs)
                    mb = mk[bs].unsqueeze(1).to_broadcast([128, 4, 128])
                    tt("dve", pt, pt, mb, ALU.mult, [B_pTu[ub], B_mk[bs]], [B_pTu[ub]])
                for hh in range(4 * q, 4 * q + 4):
                    mm(PS[q][:, (hh % 4) * 128:(hh % 4) * 128 + 65], pt[:, hh % 4, :], V, [B_pTu[ub], B_Vs, B_Vw, B_Vc], [PB[q]],
                       start=(n == 0 and hh % 4 == 0), stop=(n == len(chunks) - 1))
            gn3 = gn.rearrange("p (h b) -> p h b", b=3)
            for b in range(2):
                Ov = PS[b][:, :].rearrange("p (h d) -> p h d", h=4)
                h0 = 8 * g + 4 * b
                rsb = rs[:, 4 * b:4 * b + 4]
                ts("dve", rsb, Ov[:, :, 64], 1e-30, None, ALU.max, None, [PB[b]], [B_rs])
                S.op("dve", lambda e_, rsb=rsb: e_.reciprocal(out=rsb, in_=rsb), [B_rs], [B_rs])
                tt("dve", rsb, rsb, gn3[:, h0:h0 + 4, br], ALU.mult, [B_rs, B_gn], [B_rs])
                dst = ynsa.rearrange("p (h d) -> p h d", h=16)[:, h0:h0 + 4, :]
                rb = rsb.unsqueeze(2).to_broadcast([128, 4, 64])
                if br == 0:
                    tt("dve", dst, Ov[:, :, 0:64], rb, ALU.mult, [PB[b], B_rs], [B_yn])
                else:
                    yt = ytmp.rearrange("p (h d) -> p h d", h=4)
                    tt("dve", yt, Ov[:, :, 0:64], rb, ALU.mult, [PB[b], B_rs], [B_yt])
                    tt("pool", dst, dst, yt, ALU.add, [B_yn, B_yt], [B_yn])

        dma(xt2[0], x_ext[0:128, :], [], [B_xt[0]])
        for e in range(ntile_b1):
            s = e % 2
            if e + 1 < NEXT:
                dma(xt2[1 - s], x_ext[(e + 1) * 128:(e + 2) * 128, :], [], [B_xt[1 - s]])
            bt = brT[s]
            hTd = bt[:, 16:24, :]
            norm_T(xt2[s], B_xt[s], 0, 1, hTd, B_br[s])
            for k in range(8):
                mm(PS[2][:, 0:256], hTd[:, k, :], wb1[:, k, 1536:1792], [B_br[s], B_wb1], [PB[2]], start=(k == 0), stop=(k == 7))
            cp("act", kwb, PS[2][:, 0:256], [PB[2]], [B_kwb])
            rotary(PS[2][:, 0:128].rearrange("p (a g d) -> p a g d", a=1, g=2), kwb[:, 0:128].rearrange("p (a g d) -> p a g d", a=1, g=2),
                   cosE[:, e, :], sinE[:, e, :], rtmp, [PB[2], B_tabE], [B_kwb], B_rt)
            ts("pool", Vw[:, e, :, 0:64], kwb[:, 128:256].rearrange("p (g d) -> p g d", g=2), wval[:, e:e + 1], None, ALU.mult, None,
               [B_kwb, B_small], [B_Vw])
            for g_ in range(2):
                cp("pool", Vw[:, e, g_, 64:65], wval[:, e:e + 1], [B_small], [B_Vw])
            tr(PS[3][:, 0:128], kwb[:, 0:128], [B_kwb], [PB[3]])
            cp("act", KTw[:, e * 128:(e + 1) * 128], PS[3][:, 0:128], [PB[3]], [B_KTw])
            for k in range(8):
                mm(PS[4][:, :], hTd[:, k, :], wb1[:, k, 0:512], [B_br[s], B_wb1], [PB[4]], start=(k == 0), stop=(k == 7))
            cp("act", ub[s], PS[4][:, :], [PB[4]], [B_ub[s]])
            if e < 4:
                continue
            cp("dve", uf, PS[4][:, :], [PB[4]], [B_uf])
            i = e - 4
            for gi in range(4):
                blk = slice(gi * 128, (gi + 1) * 128)
                mm(PS[5][:, blk], Ab[:, gi, 0, :], ub[s][:, blk], [B_cst, B_ub[s]], [PB[5]], start=True, stop=False)
                mm(PS[5][:, blk], Ab[:, gi, 1, :], ub[1 - s][:, blk], [B_cst, B_ub[1 - s]], [PB[5]], start=False, stop=True)
            for gi in range(4):
                blk = slice(gi * 128, (gi + 1) * 128)
                stt(pbb[:, blk], PS[5][:, blk], invcnt[:, e * 4 + gi:e * 4 + gi + 1], uf[:, blk], ALU.mult, ALU.subtract,
                    [PB[5], B_uf, B_small], [B_pbb])
            for gi in range(4):
                blk = slice(gi * 128, (gi + 1) * 128)
                tr(PS[6][:, blk], pbb[:, blk], [B_pbb], [PB[6]])
            cp("act", pT, PS[6][:, :].rearrange("p (g t) -> p g t", g=4), [PB[6]], [B_pT])
            for gi in range(4):
                blk = slice(gi * 128, (gi + 1) * 128)
                mm(PS[5][:, blk], poolw[:, gi, :], pT[:, gi, :], [B_cst, B_pT], [PB[5]])
            for gi in range(4):
                blk = slice(gi * 128, (gi + 1) * 128)
                ts("dve", bt[:, gi, :], PS[5][:, blk], pscale[:, gi:gi + 1], None, ALU.mult, None, [PB[5], B_cst], [B_br[s]])
            for nb in range(2):
                for k in range(8):
                    mm(PS[nb][:, :], hTd[:, k, :], wb1[:, k, 512 + nb * 512:1024 + nb * 512], [B_br[s], B_wb1], [PB[nb]],
                       start=(k == 0), stop=(k == 7))
            for nb in range(2):
                qv = qb[:, nb * 512:(nb + 1) * 512]
                cp("act", qv, PS[nb][:, :], [PB[nb]], [B_qb], scale=0.125)
                rotary(PS[nb][:, :].rearrange("p (c g d) -> p c g d", c=4, g=2), qv.rearrange("p (c g d) -> p c g d", c=4, g=2),
                       cos8[:, e, :], sin8[:, e, :], rtmp, [PB[nb], B_tabE], [B_qb], B_rt)
            for k in range(8):
                tr(PS[2 + k // 4][:, (k % 4) * 128:(k % 4 + 1) * 128], qb[:, k * 128:(k + 1) * 128], [B_qb], [PB[2 + k // 4]])
            for (bk, c0) in ((2, 0), (3, 4)):
                cp("act", qTz[0][0:64, c0:c0 + 4, :], PS[bk][0:64, :].rearrange("p (k t) -> p k t", k=4), [PB[bk]], [B_qT])
                cp("dve", qTz[1][64:128, c0:c0 + 4, :], PS[bk][64:128, :].rearrange("p (k t) -> p k t", k=4), [PB[bk]], [B_qT])
            for k in range(8):
                mm(PS[4][:, 0:48], hTd[:, k, :], wb1[:, k, 1792:1840], [B_br[s], B_wb1], [PB[4]], start=(k == 0), stop=(k == 7))
            act(gn, PS[4][:, 0:48], AF.Sigmoid, [PB[4]], [B_gn])
            for k in range(8):
                mm(PS[5][:, :], hTd[:, k, :], wb1[:, k, 1840:2352], [B_br[s], B_wb1], [PB[5]], start=(k == 0), stop=(k == 7))
            cp("act", qxb, PS[5][:, :], [PB[5]], [B_qxb])
            for h in range(4):
                tr(PS[6][:, h * 128:(h + 1) * 128], qxb[:, h * 128:(h + 1) * 128], [B_qxb], [PB[6]])
            cp("dve", qxT, PS[6][:, :].rearrange("p (h t) -> p h t", h=4), [PB[6]], [B_qxT])
            for c in range(2):
                for h in range(4):
                    mm(PS[7][:, h * 128:(h + 1) * 128], KmT[:, h, c * 128:(c + 1) * 128], qxT[:, h, :], [B_km, B_qxT], [PB[7]])
                act(mpT[c], PS[7][:, :].rearrange("p (h t) -> p h t", h=4), AF.Exp, [PB[7]], [B_mpT[c]])
            for h in range(4):
                for c in range(2):
                    mm(PS[5][:, h * 128:(h + 1) * 128], mpT[c][:, h, :], Vm[:, c, h, :], [B_mpT[c], B_vm], [PB[5]],
                       start=(c == 0), stop=(c == 1))
            for h in range(4):
                for c in range(2):
                    mm(PS[6][:, h:h + 1], mpT[c][:, h, :], onesb[:, 0:1], [B_mpT[c], B_cst], [PB[6]],
                       start=(c == 0), stop=(c == 1))
            S.op("dve", lambda e_: e_.reciprocal(out=rsm, in_=PS[6][:, 0:4]), [PB[6]], [B_rsm])
            tt("dve", ymemb.rearrange("p (h d) -> p h d", h=4), PS[5][:, :].rearrange("p (h d) -> p h d", h=4),
               rsm.unsqueeze(2).to_broadcast([128, 4, 128]), ALU.mult, [PB[5], B_rsm], [B_ymem])
            for h in range(4):
                tr(PS[7][:, h * 128:(h + 1) * 128], ymemb[:, h * 128:(h + 1) * 128], [B_ymem], [PB[7]])
            cp("act", bt[:, 12:16, :], PS[7][:, :].rearrange("p (h t) -> p h t", h=4), [PB[7]], [B_br[s]])
            tqs = tqe[:, e:e + 1]
            ts("dve", cmneg, cendrow, tqs, -29952.0, ALU.is_gt, ALU.mult, [B_cst, B_small], [B_cm])
            ts("dve", tqt, qrow, tqst[:, e:e + 1], None, ALU.add, None, [B_cst], [B_tqt])
            for g in range(2):
                pr = slice(64 * g, 64 * g + 64)
                Pv = Pb[g][:, 1:513]
                for c_ in range(8):
                    a = c_ % 2
                    ss_, Bs_ = ssum2[a], B_ssum2[a]
                    mm(PS[4 + a][:, :], qTz[g][:, c_, :], KcT[:, 0:512], [B_qT, B_KcT], [PB[4 + a]], start=True, stop=False)
                    mm(PS[4 + a][:, :], ident_b, cmneg, [B_ident, B_cm], [PB[4 + a]], start=False, stop=True)
                    act(ef[a], PS[4 + a][:, :], AF.Exp, [PB[4 + a]], [B_ef[a], Bs_], accum=ss_[:, 0:1])
                    ts("dve", ss_[:, 1:2], ss_[:, 0:1], 1e-30, None, ALU.max, None, [Bs_], [Bs_])
                    S.op("dve", lambda e_, ss_=ss_: e_.reciprocal(out=ss_[:, 1:2], in_=ss_[:, 1:2]), [Bs_], [Bs_])
                    if c_ == 0:
                        ts("dve", Pv, ef[a], ss_[:, 1:2], None, ALU.mult, None, [B_ef[a], Bs_], [B_Pb[g]])
                    else:
                        stt(Pv, ef[a], ss_[:, 1:2], Pv, ALU.mult, ALU.add, [B_ef[a], Bs_, B_Pb[g]], [B_Pb[g]])
                S.op("dve", lambda e_, g=g: e_.tensor_reduce(out=imp, in_=Pb[g][:, 0:512].rearrange("p (j s) -> p j s", s=4),
                                                          axis=AX.X, op=ALU.add), [B_Pb[g]], [B_imp])
                tt("dve", imp, imp, Pb[g][:, 4:516].rearrange("p (j s) -> p j s", s=4)[:, :, 0], ALU.add, [B_Pb[g], B_imp], [B_imp])
                ts("dve", nd, bsrow, tqs, None, ALU.subtract, None, [B_cst, B_small, B_imp], [B_imp])
                ts("dve", itmp, nd, -128.0, BIG, ALU.is_gt, ALU.mult, [B_imp], [B_imp])
                tt("dve", imp, imp, itmp, ALU.add, [B_imp], [B_imp])
                ts("dve", itmp, nd, 0.0, -3.0 * BIG, ALU.is_gt, ALU.mult, [B_imp], [B_imp])
                tt("dve", imp, imp, itmp, ALU.add, [B_imp], [B_imp])
                tt("dve", imp, imp, e0big, ALU.add, [B_imp, B_cst], [B_imp])
                S.op("dve", lambda e_: e_.max(out=mx[:, 0:8], in_=imp), [B_imp], [B_imp])
                S.op("dve", lambda e_: e_.match_replace(out=wk, in_to_replace=mx[:, 0:8], in_values=imp, imm_value=-1e30), [B_imp], [B_imp])
                S.op("dve", lambda e_: e_.max(out=mx[:, 8:16], in_=wk), [B_imp], [B_imp])
                ts("dve", selb, imp, mx[:, 15:16], None, ALU.is_ge, None, [B_imp], [B_sel])
                tr(PS[6][:, g * 128:(g + 1) * 128], selb, [B_sel], [PB[6]])
                cp("act", selT[g], PS[6][:, g * 128:(g + 1) * 128], [PB[6]], [B_selT[g]])
            for g in range(2):
                pr = slice(64 * g, 64 * g + 64)

                def mk_cmp(c):
                    return (None, lambda sl, bs: ts("dve", mk[bs], tqt, cendcol[:, c:c + 1], None, ALU.is_ge, None, [B_cst, B_tqt], [B_mk[bs]]))

                def mk_win(j, ee):
                    if 1 <= j <= 3:
                        return (None, None)
                    v = 0 if j == 0 else (2 if j == 4 else 1)
                    return (None, lambda sl, bs: ts("dve", mk[bs], Mw3[:, v, :], wval[:, ee:ee + 1], None, ALU.mult, None, [B_cst, B_small], [B_mk[bs]]))

                def mk_sel(cc, g=g):
                    def f_pe(sl):
                        mm(PS[6 + sl][:, 256:384], Gb[:, cc * 128:(cc + 1) * 128], selT[g], [B_G, B_selT[g]], [PB[6 + sl]])

                    def f_dve(sl, bs):
                        stt(mk[bs], tqt, kidx[:, cc:cc + 1], PS[6 + sl][:, 256:384], ALU.is_ge, ALU.mult,
                            [B_cst, B_tqt, PB[6 + sl]], [B_mk[bs]])
                    return (f_pe, f_dve)

                attend(e, g, 0, [(KcT[:, c * 128:(c + 1) * 128], Vc[:, c, g, :]) + mk_cmp(c) for c in range(4)])
                attend(e, g, 1, [(KTs[:, cc * 128:(cc + 1) * 128], Vs[:, cc, g, :]) + mk_sel(cc) for cc in range(48 + i)])
                attend(e, g, 2, [(KTw[:, (e - 4 + j) * 128:(e - 3 + j) * 128], Vw[:, e - 4 + j, g, :]) + mk_win(j, e - 4 + j)
                                 for j in range(5)])
            cp("act", ynb, ynsa, [B_yn], [B_ynb])
            for k in range(8):
                tr(PS[2 + k // 4][:, (k % 4) * 128:(k % 4 + 1) * 128], ynb[:, k * 128:(k + 1) * 128], [B_ynb], [PB[2 + k // 4]])
            cp("act", bt[:, 4:8, :], PS[2][:, :].rearrange("p (k t) -> p k t", k=4), [PB[2]], [B_br[s]])
            cp("dve", bt[:, 8:12, :], PS[3][:, :].rearrange("p (k t) -> p k t", k=4), [PB[3]], [B_br[s]])
            lastbr = dma(br_scr[i], bt.rearrange("p k t -> p (k t)"), [B_br[s]], [])
            if dbg == "B1":
                lastbr = dma(dbg_t(f"br{i}", [128, 24 * 128], BF16), bt.rearrange("p k t -> p (k t)"), [B_br[s]], [])
                fl1 = [lastbr, dma(dbg_t(f"ynsa{i}", [128, 1024]), ynsa, [B_yn], []),
                       dma(dbg_t(f"qb{i}", [128, 1024], BF16), qb, [B_qb], []),
                       dma(dbg_t(f"gn{i}", [128, 48]), gn, [B_gn], []),
                       dma(dbg_t(f"selT{i}", [128, 128], BF16), selT[1], [B_selT[1]], []),
                       dma(dbg_t(f"Pb{i}", [128, 516]), Pb[1], [B_Pb[1]], [])]
        if dbg == "B1":
            print(S.emit(final_waits=fl1))
            return nc
        barrier()
        top[0] = persist_top
        wg = abf(8 * 3072).rearrange("p (k c) -> p k c", k=8)
        wbp = abf(4 * 1024).rearrange("p (k c) -> p k c", k=4)
        wbn = abf(8 * 1024).rearrange("p (k c) -> p k c", k=8)
        wbx = abf(4 * 1024).rearrange("p (k c) -> p k c", k=4)
        wo = abf(8 * 1024).rearrange("p (k c) -> p k c", k=8)
        B_w2p = S.buf("w_b2")
        for k in range(8):
            loadw(lambda c0, n, k=k: wg[:, k, c0:c0 + n], w_in[0][k * 128:(k + 1) * 128, 2864:5936], 3072,
                  scale=gpre[:, k:k + 1], r_extra=[B_small], w=[B_w2p])
            loadw(lambda c0, n, k=k: wbn[:, k, c0:c0 + n], w_br_nsa[0][k * 128:(k + 1) * 128, :], 1024, w=[B_w2p])
            loadw(lambda c0, n, k=k: wo[:, k, c0:c0 + n], w_out[0][k * 128:(k + 1) * 128, :], 1024, w=[B_w2p])
            if k < 4:
                loadw(lambda c0, n, k=k: wbp[:, k, c0:c0 + n], w_br_pool[0][k * 128:(k + 1) * 128, :], 1024, w=[B_w2p])
                loadw(lambda c0, n, k=k: wbx[:, k, c0:c0 + n], w_br_xa[0][k * 128:(k + 1) * 128, :], 1024, w=[B_w2p])
        gpost = af32(1024)
        B_gp = S.buf("gpost")
        dma(gpost, post_mix_g.partition_broadcast(128), [], [B_gp])
        brT = [abf(24 * 128).rearrange("p (k t) -> p k t", k=24) for _ in range(2)]
        B_br = [S.buf("c_br0"), S.buf("c_br1")]
        xt2 = [af32(1024), af32(1024)]
        B_xt = [S.buf("c_xt0"), S.buf("c_xt1")]
        sg = af32(1024)
        B_sg = S.buf("sg")
        yy = af32(1024)
        B_y = S.buf("yy")
        ytm = af32(1024)
        B_ytm = S.buf("ytm")
        yb = abf(1024)
        B_yb = S.buf("yb")
        yT = abf(1024).rearrange("p (k t) -> p k t", k=8)
        B_yT = S.buf("yT")
        junk = abf(512)
        x1t = [af32(1024), af32(1024)]
        B_x1 = [S.buf("x1t0"), S.buf("x1t1")]

        def post_norm_res(pa, pb, gp, Bg, xres, Bxres, dst, Bdst):
            act(junk[:, 0:512], PS[pa][:, :], AF.Square, [PB[pa]], [B_ss2], accum=ss2[:, 0:1])
            act(junk[:, 0:512], PS[pb][:, :], AF.Square, [PB[pb]], [B_ss2], accum=ss2[:, 1:2])
            tt("dve", ss2[:, 2:3], ss2[:, 0:1], ss2[:, 1:2], ALU.add, [B_ss2], [B_ss2])
            ts("dve", ss2[:, 2:3], ss2[:, 2:3], 1.0 / 1024, 1e-6, ALU.mult, ALU.add, [B_ss2], [B_ss2])
            act(ss2[:, 2:3], ss2[:, 2:3], AF.Sqrt, [B_ss2], [B_ss2])
            S.op("dve", lambda e_: e_.reciprocal(out=ss2[:, 3:4], in_=ss2[:, 2:3]), [B_ss2], [B_ss2])
            for nb, bank in enumerate((pa, pb)):
                blk = slice(nb * 512, (nb + 1) * 512)
                stt(dst[:, blk], PS[bank][:, :], ss2[:, 3:4], gp[:, blk], ALU.mult, ALU.mult, [PB[bank], B_ss2, Bg], [Bdst])
                tt("pool", dst[:, blk], dst[:, blk], xres[:, blk], ALU.add, [Bdst, Bxres], [Bdst])

        dma(brT[0].rearrange("p k t -> p (k t)"), br_scr[0], [], [B_br[0]])
        dma(xt2[0], x_ext[4 * 128:5 * 128, :], [], [B_xt[0]])
        for i in range(17):
            s = i % 2
            e = i + 4
            if i + 1 < 17:
                dma(brT[1 - s].rearrange("p k t -> p (k t)"), br_scr[i + 1], [], [B_br[1 - s]])
                dma(xt2[1 - s], x_ext[(e + 1) * 128:(e + 2) * 128, :], [], [B_xt[1 - s]])
            bt = brT[s]
            for br in range(3):
                for nb in range(2):
                    for k in range(8):
                        mm(PS[nb][:, :], bt[:, 16 + k, :], wg[:, k, br * 1024 + nb * 512:br * 1024 + (nb + 1) * 512],
                           [B_br[s], B_w2p], [PB[nb]], start=(k == 0), stop=(k == 7))
                wsel, off, nk = ((wbp, 0, 4), (wbn, 4, 8), (wbx, 12, 4))[br]
                for nb in range(2):
                    for k in range(nk):
                        mm(PS[2 + nb][:, :], bt[:, off + k, :], wsel[:, k, nb * 512:(nb + 1) * 512],
                           [B_br[s], B_w2p], [PB[2 + nb]], start=(k == 0), stop=(k == nk - 1))
                for nb in range(2):
                    blk = slice(nb * 512, (nb + 1) * 512)
                    act(sg[:, blk], PS[nb][:, :], AF.Sigmoid, [PB[nb]], [B_sg])
                    if br == 0:
                        tt("dve", yy[:, blk], sg[:, blk], PS[2 + nb][:, :], ALU.mult, [B_sg, PB[2 + nb]], [B_y])
                    else:
                        tt("dve", ytm[:, blk], sg[:, blk], PS[2 + nb][:, :], ALU.mult, [B_sg, PB[2 + nb]], [B_ytm])
                        tt("pool", yy[:, blk], yy[:, blk], ytm[:, blk], ALU.add, [B_y, B_ytm], [B_y])
            cp("act", yb, yy, [B_y], [B_yb])
            for k in range(8):
                tr(PS[4 + k // 4][:, (k % 4) * 128:(k % 4 + 1) * 128], yb[:, k * 128:(k + 1) * 128], [B_yb], [PB[4 + k // 4]])
            cp("act", yT[:, 0:4, :], PS[4][:, :].rearrange("p (k t) -> p k t", k=4), [PB[4]], [B_yT])
            cp("dve", yT[:, 4:8, :], PS[5][:, :].rearrange("p (k t) -> p k t", k=4), [PB[5]], [B_yT])
            for nb in range(2):
                for k in range(8):
                    mm(PS[6 + nb][:, :], yT[:, k, :], wo[:, k, nb * 512:(nb + 1) * 512], [B_yT, B_w2p], [PB[6 + nb]],
                       start=(k == 0), stop=(k == 7))
            post_norm_res(6, 7, gpost, B_gp, xt2[s], B_xt[s], x1t[s], B_x1[s])
            lx = dma(x1_scr[i], x1t[s], [B_x1[s]], [])
            if dbg == "B2":
                lx = dma(dbg_t(f"x1_{i}", [128, 1024]), x1t[s], [B_x1[s]], [])
        if dbg == "B2":
            print(S.emit(final_waits=[lx]))
            return nc
        barrier()
        top[0] = persist_top

        wup = abf(8 * 5632).rearrange("p (k c) -> p k c", k=8)
        wdn = abf(22 * 1024).rearrange("p (k c) -> p k c", k=22)
        B_w3 = S.buf("w_c")
        for k in range(8):
            loadw(lambda c0, n, k=k: wup[:, k, c0:c0 + n], w_up[0][k * 128:(k + 1) * 128, :], 5632,
                  scale=gffn[:, k:k + 1], r_extra=[B_small], w=[B_w3])
        for k in range(22):
            loadw(lambda c0, n, k=k: wdn[:, k, c0:c0 + n], w_down[0][k * 128:(k + 1) * 128, :], 1024, w=[B_w3])
        convp = af32(44 * 4).rearrange("p (j c) -> p j c", c=4)
        B_cv = S.buf("convp")
        for kk in range(3):
            dma(convp[:, :, kk], conv_w[0][kk].rearrange("(j p) -> p j", p=128), [], [B_cv], slow=True)
        dma(convp[:, :, 3], conv_b[0].rearrange("(j p) -> p j", p=128), [], [B_cv], slow=True)
        gpost2 = af32(1024)
        B_gp2 = S.buf("gpost2")
        dma(gpost2, post_ffn_g.partition_broadcast(128), [], [B_gp2])
        xt2 = [af32(1024), af32(1024)]
        B_xt = [S.buf("d_xt0"), S.buf("d_xt1")]
        xn = abf(1024)
        B_xn = S.buf("d_xn")
        junk = abf(1024)
        ssA = af32(2)
        B_ss = S.buf("d_ss")
        h2Ts = [abf(8 * 130).rearrange("p (k t) -> p k t", k=8) for _ in range(2)]
        B_h2s = [S.buf("h2T0"), S.buf("h2T1")]
        xnC = [xn, abf(1024)]
        B_xnC = [B_xn, S.buf("xnC1")]
        aT = abf(22 * 128).rearrange("p (k t) -> p k t", k=22)
        B_aT = S.buf("aT")
        cg = [af32(128), af32(128)]
        cv = [af32(128), af32(128)]
        gl = [af32(128), af32(128)]
        B_cg = [S.buf("cg0"), S.buf("cg1")]
        B_cvb = [S.buf("cv0"), S.buf("cv1")]
        B_gl = [S.buf("gl0"), S.buf("gl1")]
        ot = [af32(1024), af32(1024)]
        B_ot = [S.buf("ot0"), S.buf("ot1")]
        xt3 = [xt2[0], xt2[1], af32(1024)]
        B_xt3 = [B_xt[0], B_xt[1], S.buf("d_xt2")]
        fins = []

        def c_stage1(i):
            s_ = i % 2
            x3 = i % 3
            rms_scale(xt3[x3], xnC[s_], B_xt3[x3], B_xnC[s_], ssA[:, 0:1], junk, B_ss)
            for k in range(8):
                bank = k // 4
                tr(PS[bank][:, (k % 4) * 128:(k % 4 + 1) * 128], xnC[s_][:, k * 128:(k + 1) * 128], [B_xnC[s_]], [PB[bank]])
            cp("act", h2Ts[s_][:, 0:4, 2:130], PS[0][:, :].rearrange("p (k t) -> p k t", k=4), [PB[0]], [B_h2s[s_]])
            cp("dve", h2Ts[s_][:, 4:8, 2:130], PS[1][:, :].rearrange("p (k t) -> p k t", k=4), [PB[1]], [B_h2s[s_]])
            if i == 1:
                ts("pool", h2Ts[s_][:, :, 0:2], h2Ts[1 - s_][:, :, 128:130], hflag[:, 0:1], None, ALU.mult, None,
                   [B_h2s[1 - s_], B_small, B_h2s[s_]], [B_h2s[s_]])
            elif i > 1:
                cp("pool", h2Ts[s_][:, :, 0:2], h2Ts[1 - s_][:, :, 128:130], [B_h2s[1 - s_], B_h2s[s_]], [B_h2s[s_]])

        dma(xt3[0], x1_scr[0], [], [B_xt3[0]])
        dma(xt3[1], x1_scr[1], [], [B_xt3[1]])
        dma(xt3[2], x1_scr[2], [], [B_xt3[2]])
        c_stage1(0)
        for i in range(17):
            s = i % 2
            if i + 1 < 17:
                c_stage1(i + 1)
            if i == 0:
                dma(xt3[0], x1_scr[3], [], [B_xt3[0]])
                continue
            h2T, B_h2 = h2Ts[s], B_h2s[s]
            for j in range(22):
                a = j % 2
                bg, bv = 2 + 2 * a, 3 + 2 * a
                for k in range(8):
                    mm(PS[bg][:, 0:130], wup[:, k, j * 128:(j + 1) * 128], h2T[:, k, :], [B_w3, B_h2], [PB[bg]],
                       start=(k == 0), stop=(k == 7))
                for k in range(8):
                    mm(PS[bv][:, 0:130], wup[:, k, (22 + j) * 128:(23 + j) * 128], h2T[:, k, :], [B_w3, B_h2], [PB[bv]],
                       start=(k == 0), stop=(k == 7))
                for (bank, dstc, Bd, jj) in ((bg, cg[a], B_cg[a], j), (bv, cv[a], B_cvb[a], 22 + j)):
                    act(dstc, PS[bank][:, 2:130], AF.Identity, [PB[bank], B_cv], [Bd], bias=convp[:, jj, 3:4], scale=convp[:, jj, 2:3])
                    stt(dstc, PS[bank][:, 1:129], convp[:, jj, 1:2], dstc, ALU.mult, ALU.add, [PB[bank], B_cv, Bd], [Bd])
                    stt(dstc, PS[bank][:, 0:128], convp[:, jj, 0:1], dstc, ALU.mult, ALU.add, [PB[bank], B_cv, Bd], [Bd])
                act(gl[a], cg[a], AF.Gelu_apprx_tanh, [B_cg[a]], [B_gl[a]])
                tt("pool", aT[:, j, :], gl[a], cv[a], ALU.mult, [B_gl[a], B_cvb[a]], [B_aT])
            for nb in range(2):
                for j in range(22):
                    mm(PS[6 + nb][:, :], aT[:, j, :], wdn[:, j, nb * 512:(nb + 1) * 512], [B_aT, B_w3], [PB[6 + nb]],
                       start=(j == 0), stop=(j == 21))
            post_norm_res(6, 7, gpost2, B_gp2, xt3[i % 3], B_xt3[i % 3], ot[s], B_ot[s])
            fins.append(dma(out_d[(i - 1) * 128:i * 128, :], ot[s], [B_ot[s]], []))
            if i + 3 < 17:
                dma(xt3[i % 3], x1_scr[i + 3], [], [B_xt3[i % 3]])
        stats = S.emit(final_waits=fins)
        print("emit stats", stats, flush=True)
    return nc


def _consts():
    c = {}
    c["ident"] = np.eye(128, dtype=np.float32)
    k = np.arange(8192)
    c["Gm"] = (k[None, :] // 64 == np.arange(128)[:, None]).astype(np.float32)
    c["invf"] = (np.float32(500000.0) ** (-np.arange(8, dtype=np.float32) * np.float32(2.0 / 16))).astype(np.float32).reshape(1, 8)
    c["bsrow"] = (64.0 * np.arange(128, dtype=np.float32)).reshape(1, 128)
    ce = (16.0 * np.arange(512, dtype=np.float32) + 31.0)
    ce[511] = 1e9
    c["cendrow"] = ce.reshape(1, 512)
    c["kidx"] = (128.0 * np.arange(64)[None, :] + np.arange(128)[:, None]).astype(np.float32)
    c["cendcol"] = np.ascontiguousarray(ce.reshape(4, 128).T)
    p = np.arange(128)[:, None]
    q = np.arange(128)[None, :]
    c["Mw"] = np.concatenate([(q < p), (q >= p)], axis=1).astype(np.float32)
    e0 = np.zeros((1, 128), np.float32)
    e0[0, 0] = BIG
    c["e0big"] = e0
    A = np.zeros((128, 4, 2, 128), np.float32)
    for gi, w in enumerate((2, 4, 8, 16)):
        tp = np.arange(128)[:, None]
        t = np.arange(128)[None, :]
        A[:, gi, 0, :] = ((t - tp >= 0) & (t - tp < w))
        A[:, gi, 1, :] = ((t + 128 - tp >= 0) & (t + 128 - tp < w))
    c["Aband"] = A.reshape(128, 1024)
    c["qrow"] = np.arange(128, dtype=np.float32).reshape(1, 128)
    return c


_PROG = {}


def kernel(**inputs):
    x = np.asarray(inputs["x"], dtype=np.float32)
    mem = np.asarray(inputs["mem"], dtype=np.float32)
    positions = np.asarray(inputs["positions"]).astype(np.int32)
    if inputs.get("_return_maps"):
        nc = None
    else:
        if "nc" not in _PROG:
            _PROG["nc"] = build()
        nc = _PROG["nc"]
    consts = _consts()
    wnames = ["pre_mix_g", "w_in", "pool_w", "pool_scale", "cmp_pe", "cmp_w1", "cmp_w2", "mem_norm_g", "w_mem_kv",
              "w_br_pool", "w_br_nsa", "w_br_xa", "w_out", "post_mix_g", "pre_ffn_g", "w_up", "conv_w", "conv_b",
              "w_down", "post_ffn_g"]
    shared = {n: np.ascontiguousarray(np.asarray(inputs[n], dtype=np.float32)) for n in wnames}
    in_maps = []
    for core in range(8):
        b, r = core // 4, core % 4
        m = dict(shared)
        m.update(consts)
        m["x_all"] = np.ascontiguousarray(x[b])
        m["mem"] = np.ascontiguousarray(mem[b])
        xe = np.zeros((NEXT * 128, 1024), np.float32)
        pe = np.zeros((NEXT, 128), np.int32)
        tq = np.zeros((NEXT, 128), np.float32)
        wv = np.zeros((1, NEXT), np.float32)
        ic = np.ones((128, NEXT, 4), np.float32)
        tst = np.zeros((1, NEXT), np.float32)
        for e in range(NEXT):
            ge = 16 * r - 5 + e
            tq[e] = ge * 128 + np.arange(128)
            tst[0, e] = ge * 128
            if ge >= 0:
                xe[e * 128:(e + 1) * 128] = x[b, ge * 128:(ge + 1) * 128]
                pe[e] = positions[b, ge * 128:(ge + 1) * 128]
                wv[0, e] = 1.0
                t = ge * 128 + np.arange(128)
                for gi, w in enumerate((2, 4, 8, 16)):
                    ic[:, e, gi] = 1.0 / np.minimum(t + 1, w).astype(np.float32)
        m["x_ext"] = xe
        m["pos_ext"] = np.ascontiguousarray(pe.T)
        m["pos_all"] = np.ascontiguousarray(positions[b].reshape(NALL, 128).T)
        m["tq_ext"] = np.ascontiguousarray(tq.T)
        m["tqstart"] = tst
        m["wvalid"] = wv
        m["invcnt"] = np.ascontiguousarray(ic.reshape(128, NEXT * 4))
        m["hflag"] = np.array([[0.0 if r == 0 else 1.0]], np.float32)
        in_maps.append(m)
    if inputs.get("_return_maps"):
        return in_maps
    res = run_bass_kernel_spmd(nc, in_maps, core_ids=list(range(8)))
    out = np.zeros((2, 8192, 1024), np.float32)
    for core in range(8):
        b, r = core // 4, core % 4
        out[b, r * 2048:(r + 1) * 2048] = res.results[core]["out"]
    return out
```

```python
import numpy as np
from contextlib import ExitStack
import concourse.bass as bass
import concourse.mybir as mybir
from concourse.bass_utils import run_bass_kernel_spmd

F32 = mybir.dt.float32
BF16 = mybir.dt.bfloat16
I32 = mybir.dt.int32
AF = mybir.ActivationFunctionType
ALU = mybir.AluOpType
AX = mybir.AxisListType


import sys as _sys


def _where():
    f = _sys._getframe(2)
    out = []
    while f is not None and len(out) < 4:
        if f.f_code.co_name != "<lambda>":
            out.append(f.f_lineno)
        f = f.f_back
    return out


class Buf:
    __slots__ = ("name", "writers", "readers", "excl", "last")

    def __init__(self, name, excl=False):
        self.name = name
        self.writers = []
        self.readers = []
        self.excl = excl
        self.last = {}


class Ins:
    __slots__ = ("eng", "fn", "deps", "idx", "flag", "tok", "dma", "pre", "where")

    def __init__(self, eng, fn, dma):
        self.eng = eng
        self.fn = fn
        self.deps = []
        self.flag = False
        self.tok = None
        self.dma = dma
        self.pre = None


class Sched:
    ENGS = ("pe", "dve", "act", "pool", "sp")
    EPOCH = 8000
    NDMA = 24

    def __init__(self, nc, stack):
        self.nc = nc
        self.stack = stack
        self.q = {e: [] for e in self.ENGS}
        self.nbuf = 0

    def buf(self, name=None, excl=False):
        self.nbuf += 1
        return Buf(name or f"b{self.nbuf}", excl)

    def op(self, eng, fn, r=(), w=(), dma=False):
        ins = Ins(eng, fn, dma)
        ins.where = _where()
        deps = []
        for b in r:
            deps.extend(b.writers)
        for b in w:
            deps.extend(b.readers)
        for b in list(r) + list(w):
            if b.excl:
                for en, li in b.last.items():
                    if en != eng:
                        deps.append(li)
                b.last[eng] = ins
        for b in w:
            if b.readers or (b in r):
                b.writers = [ins]
                b.readers = []
            else:
                b.writers.append(ins)
                if len(b.writers) > 48:
                    b.writers = b.writers[-48:]
        for b in r:
            if b not in w:
                b.readers.append(ins)
                if len(b.readers) > 48:
                    b.readers = b.readers[-48:]
        seen = set()
        for d in deps:
            if d is ins or id(d) in seen:
                continue
            if d.eng == "pe" and eng == "pe" and not d.dma:
                continue
            seen.add(id(d))
            ins.deps.append(d)
            d.flag = True
        self.q[eng].append(ins)
        return ins

    def emit(self, final_waits=()):
        nc = self.nc
        stack = self.stack
        sems = {}
        for e in self.ENGS:
            n = 0
            for ins in self.q[e]:
                if ins.dma:
                    continue
                if ins.flag:
                    n += 1
                    ins.idx = n
            nep = (n + self.EPOCH - 1) // self.EPOCH
            sems[e] = [stack.enter_context(nc.semaphore(f"s_{e}_{k}")) for k in range(max(nep, 1))]
        dsems = [stack.enter_context(nc.semaphore(f"s_dma_{k}")) for k in range(self.NDMA)]
        duse = [0] * self.NDMA
        dma_engs = [e for e in self.ENGS if any(i.dma for i in self.q[e])]
        share = {}
        if dma_engs:
            per = self.NDMA // len(dma_engs)
            for k, e in enumerate(dma_engs):
                share[e] = list(range(k * per, (k + 1) * per))
        for e in dma_engs:
            j = 0
            for ins in self.q[e]:
                if not ins.dma:
                    continue
                s = share[e][j % len(share[e])]
                j += 1
                prev = duse[s]
                duse[s] += 1
                ins.tok = (dsems[s], 16 * duse[s])
                ins.pre = (dsems[s], 16 * prev) if prev > 0 else None
        for e in self.ENGS:
            for ins in self.q[e]:
                if ins.dma or not ins.flag:
                    continue
                k = (ins.idx - 1) // self.EPOCH
                ins.tok = (sems[e][k], (ins.idx - 1) % self.EPOCH + 1)
        engobj = {"pe": "tensor", "dve": "vector", "act": "scalar", "pool": "gpsimd", "sp": "sync"}
        stats = {}
        with nc.Block() as block:
            for e in self.ENGS:
                lst = self.q[e]

                def body(eng, lst=lst, e=e):
                    waited = {}
                    nw = 0

                    def wait(tok):
                        nonlocal nw
                        sem, val = tok
                        key = id(sem)
                        if waited.get(key, 0) >= val:
                            return
                        waited[key] = val
                        eng.wait_ge(sem, val)
                        nw += 1

                    for ins in lst:
                        if ins.pre is not None:
                            wait(ins.pre)
                        for d in ins.deps:
                            wait(d.tok)
                        try:
                            bi = ins.fn(eng)
                        except BaseException:
                            print("FAILED op recorded at lines", ins.where, flush=True)
                            raise
                        if ins.dma:
                            bi.then_inc(ins.tok[0], 16)
                        elif ins.flag:
                            bi.then_inc(ins.tok[0], 1)
                    if e == "sp":
                        for fw in final_waits:
                            wait(fw.tok)
                    stats[e] = (len(lst), nw)

                getattr(block, engobj[e])(body)
        return stats


import os as _os
DUMPX = int(_os.environ.get('DUMPX', '0'))
NOBAR = int(_os.environ.get('NOBAR', '0'))
NEXT = 21
NALL = 64
BIG = 100.0
TWO_PI = float(2 * np.pi)


def build(dbg=None, ntile_b1=NEXT, na=NALL, skipcmp=False):
    nc = bass.Bass("TRN2", target_bir_lowering=False)

    def din(name, shape, dt=F32):
        return nc.dram_tensor(name, list(shape), dt, kind="ExternalInput").ap()

    x_all = din("x_all", [8192, 1024])
    x_ext = din("x_ext", [NEXT * 128, 1024])
    pos_all = din("pos_all", [128, NALL], I32)
    pos_ext = din("pos_ext", [128, NEXT], I32)
    tq_ext = din("tq_ext", [128, NEXT])
    qrow_d = din("qrow", [1, 128])
    tqst_d = din("tqstart", [1, NEXT])
    wvalid_d = din("wvalid", [1, NEXT])
    invcnt_d = din("invcnt", [128, NEXT * 4])
    hflag_d = din("hflag", [1, 1])
    mem_d = din("mem", [256, 1024])
    ident_d = din("ident", [128, 128])
    G_d = din("Gm", [128, 8192])
    invf_d = din("invf", [1, 8])
    bsrow_d = din("bsrow", [1, 128])
    cendrow_d = din("cendrow", [1, 512])
    kidx_d = din("kidx", [128, 64])
    cendcol_d = din("cendcol", [128, 4])
    Mw_d = din("Mw", [128, 256])
    e0_d = din("e0big", [1, 128])
    Ab_d = din("Aband", [128, 1024])
    pre_mix_g = din("pre_mix_g", [1, 1024])
    w_in = din("w_in", [1, 1024, 5936])
    pool_w = din("pool_w", [1, 4, 128, 128])
    pool_scale = din("pool_scale", [1, 512])
    cmp_pe = din("cmp_pe", [1, 2, 32, 64])
    cmp_w1 = din("cmp_w1", [1, 2, 2048, 256])
    cmp_w2 = din("cmp_w2", [1, 2, 256, 64])
    mem_norm_g = din("mem_norm_g", [1, 1024])
    w_mem_kv = din("w_mem_kv", [1, 1024, 1024])
    w_br_pool = din("w_br_pool", [1, 512, 1024])
    w_br_nsa = din("w_br_nsa", [1, 1024, 1024])
    w_br_xa = din("w_br_xa", [1, 512, 1024])
    w_out = din("w_out", [1, 1024, 1024])
    post_mix_g = din("post_mix_g", [1, 1024])
    pre_ffn_g = din("pre_ffn_g", [1, 1024])
    w_up = din("w_up", [1, 1024, 5632])
    conv_w = din("conv_w", [1, 3, 5632])
    conv_b = din("conv_b", [1, 5632])
    w_down = din("w_down", [1, 2816, 1024])
    post_ffn_g = din("post_ffn_g", [1, 1024])
    out_d = nc.dram_tensor("out", [16 * 128, 1024], F32, kind="ExternalOutput").ap()
    br_scr = nc.dram_tensor("br_scr", [17, 128, 24 * 128], BF16, kind="Internal").ap()
    x1_scr = nc.dram_tensor("x1_scr", [17, 128, 1024], F32, kind="Internal").ap()
    dbg_out = {}

    def dbg_t(name, shape, dt=F32):
        return nc.dram_tensor("dbg_" + name, list(shape), dt, kind="ExternalOutput").ap()

    with ExitStack() as st:
        S = Sched(nc, st)
        ARN = 95800
        arena = st.enter_context(nc.sbuf_tensor("arena", [128, ARN], BF16))
        PS = [st.enter_context(nc.psum_tensor(f"ps{i}", [128, 512], F32)) for i in range(8)]
        PB = [S.buf(f"ps{i}", excl=True) for i in range(8)]
        top = [0]

        def abf(n):
            a = arena[:, top[0]:top[0] + n]
            top[0] += (n + 31) // 32 * 32
            assert top[0] <= ARN, top[0]
            return a

        def af32(n):
            a = arena[:, top[0]:top[0] + 2 * n].bitcast(F32)
            top[0] += (n + 15) // 16 * 32
            assert top[0] <= ARN, top[0]
            return a

        def ai32(n):
            a = arena[:, top[0]:top[0] + 2 * n].bitcast(I32)
            top[0] += (n + 15) // 16 * 32
            assert top[0] <= ARN, top[0]
            return a

        def mm(out, lhsT, rhs, r, w, start=True, stop=True):
            return S.op("pe", lambda e: e.matmul(out, lhsT=lhsT, rhs=rhs, start=start, stop=stop,
                                                 skip_group_check=True), r, w)

        last_func = [None]

        def act(out, in_, func, r, w, bias=None, scale=None, accum=None):
            if func != last_func[0]:
                last_func[0] = func
                S.op("act", lambda e: e.activation(out=bsc[1][:, 0:1], in_=bsc[3][:, 0:1], func=func), [B_bsc3], [])
            kw = {}
            if bias is not None:
                kw["bias"] = bias
            if scale is not None:
                kw["scale"] = scale
            if accum is not None:
                kw["accum_out"] = accum
            return S.op("act", lambda e: e.activation(out=out, in_=in_, func=func, **kw), r, w)

        def ts(eng, out, in0, s1, s2, op0, op1, r, w):
            if s2 is None:
                return S.op(eng, lambda e: e.tensor_scalar(out=out, in0=in0, scalar1=s1, scalar2=None, op0=op0), r, w)
            return S.op(eng, lambda e: e.tensor_scalar(out=out, in0=in0, scalar1=s1, scalar2=s2, op0=op0, op1=op1), r, w)

        def tt(eng, out, in0, in1, op, r, w):
            return S.op(eng, lambda e: e.tensor_tensor(out=out, in0=in0, in1=in1, op=op), r, w)

        def stt(out, in0, scalar, in1, op0, op1, r, w, accum=None):
            if accum is None:
                return S.op("dve", lambda e: e.scalar_tensor_tensor(out=out, in0=in0, scalar=scalar, in1=in1, op0=op0, op1=op1), r, w)
            return S.op("dve", lambda e: e.scalar_tensor_tensor(out=out, in0=in0, scalar=scalar, in1=in1, op0=op0, op1=op1,
                                                                accum_out=accum), r, w)

        def cp(eng, out, in_, r, w, scale=None):
            if eng == "act":
                return act(out, in_, AF.Copy, r, w, scale=scale)
            if scale is not None:
                return ts(eng, out, in_, scale, None, ALU.mult, None, r, w)
            return S.op(eng, lambda e: e.tensor_copy(out=out, in_=in_), r, w)

        def memset(eng, ap, val, w):
            return S.op(eng, lambda e: e.memset(ap, val), (), w)

        dmas = []

        def dma(out, in_, r, w, slow=False):
            if slow:
                i = S.op("sp", lambda e: e.dma_start(out=out, in_=in_, allow_slow_non_contiguous=True), r, w, dma=True)
            else:
                i = S.op("sp", lambda e: e.dma_start(out=out, in_=in_), r, w, dma=True)
            dmas.append(i)
            return i

        bsc = [af32(16) for _ in range(4)]
        bbuf = {e: S.buf("bar_" + e) for e in S.ENGS}
        bar_scr = nc.dram_tensor("bar_scr", [128, 16], F32, kind="Internal").ap()

        def barrier():
            allb = list(bbuf.values())
            mm(PS[7][:, 0:1], ident_b[:, 0:128], ident_b[:, 0:1], [PB[7], B_ident], [bbuf["pe"], PB[7]])
            memset("dve", bsc[0], 0.0, [bbuf["dve"]])
            act(bsc[1], bsc[3], AF.Copy, [B_bsc3], [bbuf["act"]])
            memset("pool", bsc[2], 0.0, [bbuf["pool"]])
            i = S.op("sp", lambda e: e.dma_start(out=bar_scr, in_=bsc[3]), [B_bsc3], [bbuf["sp"]], dma=True)
            for d in dmas:
                if d not in i.deps:
                    i.deps.append(d)
            dmas.clear()
            mm(PS[7][:, 0:1], ident_b[:, 0:128], ident_b[:, 0:1], allb + [PB[7], B_ident], [PB[7]])
            S.op("dve", lambda e: e.memset(bsc[0], 0.0), allb, [])
            act(bsc[1], bsc[3], AF.Copy, allb + [B_bsc3], [])
            S.op("pool", lambda e: e.memset(bsc[2], 0.0), allb, [])
            S.op("sp", lambda e: e.dma_start(out=bar_scr, in_=bsc[3]), allb + [B_bsc3], [], dma=True)

        ident_b = abf(128)
        B_ident = S.buf("ident")
        stage = [af32(1024), af32(1024)]
        B_stage = [S.buf("stg0"), S.buf("stg1")]
        stg_i = [0]
        cast_i = [0]
        B_bsc3 = S.buf("bsc3")
        memset("dve", bsc[3], 0.0, [B_bsc3])

        def loadw(dst_fn, src, ncols, scale=None, r_extra=(), w=None, perm_q=False):
            for c0 in range(0, ncols, 1024):
                n = min(1024, ncols - c0)
                k = stg_i[0] % 2
                stg_i[0] += 1
                np_ = src.shape[0]
                sl = stage[k][0:np_, 0:n]
                dma(sl, src[:, c0:c0 + n], [], [B_stage[k]])
                eng = ("act", "dve")[cast_i[0] % 2]
                cast_i[0] += 1
                def cast(o, i_):
                    rr = [B_stage[k]] + list(r_extra)
                    if eng == "act":
                        cp("act", o, i_, rr, w, scale=scale)
                    elif scale is not None:
                        ts(eng, o, i_, scale, None, ALU.mult, None, rr, w)
                    else:
                        cp(eng, o, i_, rr, w)

                if perm_q:
                    for g in range(2):
                        cast(dst_fn(c0, n).rearrange("p (c g d) -> p c g d", c=8, g=2)[:, :, g, :],
                             stage[k][0:np_, g * 512:(g + 1) * 512].rearrange("p (c d) -> p c d", c=8))
                else:
                    cast(dst_fn(c0, n), sl)

        def tr(out_ps, in_sb, r, w, start=True):
            return mm(out_ps, in_sb, ident_b[0:in_sb.shape[0], 0:in_sb.shape[0]], list(r) + [B_ident], w)

        dma(stage[0][:, 0:128], ident_d, [], [B_stage[0]])
        cp("dve", ident_b, stage[0][:, 0:128], [B_stage[0]], [B_ident])

        gpre = af32(8)
        gmem = af32(8)
        gffn = af32(8)
        B_small = S.buf("small")
        dma(gpre, pre_mix_g[0].rearrange("(k p) -> p k", p=128), [], [B_small], slow=True)
        dma(gmem, mem_norm_g[0].rearrange("(k p) -> p k", p=128), [], [B_small], slow=True)
        dma(gffn, pre_ffn_g[0].rearrange("(k p) -> p k", p=128), [], [B_small], slow=True)
        tqe = af32(NEXT)
        dma(tqe, tq_ext, [], [B_small])
        wval = af32(NEXT)
        dma(wval, wvalid_d.partition_broadcast(128), [], [B_small])
        hflag = af32(1)
        dma(hflag, hflag_d.partition_broadcast(128), [], [B_small])
        invcnt = af32(NEXT * 4)
        dma(invcnt, invcnt_d, [], [B_small])
        ss2 = af32(4)
        B_ss2 = S.buf("ss2")
        persist_top = top[0]

        def rms_scale(xt, xn, Bx, Bxn, ss, junk, Bss):
            act(junk, xt, AF.Square, [Bx], [Bss], accum=ss)
            ts("dve", ss, ss, 1.0 / 1024, 1e-6, ALU.mult, ALU.add, [Bss], [Bss])
            act(ss, ss, AF.Sqrt, [Bss], [Bss])
            S.op("dve", lambda e: e.reciprocal(out=ss, in_=ss), [Bss], [Bss])
            ts("dve", xn, xt, ss, None, ALU.mult, None, [Bx, Bss], [Bxn])

        def sincos(ang, n, osin, ocos, tmp, Bt, Bo):
            t, kf, g, ki = tmp
            ts("dve", ang, ang, 1.0 / TWO_PI, None, ALU.mult, None, [Bt], [Bt])
            for dst, off in ((osin, 0.0), (ocos, 0.25)):
                ts("dve", t, ang, off, None, ALU.add, None, [Bt], [Bt])
                cp("dve", ki, t, [Bt], [Bt])
                cp("dve", kf, ki, [Bt], [Bt])
                tt("dve", t, t, kf, ALU.subtract, [Bt], [Bt])
                ts("dve", g, t, 0.5, None, ALU.is_gt, None, [Bt], [Bt])
                tt("dve", t, t, g, ALU.subtract, [Bt], [Bt])
                ts("dve", g, t, -0.5, None, ALU.is_lt, None, [Bt], [Bt])
                tt("dve", t, t, g, ALU.add, [Bt], [Bt])
                act(dst, t, AF.Sin, [Bt], [Bo], scale=TWO_PI)

        def rotary(src4, dst4, cs, sn, tmp, rB, wB, Bt):
            a, b = src4.shape[1], src4.shape[2]
            n = a * b * 8
            x1 = src4[:, :, :, 0:8]
            x2 = src4[:, :, :, 8:16]
            csb = cs.unsqueeze(1).unsqueeze(1).to_broadcast([128, a, b, 8])
            snb = sn.unsqueeze(1).unsqueeze(1).to_broadcast([128, a, b, 8])
            t1 = tmp[:, 0:n].rearrange("p (a b d) -> p a b d", a=a, b=b)
            t2 = tmp[:, n:2 * n].rearrange("p (a b d) -> p a b d", a=a, b=b)
            tt("dve", t1, x1, csb, ALU.mult, rB, [Bt])
            tt("dve", t2, x2, snb, ALU.mult, rB, [Bt])
            tt("dve", dst4[:, :, :, 0:8], t1, t2, ALU.subtract, [Bt], wB)
            tt("dve", t1, x2, csb, ALU.mult, rB, [Bt])
            tt("dve", t2, x1, snb, ALU.mult, rB, [Bt])
            tt("dve", dst4[:, :, :, 8:16], t1, t2, ALU.add, [Bt], wB)

        KTs = abf(8192)
        Vs = abf(64 * 2 * 65).rearrange("p (t g d) -> p t g d", t=64, g=2)
        KTw = abf(NEXT * 128)
        Vw = abf(NEXT * 2 * 65).rearrange("p (t g d) -> p t g d", t=NEXT, g=2)
        KcT = abf(512)
        Vc = abf(4 * 2 * 65).rearrange("p (t g d) -> p t g d", t=4, g=2)
        Gb = abf(8192)
        B_KTs, B_Vs, B_KTw, B_Vw, B_KcT, B_Vc, B_G = [S.buf(n) for n in "KTs Vs KTw Vw KcT Vc G".split()]
        cosE = af32(NEXT * 8).rearrange("p (t f) -> p t f", f=8)
        sinE = af32(NEXT * 8).rearrange("p (t f) -> p t f", f=8)
        cos8 = af32(NEXT * 8).rearrange("p (t f) -> p t f", f=8)
        sin8 = af32(NEXT * 8).rearrange("p (t f) -> p t f", f=8)
        B_tabE = S.buf("tabE")
        kv_top = top[0]

        memset("pool", Vs, 1.0, [B_Vs])
        memset("pool", Vw, 1.0, [B_Vw])
        memset("pool", Vc, 0.0, [B_Vc])
        memset("pool", Vc[:, :, :, 64:65], 1.0, [B_Vc])
        loadw(lambda c0, n: Gb[:, c0:c0 + n], G_d, 8192, w=[B_G])

        cosA = af32(NALL * 8).rearrange("p (t f) -> p t f", f=8)
        sinA = af32(NALL * 8).rearrange("p (t f) -> p t f", f=8)
        B_tabA = S.buf("tabA")
        KcRaw = abf(8192)
        VcRaw = abf(8192)
        B_KcRaw, B_VcRaw = S.buf("KcRaw"), S.buf("VcRaw")
        wkvA = abf(8 * 512).rearrange("p (k c) -> p k c", k=8)
        B_wkvA = S.buf("wkvA")
        for k in range(8):
            loadw(lambda c0, n, k=k: wkvA[:, k, c0:c0 + n], w_in[0][k * 128:(k + 1) * 128, 1536:2048], 512,
                  scale=gpre[:, k:k + 1], r_extra=[B_small], w=[B_wkvA])
        mark_tab = top[0]
        posi = ai32(512)
        posf = af32(512)
        invf = af32(8)
        ang = af32(512)
        tmp3 = (af32(512), af32(512), af32(512), ai32(512))
        B_t = S.buf("tabtmp")
        dma(invf, invf_d.partition_broadcast(128), [], [B_t])
        dma(posi[:, 0:NALL], pos_all, [], [B_t])
        cp("dve", posf[:, 0:NALL], posi[:, 0:NALL], [B_t], [B_t])
        tt("dve", ang.rearrange("p (t f) -> p t f", f=8), posf[:, 0:NALL].unsqueeze(2).to_broadcast([128, NALL, 8]),
           invf.unsqueeze(1).to_broadcast([128, NALL, 8]), ALU.mult, [B_t], [B_t])
        sincos(ang, 512, sinA.rearrange("p t f -> p (t f)"), cosA.rearrange("p t f -> p (t f)"),
               tmp3, B_t, B_tabA)
        dma(posi[:, 0:NEXT], pos_ext, [B_t], [B_t])
        cp("dve", posf[:, 0:NEXT], posi[:, 0:NEXT], [B_t], [B_t])
        ne = NEXT * 8
        tt("dve", ang[:, 0:ne].rearrange("p (t f) -> p t f", f=8), posf[:, 0:NEXT].unsqueeze(2).to_broadcast([128, NEXT, 8]),
           invf.unsqueeze(1).to_broadcast([128, NEXT, 8]), ALU.mult, [B_t], [B_t])
        sincos(ang[:, 0:ne], ne, sinE.rearrange("p t f -> p (t f)"), cosE.rearrange("p t f -> p (t f)"),
               tuple(a[:, 0:ne] for a in tmp3), B_t, B_tabE)
        ts("dve", cos8.rearrange("p t f -> p (t f)"), cosE.rearrange("p t f -> p (t f)"), 0.125, None, ALU.mult, None, [B_tabE], [B_tabE])
        ts("dve", sin8.rearrange("p t f -> p (t f)"), sinE.rearrange("p t f -> p (t f)"), 0.125, None, ALU.mult, None, [B_tabE], [B_tabE])
        if dbg != "0":
            barrier()
            top[0] = mark_tab

        if dbg == "0":
            f1 = dma(dbg_t("cosA", [128, NALL * 8]), cosA.rearrange("p t f -> p (t f)"), [B_tabA], [])
            f2 = dma(dbg_t("sinA", [128, NALL * 8]), sinA.rearrange("p t f -> p (t f)"), [B_tabA], [])
            f3 = dma(dbg_t("cos8", [128, NEXT * 8]), cos8.rearrange("p t f -> p (t f)"), [B_tabE], [])
            f4 = dma(dbg_t("Gb", [128, 8192], BF16), Gb, [B_G], [])
            print(S.emit(final_waits=[f1, f2, f3, f4]))
            return nc
        xt2 = [af32(1024), af32(1024)]
        B_xt = [S.buf("xt0"), S.buf("xt1")]
        xn = abf(1024)
        B_xn = S.buf("xn")
        junk = abf(1024)
        ssA = af32(1)
        B_ss = S.buf("ss")
        hT = abf(1024).rearrange("p (k t) -> p k t", k=8)
        B_hT = S.buf("hT")
        kb = abf(512)
        B_kb = S.buf("kb")
        rtmp = af32(2 * 16 * 8)
        B_rt = S.buf("rtmp")

        def norm_T(xt, Bx, pa, pb, hTd, BhT, xnb=None, Bxnb=None):
            if xnb is None:
                xnb, Bxnb = xn, B_xn
            rms_scale(xt, xnb, Bx, Bxnb, ssA, junk, B_ss)
            for k in range(8):
                bank = pa if k < 4 else pb
                tr(PS[bank][:, (k % 4) * 128:(k % 4 + 1) * 128], xnb[:, k * 128:(k + 1) * 128], [Bxnb], [PB[bank]])
            cp("act", hTd[:, 0:4, :], PS[pa][:, :].rearrange("p (k t) -> p k t", k=4), [PB[pa]], [BhT])
            cp("dve", hTd[:, 4:8, :], PS[pb][:, :].rearrange("p (k t) -> p k t", k=4), [PB[pb]], [BhT])

        xnA = [xn, abf(1024)]
        B_xnA = [B_xn, S.buf("xnA1")]
        hTA = [hT, abf(1024).rearrange("p (k t) -> p k t", k=8)]
        B_hTA = [B_hT, S.buf("hTA1")]
        dma(xt2[0], x_all[0:128, :], [], [B_xt[0]])
        dma(xt2[1], x_all[128:256, :], [], [B_xt[1]])
        norm_T(xt2[0], B_xt[0], 0, 1, hTA[0], B_hTA[0], xnA[0], B_xnA[0])
        for T in range(na):
            s = T % 2
            if T + 1 < na:
                norm_T(xt2[1 - s], B_xt[1 - s], 0, 1, hTA[1 - s], B_hTA[1 - s], xnA[1 - s], B_xnA[1 - s])
            if T + 2 < na:
                dma(xt2[s], x_all[(T + 2) * 128:(T + 3) * 128, :], [], [B_xt[s]])
            hT, B_hT = hTA[s], B_hTA[s]
            for k in range(8):
                mm(PS[2][:, 0:512], hT[:, k, :], wkvA[:, k, :], [B_hT, B_wkvA], [PB[2]], start=(k == 0), stop=(k == 7))
            cp("act", kb, PS[2][:, 0:512], [PB[2]], [B_kb])
            v5 = PS[2][:, 0:512].rearrange("p (j2 jj g d) -> p j2 jj g d", j2=2, jj=2, g=2)
            k5 = kb.rearrange("p (j2 jj g d) -> p j2 jj g d", j2=2, jj=2, g=2)
            rotary(v5[:, :, 0, :, :], k5[:, :, 0, :, :], cosA[:, T, :], sinA[:, T, :], rtmp, [PB[2], B_tabA], [B_kb], B_rt)
            cp("pool", Vs[:, T, :, 0:64], kb[:, 384:512].rearrange("p (g d) -> p g d", g=2), [B_kb], [B_Vs])
            tr(PS[3][:, 0:128], kb[:, 0:128], [B_kb], [PB[3]])
            tr(PS[3][:, 128:256], kb[:, 128:256], [B_kb], [PB[3]])
            tr(PS[3][:, 256:384], kb[:, 256:384], [B_kb], [PB[3]])
            cp("act", KcRaw[:, T * 128:(T + 1) * 128], PS[3][:, 0:128], [PB[3]], [B_KcRaw])
            cp("dve", VcRaw[:, T * 128:(T + 1) * 128], PS[3][:, 128:256], [PB[3]], [B_VcRaw])
            cp("act", KTs[:, T * 128:(T + 1) * 128], PS[3][:, 256:384], [PB[3]], [B_KTs])

        w1z = [abf(32 * 256).rearrange("p (l m) -> p l m", l=32) for _ in range(2)]
        B_w1 = S.buf("w1b")
        memset("pool", w1z[0][64:128, :, :], 0.0, [B_w1])
        memset("pool", w1z[1][0:64, :, :], 0.0, [B_w1])
        w2b = abf(2 * 128).rearrange("p (h d) -> p h d", h=2)
        B_w2 = S.buf("w2b")
        peT = abf(32)
        pef = af32(32)
        B_pe = S.buf("pe")
        hid2 = [abf(2 * 512).rearrange("p (h c) -> p h c", h=2) for _ in range(2)]
        B_hid2 = [S.buf("hid0"), S.buf("hid1")]
        cbias = af32(2)
        B_cb = S.buf("cbias")
        for kvi, raw, Braw in (() if skipcmp else ((0, KcRaw, B_KcRaw), (1, VcRaw, B_VcRaw))):
            w1v = cmp_w1[0][kvi].rearrange("(l d) m -> d l m", d=64)
            for half in range(2):
                for l0 in range(0, 32, 4):
                    k = stg_i[0] % 2
                    stg_i[0] += 1
                    sl = stage[k][64 * half:64 * half + 64, 0:1024]
                    dma(sl.rearrange("p (l m) -> p l m", l=4), w1v[:, l0:l0 + 4, :], [], [B_stage[k]])
                    cp(("act", "dve")[(l0 // 4) % 2], w1z[half][64 * half:64 * half + 64, l0:l0 + 4, :],
                       sl.rearrange("p (l m) -> p l m", l=4), [B_stage[k]], [B_w1])
            k = stg_i[0] % 2
            stg_i[0] += 1
            dma(stage[k][:, 0:128].rearrange("p (h d) -> p h d", h=2), cmp_w2[0][kvi].rearrange("(h p) d -> p h d", p=128),
                [], [B_stage[k]])
            cp("dve", w2b[:, :, 0:64], stage[k][:, 0:128].rearrange("p (h d) -> p h d", h=2), [B_stage[k]], [B_w2])
            cp("dve", w2b[:, :, 64:128], stage[k][:, 0:128].rearrange("p (h d) -> p h d", h=2), [B_stage[k]], [B_w2])
            dma(pef[0:64, :], cmp_pe[0][kvi].rearrange("l d -> d l"), [], [B_pe], slow=True)
            memset("dve", peT[64:128, :], 0.0, [B_pe])
            cp("dve", peT[0:64, :], pef[0:64, :], [B_pe], [B_pe])
            for half in range(2):
                for l in range(32):
                    mm(PS[4][:, half:half + 1], w1z[0][:, l, half * 128:(half + 1) * 128], peT[:, l:l + 1],
                       [B_w1, B_pe], [PB[4]], start=(l == 0 and half == 0), stop=(l == 31))
            cp("dve", cbias, PS[4][:, 0:2], [PB[4]], [B_cb])
            if not NOBAR:
                barrier()
            if dbg == "A" and kvi == 0 and (DUMPX & 1):
                dma(dbg_t("w1b", [128, 8192], BF16), w1z[0].rearrange("p l m -> p (l m)"), [B_w1], [])
                dma(dbg_t("cbias", [128, 2]), cbias, [B_cb], [])
            rawv = raw.rearrange("p (i s) -> p i s", s=16)
            for g in range(2):
                if not NOBAR:
                    barrier()
                hid, B_hid = hid2[g], B_hid2[g]
                pr = slice(64 * g, 64 * g + 64)
                for half in range(2):
                    for l in range(32):
                        rhs = rawv[:, 0:511, l] if l < 16 else rawv[:, 1:512, l - 16]
                        mm(PS[half][:, 0:511], w1z[g][:, l, half * 128:(half + 1) * 128], rhs, [B_w1, Braw], [PB[half]],
                           start=(l == 0), stop=(l == 31))
                    act(hid[:, half, 0:511], PS[half][:, 0:511], AF.Gelu_apprx_tanh, [PB[half], B_cb], [B_hid],
                        bias=cbias[:, half:half + 1])
                if dbg == "A" and kvi == 0 and (DUMPX & 2):
                    dma(dbg_t(f"hid{g}", [128, 1024], BF16), hid.rearrange("p h c -> p (h c)"), [B_hid], [])
                if kvi == 0:
                    for half in range(2):
                        mm(PS[2][:, 0:511], w2b[:, half, :], hid[:, half, 0:511], [B_w2, B_hid], [PB[2]],
                           start=(half == 0), stop=(half == 1))
                    cp("act", KcT[pr, 0:511], PS[2][pr, 0:511], [PB[2]], [B_KcT])
                else:
                    for c in range(4):
                        m = 128 if c < 3 else 127
                        for half in range(2):
                            mm(PS[2][0:m, c * 64:(c + 1) * 64], hid[:, half, c * 128:c * 128 + m], w2b[:, half, 0:64],
                               [B_w2, B_hid], [PB[2]], start=(half == 0 and c == 0), stop=(half == 1))
                    for c in range(4):
                        m = 128 if c < 3 else 127
                        cp("act", Vc[0:m, c, g, 0:64], PS[2][0:m, c * 64:(c + 1) * 64], [PB[2]], [B_Vc])
        memset("dve", KcT[:, 511:512], 0.0, [B_KcT])
        if dbg == "A":
            fl = [dma(dbg_t("KTs", [128, 8192], BF16), KTs, [B_KTs], []),
                  dma(dbg_t("Vs", [128, 64 * 130], BF16), Vs.rearrange("p t g d -> p (t g d)"), [B_Vs], []),
                  dma(dbg_t("KcT", [128, 512], BF16), KcT, [B_KcT], []),
                  dma(dbg_t("Vc", [128, 4 * 130], BF16), Vc.rearrange("p t g d -> p (t g d)"), [B_Vc], []),
                  dma(dbg_t("KcRaw", [128, 8192], BF16), KcRaw, [B_KcRaw], [])]
            print(S.emit(final_waits=fl))
            return nc
        barrier()
        top[0] = kv_top
        wb1 = abf(8 * 2352).rearrange("p (k c) -> p k c", k=8)
        B_wb1 = S.buf("wb1")
        for k in range(8):
            rows = w_in[0][k * 128:(k + 1) * 128, :]
            sc = gpre[:, k:k + 1]
            loadw(lambda c0, n, k=k: wb1[:, k, c0:c0 + n], rows[:, 0:512], 512, scale=sc, r_extra=[B_small], w=[B_wb1])
            loadw(lambda c0, n, k=k: wb1[:, k, 512:1536], rows[:, 512:1536], 1024, scale=sc, r_extra=[B_small], w=[B_wb1], perm_q=True)
            loadw(lambda c0, n, k=k: wb1[:, k, 1536 + c0:1536 + c0 + n], rows[:, 2048:2864], 816, scale=sc, r_extra=[B_small], w=[B_wb1])
        poolw = abf(4 * 128).rearrange("p (g d) -> p g d", g=4)
        B_cst = S.buf("cst")
        k_ = stg_i[0] % 2
        stg_i[0] += 1
        dma(stage[k_][:, 0:512].rearrange("p (g d) -> p g d", g=4), pool_w[0].rearrange("g c d -> c g d"), [], [B_stage[k_]])
        cp("dve", poolw, stage[k_][:, 0:512].rearrange("p (g d) -> p g d", g=4), [B_stage[k_]], [B_cst])
        pscale = af32(4)
        dma(pscale, pool_scale[0].rearrange("(g d) -> d g", d=128), [], [B_cst], slow=True)
        Ab = abf(1024).rearrange("p (g c t) -> p g c t", g=4, c=2)
        k_ = stg_i[0] % 2
        stg_i[0] += 1
        dma(stage[k_][:, 0:1024], Ab_d, [], [B_stage[k_]])
        cp("dve", Ab.rearrange("p g c t -> p (g c t)"), stage[k_][:, 0:1024], [B_stage[k_]], [B_cst])
        Mw3 = abf(384).rearrange("p (j q) -> p j q", j=3)
        k_ = stg_i[0] % 2
        stg_i[0] += 1
        dma(stage[k_][:, 0:256], Mw_d, [], [B_stage[k_]])
        cp("dve", Mw3[:, 0, :], stage[k_][:, 0:128], [B_stage[k_]], [B_cst])
        cp("dve", Mw3[:, 2, :], stage[k_][:, 128:256], [B_stage[k_]], [B_cst])
        memset("dve", Mw3[:, 1, :], 1.0, [B_cst])
        onesb = abf(1)
        memset("dve", onesb, 1.0, [B_cst])
        qrow = af32(128)
        dma(qrow, qrow_d.partition_broadcast(128), [], [B_cst])
        tqst = af32(NEXT)
        dma(tqst, tqst_d.partition_broadcast(128), [], [B_cst])
        tqt = af32(128)
        B_tqt = S.buf("tqt")
        bsrow = af32(128)
        dma(bsrow, bsrow_d.partition_broadcast(128), [], [B_cst])
        cendrow = af32(512)
        dma(cendrow, cendrow_d.partition_broadcast(128), [], [B_cst])
        kidx = af32(64)
        dma(kidx, kidx_d, [], [B_cst])
        cendcol = af32(4)
        dma(cendcol, cendcol_d, [], [B_cst])
        e0big = af32(128)
        dma(e0big, e0_d.partition_broadcast(128), [], [B_cst])

        xt2 = [af32(1024), af32(1024)]
        B_xt = [S.buf("bxt0"), S.buf("bxt1")]
        xn = abf(1024)
        B_xn = S.buf("bxn")
        junk = abf(1024)
        ssA = af32(1)
        B_ss = S.buf("bss")
        brT0 = abf(24 * 128).rearrange("p (k t) -> p k t", k=24)
        brT = [brT0, brT0]
        B_br0 = S.buf("br0")
        B_br = [B_br0, B_br0]
        rtmp = af32(256)
        B_rt = S.buf("brtmp")

        KmT = abf(4 * 256).rearrange("p (h m) -> p h m", h=4)
        Vm = abf(2 * 4 * 128).rearrange("p (c h d) -> p c h d", c=2, h=4)
        kmb = abf(512)
        mark_m = top[0]
        wmem = abf(8 * 1024).rearrange("p (k c) -> p k c", k=8)
        B_wmem = S.buf("wmem")
        for k in range(8):
            loadw(lambda c0, n, k=k: wmem[:, k, c0:c0 + n], w_mem_kv[0][k * 128:(k + 1) * 128, :], 1024,
                  scale=gmem[:, k:k + 1], r_extra=[B_small], w=[B_wmem])
        B_km, B_vm, B_kmb = S.buf("KmT"), S.buf("Vm"), S.buf("kmb")
        for c in range(2):
            dma(xt2[c], mem_d[c * 128:(c + 1) * 128, :], [], [B_xt[c]])
            norm_T(xt2[c], B_xt[c], 0, 1, brT[c][:, 16:24, :], B_br[c])
            for nb in range(2):
                for k in range(8):
                    mm(PS[2 + nb][:, :], brT[c][:, 16 + k, :], wmem[:, k, nb * 512:(nb + 1) * 512], [B_br[c], B_wmem], [PB[2 + nb]],
                       start=(k == 0), stop=(k == 7))
            cp("act", kmb, PS[2][:, :], [PB[2]], [B_kmb], scale=float(128 ** -0.5))
            cp("dve", Vm[:, c, :, :], PS[3][:, :].rearrange("p (h d) -> p h d", h=4), [PB[3]], [B_vm])
            for h in range(4):
                tr(PS[4][:, h * 128:(h + 1) * 128], kmb[:, h * 128:(h + 1) * 128], [B_kmb], [PB[4]])
            cp("act", KmT[:, :, c * 128:(c + 1) * 128], PS[4][:, :].rearrange("p (h m) -> p h m", h=4), [PB[4]], [B_km])

        barrier()
        top[0] = mark_m
        kwb = abf(256)
        B_kwb = S.buf("kwb")
        ub = [abf(512), abf(512)]
        B_ub = [S.buf("ub0"), S.buf("ub1")]
        uf = af32(512)
        B_uf = S.buf("uf")
        pbb = abf(512)
        B_pbb = S.buf("pbb")
        pT = abf(512).rearrange("p (g t) -> p g t", g=4)
        B_pT = S.buf("pT")
        qb = abf(1024)
        B_qb = S.buf("qb")
        qTz = [abf(1024).rearrange("p (k t) -> p k t", k=8) for _ in range(2)]
        B_qT = S.buf("qT")
        memset("pool", qTz[0], 0.0, [B_qT])
        memset("pool", qTz[1], 0.0, [B_qT])
        gn = af32(48)
        B_gn = S.buf("gn")
        qxb = abf(512)
        B_qxb = S.buf("qxb")
        qxT = abf(512).rearrange("p (h t) -> p h t", h=4)
        B_qxT = S.buf("qxT")
        mpT = [abf(512).rearrange("p (h t) -> p h t", h=4) for _ in range(2)]
        B_mpT = [S.buf("mpT0"), S.buf("mpT1")]
        rsm = af32(4)
        B_rsm = S.buf("rsm")
        ymemb = abf(512)
        B_ymem = S.buf("ymem")
        ef = [af32(512), af32(512)]
        B_ef = [S.buf("ef0"), S.buf("ef1")]
        ssum2 = [af32(2), af32(2)]
        B_ssum2 = [S.buf("ssum0"), S.buf("ssum1")]
        Pb = [af32(516), af32(516)]
        B_Pb = [S.buf("Pb0"), S.buf("Pb1")]
        cmneg = abf(512)
        B_cm = S.buf("cmneg")
        imp = af32(128)
        nd = af32(128)
        itmp = af32(128)
        wk = af32(128)
        mx = af32(16)
        B_imp = S.buf("imp")
        selb = abf(128)
        B_sel = S.buf("sel")
        selT = [abf(128), abf(128)]
        B_selT = [S.buf("selT0"), S.buf("selT1")]
        NSL = 3
        NSU = 6
        ucount = [0]
        mk = [abf(128) for _ in range(NSL)]
        B_mk = [S.buf(f"mk{i}") for i in range(NSL)]
        pTu = [abf(512).rearrange("p (h t) -> p h t", h=4) for _ in range(NSU)]
        B_pTu = [S.buf(f"pTu{i}") for i in range(NSU)]
        ynsa = af32(1024)
        B_yn = S.buf("ynsa")
        ytmp = af32(256)
        B_yt = S.buf("ytmp")
        rs = af32(8)
        B_rs = S.buf("rs")
        ynb = abf(1024)
        B_ynb = S.buf("ynb")
        memset("dve", Pb[0], 0.0, [B_Pb[0]])
        memset("dve", Pb[1], 0.0, [B_Pb[1]])
        nchunk = [0]

        def attend(e, g, br, chunks):
            LOOK = 3
            units = [(n, q) for n in range(len(chunks)) for q in range(2)]
            nu = len(units)
            info = {}

            def stage_scores(u):
                n, q = units[u]
                KT, V, mk_pe, mk_dve = chunks[n]
                if q == 0:
                    cs = nchunk[0]
                    nchunk[0] += 1
                    info[n] = (cs % 2, cs % NSL)
                    if mk_pe is not None:
                        mk_pe(cs % 2)
                bank = 2 + (ucount[0] % 4)
                ub = ucount[0] % NSU
                ucount[0] += 1
                mm(PS[bank][:, :], KT, qTz[g][:, 4 * q:4 * q + 4, :], [B_qT, B_KTs, B_KTw, B_KcT], [PB[bank]])
                return (bank, ub)

            pend = [stage_scores(u) for u in range(min(LOOK, nu))]
            for u, (n, q) in enumerate(units):
                bank, ub = pend.pop(0)
                if u + LOOK < nu:
                    pend.append(stage_scores(u + LOOK))
                KT, V, mk_pe, mk_dve = chunks[n]
                sl, bs = info[n]
                pt = pTu[ub]
                act(pt, PS[bank][:, :].rearrange("p (h t) -> p h t", h=4), AF.Exp, [PB[bank]], [B_pTu[ub]])
                if mk_dve is not None:
                    if q == 0:
                        mk_dve(sl, bs)
                    mb = mk[bs].unsqueeze(1).to_broadcast([128, 4, 128])
                    tt("dve", pt, pt, mb, ALU.mult, [B_pTu[ub], B_mk[bs]], [B_pTu[ub]])
                for hh in range(4 * q, 4 * q + 4):
                    mm(PS[q][:, (hh % 4) * 128:(hh % 4) * 128 + 65], pt[:, hh % 4, :], V, [B_pTu[ub], B_Vs, B_Vw, B_Vc], [PB[q]],
                       start=(n == 0 and hh % 4 == 0), stop=(n == len(chunks) - 1))
            gn3 = gn.rearrange("p (h b) -> p h b", b=3)
            for b in range(2):
                Ov = PS[b][:, :].rearrange("p (h d) -> p h d", h=4)
                h0 = 8 * g + 4 * b
                rsb = rs[:, 4 * b:4 * b + 4]
                ts("dve", rsb, Ov[:, :, 64], 1e-30, None, ALU.max, None, [PB[b]], [B_rs])
                S.op("dve", lambda e_, rsb=rsb: e_.reciprocal(out=rsb, in_=rsb), [B_rs], [B_rs])
                tt("dve", rsb, rsb, gn3[:, h0:h0 + 4, br], ALU.mult, [B_rs, B_gn], [B_rs])
                dst = ynsa.rearrange("p (h d) -> p h d", h=16)[:, h0:h0 + 4, :]
                rb = rsb.unsqueeze(2).to_broadcast([128, 4, 64])
                if br == 0:
                    tt("dve", dst, Ov[:, :, 0:64], rb, ALU.mult, [PB[b], B_rs], [B_yn])
                else:
                    yt = ytmp.rearrange("p (h d) -> p h d", h=4)
                    tt("dve", yt, Ov[:, :, 0:64], rb, ALU.mult, [PB[b], B_rs], [B_yt])
                    tt("pool", dst, dst, yt, ALU.add, [B_yn, B_yt], [B_yn])

        dma(xt2[0], x_ext[0:128, :], [], [B_xt[0]])
        for e in range(ntile_b1):
            s = e % 2
            if e + 1 < NEXT:
                dma(xt2[1 - s], x_ext[(e + 1) * 128:(e + 2) * 128, :], [], [B_xt[1 - s]])
            bt = brT[s]
            hTd = bt[:, 16:24, :]
            norm_T(xt2[s], B_xt[s], 0, 1, hTd, B_br[s])
            for k in range(8):
                mm(PS[2][:, 0:256], hTd[:, k, :], wb1[:, k, 1536:1792], [B_br[s], B_wb1], [PB[2]], start=(k == 0), stop=(k == 7))
            cp("act", kwb, PS[2][:, 0:256], [PB[2]], [B_kwb])
            rotary(PS[2][:, 0:128].rearrange("p (a g d) -> p a g d", a=1, g=2), kwb[:, 0:128].rearrange("p (a g d) -> p a g d", a=1, g=2),
                   cosE[:, e, :], sinE[:, e, :], rtmp, [PB[2], B_tabE], [B_kwb], B_rt)
            ts("pool", Vw[:, e, :, 0:64], kwb[:, 128:256].rearrange("p (g d) -> p g d", g=2), wval[:, e:e + 1], None, ALU.mult, None,
               [B_kwb, B_small], [B_Vw])
            for g_ in range(2):
                cp("pool", Vw[:, e, g_, 64:65], wval[:, e:e + 1], [B_small], [B_Vw])
            tr(PS[3][:, 0:128], kwb[:, 0:128], [B_kwb], [PB[3]])
            cp("act", KTw[:, e * 128:(e + 1) * 128], PS[3][:, 0:128], [PB[3]], [B_KTw])
            for k in range(8):
                mm(PS[4][:, :], hTd[:, k, :], wb1[:, k, 0:512], [B_br[s], B_wb1], [PB[4]], start=(k == 0), stop=(k == 7))
            cp("act", ub[s], PS[4][:, :], [PB[4]], [B_ub[s]])
            if e < 4:
                continue
            cp("dve", uf, PS[4][:, :], [PB[4]], [B_uf])
            i = e - 4
            for gi in range(4):
                blk = slice(gi * 128, (gi + 1) * 128)
                mm(PS[5][:, blk], Ab[:, gi, 0, :], ub[s][:, blk], [B_cst, B_ub[s]], [PB[5]], start=True, stop=False)
                mm(PS[5][:, blk], Ab[:, gi, 1, :], ub[1 - s][:, blk], [B_cst, B_ub[1 - s]], [PB[5]], start=False, stop=True)
            for gi in range(4):
                blk = slice(gi * 128, (gi + 1) * 128)
                stt(pbb[:, blk], PS[5][:, blk], invcnt[:, e * 4 + gi:e * 4 + gi + 1], uf[:, blk], ALU.mult, ALU.subtract,
                    [PB[5], B_uf, B_small], [B_pbb])
            for gi in range(4):
                blk = slice(gi * 128, (gi + 1) * 128)
                tr(PS[6][:, blk], pbb[:, blk], [B_pbb], [PB[6]])
            cp("act", pT, PS[6][:, :].rearrange("p (g t) -> p g t", g=4), [PB[6]], [B_pT])
            for gi in range(4):
                blk = slice(gi * 128, (gi + 1) * 128)
                mm(PS[5][:, blk], poolw[:, gi, :], pT[:, gi, :], [B_cst, B_pT], [PB[5]])
            for gi in range(4):
                blk = slice(gi * 128, (gi + 1) * 128)
                ts("dve", bt[:, gi, :], PS[5][:, blk], pscale[:, gi:gi + 1], None, ALU.mult, None, [PB[5], B_cst], [B_br[s]])
            for nb in range(2):
                for k in range(8):
                    mm(PS[nb][:, :], hTd[:, k, :], wb1[:, k, 512 + nb * 512:1024 + nb * 512], [B_br[s], B_wb1], [PB[nb]],
                       start=(k == 0), stop=(k == 7))
            for nb in range(2):
                qv = qb[:, nb * 512:(nb + 1) * 512]
                cp("act", qv, PS[nb][:, :], [PB[nb]], [B_qb], scale=0.125)
                rotary(PS[nb][:, :].rearrange("p (c g d) -> p c g d", c=4, g=2), qv.rearrange("p (c g d) -> p c g d", c=4, g=2),
                       cos8[:, e, :], sin8[:, e, :], rtmp, [PB[nb], B_tabE], [B_qb], B_rt)
            for k in range(8):
                tr(PS[2 + k // 4][:, (k % 4) * 128:(k % 4 + 1) * 128], qb[:, k * 128:(k + 1) * 128], [B_qb], [PB[2 + k // 4]])
            for (bk, c0) in ((2, 0), (3, 4)):
                cp("act", qTz[0][0:64, c0:c0 + 4, :], PS[bk][0:64, :].rearrange("p (k t) -> p k t", k=4), [PB[bk]], [B_qT])
                cp("dve", qTz[1][64:128, c0:c0 + 4, :], PS[bk][64:128, :].rearrange("p (k t) -> p k t", k=4), [PB[bk]], [B_qT])
            for k in range(8):
                mm(PS[4][:, 0:48], hTd[:, k, :], wb1[:, k, 1792:1840], [B_br[s], B_wb1], [PB[4]], start=(k == 0), stop=(k == 7))
            act(gn, PS[4][:, 0:48], AF.Sigmoid, [PB[4]], [B_gn])
            for k in range(8):
                mm(PS[5][:, :], hTd[:, k, :], wb1[:, k, 1840:2352], [B_br[s], B_wb1], [PB[5]], start=(k == 0), stop=(k == 7))
            cp("act", qxb, PS[5][:, :], [PB[5]], [B_qxb])
            for h in range(4):
                tr(PS[6][:, h * 128:(h + 1) * 128], qxb[:, h * 128:(h + 1) * 128], [B_qxb], [PB[6]])
            cp("dve", qxT, PS[6][:, :].rearrange("p (h t) -> p h t", h=4), [PB[6]], [B_qxT])
            for c in range(2):
                for h in range(4):
                    mm(PS[7][:, h * 128:(h + 1) * 128], KmT[:, h, c * 128:(c + 1) * 128], qxT[:, h, :], [B_km, B_qxT], [PB[7]])
                act(mpT[c], PS[7][:, :].rearrange("p (h t) -> p h t", h=4), AF.Exp, [PB[7]], [B_mpT[c]])
            for h in range(4):
                for c in range(2):
                    mm(PS[5][:, h * 128:(h + 1) * 128], mpT[c][:, h, :], Vm[:, c, h, :], [B_mpT[c], B_vm], [PB[5]],
                       start=(c == 0), stop=(c == 1))
            for h in range(4):
                for c in range(2):
                    mm(PS[6][:, h:h + 1], mpT[c][:, h, :], onesb[:, 0:1], [B_mpT[c], B_cst], [PB[6]],
                       start=(c == 0), stop=(c == 1))
            S.op("dve", lambda e_: e_.reciprocal(out=rsm, in_=PS[6][:, 0:4]), [PB[6]], [B_rsm])
            tt("dve", ymemb.rearrange("p (h d) -> p h d", h=4), PS[5][:, :].rearrange("p (h d) -> p h d", h=4),
               rsm.unsqueeze(2).to_broadcast([128, 4, 128]), ALU.mult, [PB[5], B_rsm], [B_ymem])
            for h in range(4):
                tr(PS[7][:, h * 128:(h + 1) * 128], ymemb[:, h * 128:(h + 1) * 128], [B_ymem], [PB[7]])
            cp("act", bt[:, 12:16, :], PS[7][:, :].rearrange("p (h t) -> p h t", h=4), [PB[7]], [B_br[s]])
            tqs = tqe[:, e:e + 1]
            ts("dve", cmneg, cendrow, tqs, -29952.0, ALU.is_gt, ALU.mult, [B_cst, B_small], [B_cm])
            ts("dve", tqt, qrow, tqst[:, e:e + 1], None, ALU.add, None, [B_cst], [B_tqt])
            for g in range(2):
                pr = slice(64 * g, 64 * g + 64)
                Pv = Pb[g][:, 1:513]
                for c_ in range(8):
                    a = c_ % 2
                    ss_, Bs_ = ssum2[a], B_ssum2[a]
                    mm(PS[4 + a][:, :], qTz[g][:, c_, :], KcT[:, 0:512], [B_qT, B_KcT], [PB[4 + a]], start=True, stop=False)
                    mm(PS[4 + a][:, :], ident_b, cmneg, [B_ident, B_cm], [PB[4 + a]], start=False, stop=True)
                    act(ef[a], PS[4 + a][:, :], AF.Exp, [PB[4 + a]], [B_ef[a], Bs_], accum=ss_[:, 0:1])
                    ts("dve", ss_[:, 1:2], ss_[:, 0:1], 1e-30, None, ALU.max, None, [Bs_], [Bs_])
                    S.op("dve", lambda e_, ss_=ss_: e_.reciprocal(out=ss_[:, 1:2], in_=ss_[:, 1:2]), [Bs_], [Bs_])
                    if c_ == 0:
                        ts("dve", Pv, ef[a], ss_[:, 1:2], None, ALU.mult, None, [B_ef[a], Bs_], [B_Pb[g]])
                    else:
                        stt(Pv, ef[a], ss_[:, 1:2], Pv, ALU.mult, ALU.add, [B_ef[a], Bs_, B_Pb[g]], [B_Pb[g]])
                S.op("dve", lambda e_, g=g: e_.tensor_reduce(out=imp, in_=Pb[g][:, 0:512].rearrange("p (j s) -> p j s", s=4),
                                                          axis=AX.X, op=ALU.add), [B_Pb[g]], [B_imp])
                tt("dve", imp, imp, Pb[g][:, 4:516].rearrange("p (j s) -> p j s", s=4)[:, :, 0], ALU.add, [B_Pb[g], B_imp], [B_imp])
                ts("dve", nd, bsrow, tqs, None, ALU.subtract, None, [B_cst, B_small, B_imp], [B_imp])
                ts("dve", itmp, nd, -128.0, BIG, ALU.is_gt, ALU.mult, [B_imp], [B_imp])
                tt("dve", imp, imp, itmp, ALU.add, [B_imp], [B_imp])
                ts("dve", itmp, nd, 0.0, -3.0 * BIG, ALU.is_gt, ALU.mult, [B_imp], [B_imp])
                tt("dve", imp, imp, itmp, ALU.add, [B_imp], [B_imp])
                tt("dve", imp, imp, e0big, ALU.add, [B_imp, B_cst], [B_imp])
                S.op("dve", lambda e_: e_.max(out=mx[:, 0:8], in_=imp), [B_imp], [B_imp])
                S.op("dve", lambda e_: e_.match_replace(out=wk, in_to_replace=mx[:, 0:8], in_values=imp, imm_value=-1e30), [B_imp], [B_imp])
                S.op("dve", lambda e_: e_.max(out=mx[:, 8:16], in_=wk), [B_imp], [B_imp])
                ts("dve", selb, imp, mx[:, 15:16], None, ALU.is_ge, None, [B_imp], [B_sel])
                tr(PS[6][:, g * 128:(g + 1) * 128], selb, [B_sel], [PB[6]])
                cp("act", selT[g], PS[6][:, g * 128:(g + 1) * 128], [PB[6]], [B_selT[g]])
            for g in range(2):
                pr = slice(64 * g, 64 * g + 64)

                def mk_cmp(c):
                    return (None, lambda sl, bs: ts("dve", mk[bs], tqt, cendcol[:, c:c + 1], None, ALU.is_ge, None, [B_cst, B_tqt], [B_mk[bs]]))

                def mk_win(j, ee):
                    if 1 <= j <= 3:
                        return (None, None)
                    v = 0 if j == 0 else (2 if j == 4 else 1)
                    return (None, lambda sl, bs: ts("dve", mk[bs], Mw3[:, v, :], wval[:, ee:ee + 1], None, ALU.mult, None, [B_cst, B_small], [B_mk[bs]]))

                def mk_sel(cc, g=g):
                    def f_pe(sl):
                        mm(PS[6 + sl][:, 256:384], Gb[:, cc * 128:(cc + 1) * 128], selT[g], [B_G, B_selT[g]], [PB[6 + sl]])

                    def f_dve(sl, bs):
                        stt(mk[bs], tqt, kidx[:, cc:cc + 1], PS[6 + sl][:, 256:384], ALU.is_ge, ALU.mult,
                            [B_cst, B_tqt, PB[6 + sl]], [B_mk[bs]])
                    return (f_pe, f_dve)

                attend(e, g, 0, [(KcT[:, c * 128:(c + 1) * 128], Vc[:, c, g, :]) + mk_cmp(c) for c in range(4)])
                attend(e, g, 1, [(KTs[:, cc * 128:(cc + 1) * 128], Vs[:, cc, g, :]) + mk_sel(cc) for cc in range(48 + i)])
                attend(e, g, 2, [(KTw[:, (e - 4 + j) * 128:(e - 3 + j) * 128], Vw[:, e - 4 + j, g, :]) + mk_win(j, e - 4 + j)
                                 for j in range(5)])
            cp("act", ynb, ynsa, [B_yn], [B_ynb])
            for k in range(8):
                tr(PS[2 + k // 4][:, (k % 4) * 128:(k % 4 + 1) * 128], ynb[:, k * 128:(k + 1) * 128], [B_ynb], [PB[2 + k // 4]])
            cp("act", bt[:, 4:8, :], PS[2][:, :].rearrange("p (k t) -> p k t", k=4), [PB[2]], [B_br[s]])
            cp("dve", bt[:, 8:12, :], PS[3][:, :].rearrange("p (k t) -> p k t", k=4), [PB[3]], [B_br[s]])
            lastbr = dma(br_scr[i], bt.rearrange("p k t -> p (k t)"), [B_br[s]], [])
            if dbg == "B1":
                lastbr = dma(dbg_t(f"br{i}", [128, 24 * 128], BF16), bt.rearrange("p k t -> p (k t)"), [B_br[s]], [])
                fl1 = [lastbr, dma(dbg_t(f"ynsa{i}", [128, 1024]), ynsa, [B_yn], []),
                       dma(dbg_t(f"qb{i}", [128, 1024], BF16), qb, [B_qb], []),
                       dma(dbg_t(f"gn{i}", [128, 48]), gn, [B_gn], []),
                       dma(dbg_t(f"selT{i}", [128, 128], BF16), selT[1], [B_selT[1]], []),
                       dma(dbg_t(f"Pb{i}", [128, 516]), Pb[1], [B_Pb[1]], [])]
        if dbg == "B1":
            print(S.emit(final_waits=fl1))
            return nc
        barrier()
        top[0] = persist_top
        wg = abf(8 * 3072).rearrange("p (k c) -> p k c", k=8)
        wbp = abf(4 * 1024).rearrange("p (k c) -> p k c", k=4)
        wbn = abf(8 * 1024).rearrange("p (k c) -> p k c", k=8)
        wbx = abf(4 * 1024).rearrange("p (k c) -> p k c", k=4)
        wo = abf(8 * 1024).rearrange("p (k c) -> p k c", k=8)
        B_w2p = S.buf("w_b2")
        for k in range(8):
            loadw(lambda c0, n, k=k: wg[:, k, c0:c0 + n], w_in[0][k * 128:(k + 1) * 128, 2864:5936], 3072,
                  scale=gpre[:, k:k + 1], r_extra=[B_small], w=[B_w2p])
            loadw(lambda c0, n, k=k: wbn[:, k, c0:c0 + n], w_br_nsa[0][k * 128:(k + 1) * 128, :], 1024, w=[B_w2p])
            loadw(lambda c0, n, k=k: wo[:, k, c0:c0 + n], w_out[0][k * 128:(k + 1) * 128, :], 1024, w=[B_w2p])
            if k < 4:
                loadw(lambda c0, n, k=k: wbp[:, k, c0:c0 + n], w_br_pool[0][k * 128:(k + 1) * 128, :], 1024, w=[B_w2p])
                loadw(lambda c0, n, k=k: wbx[:, k, c0:c0 + n], w_br_xa[0][k * 128:(k + 1) * 128, :], 1024, w=[B_w2p])
        gpost = af32(1024)
        B_gp = S.buf("gpost")
        dma(gpost, post_mix_g.partition_broadcast(128), [], [B_gp])
        brT = [abf(24 * 128).rearrange("p (k t) -> p k t", k=24) for _ in range(2)]
        B_br = [S.buf("c_br0"), S.buf("c_br1")]
        xt2 = [af32(1024), af32(1024)]
        B_xt = [S.buf("c_xt0"), S.buf("c_xt1")]
        sg = af32(1024)
        B_sg = S.buf("sg")
        yy = af32(1024)
        B_y = S.buf("yy")
        ytm = af32(1024)
        B_ytm = S.buf("ytm")
        yb = abf(1024)
        B_yb = S.buf("yb")
        yT = abf(1024).rearrange("p (k t) -> p k t", k=8)
        B_yT = S.buf("yT")
        junk = abf(512)
        x1t = [af32(1024), af32(1024)]
        B_x1 = [S.buf("x1t0"), S.buf("x1t1")]

        def post_norm_res(pa, pb, gp, Bg, xres, Bxres, dst, Bdst):
            act(junk[:, 0:512], PS[pa][:, :], AF.Square, [PB[pa]], [B_ss2], accum=ss2[:, 0:1])
            act(junk[:, 0:512], PS[pb][:, :], AF.Square, [PB[pb]], [B_ss2], accum=ss2[:, 1:2])
            tt("dve", ss2[:, 2:3], ss2[:, 0:1], ss2[:, 1:2], ALU.add, [B_ss2], [B_ss2])
            ts("dve", ss2[:, 2:3], ss2[:, 2:3], 1.0 / 1024, 1e-6, ALU.mult, ALU.add, [B_ss2], [B_ss2])
            act(ss2[:, 2:3], ss2[:, 2:3], AF.Sqrt, [B_ss2], [B_ss2])
            S.op("dve", lambda e_: e_.reciprocal(out=ss2[:, 3:4], in_=ss2[:, 2:3]), [B_ss2], [B_ss2])
            for nb, bank in enumerate((pa, pb)):
                blk = slice(nb * 512, (nb + 1) * 512)
                stt(dst[:, blk], PS[bank][:, :], ss2[:, 3:4], gp[:, blk], ALU.mult, ALU.mult, [PB[bank], B_ss2, Bg], [Bdst])
                tt("pool", dst[:, blk], dst[:, blk], xres[:, blk], ALU.add, [Bdst, Bxres], [Bdst])

        dma(brT[0].rearrange("p k t -> p (k t)"), br_scr[0], [], [B_br[0]])
        dma(xt2[0], x_ext[4 * 128:5 * 128, :], [], [B_xt[0]])
        for i in range(17):
            s = i % 2
            e = i + 4
            if i + 1 < 17:
                dma(brT[1 - s].rearrange("p k t -> p (k t)"), br_scr[i + 1], [], [B_br[1 - s]])
                dma(xt2[1 - s], x_ext[(e + 1) * 128:(e + 2) * 128, :], [], [B_xt[1 - s]])
            bt = brT[s]
            for br in range(3):
                for nb in range(2):
                    for k in range(8):
                        mm(PS[nb][:, :], bt[:, 16 + k, :], wg[:, k, br * 1024 + nb * 512:br * 1024 + (nb + 1) * 512],
                           [B_br[s], B_w2p], [PB[nb]], start=(k == 0), stop=(k == 7))
                wsel, off, nk = ((wbp, 0, 4), (wbn, 4, 8), (wbx, 12, 4))[br]
                for nb in range(2):
                    for k in range(nk):
                        mm(PS[2 + nb][:, :], bt[:, off + k, :], wsel[:, k, nb * 512:(nb + 1) * 512],
                           [B_br[s], B_w2p], [PB[2 + nb]], start=(k == 0), stop=(k == nk - 1))
                for nb in range(2):
                    blk = slice(nb * 512, (nb + 1) * 512)
                    act(sg[:, blk], PS[nb][:, :], AF.Sigmoid, [PB[nb]], [B_sg])
                    if br == 0:
                        tt("dve", yy[:, blk], sg[:, blk], PS[2 + nb][:, :], ALU.mult, [B_sg, PB[2 + nb]], [B_y])
                    else:
                        tt("dve", ytm[:, blk], sg[:, blk], PS[2 + nb][:, :], ALU.mult, [B_sg, PB[2 + nb]], [B_ytm])
                        tt("pool", yy[:, blk], yy[:, blk], ytm[:, blk], ALU.add, [B_y, B_ytm], [B_y])
            cp("act", yb, yy, [B_y], [B_yb])
            for k in range(8):
                tr(PS[4 + k // 4][:, (k % 4) * 128:(k % 4 + 1) * 128], yb[:, k * 128:(k + 1) * 128], [B_yb], [PB[4 + k // 4]])
            cp("act", yT[:, 0:4, :], PS[4][:, :].rearrange("p (k t) -> p k t", k=4), [PB[4]], [B_yT])
            cp("dve", yT[:, 4:8, :], PS[5][:, :].rearrange("p (k t) -> p k t", k=4), [PB[5]], [B_yT])
            for nb in range(2):
                for k in range(8):
                    mm(PS[6 + nb][:, :], yT[:, k, :], wo[:, k, nb * 512:(nb + 1) * 512], [B_yT, B_w2p], [PB[6 + nb]],
                       start=(k == 0), stop=(k == 7))
            post_norm_res(6, 7, gpost, B_gp, xt2[s], B_xt[s], x1t[s], B_x1[s])
            lx = dma(x1_scr[i], x1t[s], [B_x1[s]], [])
            if dbg == "B2":
                lx = dma(dbg_t(f"x1_{i}", [128, 1024]), x1t[s], [B_x1[s]], [])
        if dbg == "B2":
            print(S.emit(final_waits=[lx]))
            return nc
        barrier()
        top[0] = persist_top

        wup = abf(8 * 5632).rearrange("p (k c) -> p k c", k=8)
        wdn = abf(22 * 1024).rearrange("p (k c) -> p k c", k=22)
        B_w3 = S.buf("w_c")
        for k in range(8):
            loadw(lambda c0, n, k=k: wup[:, k, c0:c0 + n], w_up[0][k * 128:(k + 1) * 128, :], 5632,
                  scale=gffn[:, k:k + 1], r_extra=[B_small], w=[B_w3])
        for k in range(22):
            loadw(lambda c0, n, k=k: wdn[:, k, c0:c0 + n], w_down[0][k * 128:(k + 1) * 128, :], 1024, w=[B_w3])
        convp = af32(44 * 4).rearrange("p (j c) -> p j c", c=4)
        B_cv = S.buf("convp")
        for kk in range(3):
            dma(convp[:, :, kk], conv_w[0][kk].rearrange("(j p) -> p j", p=128), [], [B_cv], slow=True)
        dma(convp[:, :, 3], conv_b[0].rearrange("(j p) -> p j", p=128), [], [B_cv], slow=True)
        gpost2 = af32(1024)
        B_gp2 = S.buf("gpost2")
        dma(gpost2, post_ffn_g.partition_broadcast(128), [], [B_gp2])
        xt2 = [af32(1024), af32(1024)]
        B_xt = [S.buf("d_xt0"), S.buf("d_xt1")]
        xn = abf(1024)
        B_xn = S.buf("d_xn")
        junk = abf(1024)
        ssA = af32(2)
        B_ss = S.buf("d_ss")
        h2Ts = [abf(8 * 130).rearrange("p (k t) -> p k t", k=8) for _ in range(2)]
        B_h2s = [S.buf("h2T0"), S.buf("h2T1")]
        xnC = [xn, abf(1024)]
        B_xnC = [B_xn, S.buf("xnC1")]
        aT = abf(22 * 128).rearrange("p (k t) -> p k t", k=22)
        B_aT = S.buf("aT")
        cg = [af32(128), af32(128)]
        cv = [af32(128), af32(128)]
        gl = [af32(128), af32(128)]
        B_cg = [S.buf("cg0"), S.buf("cg1")]
        B_cvb = [S.buf("cv0"), S.buf("cv1")]
        B_gl = [S.buf("gl0"), S.buf("gl1")]
        ot = [af32(1024), af32(1024)]
        B_ot = [S.buf("ot0"), S.buf("ot1")]
        xt3 = [xt2[0], xt2[1], af32(1024)]
        B_xt3 = [B_xt[0], B_xt[1], S.buf("d_xt2")]
        fins = []

        def c_stage1(i):
            s_ = i % 2
            x3 = i % 3
            rms_scale(xt3[x3], xnC[s_], B_xt3[x3], B_xnC[s_], ssA[:, 0:1], junk, B_ss)
            for k in range(8):
                bank = k // 4
                tr(PS[bank][:, (k % 4) * 128:(k % 4 + 1) * 128], xnC[s_][:, k * 128:(k + 1) * 128], [B_xnC[s_]], [PB[bank]])
            cp("act", h2Ts[s_][:, 0:4, 2:130], PS[0][:, :].rearrange("p (k t) -> p k t", k=4), [PB[0]], [B_h2s[s_]])
            cp("dve", h2Ts[s_][:, 4:8, 2:130], PS[1][:, :].rearrange("p (k t) -> p k t", k=4), [PB[1]], [B_h2s[s_]])
            if i == 1:
                ts("pool", h2Ts[s_][:, :, 0:2], h2Ts[1 - s_][:, :, 128:130], hflag[:, 0:1], None, ALU.mult, None,
                   [B_h2s[1 - s_], B_small, B_h2s[s_]], [B_h2s[s_]])
            elif i > 1:
                cp("pool", h2Ts[s_][:, :, 0:2], h2Ts[1 - s_][:, :, 128:130], [B_h2s[1 - s_], B_h2s[s_]], [B_h2s[s_]])

        dma(xt3[0], x1_scr[0], [], [B_xt3[0]])
        dma(xt3[1], x1_scr[1], [], [B_xt3[1]])
        dma(xt3[2], x1_scr[2], [], [B_xt3[2]])
        c_stage1(0)
        for i in range(17):
            s = i % 2
            if i + 1 < 17:
                c_stage1(i + 1)
            if i == 0:
                dma(xt3[0], x1_scr[3], [], [B_xt3[0]])
                continue
            h2T, B_h2 = h2Ts[s], B_h2s[s]
            for j in range(22):
                a = j % 2
                bg, bv = 2 + 2 * a, 3 + 2 * a
                for k in range(8):
                    mm(PS[bg][:, 0:130], wup[:, k, j * 128:(j + 1) * 128], h2T[:, k, :], [B_w3, B_h2], [PB[bg]],
                       start=(k == 0), stop=(k == 7))
                for k in range(8):
                    mm(PS[bv][:, 0:130], wup[:, k, (22 + j) * 128:(23 + j) * 128], h2T[:, k, :], [B_w3, B_h2], [PB[bv]],
                       start=(k == 0), stop=(k == 7))
                for (bank, dstc, Bd, jj) in ((bg, cg[a], B_cg[a], j), (bv, cv[a], B_cvb[a], 22 + j)):
                    act(dstc, PS[bank][:, 2:130], AF.Identity, [PB[bank], B_cv], [Bd], bias=convp[:, jj, 3:4], scale=convp[:, jj, 2:3])
                    stt(dstc, PS[bank][:, 1:129], convp[:, jj, 1:2], dstc, ALU.mult, ALU.add, [PB[bank], B_cv, Bd], [Bd])
                    stt(dstc, PS[bank][:, 0:128], convp[:, jj, 0:1], dstc, ALU.mult, ALU.add, [PB[bank], B_cv, Bd], [Bd])
                act(gl[a], cg[a], AF.Gelu_apprx_tanh, [B_cg[a]], [B_gl[a]])
                tt("pool", aT[:, j, :], gl[a], cv[a], ALU.mult, [B_gl[a], B_cvb[a]], [B_aT])
            for nb in range(2):
                for j in range(22):
                    mm(PS[6 + nb][:, :], aT[:, j, :], wdn[:, j, nb * 512:(nb + 1) * 512], [B_aT, B_w3], [PB[6 + nb]],
                       start=(j == 0), stop=(j == 21))
            post_norm_res(6, 7, gpost2, B_gp2, xt3[i % 3], B_xt3[i % 3], ot[s], B_ot[s])
            fins.append(dma(out_d[(i - 1) * 128:i * 128, :], ot[s], [B_ot[s]], []))
            if i + 3 < 17:
                dma(xt3[i % 3], x1_scr[i + 3], [], [B_xt3[i % 3]])
        stats = S.emit(final_waits=fins)
        print("emit stats", stats, flush=True)
    return nc


def _consts():
    c = {}
    c["ident"] = np.eye(128, dtype=np.float32)
    k = np.arange(8192)
    c["Gm"] = (k[None, :] // 64 == np.arange(128)[:, None]).astype(np.float32)
    c["invf"] = (np.float32(500000.0) ** (-np.arange(8, dtype=np.float32) * np.float32(2.0 / 16))).astype(np.float32).reshape(1, 8)
    c["bsrow"] = (64.0 * np.arange(128, dtype=np.float32)).reshape(1, 128)
    ce = (16.0 * np.arange(512, dtype=np.float32) + 31.0)
    ce[511] = 1e9
    c["cendrow"] = ce.reshape(1, 512)
    c["kidx"] = (128.0 * np.arange(64)[None, :] + np.arange(128)[:, None]).astype(np.float32)
    c["cendcol"] = np.ascontiguousarray(ce.reshape(4, 128).T)
    p = np.arange(128)[:, None]
    q = np.arange(128)[None, :]
    c["Mw"] = np.concatenate([(q < p), (q >= p)], axis=1).astype(np.float32)
    e0 = np.zeros((1, 128), np.float32)
    e0[0, 0] = BIG
    c["e0big"] = e0
    A = np.zeros((128, 4, 2, 128), np.float32)
    for gi, w in enumerate((2, 4, 8, 16)):
        tp = np.arange(128)[:, None]
        t = np.arange(128)[None, :]
        A[:, gi, 0, :] = ((t - tp >= 0) & (t - tp < w))
        A[:, gi, 1, :] = ((t + 128 - tp >= 0) & (t + 128 - tp < w))
    c["Aband"] = A.reshape(128, 1024)
    c["qrow"] = np.arange(128, dtype=np.float32).reshape(1, 128)
    return c


_PROG = {}


def kernel(**inputs):
    x = np.asarray(inputs["x"], dtype=np.float32)
    mem = np.asarray(inputs["mem"], dtype=np.float32)
    positions = np.asarray(inputs["positions"]).astype(np.int32)
    if inputs.get("_return_maps"):
        nc = None
    else:
        if "nc" not in _PROG:
            _PROG["nc"] = build()
        nc = _PROG["nc"]
    consts = _consts()
    wnames = ["pre_mix_g", "w_in", "pool_w", "pool_scale", "cmp_pe", "cmp_w1", "cmp_w2", "mem_norm_g", "w_mem_kv",
              "w_br_pool", "w_br_nsa", "w_br_xa", "w_out", "post_mix_g", "pre_ffn_g", "w_up", "conv_w", "conv_b",
              "w_down", "post_ffn_g"]
    shared = {n: np.ascontiguousarray(np.asarray(inputs[n], dtype=np.float32)) for n in wnames}
    in_maps = []
    for core in range(8):
        b, r = core // 4, core % 4
        m = dict(shared)
        m.update(consts)
        m["x_all"] = np.ascontiguousarray(x[b])
        m["mem"] = np.ascontiguousarray(mem[b])
        xe = np.zeros((NEXT * 128, 1024), np.float32)
        pe = np.zeros((NEXT, 128), np.int32)
        tq = np.zeros((NEXT, 128), np.float32)
        wv = np.zeros((1, NEXT), np.float32)
        ic = np.ones((128, NEXT, 4), np.float32)
        tst = np.zeros((1, NEXT), np.float32)
        for e in range(NEXT):
            ge = 16 * r - 5 + e
            tq[e] = ge * 128 + np.arange(128)
            tst[0, e] = ge * 128
            if ge >= 0:
                xe[e * 128:(e + 1) * 128] = x[b, ge * 128:(ge + 1) * 128]
                pe[e] = positions[b, ge * 128:(ge + 1) * 128]
                wv[0, e] = 1.0
                t = ge * 128 + np.arange(128)
                for gi, w in enumerate((2, 4, 8, 16)):
                    ic[:, e, gi] = 1.0 / np.minimum(t + 1, w).astype(np.float32)
        m["x_ext"] = xe
        m["pos_ext"] = np.ascontiguousarray(pe.T)
        m["pos_all"] = np.ascontiguousarray(positions[b].reshape(NALL, 128).T)
        m["tq_ext"] = np.ascontiguousarray(tq.T)
        m["tqstart"] = tst
        m["wvalid"] = wv
        m["invcnt"] = np.ascontiguousarray(ic.reshape(128, NEXT * 4))
        m["hflag"] = np.array([[0.0 if r == 0 else 1.0]], np.float32)
        in_maps.append(m)
    if inputs.get("_return_maps"):
        return in_maps
    res = run_bass_kernel_spmd(nc, in_maps, core_ids=list(range(8)))
    out = np.zeros((2, 8192, 1024), np.float32)
    for core in range(8):
        b, r = core // 4, core % 4
        out[b, r * 2048:(r + 1) * 2048] = res.results[core]["out"]
    return out
```
